# Optimizing a Trainium2 kernel written in Bass

```python
import jax, jax.numpy as jnp
from jax import lax
import numpy as np

D_MODEL = 2048
BATCH = 16
SEQ = 2048
DEPTH = 4

D_MIX = D_MODEL
A_WIDTH = D_MIX // 2
B_WIDTH = D_MIX // 4
C_WIDTH = D_MIX - A_WIDTH - B_WIDTH

V_DIM = 128
NOPE_DIM = 128
ROPE_DIM = 64
A_HEADS = A_WIDTH // V_DIM
Q_LORA = D_MODEL // 4
KV_LORA = D_MODEL // 8
ROPE_THETA = 10000.0
Q_BLOCK = 128

B_HEAD = 64
B_HEADS = B_WIDTH // B_HEAD
DECAY_LORA = 32
ICLR_LORA = 32
GATE_LORA = 96
HEAD_NORM_EPS = 64e-5

C_GROUPS = 8
CONV_K = 3

D_FF = ((8 * D_MODEL // 3 + 255) // 256) * 256
RMS_EPS = 1e-6

A_IN = Q_LORA + KV_LORA + ROPE_DIM
B_IN = 3 * B_WIDTH + DECAY_LORA + ICLR_LORA + GATE_LORA
C_IN = 3 * C_WIDTH
N_IN = A_IN + B_IN + C_IN

kernel_name = "hymba_style_mla_rwkv7_shortconv_macaron"


def rmsnorm(x, g, eps=RMS_EPS):
    xf = x.astype(jnp.float32)
    y = xf * lax.rsqrt(jnp.mean(xf * xf, axis=-1, keepdims=True) + eps)
    return (y * g.astype(jnp.float32)).astype(x.dtype)


def group_rmsnorm(y, g, n_groups, eps=RMS_EPS):
    shp = y.shape
    yg = y.reshape(shp[:-1] + (n_groups, shp[-1] // n_groups)).astype(jnp.float32)
    yg = yg * lax.rsqrt(jnp.mean(yg * yg, axis=-1, keepdims=True) + eps)
    return (yg.reshape(shp) * g.astype(jnp.float32)).astype(y.dtype)


def swiglu(x, w_gate, w_up, w_down):
    return (jax.nn.silu(x @ w_gate) * (x @ w_up)) @ w_down


def rope_angles(positions):
    inv_freq = 1.0 / (ROPE_THETA ** (jnp.arange(0, ROPE_DIM, 2, dtype=jnp.float32) / ROPE_DIM))
    ang = positions.astype(jnp.float32)[..., None] * inv_freq
    return jnp.cos(ang), jnp.sin(ang)


def apply_rope(x, cos, sin):
    half = ROPE_DIM // 2
    xf = x.astype(jnp.float32)
    x1, x2 = xf[..., :half], xf[..., half:]
    return jnp.concatenate([x1 * cos - x2 * sin, x1 * sin + x2 * cos], axis=-1).astype(x.dtype)


def causal_block_attention(q_nope, q_rope, k_nope, k_rope, v):
    B, S, H, _ = q_nope.shape
    scale = (NOPE_DIM + ROPE_DIM) ** -0.5
    kpos = jnp.arange(S)

    def one_block(i):
        start = i * Q_BLOCK
        qn = lax.dynamic_slice_in_dim(q_nope, start, Q_BLOCK, axis=1)
        qr = lax.dynamic_slice_in_dim(q_rope, start, Q_BLOCK, axis=1)
        s = (jnp.einsum('bqhd,bkhd->bhqk', qn, k_nope)
             + jnp.einsum('bqhr,bkr->bhqk', qr, k_rope)).astype(jnp.float32) * scale
        qpos = start + jnp.arange(Q_BLOCK)
        mask = kpos[None, :] <= qpos[:, None]
        s = jnp.where(mask[None, None], s, -1e30)
        p = jax.nn.softmax(s, axis=-1).astype(v.dtype)
        return jnp.einsum('bhqk,bkhd->bqhd', p, v)

    out = lax.map(one_block, jnp.arange(S // Q_BLOCK))
    return jnp.moveaxis(out, 0, 1).reshape(B, S, H, v.shape[-1])


def mla_mixer(pa, cos, sin, q_norm, kv_norm, w_uq, w_ukv, out_norm):
    B, S, _ = pa.shape
    c_q = pa[..., :Q_LORA]
    c_kv = pa[..., Q_LORA:Q_LORA + KV_LORA]
    k_rope = pa[..., Q_LORA + KV_LORA:]
    q = (rmsnorm(c_q, q_norm) @ w_uq).reshape(B, S, A_HEADS, NOPE_DIM + ROPE_DIM)
    kv = (rmsnorm(c_kv, kv_norm) @ w_ukv).reshape(B, S, A_HEADS, NOPE_DIM + V_DIM)
    q_nope = q[..., :NOPE_DIM]
    q_rope = apply_rope(q[..., NOPE_DIM:], cos[:, :, None, :], sin[:, :, None, :])
    k_nope, v = kv[..., :NOPE_DIM], kv[..., NOPE_DIM:]
    k_rope = apply_rope(k_rope, cos, sin)
    o = causal_block_attention(q_nope, q_rope, k_nope, k_rope, v)
    return group_rmsnorm(o.reshape(B, S, A_WIDTH), out_norm, A_HEADS)


def wkv7_scan(r, w, k, v, a, b):
    B, S, H, N = r.shape

    def step(state, inp):
        r_t, w_t, k_t, v_t, a_t, b_t = inp
        sa = jnp.einsum('bhvk,bhk->bhv', state, a_t)
        state = (state * w_t[:, :, None, :]
                 + sa[..., None] * b_t[:, :, None, :]
                 + v_t[..., None] * k_t[:, :, None, :])
        return state, jnp.einsum('bhvk,bhk->bhv', state, r_t)

    xs = tuple(jnp.moveaxis(t, 1, 0) for t in (r, w, k, v, a, b))
    _, y = lax.scan(step, jnp.zeros((B, H, N, N), jnp.float32), xs)
    return jnp.moveaxis(y, 0, 1)


def rwkv7_mixer(pb, shift_mu, decay_w0, decay_up, iclr_a0, iclr_up, gate_up,
                k_k, k_a, r_k, lnx_gain, lnx_bias):
    B, S, _ = pb.shape
    prev = jnp.pad(pb, ((0, 0), (1, 0), (0, 0)))[:, :-1]
    pb = pb + (prev - pb) * shift_mu
    o1, o2, o3 = B_WIDTH, 2 * B_WIDTH, 3 * B_WIDTH
    o4, o5 = o3 + DECAY_LORA, o3 + DECAY_LORA + ICLR_LORA
    r, k, v = pb[..., :o1], pb[..., o1:o2], pb[..., o2:o3]
    w_lo, a_lo, g_lo = pb[..., o3:o4], pb[..., o4:o5], pb[..., o5:]

    w_log = -jax.nn.softplus(-(decay_w0 + jnp.tanh(w_lo) @ decay_up)) - 0.5
    a = jax.nn.sigmoid(iclr_a0 + a_lo @ iclr_up)
    g = jax.nn.sigmoid(g_lo) @ gate_up

    heads = lambda t: t.reshape(B, S, B_HEADS, B_HEAD).astype(jnp.float32)
    kk = heads(k * k_k)
    kk = kk / jnp.maximum(jnp.sqrt(jnp.sum(kk * kk, axis=-1, keepdims=True)), 1e-12)
    k = k * (1.0 + (a - 1.0) * k_a)
    rh, kh, vh, ah = heads(r), heads(k), heads(v), heads(a)
    decay = jnp.exp(-jnp.exp(heads(w_log)))

    y = wkv7_scan(rh, decay, kh, vh, -kk, kk * ah)
    mu = jnp.mean(y, axis=-1, keepdims=True)
    var = jnp.mean(jnp.square(y - mu), axis=-1, keepdims=True)
    yn = ((y - mu) * lax.rsqrt(var + HEAD_NORM_EPS)).reshape(B, S, B_WIDTH)
    yn = yn * lnx_gain.astype(jnp.float32) + lnx_bias.astype(jnp.float32)
    bonus = jnp.sum(rh * kh * r_k.astype(jnp.float32), axis=-1, keepdims=True) * vh
    out = (yn + bonus.reshape(B, S, B_WIDTH)) * g.astype(jnp.float32)
    return out.astype(pb.dtype)


def short_conv_mixer(pc, conv_w, out_norm):
    b_gate = pc[..., :C_WIDTH]
    c_gate = pc[..., C_WIDTH:2 * C_WIDTH]
    h = pc[..., 2 * C_WIDTH:]
    u = c_gate * h
    y = lax.conv_general_dilated(u, conv_w[:, None, :], window_strides=(1,),
                                 padding=[(CONV_K - 1, 0)],
                                 dimension_numbers=('NWC', 'WIO', 'NWC'),
                                 feature_group_count=C_WIDTH)
    return group_rmsnorm(b_gate * y, out_norm, C_GROUPS)


def setup_inputs(seed: int = 0) -> dict:
    key = jax.random.key(seed)
    ks = iter(jax.random.split(key, 40))
    f32 = jnp.float32

    def nrm(shape, scale):
        return scale * jax.random.normal(next(ks), shape, f32)

    def gain(shape):
        return 1.0 + nrm(shape, 0.02)

    L = DEPTH
    x = nrm((BATCH, SEQ, D_MODEL), 1.0)
    positions = (jax.random.randint(next(ks), (BATCH, 1), 0, 1024, jnp.int32)
                 + jnp.arange(SEQ, dtype=jnp.int32)[None, :])
    return {
        "x": x,
        "positions": positions,
        "norm_ffn1": gain((L, D_MODEL)),
        "ffn1_gate": nrm((L, D_MODEL, D_FF), D_MODEL ** -0.5),
        "ffn1_up": nrm((L, D_MODEL, D_FF), D_MODEL ** -0.5),
        "ffn1_down": nrm((L, D_FF, D_MODEL), D_FF ** -0.5),
        "norm_mix": gain((L, D_MODEL)),
        "w_in": nrm((L, D_MODEL, N_IN), D_MODEL ** -0.5),
        "q_norm": gain((L, Q_LORA)),
        "kv_norm": gain((L, KV_LORA)),
        "w_uq": nrm((L, Q_LORA, A_HEADS * (NOPE_DIM + ROPE_DIM)), Q_LORA ** -0.5),
        "w_ukv": nrm((L, KV_LORA, A_HEADS * (NOPE_DIM + V_DIM)), KV_LORA ** -0.5),
        "attn_out_norm": gain((L, A_WIDTH)),
        "shift_mu": jax.random.uniform(next(ks), (L, B_IN), f32),
        "decay_w0": -2.0 + nrm((L, B_WIDTH), 0.5),
        "decay_up": nrm((L, DECAY_LORA, B_WIDTH), 0.5 * DECAY_LORA ** -0.5),
        "iclr_a0": nrm((L, B_WIDTH), 0.1),
        "iclr_up": nrm((L, ICLR_LORA, B_WIDTH), ICLR_LORA ** -0.5),
        "gate_up": nrm((L, GATE_LORA, B_WIDTH), GATE_LORA ** -0.5),
        "k_k": 0.85 + nrm((L, B_WIDTH), 0.02),
        "k_a": gain((L, B_WIDTH)),
        "r_k": nrm((L, B_HEADS, B_HEAD), 0.1),
        "lnx_gain": gain((L, B_WIDTH)),
        "lnx_bias": nrm((L, B_WIDTH), 0.01),
        "conv_w": nrm((L, CONV_K, C_WIDTH), CONV_K ** -0.5),
        "conv_out_norm": gain((L, C_WIDTH)),
        "w_out": nrm((L, D_MIX, D_MODEL), D_MIX ** -0.5),
        "norm_ffn2": gain((L, D_MODEL)),
        "ffn2_gate": nrm((L, D_MODEL, D_FF), D_MODEL ** -0.5),
        "ffn2_up": nrm((L, D_MODEL, D_FF), D_MODEL ** -0.5),
        "ffn2_down": nrm((L, D_FF, D_MODEL), D_FF ** -0.5),
        "norm_final": gain((D_MODEL,)),
    }


def reference(x, positions, norm_ffn1, ffn1_gate, ffn1_up, ffn1_down, norm_mix, w_in,
              q_norm, kv_norm, w_uq, w_ukv, attn_out_norm, shift_mu, decay_w0, decay_up,
              iclr_a0, iclr_up, gate_up, k_k, k_a, r_k, lnx_gain, lnx_bias, conv_w,
              conv_out_norm, w_out, norm_ffn2, ffn2_gate, ffn2_up, ffn2_down, norm_final):
    cos, sin = rope_angles(positions)
    h = x
    for l in range(DEPTH):
        h = h + 0.5 * swiglu(rmsnorm(h, norm_ffn1[l]), ffn1_gate[l], ffn1_up[l], ffn1_down[l])
        p = rmsnorm(h, norm_mix[l]) @ w_in[l]
        pa, pb, pc = p[..., :A_IN], p[..., A_IN:A_IN + B_IN], p[..., A_IN + B_IN:]
        ya = mla_mixer(pa, cos, sin, q_norm[l], kv_norm[l], w_uq[l], w_ukv[l], attn_out_norm[l])
        yb = rwkv7_mixer(pb, shift_mu[l], decay_w0[l], decay_up[l], iclr_a0[l], iclr_up[l],
                         gate_up[l], k_k[l], k_a[l], r_k[l], lnx_gain[l], lnx_bias[l])
        yc = short_conv_mixer(pc, conv_w[l], conv_out_norm[l])
        h = h + jnp.concatenate([ya, yb, yc], axis=-1) @ w_out[l]
        h = h + 0.5 * swiglu(rmsnorm(h, norm_ffn2[l]), ffn2_gate[l], ffn2_up[l], ffn2_down[l])
    return rmsnorm(h, norm_final)
```

```python
import contextlib
import numpy as np
import concourse.bass as bass
import concourse.mybir as mybir
from concourse.bass_utils import run_bass_kernel_spmd

F32 = mybir.dt.float32
BF16 = mybir.dt.bfloat16
I32 = mybir.dt.int32
AF = mybir.ActivationFunctionType
ALU = mybir.AluOpType
AX = mybir.AxisListType

D = 2048
DFF = 5632
S = 2048
NSEQ = 2
NT = NSEQ * S
DEPTH = 4
KC = D // 128
FC = DFF // 128
EPS = 1e-6
NIN = 4064
NINX = 4128


class Sem:
    def __init__(self, h):
        self.h = h
        self.n = 0


class Buf:
    def __init__(self, cx, t, dma=False):
        self.t = t
        self.wr = None
        self.rd = []
        self.sem = cx.new_sem() if dma else None

    def __getitem__(self, k):
        return self.t[k]


class Ctx:
    ENG = ('pe', 'act', 'dve', 'pool', 'sp')

    def __init__(self, nc, sem_handles):
        self.nc = nc
        self.sems = [Sem(h) for h in sem_handles]
        self.free = list(self.sems)
        self.esem = {e: self.new_sem() for e in ('pe', 'act', 'dve', 'pool')}
        self.q = {e: [] for e in self.ENG}
        self.seen = {e: {} for e in self.ENG}
        self.stage_sems = []
        self.pending_dma = {e: [] for e in self.ENG}

    def new_sem(self):
        return self.free.pop()

    def begin_stage(self):
        self.mark = len(self.free)
        self.taken = []

    def buf(self, t, dma=False):
        b = Buf.__new__(Buf)
        b.t = t
        b.wr = None
        b.rd = []
        b.sem = None
        if dma:
            b.sem = self.free.pop()
            self.taken.append(b.sem)
        return b

    def end_stage(self, block):
        for e in self.ENG:
            for tok in self.pending_dma[e]:
                self._wait(e, tok)
            self.pending_dma[e] = []
        self.flush(block)
        self.free.extend(self.taken)
        self.taken = []

    def _wait(self, eng, tok):
        if tok is None:
            return
        sem, val, src = tok
        if src == eng and src in ('pe', 'act', 'dve'):
            return
        if self.seen[eng].get(id(sem), 0) >= val:
            return
        self.seen[eng][id(sem)] = val
        self.q[eng].append(('wait', sem, val))

    def _deps(self, eng, reads, writes):
        best = {}
        toks = [b.wr for b in reads] + [b.wr for b in writes]
        for b in writes:
            toks.extend(b.rd)
        for t in toks:
            if t is None:
                continue
            k = id(t[0])
            if k not in best or best[k][1] < t[1]:
                best[k] = t
        for t in best.values():
            self._wait(eng, t)

    def op(self, eng, fn, reads=(), writes=(), sig=True):
        self._deps(eng, reads, writes)
        tok = None
        if sig:
            s = self.esem[eng]
            s.n += 1
            tok = (s, s.n, eng)
            self.q[eng].append(('op', fn, s, 1))
        else:
            self.q[eng].append(('op', fn, None, 0))
        for b in writes:
            b.wr = tok
            b.rd = []
        for b in reads:
            if tok is not None:
                b.rd = [t for t in b.rd if t[0] is not tok[0]] + [tok]
        return tok

    def dma(self, eng, fn, reads=(), writes=(), sembuf=None):
        self._deps(eng, reads, writes)
        s = sembuf.sem
        s.n += 16
        tok = (s, s.n, 'dma')
        self.q[eng].append(('op', fn, s, 16))
        for b in writes:
            b.wr = tok
            b.rd = []
        for b in reads:
            b.rd = [t for t in b.rd if t[0] is not tok[0]] + [tok]
        if not writes:
            self.pending_dma[eng].append(tok)
        return tok

    def flush(self, block):
        m = {'pe': block.tensor, 'act': block.scalar, 'dve': block.vector,
             'pool': block.gpsimd, 'sp': block.sync}
        for e in self.ENG:
            lst = self.q[e]
            if not lst:
                continue

            def body(eng, lst=lst):
                for it in lst:
                    if it[0] == 'wait':
                        eng.wait_ge(it[1].h, it[2])
                    else:
                        ins = it[1](eng)
                        if it[2] is not None:
                            ins.then_inc(it[2].h, it[3])
            m[e](body)
            self.q[e] = []


class Ring:
    def __init__(self, bufs):
        self.bufs = bufs
        self.i = 0

    def next(self):
        b = self.bufs[self.i % len(self.bufs)]
        self.i += 1
        return b


def mm_group(cx, out_buf, pairs, reads, out_ap=None):
    nc = cx.nc
    oap = out_buf.t[:] if out_ap is None else out_ap
    n = len(pairs)

    def fn(eng):
        ins = None
        for i, (l, r) in enumerate(pairs):
            ins = nc.tensor.matmul(oap, l, r, start=(i == 0), stop=(i == n - 1))
        return ins
    return cx.op('pe', fn, reads=reads, writes=[out_buf])


_UID = [0]


def alloc(es, nc, name, shape, dt, psum=False):
    _UID[0] += 1
    name = f"{name}_{_UID[0]}"
    if psum:
        return es.enter_context(nc.psum_tensor(name, shape, dt))
    return es.enter_context(nc.sbuf_tensor(name, shape, dt))


def stage_in(cx, x, hT, ident_dram):
    nc = cx.nc
    with contextlib.ExitStack() as es:
        cx.begin_stage()
        ident = cx.buf(alloc(es, nc, "ident", [128, 128], F32), dma=True)
        xin = Ring([cx.buf(alloc(es, nc, f"xin{i}", [128, D], F32), dma=True) for i in range(3)])
        xo = Ring([cx.buf(alloc(es, nc, f"xo{i}", [128, KC, 128], F32), dma=True) for i in range(3)])
        ps = Ring([cx.buf(alloc(es, nc, f"ps{i}", [128, 512], F32, psum=True)) for i in range(4)])
        blk = es.enter_context(nc.Block())
        cx.dma('sp', lambda e: e.dma_start(out=ident.t[:], in_=ident_dram[:, :]), writes=[ident], sembuf=ident)
        for tt in range(NT // 128):
            xb = xin.next()
            cx.dma('sp', lambda e, xb=xb, tt=tt: e.dma_start(out=xb.t[:], in_=x[tt * 128:(tt + 1) * 128, :]),
                   writes=[xb], sembuf=xb)
            ob = xo.next()
            for g in range(KC // 4):
                p = ps.next()

                def fn(eng, p=p, xb=xb, g=g):
                    ins = None
                    for j in range(4):
                        kc = g * 4 + j
                        ins = nc.tensor.transpose(p.t[:, j * 128:(j + 1) * 128], xb.t[:, kc * 128:(kc + 1) * 128], ident.t[:])
                    return ins
                cx.op('pe', fn, reads=[xb, ident], writes=[p])
                eng = 'dve' if g % 2 == 0 else 'act'
                if eng == 'dve':
                    cx.op('dve', lambda e, p=p, ob=ob, g=g: nc.vector.tensor_copy(
                        ob.t[:, g * 4:(g + 1) * 4, :], p.t[:].rearrange("p (a b) -> p a b", a=4)), reads=[p], writes=[ob])
                else:
                    cx.op('act', lambda e, p=p, ob=ob, g=g: nc.scalar.copy(
                        ob.t[:, g * 4:(g + 1) * 4, :], p.t[:].rearrange("p (a b) -> p a b", a=4)), reads=[p], writes=[ob])
            cx.dma('sp', lambda e, ob=ob, tt=tt: e.dma_start(
                out=hT[:, :, tt * 128:(tt + 1) * 128].rearrange("k p t -> p k t"), in_=ob.t[:]),
                reads=[ob], sembuf=ob)
        cx.end_stage(blk)


def rms_stats(cx, nc, chunks_fn, nchunks, T, hring, sqring, ps_ss, ones_f, nfeat, eps_t, sd, rstd, src_rows=128):
    nsub = T // 512
    for kc in range(nchunks):
        hb = hring.next()
        cx.dma('sp', chunks_fn(kc, hb), writes=[hb], sembuf=hb)
        sq = sqring.next()
        cx.op('act', lambda e, hb=hb, sq=sq: nc.scalar.activation(out=sq.t[:src_rows, :T], in_=hb.t[:src_rows, :T], func=AF.Square),
              reads=[hb], writes=[sq])
        for sub in range(nsub):
            p = ps_ss[sub]

            def fn(eng, p=p, sq=sq, sub=sub, kc=kc):
                return nc.tensor.matmul(p.t[:], ones_f.t[:src_rows, :], sq.t[:src_rows, sub * 512:(sub + 1) * 512],
                                        start=(kc == 0), stop=(kc == nchunks - 1))
            cx.op('pe', fn, reads=[sq, ones_f], writes=[p] if kc == 0 else [], sig=True)
            if kc != 0:
                pass
        if kc == nchunks - 1:
            last_tok = (cx.esem['pe'], cx.esem['pe'].n, 'pe')
            for sub in range(nsub):
                ps_ss[sub].wr = last_tok
    for sub in range(nsub):
        p = ps_ss[sub]
        cx.op('act', lambda e, p=p, sub=sub: nc.scalar.activation(
            out=sd.t[:, sub * 512:(sub + 1) * 512], in_=p.t[:], func=AF.Sqrt, bias=eps_t.t[:, 0:1], scale=1.0 / nfeat),
            reads=[p, eps_t], writes=[sd] if sub == 0 else [])
    sd.wr = (cx.esem['act'], cx.esem['act'].n, 'act')
    cx.op('dve', lambda e: nc.vector.reciprocal(rstd.t[:, :T], sd.t[:, :T]), reads=[sd], writes=[rstd])


def stage_ffn(cx, hT, wg, wu, wd, gvec, consts, T=1024, tiles=None):
    nc = cx.nc
    nsub = T // 512
    NJ = 256
    with contextlib.ExitStack() as es:
        cx.begin_stage()
        ones_f = cx.buf(alloc(es, nc, "ones_f", [128, 128], F32), dma=True)
        eps_t = cx.buf(alloc(es, nc, "eps_t", [128, 1], F32), dma=True)
        g_t = cx.buf(alloc(es, nc, "g_t", [128, KC], F32), dma=True)
        xn = cx.buf(alloc(es, nc, "xn", [128, KC, T], BF16))
        act = [cx.buf(alloc(es, nc, f"act{j}", [128, T], BF16)) for j in range(FC)]
        wring = Ring([cx.buf(alloc(es, nc, f"w{i}", [128, 16, NJ], BF16), dma=True) for i in range(6)])
        hring = Ring([cx.buf(alloc(es, nc, f"hb{i}", [128, T], F32), dma=True) for i in range(2)])
        sqring = Ring([cx.buf(alloc(es, nc, f"sq{i}", [128, T], F32)) for i in range(2)])
        rstd = cx.buf(alloc(es, nc, "rstd", [128, T], F32))
        sd = rstd
        sgr = Ring([cx.buf(alloc(es, nc, f"sg{i}", [128, 512], F32)) for i in range(2)])
        outr = Ring([cx.buf(alloc(es, nc, f"ob{i}", [128, 512], F32), dma=True) for i in range(2)])
        hres = Ring([cx.buf(alloc(es, nc, f"hr{i}", [128, 512], F32), dma=True) for i in range(2)])
        ps_g = Ring([cx.buf(alloc(es, nc, f"psg{i}", [128, 512], F32, psum=True)) for i in range(2)])
        ps_u = Ring([cx.buf(alloc(es, nc, f"psu{i}", [128, 512], F32, psum=True)) for i in range(2)])
        ps_d = Ring([cx.buf(alloc(es, nc, f"psd{i}", [128, 512], F32, psum=True)) for i in range(2)])
        ps_ss = [cx.buf(alloc(es, nc, f"pss{i}", [128, 512], F32, psum=True)) for i in range(nsub)]
        blk = es.enter_context(nc.Block())

        cx.dma('sp', lambda e: e.dma_start(out=ones_f.t[:], in_=consts[:, 0:128]), writes=[ones_f], sembuf=ones_f)
        cx.dma('sp', lambda e: e.dma_start(out=eps_t.t[:], in_=consts[:, 128:129], allow_slow_non_contiguous=True), writes=[eps_t], sembuf=eps_t)
        cx.dma('sp', lambda e: e.dma_start(out=g_t.t[:], in_=gvec[:, :]), writes=[g_t], sembuf=g_t)
        wgv = wg.rearrange("(kc p) n -> p kc n", p=128)
        wuv = wu.rearrange("(kc p) n -> p kc n", p=128)
        wdv = wd.rearrange("(j p) n -> p j n", p=128)
        tl = list(range(NT // T)) if tiles is None else tiles
        def tile_body(t0):
            rms_stats(cx, nc, lambda kc, hb: (lambda e: e.dma_start(out=hb.t[:, :T], in_=hT[kc, :, t0:t0 + T])),
                      KC, T, hring, sqring, ps_ss, ones_f, D, eps_t, sd, rstd)
            for kc in range(KC):
                hb = hring.next()
                cx.dma('sp', lambda e, hb=hb, kc=kc: e.dma_start(out=hb.t[:, :T], in_=hT[kc, :, t0:t0 + T]),
                       writes=[hb], sembuf=hb)
                cx.op('dve', lambda e, hb=hb, kc=kc: nc.vector.scalar_tensor_tensor(
                    out=xn.t[:, kc, :], in0=hb.t[:, :T], scalar=g_t.t[:, kc:kc + 1], in1=rstd.t[:, :T],
                    op0=ALU.mult, op1=ALU.mult), reads=[hb, g_t, rstd], writes=[xn] if kc == 0 else [])
            xn.wr = (cx.esem['dve'], cx.esem['dve'].n, 'dve')
            for jt in range(DFF // NJ):
                wgb = wring.next()
                cx.dma('pool', lambda e, b=wgb, jt=jt: e.dma_start(out=b.t[:], in_=wgv[:, :, jt * NJ:(jt + 1) * NJ]),
                       writes=[wgb], sembuf=wgb)
                wub = wring.next()
                cx.dma('pool', lambda e, b=wub, jt=jt: e.dma_start(out=b.t[:], in_=wuv[:, :, jt * NJ:(jt + 1) * NJ]),
                       writes=[wub], sembuf=wub)
                for jj in range(NJ // 128):
                    j = jt * (NJ // 128) + jj
                    for sub in range(nsub):
                        pg = ps_g.next()
                        pu = ps_u.next()
                        mm_group(cx, pg, [(wgb.t[:, kc, jj * 128:(jj + 1) * 128], xn.t[:, kc, sub * 512:(sub + 1) * 512])
                                          for kc in range(KC)], reads=[wgb, xn])
                        mm_group(cx, pu, [(wub.t[:, kc, jj * 128:(jj + 1) * 128], xn.t[:, kc, sub * 512:(sub + 1) * 512])
                                          for kc in range(KC)], reads=[wub, xn])
                        sg = sgr.next()
                        cx.op('act', lambda e, pg=pg, sg=sg: nc.scalar.activation(out=sg.t[:], in_=pg.t[:], func=AF.Silu),
                              reads=[pg], writes=[sg])
                        cx.op('dve', lambda e, sg=sg, pu=pu, j=j, sub=sub: nc.vector.tensor_tensor(
                            out=act[j].t[:, sub * 512:(sub + 1) * 512], in0=sg.t[:], in1=pu.t[:], op=ALU.mult),
                            reads=[sg, pu], writes=[act[j]])
            JD = 11
            for ct in range(D // NJ):
                wds = []
                for q4 in range(FC // JD):
                    wb = wring.next()
                    cx.dma('pool', lambda e, b=wb, q4=q4, ct=ct: e.dma_start(
                        out=b.t[:, 0:JD, :], in_=wdv[:, q4 * JD:(q4 + 1) * JD, ct * NJ:(ct + 1) * NJ]),
                        writes=[wb], sembuf=wb)
                    wds.append(wb)
                for cc in range(NJ // 128):
                    c = ct * (NJ // 128) + cc
                    for sub in range(nsub):
                        hr = hres.next()
                        cx.dma('sp', lambda e, hr=hr, c=c, sub=sub: e.dma_start(
                            out=hr.t[:], in_=hT[c, :, t0 + sub * 512:t0 + (sub + 1) * 512]), writes=[hr], sembuf=hr)
                        pd = ps_d.next()
                        mm_group(cx, pd, [(wds[j // JD].t[:, j % JD, cc * 128:(cc + 1) * 128], act[j].t[:, sub * 512:(sub + 1) * 512])
                                          for j in range(FC)], reads=wds + act)
                        ob = outr.next()
                        cx.op('dve', lambda e, pd=pd, hr=hr, ob=ob: nc.vector.scalar_tensor_tensor(
                            out=ob.t[:], in0=pd.t[:], scalar=0.5, in1=hr.t[:], op0=ALU.mult, op1=ALU.add),
                            reads=[pd, hr], writes=[ob])
                        cx.dma('sp', lambda e, ob=ob, c=c, sub=sub: e.dma_start(
                            out=hT[c, :, t0 + sub * 512:t0 + (sub + 1) * 512], in_=ob.t[:]), reads=[ob], sembuf=ob)
        for ti in tl:
            tile_body(ti * T)
        cx.end_stage(blk)


def stage_out(cx, hT, gvec, consts, ident_dram, out):
    nc = cx.nc
    T = 512
    with contextlib.ExitStack() as es:
        cx.begin_stage()
        ones_f = cx.buf(alloc(es, nc, "ones_f", [128, 128], F32), dma=True)
        ident = cx.buf(alloc(es, nc, "ident", [128, 128], F32), dma=True)
        eps_t = cx.buf(alloc(es, nc, "eps_t", [128, 1], F32), dma=True)
        g_t = cx.buf(alloc(es, nc, "g_t", [128, KC], F32), dma=True)
        hring = Ring([cx.buf(alloc(es, nc, f"hb{i}", [128, T], F32), dma=True) for i in range(3)])
        sqring = Ring([cx.buf(alloc(es, nc, f"sq{i}", [128, T], F32)) for i in range(2)])
        sd = cx.buf(alloc(es, nc, "sd", [128, T], F32))
        rstd = cx.buf(alloc(es, nc, "rstd", [128, T], F32))
        xn = cx.buf(alloc(es, nc, "xnf", [128, KC, T], F32))
        ps_ss = [cx.buf(alloc(es, nc, "pss0", [128, 512], F32, psum=True))]
        ps = Ring([cx.buf(alloc(es, nc, f"ps{i}", [128, 512], F32, psum=True)) for i in range(4)])
        orow = Ring([cx.buf(alloc(es, nc, f"orow{i}", [128, D], F32), dma=True) for i in range(3)])
        blk = es.enter_context(nc.Block())
        cx.dma('sp', lambda e: e.dma_start(out=ones_f.t[:], in_=consts[:, 0:128]), writes=[ones_f], sembuf=ones_f)
        cx.dma('sp', lambda e: e.dma_start(out=eps_t.t[:], in_=consts[:, 128:129], allow_slow_non_contiguous=True), writes=[eps_t], sembuf=eps_t)
        cx.dma('sp', lambda e: e.dma_start(out=g_t.t[:], in_=gvec[:, :]), writes=[g_t], sembuf=g_t)
        cx.dma('sp', lambda e: e.dma_start(out=ident.t[:], in_=ident_dram[:, :]), writes=[ident], sembuf=ident)
        def tile_body(t0):
            rms_stats(cx, nc, lambda kc, hb: (lambda e: e.dma_start(out=hb.t[:, :T], in_=hT[kc, :, t0:t0 + T])),
                      KC, T, hring, sqring, ps_ss, ones_f, D, eps_t, sd, rstd)
            for kc in range(KC):
                hb = hring.next()
                cx.dma('sp', lambda e, hb=hb, kc=kc: e.dma_start(out=hb.t[:, :T], in_=hT[kc, :, t0:t0 + T]),
                       writes=[hb], sembuf=hb)
                cx.op('dve', lambda e, hb=hb, kc=kc: nc.vector.scalar_tensor_tensor(
                    out=xn.t[:, kc, :], in0=hb.t[:, :T], scalar=g_t.t[:, kc:kc + 1], in1=rstd.t[:, :T],
                    op0=ALU.mult, op1=ALU.mult), reads=[hb, g_t, rstd], writes=[xn] if kc == 0 else [])
            xn.wr = (cx.esem['dve'], cx.esem['dve'].n, 'dve')
            for tb in range(T // 128):
                ob = orow.next()
                for g in range(KC // 4):
                    p = ps.next()

                    def fn(eng, p=p, g=g, tb=tb):
                        ins = None
                        for j in range(4):
                            kc = g * 4 + j
                            ins = nc.tensor.transpose(p.t[:, j * 128:(j + 1) * 128], xn.t[:, kc, tb * 128:(tb + 1) * 128], ident.t[:])
                        return ins
                    cx.op('pe', fn, reads=[xn, ident], writes=[p])
                    if g % 2 == 0:
                        cx.op('dve', lambda e, p=p, ob=ob, g=g: nc.vector.tensor_copy(ob.t[:, g * 512:(g + 1) * 512], p.t[:]),
                              reads=[p], writes=[ob])
                    else:
                        cx.op('act', lambda e, p=p, ob=ob, g=g: nc.scalar.copy(ob.t[:, g * 512:(g + 1) * 512], p.t[:]),
                              reads=[p], writes=[ob])
                cx.dma('sp', lambda e, ob=ob, tb=tb: e.dma_start(out=out[t0 + tb * 128:t0 + (tb + 1) * 128, :], in_=ob.t[:]),
                       reads=[ob], sembuf=ob)
        for ti in range(NT // T):
            tile_body(ti * T)
        cx.end_stage(blk)


def load_consts(cx, es, nc, consts, ident_dram=None):
    ones_f = cx.buf(alloc(es, nc, "ones_f", [128, 128], F32), dma=True)
    eps_t = cx.buf(alloc(es, nc, "eps_t", [128, 1], F32), dma=True)
    cx.dma('sp', lambda e: e.dma_start(out=ones_f.t[:], in_=consts[:, 0:128]), writes=[ones_f], sembuf=ones_f)
    cx.dma('sp', lambda e: e.dma_start(out=eps_t.t[:], in_=consts[:, 128:129], allow_slow_non_contiguous=True),
           writes=[eps_t], sembuf=eps_t)
    return ones_f, eps_t


def stage_proj(cx, hT, w, gvec, consts, pT, ncols, T=1024):
    nc = cx.nc
    nsub = T // 512
    NJ = 384
    with contextlib.ExitStack() as es:
        cx.begin_stage()
        ones_f, eps_t = load_consts(cx, es, nc, consts)
        g_t = cx.buf(alloc(es, nc, "g_t", [128, KC], F32), dma=True)
        xn = cx.buf(alloc(es, nc, "xn", [128, KC, T], BF16))
        wring = Ring([cx.buf(alloc(es, nc, f"w{i}", [128, 16, NJ], BF16), dma=True) for i in range(3)])
        hring = Ring([cx.buf(alloc(es, nc, f"hb{i}", [128, T], F32), dma=True) for i in range(2)])
        sqring = Ring([cx.buf(alloc(es, nc, f"sq{i}", [128, T], F32)) for i in range(2)])
        rstd = cx.buf(alloc(es, nc, "rstd", [128, T], F32))
        outr = Ring([cx.buf(alloc(es, nc, f"ob{i}", [128, 512], F32), dma=True) for i in range(4)])
        ps_o = Ring([cx.buf(alloc(es, nc, f"pso{i}", [128, 512], F32, psum=True)) for i in range(4)])
        ps_ss = [cx.buf(alloc(es, nc, f"pss{i}", [128, 512], F32, psum=True)) for i in range(nsub)]
        blk = es.enter_context(nc.Block())
        cx.dma('sp', lambda e: e.dma_start(out=g_t.t[:], in_=gvec[:, :]), writes=[g_t], sembuf=g_t)
        wv = w.rearrange("(kc p) n -> p kc n", p=128)

        def tile_body(t0):
            rms_stats(cx, nc, lambda kc, hb: (lambda e: e.dma_start(out=hb.t[:, :T], in_=hT[kc, :, t0:t0 + T])),
                      KC, T, hring, sqring, ps_ss, ones_f, D, eps_t, rstd, rstd)
            for kc in range(KC):
                hb = hring.next()
                cx.dma('sp', lambda e, hb=hb, kc=kc: e.dma_start(out=hb.t[:, :T], in_=hT[kc, :, t0:t0 + T]),
                       writes=[hb], sembuf=hb)
                cx.op('dve', lambda e, hb=hb, kc=kc: nc.vector.scalar_tensor_tensor(
                    out=xn.t[:, kc, :], in0=hb.t[:, :T], scalar=g_t.t[:, kc:kc + 1], in1=rstd.t[:, :T],
                    op0=ALU.mult, op1=ALU.mult), reads=[hb, g_t, rstd], writes=[xn] if kc == 0 else [])
            xn.wr = (cx.esem['dve'], cx.esem['dve'].n, 'dve')
            for jt in range(ncols // NJ):
                wb = wring.next()
                cx.dma('pool', lambda e, b=wb, jt=jt: e.dma_start(out=b.t[:], in_=wv[:, :, jt * NJ:(jt + 1) * NJ]),
                       writes=[wb], sembuf=wb)
                for jj in range(NJ // 128):
                    r0 = jt * NJ + jj * 128
                    for sub in range(nsub):
                        po = ps_o.next()
                        mm_group(cx, po, [(wb.t[:, kc, jj * 128:(jj + 1) * 128], xn.t[:, kc, sub * 512:(sub + 1) * 512])
                                          for kc in range(KC)], reads=[wb, xn])
                        ob = outr.next()
                        if (jj + sub) % 2 == 0:
                            cx.op('act', lambda e, po=po, ob=ob: nc.scalar.copy(ob.t[:], po.t[:]), reads=[po], writes=[ob])
                        else:
                            cx.op('dve', lambda e, po=po, ob=ob: nc.vector.tensor_copy(ob.t[:], po.t[:]), reads=[po], writes=[ob])
                        cx.dma('sp', lambda e, ob=ob, r0=r0, sub=sub: e.dma_start(
                            out=pT[r0:r0 + 128, t0 + sub * 512:t0 + (sub + 1) * 512], in_=ob.t[:]), reads=[ob], sembuf=ob)
        for ti in range(NT // T):
            tile_body(ti * T)
        cx.end_stage(blk)


def stage_wout(cx, hT, yT, w, T=1024):
    nc = cx.nc
    nsub = T // 512
    NJ = 256
    with contextlib.ExitStack() as es:
        cx.begin_stage()
        yb = cx.buf(alloc(es, nc, "yb", [128, KC, T], BF16), dma=True)
        wring = Ring([cx.buf(alloc(es, nc, f"w{i}", [128, 16, NJ], BF16), dma=True) for i in range(3)])
        outr = Ring([cx.buf(alloc(es, nc, f"ob{i}", [128, 512], F32), dma=True) for i in range(3)])
        hres = Ring([cx.buf(alloc(es, nc, f"hr{i}", [128, 512], F32), dma=True) for i in range(3)])
        ps_o = Ring([cx.buf(alloc(es, nc, f"pso{i}", [128, 512], F32, psum=True)) for i in range(4)])
        blk = es.enter_context(nc.Block())
        wv = w.rearrange("(kc p) n -> p kc n", p=128)
        yv = yT.rearrange("(kc p) t -> p kc t", p=128)

        def tile_body(t0):
            cx.dma('sp', lambda e: e.dma_start(out=yb.t[:], in_=yv[:, :, t0:t0 + T]), writes=[yb], sembuf=yb)
            for jt in range(D // NJ):
                wb = wring.next()
                cx.dma('pool', lambda e, b=wb, jt=jt: e.dma_start(out=b.t[:], in_=wv[:, :, jt * NJ:(jt + 1) * NJ]),
                       writes=[wb], sembuf=wb)
                for jj in range(NJ // 128):
                    c = jt * (NJ // 128) + jj
                    for sub in range(nsub):
                        hr = hres.next()
                        cx.dma('sp', lambda e, hr=hr, c=c, sub=sub: e.dma_start(
                            out=hr.t[:], in_=hT[c, :, t0 + sub * 512:t0 + (sub + 1) * 512]), writes=[hr], sembuf=hr)
                        po = ps_o.next()
                        mm_group(cx, po, [(wb.t[:, kc, jj * 128:(jj + 1) * 128], yb.t[:, kc, sub * 512:(sub + 1) * 512])
                                          for kc in range(KC)], reads=[wb, yb])
                        ob = outr.next()
                        cx.op('dve', lambda e, po=po, hr=hr, ob=ob: nc.vector.tensor_tensor(
                            out=ob.t[:], in0=po.t[:], in1=hr.t[:], op=ALU.add), reads=[po, hr], writes=[ob])
                        cx.dma('sp', lambda e, ob=ob, c=c, sub=sub: e.dma_start(
                            out=hT[c, :, t0 + sub * 512:t0 + (sub + 1) * 512], in_=ob.t[:]), reads=[ob], sembuf=ob)
        for ti in range(NT // T):
            tile_body(ti * T)
        cx.end_stage(blk)


C_ONES, C_EPS, C_BD64, C_ID, C_SU, C_SL, C_IU, C_RM = 0, 128, 256, 384, 512, 576, 640, 704
B0 = 832
C0 = 2528
R_KRS = 4064


def stage_conv(cx, pT, prm, consts, yT):
    nc = cx.nc
    with contextlib.ExitStack() as es:
        cx.begin_stage()
        bd = cx.buf(alloc(es, nc, "bd", [128, 128], F32), dma=True)
        eps_t = cx.buf(alloc(es, nc, "eps_t", [128, 1], F32), dma=True)
        pr = cx.buf(alloc(es, nc, "pr", [128, 4, 4], F32), dma=True)
        bg = Ring([cx.buf(alloc(es, nc, f"bg{i}", [128, S], F32), dma=True) for i in range(2)])
        cg = Ring([cx.buf(alloc(es, nc, f"cg{i}", [128, S], F32), dma=True) for i in range(2)])
        hh = Ring([cx.buf(alloc(es, nc, f"hh{i}", [128, S], F32), dma=True) for i in range(2)])
        u = cx.buf(alloc(es, nc, "u", [128, S + 2], F32))
        y = cx.buf(alloc(es, nc, "y", [128, S], F32))
        z = cx.buf(alloc(es, nc, "z", [128, S], F32))
        zsq = cx.buf(alloc(es, nc, "zsq", [128, S], F32))
        rs = cx.buf(alloc(es, nc, "rs", [128, S], F32))
        ob = Ring([cx.buf(alloc(es, nc, f"ob{i}", [128, S], BF16), dma=True) for i in range(2)])
        ps = Ring([cx.buf(alloc(es, nc, f"ps{i}", [128, 512], F32, psum=True)) for i in range(4)])
        blk = es.enter_context(nc.Block())
        cx.dma('sp', lambda e: e.dma_start(out=bd.t[:], in_=consts[:, C_BD64:C_BD64 + 128]), writes=[bd], sembuf=bd)
        cx.dma('sp', lambda e: e.dma_start(out=eps_t.t[:], in_=consts[:, C_EPS:C_EPS + 1], allow_slow_non_contiguous=True),
               writes=[eps_t], sembuf=eps_t)
        cx.dma('sp', lambda e: e.dma_start(out=pr.t[:], in_=prm[:, :, :]), writes=[pr], sembuf=pr)
        cx.op('dve', lambda e: nc.vector.memset(u.t[:, 0:2], 0.0), writes=[u])

        def body(b, ch):
            t0 = b * S
            bgb, cgb, hhb = bg.next(), cg.next(), hh.next()
            for (buf, r0) in ((bgb, C0 + ch * 128), (cgb, C0 + 512 + ch * 128), (hhb, C0 + 1024 + ch * 128)):
                cx.dma('sp', lambda e, buf=buf, r0=r0: e.dma_start(out=buf.t[:], in_=pT[r0:r0 + 128, t0:t0 + S]),
                       writes=[buf], sembuf=buf)
            cx.op('dve', lambda e: nc.vector.tensor_tensor(out=u.t[:, 2:S + 2], in0=cgb.t[:], in1=hhb.t[:], op=ALU.mult),
                  reads=[cgb, hhb], writes=[u])
            cx.op('act', lambda e: nc.scalar.activation(out=y.t[:], in_=u.t[:, 2:S + 2], func=AF.Copy, scale=pr.t[:, ch, 2:3]),
                  reads=[u, pr], writes=[y])
            cx.op('dve', lambda e: nc.vector.scalar_tensor_tensor(out=y.t[:], in0=u.t[:, 1:S + 1], scalar=pr.t[:, ch, 1:2],
                                                                  in1=y.t[:], op0=ALU.mult, op1=ALU.add), reads=[u, pr], writes=[y])
            cx.op('dve', lambda e: nc.vector.scalar_tensor_tensor(out=y.t[:], in0=u.t[:, 0:S], scalar=pr.t[:, ch, 0:1],
                                                                  in1=y.t[:], op0=ALU.mult, op1=ALU.add), reads=[u, pr], writes=[y])
            cx.op('dve', lambda e: nc.vector.tensor_tensor(out=z.t[:], in0=bgb.t[:], in1=y.t[:], op=ALU.mult),
                  reads=[bgb, y], writes=[z])
            cx.op('act', lambda e: nc.scalar.activation(out=zsq.t[:], in_=z.t[:], func=AF.Square), reads=[z], writes=[zsq])
            for sub in range(S // 512):
                p = ps.next()
                sl = slice(sub * 512, (sub + 1) * 512)
                cx.op('pe', lambda e, p=p, sl=sl: nc.tensor.matmul(p.t[:], bd.t[:], zsq.t[:, sl], start=True, stop=True),
                      reads=[bd, zsq], writes=[p])
                cx.op('act', lambda e, p=p, sl=sl: nc.scalar.activation(out=rs.t[:, sl], in_=p.t[:], func=AF.Sqrt,
                                                                        bias=eps_t.t[:, 0:1], scale=1.0 / 64),
                      reads=[p, eps_t], writes=[rs])
            cx.op('dve', lambda e: nc.vector.reciprocal(rs.t[:], rs.t[:]), reads=[rs], writes=[rs])
            o = ob.next()
            cx.op('dve', lambda e, o=o: nc.vector.scalar_tensor_tensor(out=o.t[:], in0=z.t[:], scalar=pr.t[:, ch, 3:4],
                                                                       in1=rs.t[:], op0=ALU.mult, op1=ALU.mult),
                  reads=[z, pr, rs], writes=[o])
            cx.dma('sp', lambda e, o=o: e.dma_start(out=yT[1536 + ch * 128:1536 + (ch + 1) * 128, t0:t0 + S], in_=o.t[:]),
                   reads=[o], sembuf=o)
        for b in range(NSEQ):
            for ch in range(4):
                body(b, ch)
        cx.end_stage(blk)


def stage_mla(cx, pT, wq, wkv, prm, cs, consts, maskd, yT):
    nc = cx.nc
    T = 1024
    scale = float((128 + 64) ** -0.5)
    with contextlib.ExitStack() as es:
        cx.begin_stage()
        ones_f, eps_t = load_consts(cx, es, nc, consts)
        ones_b = cx.buf(alloc(es, nc, "ones_b", [128, 128], BF16), dma=True)
        id_b = cx.buf(alloc(es, nc, "id_b", [128, 128], BF16), dma=True)
        mask = cx.buf(alloc(es, nc, "mask", [128, 4, 512], BF16), dma=True)
        pr = cx.buf(alloc(es, nc, "pr", [128, 16], F32), dma=True)
        wqb = cx.buf(alloc(es, nc, "wqb", [128, 4, 2048], BF16), dma=True)
        wkvb = cx.buf(alloc(es, nc, "wkvb", [128, 2, 2048], BF16), dma=True)
        cqn = cx.buf(alloc(es, nc, "cqn", [128, 4, S], BF16))
        ckvn = cx.buf(alloc(es, nc, "ckvn", [128, 2, S], BF16))
        cc = cx.buf(alloc(es, nc, "cc", [64, S], F32), dma=True)
        ss = cx.buf(alloc(es, nc, "ss", [64, S], F32), dma=True)
        kx = cx.buf(alloc(es, nc, "kx", [64, S], F32), dma=True)
        kxs = cx.buf(alloc(es, nc, "kxs", [64, S], F32), dma=True)
        k_r = cx.buf(alloc(es, nc, "k_r", [64, S], BF16))
        q_n = cx.buf(alloc(es, nc, "q_n", [128, S], BF16))
        q_r = cx.buf(alloc(es, nc, "q_r", [64, S], BF16))
        k_n = cx.buf(alloc(es, nc, "k_n", [128, S], BF16))
        V = cx.buf(alloc(es, nc, "V", [128, 16, 128], BF16))
        xt = cx.buf(alloc(es, nc, "xt", [64, 512], F32))
        PT = Ring([cx.buf(alloc(es, nc, f"PT{i}", [128, 512], BF16)) for i in range(3)])
        hring = Ring([cx.buf(alloc(es, nc, f"hb{i}", [128, T], F32), dma=True) for i in range(2)])
        sqring = Ring([cx.buf(alloc(es, nc, f"sq{i}", [128, T], F32)) for i in range(2)])
        rstd = cx.buf(alloc(es, nc, "rstd", [128, T], F32))
        rinv = cx.buf(alloc(es, nc, "rinv", [128, 512], F32))
        yv = cx.buf(alloc(es, nc, "yv", [128, 512], F32))
        ysq = cx.buf(alloc(es, nc, "ysq", [128, 512], F32))
        sd2 = cx.buf(alloc(es, nc, "sd2", [128, 512], F32))
        obr = Ring([cx.buf(alloc(es, nc, f"ob{i}", [128, 512], BF16), dma=True) for i in range(2)])
        ps_ss = [cx.buf(alloc(es, nc, f"pss{i}", [128, 512], F32, psum=True)) for i in range(2)]
        ps_p = Ring([cx.buf(alloc(es, nc, f"psp{i}", [128, 512], F32, psum=True)) for i in range(2)])
        ps_s = Ring([cx.buf(alloc(es, nc, f"pst{i}", [128, 512], F32, psum=True)) for i in range(2)])
        ps_o = cx.buf(alloc(es, nc, "pso", [128, 512], F32, psum=True))
        ps_r = cx.buf(alloc(es, nc, "psr", [128, 512], F32, psum=True))
        blk = es.enter_context(nc.Block())

        cx.dma('pool', lambda e: e.dma_start(out=ones_b.t[:], in_=consts[:, C_ONES:C_ONES + 128]), writes=[ones_b], sembuf=ones_b)
        cx.dma('pool', lambda e: e.dma_start(out=id_b.t[:], in_=consts[:, C_ID:C_ID + 128]), writes=[id_b], sembuf=id_b)
        cx.dma('pool', lambda e: e.dma_start(out=mask.t[:], in_=maskd[:, :, :]), writes=[mask], sembuf=mask)
        cx.dma('sp', lambda e: e.dma_start(out=pr.t[:], in_=prm[:, :]), writes=[pr], sembuf=pr)
        cx.dma('pool', lambda e: e.dma_start(out=wqb.t[:], in_=wq.rearrange("(kc p) n -> p kc n", p=128)), writes=[wqb], sembuf=wqb)
        cx.dma('pool', lambda e: e.dma_start(out=wkvb.t[:], in_=wkv.rearrange("(kc p) n -> p kc n", p=128)), writes=[wkvb], sembuf=wkvb)

        def norm_into(dst, row0, nch, gcol0, t0):
            for half in range(S // T):
                tt0 = t0 + half * T
                ld = lambda kc, hb, tt0=tt0: (lambda e: e.dma_start(out=hb.t[:, :T], in_=pT[row0 + kc * 128:row0 + (kc + 1) * 128, tt0:tt0 + T]))
                rms_stats(cx, nc, ld, nch, T, hring, sqring, ps_ss, ones_f, nch * 128, eps_t, rstd, rstd)
                for kc in range(nch):
                    hb = hring.next()
                    cx.dma('sp', ld(kc, hb), writes=[hb], sembuf=hb)
                    cx.op('dve', lambda e, hb=hb, kc=kc, half=half: nc.vector.scalar_tensor_tensor(
                        out=dst.t[:, kc, half * T:(half + 1) * T], in0=hb.t[:, :T], scalar=pr.t[:, gcol0 + kc:gcol0 + kc + 1],
                        in1=rstd.t[:, :T], op0=ALU.mult, op1=ALU.mult), reads=[hb, pr, rstd], writes=[dst])

        def rope(dst, x_ap, xs_ap, sl, xbufs):
            cx.op('dve', lambda e: nc.vector.tensor_tensor(out=xt.t[:, :sl.stop - sl.start], in0=xs_ap, in1=ss.t[:, sl], op=ALU.mult),
                  reads=xbufs + [ss], writes=[xt])
            cx.op('dve', lambda e: nc.vector.tensor_tensor(out=yv.t[0:64, :sl.stop - sl.start], in0=x_ap, in1=cc.t[:, sl], op=ALU.mult),
                  reads=xbufs + [cc], writes=[yv])
            cx.op('dve', lambda e: nc.vector.tensor_tensor(out=dst.t[0:64, sl], in0=yv.t[0:64, :sl.stop - sl.start],
                                                           in1=xt.t[:, :sl.stop - sl.start], op=ALU.add),
                  reads=[yv, xt], writes=[dst])

        def seq_body(b):
            t0 = b * S
            cx.dma('sp', lambda e: e.dma_start(out=cc.t[:], in_=cs[b, 0, :, :]), writes=[cc], sembuf=cc)
            cx.dma('sp', lambda e: e.dma_start(out=ss.t[:], in_=cs[b, 1, :, :]), writes=[ss], sembuf=ss)
            cx.dma('sp', lambda e: e.dma_start(out=kx.t[:], in_=pT[768:832, t0:t0 + S]), writes=[kx], sembuf=kx)
            cx.dma('sp', lambda e: e.dma_start(out=kxs.t[:], in_=pT[R_KRS:R_KRS + 64, t0:t0 + S]), writes=[kxs], sembuf=kxs)
            norm_into(cqn, 0, 4, 0, t0)
            norm_into(ckvn, 512, 2, 4, t0)
            for sub in range(4):
                sl = slice(sub * 512, (sub + 1) * 512)
                rope(k_r, kx.t[:, sl], kxs.t[:, sl], sl, [kx, kxs])
            for h in range(8):
                head_body(h, t0)

        def head_body(h, t0):
            if True:
                c0 = h * 256
                for sub in range(4):
                    sl = slice(sub * 512, (sub + 1) * 512)
                    p = ps_p.next()
                    mm_group(cx, p, [(wqb.t[:, kc, c0:c0 + 128], cqn.t[:, kc, sl]) for kc in range(4)], reads=[wqb, cqn])
                    cx.op('act', lambda e, p=p, sl=sl: nc.scalar.copy(q_n.t[:, sl], p.t[:]), reads=[p], writes=[q_n])
                    p = ps_p.next()
                    mm_group(cx, p, [(wkvb.t[:, kc, c0:c0 + 128], ckvn.t[:, kc, sl]) for kc in range(2)], reads=[wkvb, ckvn])
                    cx.op('act', lambda e, p=p, sl=sl: nc.scalar.copy(k_n.t[:, sl], p.t[:]), reads=[p], writes=[k_n])
                    p1 = ps_p.next()
                    mm_group(cx, p1, [(wqb.t[:, kc, c0 + 128:c0 + 192], cqn.t[:, kc, sl]) for kc in range(4)], reads=[wqb, cqn],
                             out_ap=p1.t[0:64, :])
                    p2 = ps_p.next()
                    mm_group(cx, p2, [(wqb.t[:, kc, c0 + 192:c0 + 256], cqn.t[:, kc, sl]) for kc in range(4)], reads=[wqb, cqn],
                             out_ap=p2.t[0:64, :])
                    rope(q_r, p1.t[0:64, :], p2.t[0:64, :], sl, [p1, p2])
                    p = ps_p.next()

                    def vfn(eng, p=p, sub=sub):
                        ins = None
                        for j in range(4):
                            tb = sub * 4 + j
                            for kc in range(2):
                                ins = nc.tensor.matmul(p.t[:, j * 128:(j + 1) * 128], ckvn.t[:, kc, tb * 128:(tb + 1) * 128],
                                                       wkvb.t[:, kc, c0 + 128:c0 + 256], start=(kc == 0), stop=(kc == 1))
                        return ins
                    cx.op('pe', vfn, reads=[ckvn, wkvb], writes=[p])
                    cx.op('act', lambda e, p=p, sub=sub: nc.scalar.copy(
                        V.t[:, sub * 4:(sub + 1) * 4, :], p.t[:].rearrange("p (a b) -> p a b", a=4)), reads=[p], writes=[V])
                for qt in range(4):
                    qs = slice(qt * 512, (qt + 1) * 512)
                    nkb = 4 * (qt + 1)
                    for kb in range(nkb):
                        ks = slice(kb * 128, (kb + 1) * 128)
                        st = ps_s.next()
                        pairs = [(k_n.t[:, ks], q_n.t[:, qs]), (k_r.t[0:64, ks], q_r.t[0:64, qs])]
                        rds = [k_n, q_n, k_r, q_r]
                        if kb >= 4 * qt:
                            pairs.append((id_b.t[:], mask.t[:, kb - 4 * qt, :]))
                            rds += [id_b, mask]
                        mm_group(cx, st, pairs, reads=rds)
                        pt = PT.next()
                        cx.op('act', lambda e, st=st, pt=pt: nc.scalar.activation(out=pt.t[:], in_=st.t[:], func=AF.Exp, scale=scale),
                              reads=[st], writes=[pt])

                        def pv(eng, pt=pt, kb=kb, nkb=nkb):
                            nc.tensor.matmul(ps_o.t[:], V.t[:, kb, :], pt.t[:], start=(kb == 0), stop=(kb == nkb - 1))
                            return nc.tensor.matmul(ps_r.t[:], ones_b.t[:], pt.t[:], start=(kb == 0), stop=(kb == nkb - 1))
                        cx.op('pe', pv, reads=[pt, V, ones_b], writes=[ps_o, ps_r])
                    cx.op('dve', lambda e: nc.vector.reciprocal(rinv.t[:], ps_r.t[:]), reads=[ps_r], writes=[rinv])
                    cx.op('dve', lambda e: nc.vector.tensor_tensor(out=yv.t[:], in0=ps_o.t[:], in1=rinv.t[:], op=ALU.mult),
                          reads=[ps_o, rinv], writes=[yv])
                    cx.op('act', lambda e: nc.scalar.activation(out=ysq.t[:], in_=yv.t[:], func=AF.Square), reads=[yv], writes=[ysq])
                    pm = ps_ss[qt % 2]
                    cx.op('pe', lambda e, pm=pm: nc.tensor.matmul(pm.t[:], ones_f.t[:], ysq.t[:], start=True, stop=True),
                          reads=[ones_f, ysq], writes=[pm])
                    cx.op('act', lambda e, pm=pm: nc.scalar.activation(out=sd2.t[:], in_=pm.t[:], func=AF.Sqrt,
                                                                       bias=eps_t.t[:, 0:1], scale=1.0 / 128),
                          reads=[pm, eps_t], writes=[sd2])
                    cx.op('dve', lambda e: nc.vector.reciprocal(sd2.t[:], sd2.t[:]), reads=[sd2], writes=[sd2])
                    o = obr.next()
                    cx.op('dve', lambda e, o=o: nc.vector.scalar_tensor_tensor(
                        out=o.t[:], in0=yv.t[:], scalar=pr.t[:, 6 + h:7 + h], in1=sd2.t[:], op0=ALU.mult, op1=ALU.mult),
                        reads=[yv, pr, sd2], writes=[o])
                    cx.dma('sp', lambda e, o=o, qt=qt: e.dma_start(
                        out=yT[h * 128:(h + 1) * 128, t0 + qt * 512:t0 + (qt + 1) * 512], in_=o.t[:]), reads=[o], sembuf=o)
        for b in range(NSEQ):
            seq_body(b)
        cx.end_stage(blk)


def stage_rwkv(cx, pT, rp, dup, iup, gup, consts, yT):
    nc = cx.nc
    TB = 128
    c1 = -0.6065306597126334
    with contextlib.ExitStack() as es:
        cx.begin_stage()

        def sb(name, shape, dt=F32, dma=False):
            return cx.buf(alloc(es, nc, name, shape, dt), dma=dma)
        cst = sb("cst", [64, 1024], dma=True)
        rpt = sb("rpt", [128, 96], dma=True)
        dupt = sb("dupt", [32, 512], dma=True)
        iupt = sb("iupt", [32, 512], dma=True)
        gupt = sb("gupt", [96, 512], dma=True)
        omka = sb("omka", [64, 8])
        cur3 = sb("cur3", [64, 3, 8, TB], dma=True)
        prv3 = sb("prv3", [64, 3, 8, TB], dma=True)
        loc = sb("loc", [96, 3, TB], dma=True)
        lop = sb("lop", [96, 3, TB], dma=True)
        names = ["sw", "a", "g", "cum", "E1", "E2", "E3", "E4", "kk", "nr", "tmp", "kp", "bv", "aT", "bb", "bT", "bhT", "kT", "khT", "rT"]
        Tt = {n: sb(n, [64, 8, TB]) for n in names}
        cn = ["V", "Bh", "Kh", "N", "L", "AKu", "BRu", "KRu", "P", "Q", "Na", "La", "Nb", "Lb", "Zs", "Us", "H", "tH", "ysb", "yc", "sq", "sdv"]
        Ct = {n: sb(n, [64, 8, 64]) for n in cn}
        obr = Ring([sb(f"ob{i}", [64, 8, 64], BF16, dma=True) for i in range(2)])
        pp = Ring([cx.buf(alloc(es, nc, f"pp{i}", [128, 512], F32, psum=True)) for i in range(8)])
        blk = es.enter_context(nc.Block())

        cx.dma('sp', lambda e: e.dma_start(out=cst.t[:], in_=consts[0:64, :]), writes=[cst], sembuf=cst)
        cx.dma('sp', lambda e: e.dma_start(out=rpt.t[:], in_=rp[:, :]), writes=[rpt], sembuf=rpt)
        cx.dma('sp', lambda e: e.dma_start(out=dupt.t[:], in_=dup[:, :]), writes=[dupt], sembuf=dupt)
        cx.dma('sp', lambda e: e.dma_start(out=iupt.t[:], in_=iup[:, :]), writes=[iupt], sembuf=iupt)
        cx.dma('sp', lambda e: e.dma_start(out=gupt.t[:], in_=gup[:, :]), writes=[gupt], sembuf=gupt)
        ones64 = cst.t[:, C_ONES:C_ONES + 64]
        id64 = cst.t[:, C_ID:C_ID + 64]
        eps64 = cst.t[:, C_EPS + 1:C_EPS + 2]

        def mk(c):
            return cst.t[:, c:c + 64].unsqueeze(1).broadcast_to([64, 8, 64])
        SUb, SLb, IUb, IDb = mk(C_SU), mk(C_SL), mk(C_IU), mk(C_ID)
        rmask = cst.t[:, C_RM:C_RM + TB]

        def pb(col):
            return rpt.t[0:64, col:col + 8]

        def bc(ap2, n):
            return ap2.unsqueeze(2).broadcast_to([64, 8, n])

        def TTo(o, oap, ins, op, eng='dve'):
            bufs = [x[0] for x in ins]
            aps = [x[1] for x in ins]
            cx.op(eng, lambda e: nc.vector.tensor_tensor(out=oap, in0=aps[0], in1=aps[1], op=op),
                  reads=[b for b in bufs if b is not None], writes=[o])

        def ACTo(o, oap, i, iap, func, scale=1.0, bias=None, extra=()):
            def fn(e):
                if bias is None:
                    return nc.scalar.activation(out=oap, in_=iap, func=func, scale=scale)
                return nc.scalar.activation(out=oap, in_=iap, func=func, scale=scale, bias=bias)
            cx.op('act', fn, reads=[i] + list(extra), writes=[o])

        for bt in (loc, lop):
            cx.op('dve', lambda e, bt=bt: nc.vector.memset(bt.t[:], 0.0), writes=[bt])
        cx.op('dve', lambda e: nc.vector.tensor_scalar(out=omka.t[:], in0=pb(48), scalar1=-1.0, scalar2=1.0,
                                                       op0=ALU.mult, op1=ALU.add), reads=[rpt], writes=[omka])

        def rows3(t_lo, t_hi):
            return [pT[B0 + kd * 512:B0 + (kd + 1) * 512, t_lo:t_hi].rearrange("(h f) t -> f h t", f=64) for kd in range(3)]

        def block_body(b, k):
            t0 = b * S + k * TB
            T = Tt
            for kd, src in enumerate(rows3(t0, t0 + TB)):
                cx.dma('sp', lambda e, kd=kd, src=src: e.dma_start(out=cur3.t[:, kd], in_=src), writes=[cur3], sembuf=cur3)
            cx.dma('sp', lambda e: e.dma_start(out=loc.t[0:32, 0, :], in_=pT[B0 + 1536:B0 + 1568, t0:t0 + TB]), writes=[loc], sembuf=loc)
            cx.dma('sp', lambda e: e.dma_start(out=loc.t[0:32, 1, :], in_=pT[B0 + 1568:B0 + 1600, t0:t0 + TB]), writes=[loc], sembuf=loc)
            cx.dma('sp', lambda e: e.dma_start(out=loc.t[0:96, 2, :], in_=pT[B0 + 1600:B0 + 1696, t0:t0 + TB]), writes=[loc], sembuf=loc)
            if k == 0:
                cx.op('dve', lambda e: nc.vector.memset(prv3.t[:, :, :, 0:1], 0.0), writes=[prv3])
                cx.op('dve', lambda e: nc.vector.memset(lop.t[:, :, 0:1], 0.0), writes=[lop])
                cx.op('dve', lambda e: nc.vector.memset(Ct["H"].t[:], 0.0), writes=[Ct["H"]])
                for kd, src in enumerate(rows3(t0, t0 + TB - 1)):
                    cx.dma('sp', lambda e, kd=kd, src=src: e.dma_start(out=prv3.t[:, kd, :, 1:TB], in_=src), writes=[prv3], sembuf=prv3)
                o1, lo_, hi_ = 1, t0, t0 + TB - 1
            else:
                for kd, src in enumerate(rows3(t0 - 1, t0 + TB - 1)):
                    cx.dma('sp', lambda e, kd=kd, src=src: e.dma_start(out=prv3.t[:, kd], in_=src), writes=[prv3], sembuf=prv3)
                o1, lo_, hi_ = 0, t0 - 1, t0 + TB - 1
            cx.dma('sp', lambda e: e.dma_start(out=lop.t[0:32, 0, o1:TB], in_=pT[B0 + 1536:B0 + 1568, lo_:hi_]), writes=[lop], sembuf=lop)
            cx.dma('sp', lambda e: e.dma_start(out=lop.t[0:32, 1, o1:TB], in_=pT[B0 + 1568:B0 + 1600, lo_:hi_]), writes=[lop], sembuf=lop)
            cx.dma('sp', lambda e: e.dma_start(out=lop.t[0:96, 2, o1:TB], in_=pT[B0 + 1600:B0 + 1696, lo_:hi_]), writes=[lop], sembuf=lop)
            c3 = cur3.t[:].rearrange("p a h t -> p (a h) t")
            p3 = prv3.t[:].rearrange("p a h t -> p (a h) t")
            mu3 = rpt.t[0:64, 0:24].unsqueeze(2).broadcast_to([64, 24, TB])
            TTo(prv3, p3, [(prv3, p3), (cur3, c3)], ALU.subtract)
            TTo(prv3, p3, [(prv3, p3), (rpt, mu3)], ALU.mult)
            TTo(prv3, p3, [(prv3, p3), (cur3, c3)], ALU.add)
            mul = rpt.t[0:96, 80:83].unsqueeze(2).broadcast_to([96, 3, TB])
            TTo(lop, lop.t[:], [(lop, lop.t[:]), (loc, loc.t[:])], ALU.subtract)
            TTo(lop, lop.t[:], [(lop, lop.t[:]), (rpt, mul)], ALU.mult)
            TTo(lop, lop.t[:], [(lop, lop.t[:]), (loc, loc.t[:])], ALU.add)
            rs_, ks_, vs_ = prv3.t[:, 0], prv3.t[:, 1], prv3.t[:, 2]
            ACTo(lop, lop.t[0:32, 0, :], lop, lop.t[0:32, 0, :], AF.Tanh)
            ACTo(lop, lop.t[0:96, 2, :], lop, lop.t[0:96, 2, :], AF.Sigmoid)
            for (dst, wt, kdim, li, bcol) in ((T["sw"], dupt, 32, 0, 24), (T["a"], iupt, 32, 1, 32), (T["g"], gupt, 96, 2, None)):
                for half in range(2):
                    p = pp.next()

                    def fn(e, p=p, wt=wt, kdim=kdim, li=li, half=half):
                        ins = None
                        for hh in range(4):
                            h = half * 4 + hh
                            ins = nc.tensor.matmul(p.t[0:64, hh * TB:(hh + 1) * TB], wt.t[0:kdim, h * 64:(h + 1) * 64],
                                                   lop.t[0:kdim, li, :], start=True, stop=True)
                        return ins
                    cx.op('pe', fn, reads=[wt, lop], writes=[p])
                    dap = dst.t[:, half * 4:(half + 1) * 4, :]
                    pap = p.t[0:64, :].rearrange("p (h t) -> p h t", h=4)
                    if bcol is None:
                        ACTo(dst, dap, p, pap, AF.Copy)
                    else:
                        bb_ = rpt.t[0:64, bcol + half * 4:bcol + half * 4 + 4].unsqueeze(2).broadcast_to([64, 4, TB])
                        TTo(dst, dap, [(p, pap), (rpt, bb_)], ALU.add)
                if bcol is not None:
                    ACTo(dst, dst.t[:], dst, dst.t[:], AF.Sigmoid)
            for h in range(8):
                cx.op('dve', lambda e, h=h: nc.vector.tensor_tensor_scan(
                    out=T["cum"].t[:, h, :], data0=rmask, data1=T["sw"].t[:, h, :], initial=0.0, op0=ALU.mult, op1=ALU.add),
                    reads=[cst, T["sw"]], writes=[T["cum"]])
            ACTo(T["E1"], T["E1"].t[:], T["cum"], T["cum"].t[:], AF.Exp, scale=c1)
            ACTo(T["E2"], T["E2"].t[:], T["cum"], T["cum"].t[:], AF.Exp, scale=-c1)
            TTo(T["E3"], T["E3"].t[:], [(T["cum"], T["cum"].t[:]), (T["sw"], T["sw"].t[:])], ALU.subtract)
            ACTo(T["E3"], T["E3"].t[:], T["E3"], T["E3"].t[:], AF.Exp, scale=c1)
            cum4 = T["cum"].t[:].rearrange("p h (c t) -> p h c t", t=64)
            cumC = cum4[:, :, :, 63:64].broadcast_to([64, 8, TB // 64, 64])
            e44 = T["E4"].t[:].rearrange("p h (c t) -> p h c t", t=64)
            TTo(T["E4"], e44, [(T["cum"], cumC), (T["cum"], cum4)], ALU.subtract)
            ACTo(T["E4"], T["E4"].t[:], T["E4"], T["E4"].t[:], AF.Exp, scale=c1)
            TTo(T["kk"], T["kk"].t[:], [(prv3, ks_), (rpt, bc(pb(40), TB))], ALU.mult)
            ACTo(T["nr"], T["nr"].t[:], T["kk"], T["kk"].t[:], AF.Square)
            for half in range(2):
                p = pp.next()
                hs = slice(half * 4, (half + 1) * 4)
                cx.op('pe', lambda e, p=p, hs=hs: nc.tensor.matmul(p.t[0:64, :], ones64, T["nr"].t[:, hs, :], start=True, stop=True),
                      reads=[cst, T["nr"]], writes=[p])
                ACTo(T["tmp"], T["tmp"].t[:, hs, :], p, p.t[0:64, :].rearrange("p (h t) -> p h t", h=4), AF.Sqrt)
            cx.op('dve', lambda e: nc.vector.tensor_scalar(out=T["tmp"].t[:], in0=T["tmp"].t[:], scalar1=1e-12, scalar2=None, op0=ALU.max),
                  reads=[T["tmp"]], writes=[T["tmp"]])
            cx.op('dve', lambda e: nc.vector.reciprocal(T["tmp"].t[:], T["tmp"].t[:]), reads=[T["tmp"]], writes=[T["tmp"]])
            TTo(T["kk"], T["kk"].t[:], [(T["kk"], T["kk"].t[:]), (T["tmp"], T["tmp"].t[:])], ALU.mult)
            TTo(T["tmp"], T["tmp"].t[:], [(T["a"], T["a"].t[:]), (rpt, bc(pb(48), TB))], ALU.mult)
            TTo(T["tmp"], T["tmp"].t[:], [(T["tmp"], T["tmp"].t[:]), (omka, bc(omka.t[:], TB))], ALU.add)
            TTo(T["kp"], T["kp"].t[:], [(prv3, ks_), (T["tmp"], T["tmp"].t[:])], ALU.mult)
            TTo(T["tmp"], T["tmp"].t[:], [(prv3, rs_), (T["kp"], T["kp"].t[:])], ALU.mult)
            TTo(T["tmp"], T["tmp"].t[:], [(T["tmp"], T["tmp"].t[:]), (rpt, bc(pb(56), TB))], ALU.mult)
            for half in range(2):
                p = pp.next()
                hs = slice(half * 4, (half + 1) * 4)
                cx.op('pe', lambda e, p=p, hs=hs: nc.tensor.matmul(p.t[0:64, :], ones64, T["tmp"].t[:, hs, :], start=True, stop=True),
                      reads=[cst, T["tmp"]], writes=[p])
                TTo(T["bv"], T["bv"].t[:, hs, :], [(p, p.t[0:64, :].rearrange("p (h t) -> p h t", h=4)), (prv3, prv3.t[:, 2, hs, :])], ALU.mult)
            cx.op('dve', lambda e: nc.vector.scalar_tensor_tensor(out=T["aT"].t[:], in0=T["kk"].t[:], scalar=-1.0, in1=T["E3"].t[:],
                                                                  op0=ALU.mult, op1=ALU.mult), reads=[T["kk"], T["E3"]], writes=[T["aT"]])
            TTo(T["bb"], T["bb"].t[:], [(T["kk"], T["kk"].t[:]), (T["a"], T["a"].t[:])], ALU.mult)
            TTo(T["bT"], T["bT"].t[:], [(T["bb"], T["bb"].t[:]), (T["E2"], T["E2"].t[:])], ALU.mult)
            TTo(T["bhT"], T["bhT"].t[:], [(T["bb"], T["bb"].t[:]), (T["E4"], T["E4"].t[:])], ALU.mult)
            TTo(T["kT"], T["kT"].t[:], [(T["kp"], T["kp"].t[:]), (T["E2"], T["E2"].t[:])], ALU.mult)
            TTo(T["khT"], T["khT"].t[:], [(T["kp"], T["kp"].t[:]), (T["E4"], T["E4"].t[:])], ALU.mult)
            TTo(T["rT"], T["rT"].t[:], [(prv3, rs_), (T["E1"], T["E1"].t[:])], ALU.mult)
            for c in range(TB // 64):
                chunk_body(b, k, c)

        def pgroup(builder, reads):
            p = pp.next()

            def fn(e, p=p):
                ins = None
                for h in range(8):
                    ins = builder(p.t[0:64, h * 64:(h + 1) * 64], h)
                return ins
            cx.op('pe', fn, reads=reads, writes=[p])
            return p, p.t[0:64, :].rearrange("p (h t) -> p h t", h=8)

        def chunk_body(b, k, c):
            T, C = Tt, Ct
            cs_ = slice(c * 64, (c + 1) * 64)
            tc0 = b * S + k * TB + c * 64
            for dst, src_b, src_ap in ((C["V"], prv3, prv3.t[:, 2]), (C["Bh"], T["bhT"], T["bhT"].t[:]), (C["Kh"], T["khT"], T["khT"].t[:])):
                p, pv = pgroup(lambda o, h, src_ap=src_ap: nc.tensor.transpose(o, src_ap[:, h, cs_], id64), [src_b, cst])
                ACTo(dst, dst.t[:], p, pv, AF.Copy)
            def pw(dst, lt, rt, mb):
                p, pv = pgroup(lambda o, h: nc.tensor.matmul(o, lt.t[:, h, cs_], rt.t[:, h, cs_], start=True, stop=True), [lt, rt])
                TTo(dst, dst.t[:], [(p, pv), (cst, mb)], ALU.mult)
            pw(C["N"], T["bT"], T["aT"], SUb)
            pw(C["L"], T["aT"], T["bT"], SLb)
            pw(C["AKu"], T["kT"], T["aT"], SUb)
            pw(C["BRu"], T["bT"], T["rT"], IUb)
            pw(C["KRu"], T["kT"], T["rT"], IUb)
            TTo(C["P"], C["P"].t[:], [(C["N"], C["N"].t[:]), (cst, IDb)], ALU.add)
            TTo(C["Q"], C["Q"].t[:], [(C["L"], C["L"].t[:]), (cst, IDb)], ALU.add)
            Nc, Lc = C["N"], C["L"]
            nxt = [(C["Na"], C["La"]), (C["Nb"], C["Lb"])]
            for lvl in range(5):
                Nn, Ln = nxt[lvl % 2]
                p, pv = pgroup(lambda o, h, Nc=Nc, Lc=Lc: nc.tensor.matmul(o, Lc.t[:, h, :], Nc.t[:, h, :], start=True, stop=True), [Nc, Lc])
                if lvl < 4:
                    p2, pv2 = pgroup(lambda o, h, Nc=Nc, Lc=Lc: nc.tensor.matmul(o, Nc.t[:, h, :], Lc.t[:, h, :], start=True, stop=True), [Nc, Lc])
                ACTo(Nn, Nn.t[:], p, pv, AF.Copy)
                if lvl < 4:
                    cx.op('dve', lambda e, Ln=Ln, pv2=pv2: nc.vector.tensor_copy(Ln.t[:], pv2), reads=[p2], writes=[Ln])
                p, pv = pgroup(lambda o, h, Nn=Nn: nc.tensor.matmul(o, C["Q"].t[:, h, :], Nn.t[:, h, :], start=True, stop=True), [C["Q"], Nn])
                if lvl < 4:
                    p2, pv2 = pgroup(lambda o, h, Ln=Ln: nc.tensor.matmul(o, C["P"].t[:, h, :], Ln.t[:, h, :], start=True, stop=True), [C["P"], Ln])
                TTo(C["P"], C["P"].t[:], [(C["P"], C["P"].t[:]), (p, pv)], ALU.add)
                if lvl < 4:
                    TTo(C["Q"], C["Q"].t[:], [(C["Q"], C["Q"].t[:]), (p2, pv2)], ALU.add)
                Nc, Lc = Nn, Ln
            H = C["H"]

            def zb(o, h):
                nc.tensor.matmul(o, T["aT"].t[:, h, cs_], H.t[:, h, :], start=True, stop=False)
                return nc.tensor.matmul(o, C["AKu"].t[:, h, :], C["V"].t[:, h, :], start=False, stop=True)
            p, pv = pgroup(zb, [T["aT"], H, C["AKu"], C["V"]])
            ACTo(C["Zs"], C["Zs"].t[:], p, pv, AF.Copy)
            p, pv = pgroup(lambda o, h: nc.tensor.matmul(o, C["P"].t[:, h, :], C["Zs"].t[:, h, :], start=True, stop=True), [C["P"], C["Zs"]])
            cx.op('dve', lambda e, pv=pv: nc.vector.tensor_copy(C["Us"].t[:], pv), reads=[p], writes=[C["Us"]])

            def yb_(o, h):
                nc.tensor.matmul(o, H.t[:, h, :], T["rT"].t[:, h, cs_], start=True, stop=False)
                nc.tensor.matmul(o, C["Us"].t[:, h, :], C["BRu"].t[:, h, :], start=False, stop=False)
                return nc.tensor.matmul(o, C["V"].t[:, h, :], C["KRu"].t[:, h, :], start=False, stop=True)
            py, pyv = pgroup(yb_, [H, T["rT"], C["Us"], C["BRu"], C["V"], C["KRu"]])

            def hb_(o, h):
                nc.tensor.matmul(o, C["Bh"].t[:, h, :], C["Us"].t[:, h, :], start=True, stop=False)
                return nc.tensor.matmul(o, C["Kh"].t[:, h, :], C["V"].t[:, h, :], start=False, stop=True)
            ph, phv = pgroup(hb_, [C["Bh"], C["Us"], C["Kh"], C["V"]])
            gC = T["E1"].t[:, :, c * 64 + 63:c * 64 + 64].broadcast_to([64, 8, 64])
            TTo(C["tH"], C["tH"].t[:], [(H, H.t[:]), (T["E1"], gC)], ALU.mult)
            TTo(H, H.t[:], [(C["tH"], C["tH"].t[:]), (ph, phv)], ALU.add)
            ACTo(C["ysb"], C["ysb"].t[:], py, pyv, AF.Copy)
            ysf = C["ysb"].t[:].rearrange("p h t -> p (h t)")
            pm = pp.next()
            cx.op('pe', lambda e, pm=pm: nc.tensor.matmul(pm.t[0:64, :], ones64, ysf, start=True, stop=True), reads=[cst, C["ysb"]], writes=[pm])
            cx.op('dve', lambda e, pm=pm: nc.vector.scalar_tensor_tensor(
                out=C["yc"].t[:].rearrange("p h t -> p (h t)"), in0=pm.t[0:64, :], scalar=-1.0 / 64, in1=ysf, op0=ALU.mult, op1=ALU.add),
                reads=[pm, C["ysb"]], writes=[C["yc"]])
            ACTo(C["sq"], C["sq"].t[:], C["yc"], C["yc"].t[:], AF.Square)
            pv_ = pp.next()
            cx.op('pe', lambda e, pv_=pv_: nc.tensor.matmul(pv_.t[0:64, :], ones64, C["sq"].t[:].rearrange("p h t -> p (h t)"), start=True, stop=True),
                  reads=[cst, C["sq"]], writes=[pv_])
            ACTo(C["sdv"], C["sdv"].t[:].rearrange("p h t -> p (h t)"), pv_, pv_.t[0:64, :], AF.Sqrt, scale=1.0 / 64, bias=eps64, extra=[cst])
            cx.op('dve', lambda e: nc.vector.reciprocal(C["sdv"].t[:], C["sdv"].t[:]), reads=[C["sdv"]], writes=[C["sdv"]])
            yc = C["yc"]
            TTo(yc, yc.t[:], [(yc, yc.t[:]), (C["sdv"], C["sdv"].t[:])], ALU.mult)
            TTo(yc, yc.t[:], [(yc, yc.t[:]), (rpt, bc(pb(64), 64))], ALU.mult)
            TTo(yc, yc.t[:], [(yc, yc.t[:]), (rpt, bc(pb(72), 64))], ALU.add)
            TTo(yc, yc.t[:], [(yc, yc.t[:]), (T["bv"], T["bv"].t[:, :, cs_])], ALU.add)
            o = obr.next()
            TTo(o, o.t[:], [(yc, yc.t[:]), (T["g"], T["g"].t[:, :, cs_])], ALU.mult)
            cx.dma('sp', lambda e, o=o: e.dma_start(
                out=yT[1024:1536, tc0:tc0 + 64].rearrange("(h f) t -> f h t", f=64), in_=o.t[:]), reads=[o], sembuf=o)

        for b in range(NSEQ):
            for k in range(S // TB):
                block_body(b, k)
        cx.end_stage(blk)


def stage_rope(cx, pos, invf, cs):
    nc = cx.nc
    PI = float(np.pi)
    with contextlib.ExitStack() as es:
        cx.begin_stage()
        pi_ = cx.buf(alloc(es, nc, "pi_", [32, S], I32), dma=True)
        fr = cx.buf(alloc(es, nc, "fr", [32, 1], F32), dma=True)
        ang = cx.buf(alloc(es, nc, "ang", [32, S], F32))
        tf = cx.buf(alloc(es, nc, "tf", [32, S], F32))
        ki = cx.buf(alloc(es, nc, "ki", [32, S], I32))
        r = cx.buf(alloc(es, nc, "r", [32, S], F32))
        m = cx.buf(alloc(es, nc, "m", [32, S], F32))
        outs = [cx.buf(alloc(es, nc, f"o{i}", [32, S], F32), dma=True) for i in range(3)]
        blk = es.enter_context(nc.Block())
        cx.dma('sp', lambda e: e.dma_start(out=fr.t[:], in_=invf[:, :]), writes=[fr], sembuf=fr)

        def wrap(buf):
            cx.op('dve', lambda e: nc.vector.tensor_scalar(out=m.t[:], in0=buf.t[:], scalar1=PI, scalar2=-2 * PI, op0=ALU.is_gt, op1=ALU.mult),
                  reads=[buf], writes=[m])
            cx.op('dve', lambda e: nc.vector.tensor_tensor(out=buf.t[:], in0=buf.t[:], in1=m.t[:], op=ALU.add), reads=[m], writes=[buf])
            cx.op('dve', lambda e: nc.vector.tensor_scalar(out=m.t[:], in0=buf.t[:], scalar1=-PI, scalar2=2 * PI, op0=ALU.is_lt, op1=ALU.mult),
                  reads=[buf], writes=[m])
            cx.op('dve', lambda e: nc.vector.tensor_tensor(out=buf.t[:], in0=buf.t[:], in1=m.t[:], op=ALU.add), reads=[m], writes=[buf])

        def body(b):
            cx.dma('sp', lambda e: e.dma_start(out=pi_.t[:], in_=pos[b:b + 1, :].broadcast_to([32, S])), writes=[pi_], sembuf=pi_)
            cx.op('dve', lambda e: nc.vector.tensor_copy(ang.t[:], pi_.t[:]), reads=[pi_], writes=[ang])
            cx.op('dve', lambda e: nc.vector.tensor_scalar(out=ang.t[:], in0=ang.t[:], scalar1=fr.t[:, 0:1], scalar2=None, op0=ALU.mult),
                  reads=[fr], writes=[ang])
            cx.op('dve', lambda e: nc.vector.tensor_scalar(out=tf.t[:], in0=ang.t[:], scalar1=1.0 / (2 * PI), scalar2=None, op0=ALU.mult),
                  reads=[ang], writes=[tf])
            cx.op('dve', lambda e: nc.vector.tensor_copy(ki.t[:], tf.t[:]), reads=[tf], writes=[ki])
            cx.op('dve', lambda e: nc.vector.tensor_copy(tf.t[:], ki.t[:]), reads=[ki], writes=[tf])
            cx.op('dve', lambda e: nc.vector.scalar_tensor_tensor(out=r.t[:], in0=tf.t[:], scalar=-2 * PI, in1=ang.t[:], op0=ALU.mult, op1=ALU.add),
                  reads=[tf, ang], writes=[r])
            wrap(r)
            cx.op('act', lambda e: nc.scalar.activation(out=outs[1].t[:], in_=r.t[:], func=AF.Sin), reads=[r], writes=[outs[1]])
            cx.op('act', lambda e: nc.scalar.activation(out=outs[2].t[:], in_=r.t[:], func=AF.Sin, scale=-1.0), reads=[r], writes=[outs[2]])
            cx.op('dve', lambda e: nc.vector.tensor_scalar(out=r.t[:], in0=r.t[:], scalar1=PI / 2, scalar2=None, op0=ALU.add),
                  reads=[outs[1], outs[2]], writes=[r])
            wrap(r)
            cx.op('act', lambda e: nc.scalar.activation(out=outs[0].t[:], in_=r.t[:], func=AF.Sin), reads=[r], writes=[outs[0]])
            for (src, j, r0) in ((outs[0], 0, 0), (outs[0], 0, 32), (outs[2], 1, 0), (outs[1], 1, 32)):
                cx.dma('sp', lambda e, src=src, j=j, r0=r0: e.dma_start(out=cs[b, j, r0:r0 + 32, :], in_=src.t[:]), reads=[src], sembuf=src)
        for b in range(NSEQ):
            body(b)
        cx.end_stage(blk)


NPAD = 4224
BIGW = ["ffn1_gate", "ffn1_up", "ffn1_down", "ffn2_gate", "ffn2_up", "ffn2_down", "w_out", "w_ukv",
        "decay_up", "iclr_up", "gate_up"]
BIGW_SHAPES = {"ffn1_gate": [D, DFF], "ffn1_up": [D, DFF], "ffn1_down": [DFF, D], "ffn2_gate": [D, DFF],
               "ffn2_up": [D, DFF], "ffn2_down": [DFF, D], "w_out": [D, D], "w_ukv": [256, 2048],
               "decay_up": [32, 512], "iclr_up": [32, 512], "gate_up": [96, 512]}


def build_program(nlayers=DEPTH, dbg=False):
    nc = bass.Bass("TRN2", target_bir_lowering=False)
    dt = lambda n, s, d=F32, kind="ExternalInput": nc.dram_tensor(n, s, d, kind=kind).ap()
    x = dt("x", [NT, D])
    pos = dt("pos", [NSEQ, S], I32)
    W = {n: dt(n, [DEPTH] + BIGW_SHAPES[n]) for n in BIGW}
    w_inx = dt("w_inx", [DEPTH, D, NPAD])
    wq = dt("wq", [DEPTH, 512, 2048])
    spk = dt("spk", [DEPTH, 128, 192])
    gfin = dt("gfin", [128, KC])
    consts = dt("consts", [128, 1024])
    maskd = dt("maskd", [128, 4, 512])
    invf = dt("invf", [32, 1])
    out = dt("out", [NT, D], kind="ExternalOutput")
    sk = "ExternalOutput" if dbg else "Internal"
    hT = dt("hT", [KC, 128, NT], kind=sk)
    pT = dt("pT", [NPAD, NT], kind=sk)
    yT = dt("yT", [D, NT], BF16, kind=sk)
    cs = dt("cs", [NSEQ, 2, 64, S], kind=sk)
    with contextlib.ExitStack() as es:
        sems = [es.enter_context(nc.semaphore(f"s{i}")) for i in range(96)]
        cx = Ctx(nc, sems)
        stage_rope(cx, pos, invf, cs)
        stage_in(cx, x, hT, consts[:, C_ID:C_ID + 128])
        for l in range(nlayers):
            sp = spk[l]
            stage_ffn(cx, hT, W["ffn1_gate"][l], W["ffn1_up"][l], W["ffn1_down"][l], sp[:, 0:16], consts)
            stage_proj(cx, hT, w_inx[l], sp[:, 16:32], consts, pT, NPAD)
            stage_mla(cx, pT, wq[l], W["w_ukv"][l], sp[:, 48:64], cs, consts, maskd, yT)
            stage_rwkv(cx, pT, sp[:, 64:160], W["decay_up"][l], W["iclr_up"][l], W["gate_up"][l], consts, yT)
            stage_conv(cx, pT, sp[:, 160:176].rearrange("p (a b) -> p a b", a=4), consts, yT)
            stage_wout(cx, hT, yT, W["w_out"][l])
            stage_ffn(cx, hT, W["ffn2_gate"][l], W["ffn2_up"][l], W["ffn2_down"][l], sp[:, 32:48], consts)
        stage_out(cx, hT, gfin, consts, consts[:, C_ID:C_ID + 128], out)
    return nc


def _fm(v, nch):
    return np.ascontiguousarray(np.asarray(v, np.float32).reshape(nch, 128).T)


def _hd(v):
    return np.ascontiguousarray(np.asarray(v, np.float32).reshape(8, 64).T)


def host_layout(inp):
    f32 = np.float32
    w_in = np.asarray(inp["w_in"], f32)
    w_inx = np.zeros((DEPTH, D, NPAD), f32)
    w_inx[:, :, :NIN] = w_in
    w_inx[:, :, NIN:NIN + 32] = w_in[:, :, 800:832]
    w_inx[:, :, NIN + 32:NIN + 64] = w_in[:, :, 768:800]
    w_uq = np.asarray(inp["w_uq"], f32).reshape(DEPTH, 512, 8, 192)
    wq = np.concatenate([w_uq[..., :128], w_uq[..., 128:192], w_uq[..., 160:192], w_uq[..., 128:160]], axis=-1)
    wq = np.ascontiguousarray(wq.reshape(DEPTH, 512, 2048))
    spk = np.zeros((DEPTH, 128, 192), f32)
    for l in range(DEPTH):
        spk[l, :, 0:16] = _fm(inp["norm_ffn1"][l], 16)
        spk[l, :, 16:32] = _fm(inp["norm_mix"][l], 16)
        spk[l, :, 32:48] = _fm(inp["norm_ffn2"][l], 16)
        spk[l, :, 48:52] = _fm(inp["q_norm"][l], 4)
        spk[l, :, 52:54] = _fm(inp["kv_norm"][l], 2)
        spk[l, :, 54:62] = _fm(inp["attn_out_norm"][l], 8)
        rp = spk[l, :, 64:160]
        mu = np.asarray(inp["shift_mu"][l], f32)
        rp[0:64, 0:8] = _hd(mu[0:512])
        rp[0:64, 8:16] = _hd(mu[512:1024])
        rp[0:64, 16:24] = _hd(mu[1024:1536])
        rp[0:64, 24:32] = _hd(inp["decay_w0"][l])
        rp[0:64, 32:40] = _hd(inp["iclr_a0"][l])
        rp[0:64, 40:48] = _hd(inp["k_k"][l])
        rp[0:64, 48:56] = _hd(inp["k_a"][l])
        rp[0:64, 56:64] = _hd(np.asarray(inp["r_k"][l], f32).reshape(512))
        rp[0:64, 64:72] = _hd(inp["lnx_gain"][l])
        rp[0:64, 72:80] = _hd(inp["lnx_bias"][l])
        rp[0:32, 80] = mu[1536:1568]
        rp[0:32, 81] = mu[1568:1600]
        rp[0:96, 82] = mu[1600:1696]
        cw = np.asarray(inp["conv_w"][l], f32)
        cp = np.zeros((128, 4, 4), f32)
        for k in range(3):
            cp[:, :, k] = cw[k].reshape(4, 128).T
        cp[:, :, 3] = np.asarray(inp["conv_out_norm"][l], f32).reshape(4, 128).T
        spk[l, :, 160:176] = cp.reshape(128, 16)
    consts = np.zeros((128, 1024), f32)
    consts[:, C_ONES:C_ONES + 128] = 1.0
    consts[:, C_EPS] = 1e-6
    consts[:, C_EPS + 1] = 64e-5
    consts[0:64, C_BD64:C_BD64 + 64] = 1.0
    consts[64:128, C_BD64 + 64:C_BD64 + 128] = 1.0
    consts[:, C_ID:C_ID + 128] = np.eye(128, dtype=f32)
    i = np.arange(64)
    consts[0:64, C_SU:C_SU + 64] = (i[:, None] < i[None, :])
    consts[0:64, C_SL:C_SL + 64] = (i[:, None] > i[None, :])
    consts[0:64, C_IU:C_IU + 64] = (i[:, None] <= i[None, :])
    rm = np.ones(128, f32)
    rm[0::64] = 0.0
    consts[:, C_RM:C_RM + 128] = rm[None, :]
    kl = np.arange(128)[:, None]
    ql = np.arange(512)[None, :]
    maskd = np.zeros((128, 4, 512), f32)
    for j in range(4):
        maskd[:, j, :] = np.where(128 * j + kl > ql, -30000.0, 0.0)
    invf = (1.0 / (np.float32(10000.0) ** (np.arange(0, 64, 2, dtype=f32) / np.float32(64)))).astype(f32).reshape(32, 1)
    shared = {n: np.ascontiguousarray(np.asarray(inp[n], f32)) for n in BIGW}
    shared.update({"w_inx": w_inx, "wq": wq, "spk": spk, "gfin": _fm(inp["norm_final"], 16), "consts": consts,
                   "maskd": maskd, "invf": invf})
    return shared


_PROG = {}


def kernel(**inp):
    shared = host_layout(inp)
    x = np.asarray(inp["x"], np.float32)
    pos = np.asarray(inp["positions"], np.int32)
    if "nc" not in _PROG:
        _PROG["nc"] = build_program()
    nc = _PROG["nc"]
    in_maps = []
    for c in range(8):
        m = dict(shared)
        m["x"] = np.ascontiguousarray(x[c * NSEQ:(c + 1) * NSEQ].reshape(NT, D))
        m["pos"] = np.ascontiguousarray(pos[c * NSEQ:(c + 1) * NSEQ])
        in_maps.append(m)
    res = run_bass_kernel_spmd(nc, in_maps, core_ids=list(range(8)))
    out = np.stack([np.asarray(r["out"]).reshape(NSEQ, S, D) for r in res.results], axis=0)
    return out.reshape(16, S, D).astype(np.float32)
```

```python
import contextlib
import numpy as np
import concourse.bass as bass
import concourse.mybir as mybir
from concourse.bass_utils import run_bass_kernel_spmd

F32 = mybir.dt.float32
BF16 = mybir.dt.bfloat16
I32 = mybir.dt.int32
AF = mybir.ActivationFunctionType
ALU = mybir.AluOpType
AX = mybir.AxisListType

D = 2048
DFF = 5632
S = 2048
NSEQ = 2
NT = NSEQ * S
DEPTH = 4
KC = D // 128
FC = DFF // 128
EPS = 1e-6
NIN = 4064
NINX = 4128


class Sem:
    def __init__(self, h):
        self.h = h
        self.n = 0


class Buf:
    def __init__(self, cx, t, dma=False):
        self.t = t
        self.wr = None
        self.rd = []
        self.sem = cx.new_sem() if dma else None

    def __getitem__(self, k):
        return self.t[k]


class Ctx:
    ENG = ('pe', 'act', 'dve', 'pool', 'sp')

    def __init__(self, nc, sem_handles):
        self.nc = nc
        self.sems = [Sem(h) for h in sem_handles]
        self.free = list(self.sems)
        self.esem = {e: self.new_sem() for e in ('pe', 'act', 'dve', 'pool')}
        self.q = {e: [] for e in self.ENG}
        self.seen = {e: {} for e in self.ENG}
        self.stage_sems = []
        self.pending_dma = {e: [] for e in self.ENG}

    def new_sem(self):
        return self.free.pop()

    def begin_stage(self):
        self.mark = len(self.free)
        self.taken = []

    def buf(self, t, dma=False):
        b = Buf.__new__(Buf)
        b.t = t
        b.wr = None
        b.rd = []
        b.sem = None
        if dma:
            b.sem = self.free.pop()
            self.taken.append(b.sem)
        return b

    def end_stage(self, block):
        for e in self.ENG:
            for tok in self.pending_dma[e]:
                self._wait(e, tok)
            self.pending_dma[e] = []
        self.flush(block)
        self.free.extend(self.taken)
        self.taken = []

    def _wait(self, eng, tok):
        if tok is None:
            return
        sem, val, src = tok
        if src == eng and src in ('pe', 'act', 'dve'):
            return
        if self.seen[eng].get(id(sem), 0) >= val:
            return
        self.seen[eng][id(sem)] = val
        self.q[eng].append(('wait', sem, val))

    def _deps(self, eng, reads, writes):
        best = {}
        toks = [b.wr for b in reads] + [b.wr for b in writes]
        for b in writes:
            toks.extend(b.rd)
        for t in toks:
            if t is None:
                continue
            k = id(t[0])
            if k not in best or best[k][1] < t[1]:
                best[k] = t
        for t in best.values():
            self._wait(eng, t)

    def op(self, eng, fn, reads=(), writes=(), sig=True):
        self._deps(eng, reads, writes)
        tok = None
        if sig:
            s = self.esem[eng]
            s.n += 1
            tok = (s, s.n, eng)
            self.q[eng].append(('op', fn, s, 1))
        else:
            self.q[eng].append(('op', fn, None, 0))
        for b in writes:
            b.wr = tok
            b.rd = []
        for b in reads:
            if tok is not None:
                b.rd = [t for t in b.rd if t[0] is not tok[0]] + [tok]
        return tok

    def dma(self, eng, fn, reads=(), writes=(), sembuf=None):
        s = sembuf.sem
        saved = []
        for b in writes:
            if b.wr is not None and b.wr[2] == 'dma' and b.wr[0] is s and not b.rd:
                saved.append((b, b.wr))
                b.wr = None
        self._deps(eng, reads, writes)
        for b, w in saved:
            b.wr = w
        s.n += 16
        tok = (s, s.n, 'dma')
        self.q[eng].append(('op', fn, s, 16))
        for b in writes:
            b.wr = tok
            b.rd = []
        for b in reads:
            b.rd = [t for t in b.rd if t[0] is not tok[0]] + [tok]
        if not writes:
            self.pending_dma[eng].append(tok)
        return tok

    def flush(self, block):
        m = {'pe': block.tensor, 'act': block.scalar, 'dve': block.vector,
             'pool': block.gpsimd, 'sp': block.sync}
        for e in self.ENG:
            lst = self.q[e]
            if not lst:
                continue

            def body(eng, lst=lst):
                for it in lst:
                    if it[0] == 'wait':
                        eng.wait_ge(it[1].h, it[2])
                    else:
                        ins = it[1](eng)
                        if it[2] is not None:
                            ins.then_inc(it[2].h, it[3])
            m[e](body)
            self.q[e] = []


class Ring:
    def __init__(self, bufs):
        self.bufs = bufs
        self.i = 0

    def next(self):
        b = self.bufs[self.i % len(self.bufs)]
        self.i += 1
        return b


def mm_group(cx, out_buf, pairs, reads, out_ap=None):
    nc = cx.nc
    oap = out_buf.t[:] if out_ap is None else out_ap
    n = len(pairs)

    def fn(eng):
        ins = None
        for i, (l, r) in enumerate(pairs):
            ins = nc.tensor.matmul(oap, l, r, start=(i == 0), stop=(i == n - 1))
        return ins
    return cx.op('pe', fn, reads=reads, writes=[out_buf])


_UID = [0]


def alloc(es, nc, name, shape, dt, psum=False):
    _UID[0] += 1
    name = f"{name}_{_UID[0]}"
    if psum:
        return es.enter_context(nc.psum_tensor(name, shape, dt))
    return es.enter_context(nc.sbuf_tensor(name, shape, dt))


def stage_in(cx, x, hT, ident_dram):
    nc = cx.nc
    with contextlib.ExitStack() as es:
        cx.begin_stage()
        ident = cx.buf(alloc(es, nc, "ident", [128, 128], F32), dma=True)
        xin = Ring([cx.buf(alloc(es, nc, f"xin{i}", [128, D], F32), dma=True) for i in range(3)])
        xo = Ring([cx.buf(alloc(es, nc, f"xo{i}", [128, KC, 128], F32), dma=True) for i in range(3)])
        ps = Ring([cx.buf(alloc(es, nc, f"ps{i}", [128, 512], F32, psum=True)) for i in range(4)])
        blk = es.enter_context(nc.Block())
        cx.dma('sp', lambda e: e.dma_start(out=ident.t[:], in_=ident_dram[:, :]), writes=[ident], sembuf=ident)
        for tt in range(NT // 128):
            xb = xin.next()
            cx.dma('sp', lambda e, xb=xb, tt=tt: e.dma_start(out=xb.t[:], in_=x[tt * 128:(tt + 1) * 128, :]),
                   writes=[xb], sembuf=xb)
            ob = xo.next()
            for g in range(KC // 4):
                p = ps.next()

                def fn(eng, p=p, xb=xb, g=g):
                    ins = None
                    for j in range(4):
                        kc = g * 4 + j
                        ins = nc.tensor.transpose(p.t[:, j * 128:(j + 1) * 128], xb.t[:, kc * 128:(kc + 1) * 128], ident.t[:])
                    return ins
                cx.op('pe', fn, reads=[xb, ident], writes=[p])
                eng = 'dve' if g % 2 == 0 else 'act'
                if eng == 'dve':
                    cx.op('dve', lambda e, p=p, ob=ob, g=g: nc.vector.tensor_copy(
                        ob.t[:, g * 4:(g + 1) * 4, :], p.t[:].rearrange("p (a b) -> p a b", a=4)), reads=[p], writes=[ob])
                else:
                    cx.op('act', lambda e, p=p, ob=ob, g=g: nc.scalar.copy(
                        ob.t[:, g * 4:(g + 1) * 4, :], p.t[:].rearrange("p (a b) -> p a b", a=4)), reads=[p], writes=[ob])
            cx.dma('sp', lambda e, ob=ob, tt=tt: e.dma_start(
                out=hT[:, :, tt * 128:(tt + 1) * 128].rearrange("k p t -> p k t"), in_=ob.t[:]),
                reads=[ob], sembuf=ob)
        cx.end_stage(blk)


def rms_stats(cx, nc, chunks_fn, nchunks, T, hring, sqring, ps_ss, ones_f, nfeat, eps_t, sd, rstd, src_rows=128):
    nsub = T // 512
    for kc in range(nchunks):
        hb = hring.next()
        cx.dma('sp', chunks_fn(kc, hb), writes=[hb], sembuf=hb)
        sq = sqring.next()
        cx.op('act', lambda e, hb=hb, sq=sq: nc.scalar.activation(out=sq.t[:src_rows, :T], in_=hb.t[:src_rows, :T], func=AF.Square),
              reads=[hb], writes=[sq])
        for sub in range(nsub):
            p = ps_ss[sub]

            def fn(eng, p=p, sq=sq, sub=sub, kc=kc):
                return nc.tensor.matmul(p.t[:], ones_f.t[:src_rows, :], sq.t[:src_rows, sub * 512:(sub + 1) * 512],
                                        start=(kc == 0), stop=(kc == nchunks - 1))
            cx.op('pe', fn, reads=[sq, ones_f], writes=[p] if kc == 0 else [], sig=True)
            if kc != 0:
                pass
        if kc == nchunks - 1:
            last_tok = (cx.esem['pe'], cx.esem['pe'].n, 'pe')
            for sub in range(nsub):
                ps_ss[sub].wr = last_tok
    for sub in range(nsub):
        p = ps_ss[sub]
        cx.op('act', lambda e, p=p, sub=sub: nc.scalar.activation(
            out=sd.t[:, sub * 512:(sub + 1) * 512], in_=p.t[:], func=AF.Sqrt, bias=eps_t.t[:, 0:1], scale=1.0 / nfeat),
            reads=[p, eps_t], writes=[sd] if sub == 0 else [])
    sd.wr = (cx.esem['act'], cx.esem['act'].n, 'act')
    cx.op('dve', lambda e: nc.vector.reciprocal(rstd.t[:, :T], sd.t[:, :T]), reads=[sd], writes=[rstd])


def stage_ffn(cx, hT, wg, wu, wd, gvec, consts, T=1024, tiles=None):
    nc = cx.nc
    nsub = T // 512
    NJ = 256
    with contextlib.ExitStack() as es:
        cx.begin_stage()
        ones_f = cx.buf(alloc(es, nc, "ones_f", [128, 128], F32), dma=True)
        eps_t = cx.buf(alloc(es, nc, "eps_t", [128, 1], F32), dma=True)
        g_t = cx.buf(alloc(es, nc, "g_t", [128, KC], F32), dma=True)
        xn = cx.buf(alloc(es, nc, "xn", [128, KC, T], BF16))
        act = [cx.buf(alloc(es, nc, f"act{j}", [128, T], BF16)) for j in range(FC)]
        wring = Ring([cx.buf(alloc(es, nc, f"w{i}", [128, 16, NJ], BF16), dma=True) for i in range(6)])
        hring = Ring([cx.buf(alloc(es, nc, f"hb{i}", [128, T], F32), dma=True) for i in range(2)])
        sqring = Ring([cx.buf(alloc(es, nc, f"sq{i}", [128, T], F32)) for i in range(2)])
        rstd = cx.buf(alloc(es, nc, "rstd", [128, T], F32))
        sd = rstd
        sgr = Ring([cx.buf(alloc(es, nc, f"sg{i}", [128, 512], F32)) for i in range(2)])
        outr = Ring([cx.buf(alloc(es, nc, f"ob{i}", [128, 512], F32), dma=True) for i in range(2)])
        hres = Ring([cx.buf(alloc(es, nc, f"hr{i}", [128, 512], F32), dma=True) for i in range(2)])
        ps_g = Ring([cx.buf(alloc(es, nc, f"psg{i}", [128, 512], F32, psum=True)) for i in range(2)])
        ps_u = Ring([cx.buf(alloc(es, nc, f"psu{i}", [128, 512], F32, psum=True)) for i in range(2)])
        ps_d = Ring([cx.buf(alloc(es, nc, f"psd{i}", [128, 512], F32, psum=True)) for i in range(2)])
        ps_ss = [cx.buf(alloc(es, nc, f"pss{i}", [128, 512], F32, psum=True)) for i in range(nsub)]
        blk = es.enter_context(nc.Block())

        cx.dma('sp', lambda e: e.dma_start(out=ones_f.t[:], in_=consts[:, 0:128]), writes=[ones_f], sembuf=ones_f)
        cx.dma('sp', lambda e: e.dma_start(out=eps_t.t[:], in_=consts[:, 128:129], allow_slow_non_contiguous=True), writes=[eps_t], sembuf=eps_t)
        cx.dma('sp', lambda e: e.dma_start(out=g_t.t[:], in_=gvec[:, :]), writes=[g_t], sembuf=g_t)
        wgv = wg.rearrange("(kc p) n -> p kc n", p=128)
        wuv = wu.rearrange("(kc p) n -> p kc n", p=128)
        wdv = wd.rearrange("(j p) n -> p j n", p=128)
        tl = list(range(NT // T)) if tiles is None else tiles
        def tile_body(t0):
            rms_stats(cx, nc, lambda kc, hb: (lambda e: e.dma_start(out=hb.t[:, :T], in_=hT[kc, :, t0:t0 + T])),
                      KC, T, hring, sqring, ps_ss, ones_f, D, eps_t, sd, rstd)
            for kc in range(KC):
                hb = hring.next()
                cx.dma('sp', lambda e, hb=hb, kc=kc: e.dma_start(out=hb.t[:, :T], in_=hT[kc, :, t0:t0 + T]),
                       writes=[hb], sembuf=hb)
                cx.op('dve', lambda e, hb=hb, kc=kc: nc.vector.scalar_tensor_tensor(
                    out=xn.t[:, kc, :], in0=hb.t[:, :T], scalar=g_t.t[:, kc:kc + 1], in1=rstd.t[:, :T],
                    op0=ALU.mult, op1=ALU.mult), reads=[hb, g_t, rstd], writes=[xn] if kc == 0 else [])
            xn.wr = (cx.esem['dve'], cx.esem['dve'].n, 'dve')
            for jt in range(DFF // NJ):
                wgb = wring.next()
                cx.dma('pool', lambda e, b=wgb, jt=jt: e.dma_start(out=b.t[:], in_=wgv[:, :, jt * NJ:(jt + 1) * NJ]),
                       writes=[wgb], sembuf=wgb)
                wub = wring.next()
                cx.dma('pool', lambda e, b=wub, jt=jt: e.dma_start(out=b.t[:], in_=wuv[:, :, jt * NJ:(jt + 1) * NJ]),
                       writes=[wub], sembuf=wub)
                for jj in range(NJ // 128):
                    j = jt * (NJ // 128) + jj
                    for sub in range(nsub):
                        pg = ps_g.next()
                        pu = ps_u.next()
                        mm_group(cx, pg, [(wgb.t[:, kc, jj * 128:(jj + 1) * 128], xn.t[:, kc, sub * 512:(sub + 1) * 512])
                                          for kc in range(KC)], reads=[wgb, xn])
                        mm_group(cx, pu, [(wub.t[:, kc, jj * 128:(jj + 1) * 128], xn.t[:, kc, sub * 512:(sub + 1) * 512])
                                          for kc in range(KC)], reads=[wub, xn])
                        sg = sgr.next()
                        cx.op('act', lambda e, pg=pg, sg=sg: nc.scalar.activation(out=sg.t[:], in_=pg.t[:], func=AF.Silu),
                              reads=[pg], writes=[sg])
                        cx.op('dve', lambda e, sg=sg, pu=pu, j=j, sub=sub: nc.vector.tensor_tensor(
                            out=act[j].t[:, sub * 512:(sub + 1) * 512], in0=sg.t[:], in1=pu.t[:], op=ALU.mult),
                            reads=[sg, pu], writes=[act[j]])
            JD = 11
            for ct in range(D // NJ):
                wds = []
                for q4 in range(FC // JD):
                    wb = wring.next()
                    cx.dma('pool', lambda e, b=wb, q4=q4, ct=ct: e.dma_start(
                        out=b.t[:, 0:JD, :], in_=wdv[:, q4 * JD:(q4 + 1) * JD, ct * NJ:(ct + 1) * NJ]),
                        writes=[wb], sembuf=wb)
                    wds.append(wb)
                for cc in range(NJ // 128):
                    c = ct * (NJ // 128) + cc
                    for sub in range(nsub):
                        hr = hres.next()
                        cx.dma('sp', lambda e, hr=hr, c=c, sub=sub: e.dma_start(
                            out=hr.t[:], in_=hT[c, :, t0 + sub * 512:t0 + (sub + 1) * 512]), writes=[hr], sembuf=hr)
                        pd = ps_d.next()
                        mm_group(cx, pd, [(wds[j // JD].t[:, j % JD, cc * 128:(cc + 1) * 128], act[j].t[:, sub * 512:(sub + 1) * 512])
                                          for j in range(FC)], reads=wds + act)
                        ob = outr.next()
                        cx.op('dve', lambda e, pd=pd, hr=hr, ob=ob: nc.vector.scalar_tensor_tensor(
                            out=ob.t[:], in0=pd.t[:], scalar=0.5, in1=hr.t[:], op0=ALU.mult, op1=ALU.add),
                            reads=[pd, hr], writes=[ob])
                        cx.dma('sp', lambda e, ob=ob, c=c, sub=sub: e.dma_start(
                            out=hT[c, :, t0 + sub * 512:t0 + (sub + 1) * 512], in_=ob.t[:]), reads=[ob], sembuf=ob)
        for ti in tl:
            tile_body(ti * T)
        cx.end_stage(blk)


def stage_out(cx, hT, gvec, consts, ident_dram, out):
    nc = cx.nc
    T = 512
    with contextlib.ExitStack() as es:
        cx.begin_stage()
        ones_f = cx.buf(alloc(es, nc, "ones_f", [128, 128], F32), dma=True)
        ident = cx.buf(alloc(es, nc, "ident", [128, 128], F32), dma=True)
        eps_t = cx.buf(alloc(es, nc, "eps_t", [128, 1], F32), dma=True)
        g_t = cx.buf(alloc(es, nc, "g_t", [128, KC], F32), dma=True)
        hring = Ring([cx.buf(alloc(es, nc, f"hb{i}", [128, T], F32), dma=True) for i in range(3)])
        sqring = Ring([cx.buf(alloc(es, nc, f"sq{i}", [128, T], F32)) for i in range(2)])
        sd = cx.buf(alloc(es, nc, "sd", [128, T], F32))
        rstd = cx.buf(alloc(es, nc, "rstd", [128, T], F32))
        xn = cx.buf(alloc(es, nc, "xnf", [128, KC, T], F32))
        ps_ss = [cx.buf(alloc(es, nc, "pss0", [128, 512], F32, psum=True))]
        ps = Ring([cx.buf(alloc(es, nc, f"ps{i}", [128, 512], F32, psum=True)) for i in range(4)])
        orow = Ring([cx.buf(alloc(es, nc, f"orow{i}", [128, D], F32), dma=True) for i in range(3)])
        blk = es.enter_context(nc.Block())
        cx.dma('sp', lambda e: e.dma_start(out=ones_f.t[:], in_=consts[:, 0:128]), writes=[ones_f], sembuf=ones_f)
        cx.dma('sp', lambda e: e.dma_start(out=eps_t.t[:], in_=consts[:, 128:129], allow_slow_non_contiguous=True), writes=[eps_t], sembuf=eps_t)
        cx.dma('sp', lambda e: e.dma_start(out=g_t.t[:], in_=gvec[:, :]), writes=[g_t], sembuf=g_t)
        cx.dma('sp', lambda e: e.dma_start(out=ident.t[:], in_=ident_dram[:, :]), writes=[ident], sembuf=ident)
        def tile_body(t0):
            rms_stats(cx, nc, lambda kc, hb: (lambda e: e.dma_start(out=hb.t[:, :T], in_=hT[kc, :, t0:t0 + T])),
                      KC, T, hring, sqring, ps_ss, ones_f, D, eps_t, sd, rstd)
            for kc in range(KC):
                hb = hring.next()
                cx.dma('sp', lambda e, hb=hb, kc=kc: e.dma_start(out=hb.t[:, :T], in_=hT[kc, :, t0:t0 + T]),
                       writes=[hb], sembuf=hb)
                cx.op('dve', lambda e, hb=hb, kc=kc: nc.vector.scalar_tensor_tensor(
                    out=xn.t[:, kc, :], in0=hb.t[:, :T], scalar=g_t.t[:, kc:kc + 1], in1=rstd.t[:, :T],
                    op0=ALU.mult, op1=ALU.mult), reads=[hb, g_t, rstd], writes=[xn] if kc == 0 else [])
            xn.wr = (cx.esem['dve'], cx.esem['dve'].n, 'dve')
            for tb in range(T // 128):
                ob = orow.next()
                for g in range(KC // 4):
                    p = ps.next()

                    def fn(eng, p=p, g=g, tb=tb):
                        ins = None
                        for j in range(4):
                            kc = g * 4 + j
                            ins = nc.tensor.transpose(p.t[:, j * 128:(j + 1) * 128], xn.t[:, kc, tb * 128:(tb + 1) * 128], ident.t[:])
                        return ins
                    cx.op('pe', fn, reads=[xn, ident], writes=[p])
                    if g % 2 == 0:
                        cx.op('dve', lambda e, p=p, ob=ob, g=g: nc.vector.tensor_copy(ob.t[:, g * 512:(g + 1) * 512], p.t[:]),
                              reads=[p], writes=[ob])
                    else:
                        cx.op('act', lambda e, p=p, ob=ob, g=g: nc.scalar.copy(ob.t[:, g * 512:(g + 1) * 512], p.t[:]),
                              reads=[p], writes=[ob])
                cx.dma('sp', lambda e, ob=ob, tb=tb: e.dma_start(out=out[t0 + tb * 128:t0 + (tb + 1) * 128, :], in_=ob.t[:]),
                       reads=[ob], sembuf=ob)
        for ti in range(NT // T):
            tile_body(ti * T)
        cx.end_stage(blk)


def load_consts(cx, es, nc, consts, ident_dram=None):
    ones_f = cx.buf(alloc(es, nc, "ones_f", [128, 128], F32), dma=True)
    eps_t = cx.buf(alloc(es, nc, "eps_t", [128, 1], F32), dma=True)
    cx.dma('sp', lambda e: e.dma_start(out=ones_f.t[:], in_=consts[:, 0:128]), writes=[ones_f], sembuf=ones_f)
    cx.dma('sp', lambda e: e.dma_start(out=eps_t.t[:], in_=consts[:, 128:129], allow_slow_non_contiguous=True),
           writes=[eps_t], sembuf=eps_t)
    return ones_f, eps_t


def stage_proj(cx, hT, w, gvec, consts, pT, ncols, T=1024):
    nc = cx.nc
    nsub = T // 512
    NJ = 384
    with contextlib.ExitStack() as es:
        cx.begin_stage()
        ones_f, eps_t = load_consts(cx, es, nc, consts)
        g_t = cx.buf(alloc(es, nc, "g_t", [128, KC], F32), dma=True)
        xn = cx.buf(alloc(es, nc, "xn", [128, KC, T], BF16))
        wring = Ring([cx.buf(alloc(es, nc, f"w{i}", [128, 16, NJ], BF16), dma=True) for i in range(3)])
        hring = Ring([cx.buf(alloc(es, nc, f"hb{i}", [128, T], F32), dma=True) for i in range(2)])
        sqring = Ring([cx.buf(alloc(es, nc, f"sq{i}", [128, T], F32)) for i in range(2)])
        rstd = cx.buf(alloc(es, nc, "rstd", [128, T], F32))
        outr = Ring([cx.buf(alloc(es, nc, f"ob{i}", [128, 512], F32), dma=True) for i in range(4)])
        ps_o = Ring([cx.buf(alloc(es, nc, f"pso{i}", [128, 512], F32, psum=True)) for i in range(4)])
        ps_ss = [cx.buf(alloc(es, nc, f"pss{i}", [128, 512], F32, psum=True)) for i in range(nsub)]
        blk = es.enter_context(nc.Block())
        cx.dma('sp', lambda e: e.dma_start(out=g_t.t[:], in_=gvec[:, :]), writes=[g_t], sembuf=g_t)
        wv = w.rearrange("(kc p) n -> p kc n", p=128)

        def tile_body(t0):
            rms_stats(cx, nc, lambda kc, hb: (lambda e: e.dma_start(out=hb.t[:, :T], in_=hT[kc, :, t0:t0 + T])),
                      KC, T, hring, sqring, ps_ss, ones_f, D, eps_t, rstd, rstd)
            for kc in range(KC):
                hb = hring.next()
                cx.dma('sp', lambda e, hb=hb, kc=kc: e.dma_start(out=hb.t[:, :T], in_=hT[kc, :, t0:t0 + T]),
                       writes=[hb], sembuf=hb)
                cx.op('dve', lambda e, hb=hb, kc=kc: nc.vector.scalar_tensor_tensor(
                    out=xn.t[:, kc, :], in0=hb.t[:, :T], scalar=g_t.t[:, kc:kc + 1], in1=rstd.t[:, :T],
                    op0=ALU.mult, op1=ALU.mult), reads=[hb, g_t, rstd], writes=[xn] if kc == 0 else [])
            xn.wr = (cx.esem['dve'], cx.esem['dve'].n, 'dve')
            for jt in range(ncols // NJ):
                wb = wring.next()
                cx.dma('pool', lambda e, b=wb, jt=jt: e.dma_start(out=b.t[:], in_=wv[:, :, jt * NJ:(jt + 1) * NJ]),
                       writes=[wb], sembuf=wb)
                for jj in range(NJ // 128):
                    r0 = jt * NJ + jj * 128
                    for sub in range(nsub):
                        po = ps_o.next()
                        mm_group(cx, po, [(wb.t[:, kc, jj * 128:(jj + 1) * 128], xn.t[:, kc, sub * 512:(sub + 1) * 512])
                                          for kc in range(KC)], reads=[wb, xn])
                        ob = outr.next()
                        if (jj + sub) % 2 == 0:
                            cx.op('act', lambda e, po=po, ob=ob: nc.scalar.copy(ob.t[:], po.t[:]), reads=[po], writes=[ob])
                        else:
                            cx.op('dve', lambda e, po=po, ob=ob: nc.vector.tensor_copy(ob.t[:], po.t[:]), reads=[po], writes=[ob])
                        cx.dma('sp', lambda e, ob=ob, r0=r0, sub=sub: e.dma_start(
                            out=pT[r0:r0 + 128, t0 + sub * 512:t0 + (sub + 1) * 512], in_=ob.t[:]), reads=[ob], sembuf=ob)
        for ti in range(NT // T):
            tile_body(ti * T)
        cx.end_stage(blk)


def stage_wout(cx, hT, yT, w, T=1024):
    nc = cx.nc
    nsub = T // 512
    NJ = 256
    with contextlib.ExitStack() as es:
        cx.begin_stage()
        yb = cx.buf(alloc(es, nc, "yb", [128, KC, T], BF16), dma=True)
        wring = Ring([cx.buf(alloc(es, nc, f"w{i}", [128, 16, NJ], BF16), dma=True) for i in range(3)])
        outr = Ring([cx.buf(alloc(es, nc, f"ob{i}", [128, 512], F32), dma=True) for i in range(3)])
        hres = Ring([cx.buf(alloc(es, nc, f"hr{i}", [128, 512], F32), dma=True) for i in range(3)])
        ps_o = Ring([cx.buf(alloc(es, nc, f"pso{i}", [128, 512], F32, psum=True)) for i in range(4)])
        blk = es.enter_context(nc.Block())
        wv = w.rearrange("(kc p) n -> p kc n", p=128)
        yv = yT.rearrange("(kc p) t -> p kc t", p=128)

        def tile_body(t0):
            cx.dma('sp', lambda e: e.dma_start(out=yb.t[:], in_=yv[:, :, t0:t0 + T]), writes=[yb], sembuf=yb)
            for jt in range(D // NJ):
                wb = wring.next()
                cx.dma('pool', lambda e, b=wb, jt=jt: e.dma_start(out=b.t[:], in_=wv[:, :, jt * NJ:(jt + 1) * NJ]),
                       writes=[wb], sembuf=wb)
                for jj in range(NJ // 128):
                    c = jt * (NJ // 128) + jj
                    for sub in range(nsub):
                        hr = hres.next()
                        cx.dma('sp', lambda e, hr=hr, c=c, sub=sub: e.dma_start(
                            out=hr.t[:], in_=hT[c, :, t0 + sub * 512:t0 + (sub + 1) * 512]), writes=[hr], sembuf=hr)
                        po = ps_o.next()
                        mm_group(cx, po, [(wb.t[:, kc, jj * 128:(jj + 1) * 128], yb.t[:, kc, sub * 512:(sub + 1) * 512])
                                          for kc in range(KC)], reads=[wb, yb])
                        ob = outr.next()
                        cx.op('dve', lambda e, po=po, hr=hr, ob=ob: nc.vector.tensor_tensor(
                            out=ob.t[:], in0=po.t[:], in1=hr.t[:], op=ALU.add), reads=[po, hr], writes=[ob])
                        cx.dma('sp', lambda e, ob=ob, c=c, sub=sub: e.dma_start(
                            out=hT[c, :, t0 + sub * 512:t0 + (sub + 1) * 512], in_=ob.t[:]), reads=[ob], sembuf=ob)
        for ti in range(NT // T):
            tile_body(ti * T)
        cx.end_stage(blk)


C_ONES, C_EPS, C_BD64, C_ID, C_SU, C_SL, C_IU, C_RM = 0, 128, 256, 384, 512, 576, 640, 704
B0 = 832
C0 = 2528
R_KRS = 4064


def stage_conv(cx, pT, prm, consts, yT):
    nc = cx.nc
    with contextlib.ExitStack() as es:
        cx.begin_stage()
        bd = cx.buf(alloc(es, nc, "bd", [128, 128], F32), dma=True)
        eps_t = cx.buf(alloc(es, nc, "eps_t", [128, 1], F32), dma=True)
        pr = cx.buf(alloc(es, nc, "pr", [128, 4, 4], F32), dma=True)
        bg = Ring([cx.buf(alloc(es, nc, f"bg{i}", [128, S], F32), dma=True) for i in range(2)])
        cg = Ring([cx.buf(alloc(es, nc, f"cg{i}", [128, S], F32), dma=True) for i in range(2)])
        hh = Ring([cx.buf(alloc(es, nc, f"hh{i}", [128, S], F32), dma=True) for i in range(2)])
        u = cx.buf(alloc(es, nc, "u", [128, S + 2], F32))
        y = cx.buf(alloc(es, nc, "y", [128, S], F32))
        z = cx.buf(alloc(es, nc, "z", [128, S], F32))
        zsq = cx.buf(alloc(es, nc, "zsq", [128, S], F32))
        rs = cx.buf(alloc(es, nc, "rs", [128, S], F32))
        ob = Ring([cx.buf(alloc(es, nc, f"ob{i}", [128, S], BF16), dma=True) for i in range(2)])
        ps = Ring([cx.buf(alloc(es, nc, f"ps{i}", [128, 512], F32, psum=True)) for i in range(4)])
        blk = es.enter_context(nc.Block())
        cx.dma('sp', lambda e: e.dma_start(out=bd.t[:], in_=consts[:, C_BD64:C_BD64 + 128]), writes=[bd], sembuf=bd)
        cx.dma('sp', lambda e: e.dma_start(out=eps_t.t[:], in_=consts[:, C_EPS:C_EPS + 1], allow_slow_non_contiguous=True),
               writes=[eps_t], sembuf=eps_t)
        cx.dma('sp', lambda e: e.dma_start(out=pr.t[:], in_=prm[:, :, :]), writes=[pr], sembuf=pr)
        cx.op('dve', lambda e: nc.vector.memset(u.t[:, 0:2], 0.0), writes=[u])

        def body(b, ch):
            t0 = b * S
            bgb, cgb, hhb = bg.next(), cg.next(), hh.next()
            for (buf, r0) in ((bgb, C0 + ch * 128), (cgb, C0 + 512 + ch * 128), (hhb, C0 + 1024 + ch * 128)):
                cx.dma('sp', lambda e, buf=buf, r0=r0: e.dma_start(out=buf.t[:], in_=pT[r0:r0 + 128, t0:t0 + S]),
                       writes=[buf], sembuf=buf)
            cx.op('dve', lambda e: nc.vector.tensor_tensor(out=u.t[:, 2:S + 2], in0=cgb.t[:], in1=hhb.t[:], op=ALU.mult),
                  reads=[cgb, hhb], writes=[u])
            cx.op('act', lambda e: nc.scalar.activation(out=y.t[:], in_=u.t[:, 2:S + 2], func=AF.Copy, scale=pr.t[:, ch, 2:3]),
                  reads=[u, pr], writes=[y])
            cx.op('dve', lambda e: nc.vector.scalar_tensor_tensor(out=y.t[:], in0=u.t[:, 1:S + 1], scalar=pr.t[:, ch, 1:2],
                                                                  in1=y.t[:], op0=ALU.mult, op1=ALU.add), reads=[u, pr], writes=[y])
            cx.op('dve', lambda e: nc.vector.scalar_tensor_tensor(out=y.t[:], in0=u.t[:, 0:S], scalar=pr.t[:, ch, 0:1],
                                                                  in1=y.t[:], op0=ALU.mult, op1=ALU.add), reads=[u, pr], writes=[y])
            cx.op('dve', lambda e: nc.vector.tensor_tensor(out=z.t[:], in0=bgb.t[:], in1=y.t[:], op=ALU.mult),
                  reads=[bgb, y], writes=[z])
            cx.op('act', lambda e: nc.scalar.activation(out=zsq.t[:], in_=z.t[:], func=AF.Square), reads=[z], writes=[zsq])
            for sub in range(S // 512):
                p = ps.next()
                sl = slice(sub * 512, (sub + 1) * 512)
                cx.op('pe', lambda e, p=p, sl=sl: nc.tensor.matmul(p.t[:], bd.t[:], zsq.t[:, sl], start=True, stop=True),
                      reads=[bd, zsq], writes=[p])
                cx.op('act', lambda e, p=p, sl=sl: nc.scalar.activation(out=rs.t[:, sl], in_=p.t[:], func=AF.Sqrt,
                                                                        bias=eps_t.t[:, 0:1], scale=1.0 / 64),
                      reads=[p, eps_t], writes=[rs])
            cx.op('dve', lambda e: nc.vector.reciprocal(rs.t[:], rs.t[:]), reads=[rs], writes=[rs])
            o = ob.next()
            cx.op('dve', lambda e, o=o: nc.vector.scalar_tensor_tensor(out=o.t[:], in0=z.t[:], scalar=pr.t[:, ch, 3:4],
                                                                       in1=rs.t[:], op0=ALU.mult, op1=ALU.mult),
                  reads=[z, pr, rs], writes=[o])
            cx.dma('sp', lambda e, o=o: e.dma_start(out=yT[1536 + ch * 128:1536 + (ch + 1) * 128, t0:t0 + S], in_=o.t[:]),
                   reads=[o], sembuf=o)
        for b in range(NSEQ):
            for ch in range(4):
                body(b, ch)
        cx.end_stage(blk)


def stage_mla(cx, pT, wq, wkv, prm, cs, consts, maskd, yT):
    nc = cx.nc
    T = 1024
    scale = float((128 + 64) ** -0.5)
    with contextlib.ExitStack() as es:
        cx.begin_stage()
        ones_f, eps_t = load_consts(cx, es, nc, consts)
        ones_b = cx.buf(alloc(es, nc, "ones_b", [128, 128], BF16), dma=True)
        id_b = cx.buf(alloc(es, nc, "id_b", [128, 128], BF16), dma=True)
        mask = cx.buf(alloc(es, nc, "mask", [128, 4, 512], BF16), dma=True)
        pr = cx.buf(alloc(es, nc, "pr", [128, 16], F32), dma=True)
        wqb = cx.buf(alloc(es, nc, "wqb", [128, 4, 2048], BF16), dma=True)
        wkvb = cx.buf(alloc(es, nc, "wkvb", [128, 2, 2048], BF16), dma=True)
        cqn = cx.buf(alloc(es, nc, "cqn", [128, 4, S], BF16))
        ckvn = cx.buf(alloc(es, nc, "ckvn", [128, 2, S], BF16))
        cc = cx.buf(alloc(es, nc, "cc", [64, S], F32), dma=True)
        ss = cx.buf(alloc(es, nc, "ss", [64, S], F32), dma=True)
        kx = cx.buf(alloc(es, nc, "kx", [64, S], F32), dma=True)
        kxs = cx.buf(alloc(es, nc, "kxs", [64, S], F32), dma=True)
        k_r = cx.buf(alloc(es, nc, "k_r", [64, S], BF16))
        q_n = cx.buf(alloc(es, nc, "q_n", [128, S], BF16))
        q_r = cx.buf(alloc(es, nc, "q_r", [64, S], BF16))
        k_n = cx.buf(alloc(es, nc, "k_n", [128, S], BF16))
        V = cx.buf(alloc(es, nc, "V", [128, 16, 128], BF16))
        xt = cx.buf(alloc(es, nc, "xt", [64, 512], F32))
        PT = Ring([cx.buf(alloc(es, nc, f"PT{i}", [128, 512], BF16)) for i in range(3)])
        hring = Ring([cx.buf(alloc(es, nc, f"hb{i}", [128, T], F32), dma=True) for i in range(2)])
        sqring = Ring([cx.buf(alloc(es, nc, f"sq{i}", [128, T], F32)) for i in range(2)])
        rstd = cx.buf(alloc(es, nc, "rstd", [128, T], F32))
        rinv = cx.buf(alloc(es, nc, "rinv", [128, 512], F32))
        yv = cx.buf(alloc(es, nc, "yv", [128, 512], F32))
        ysq = cx.buf(alloc(es, nc, "ysq", [128, 512], F32))
        sd2 = cx.buf(alloc(es, nc, "sd2", [128, 512], F32))
        obr = Ring([cx.buf(alloc(es, nc, f"ob{i}", [128, 512], BF16), dma=True) for i in range(2)])
        ps_p = Ring([cx.buf(alloc(es, nc, f"psp{i}", [128, 512], F32, psum=True)) for i in range(2)])
        ps_ss = ps_p.bufs
        ps_s = Ring([cx.buf(alloc(es, nc, f"pst{i}", [128, 512], F32, psum=True)) for i in range(2)])
        ps_o_r = Ring([cx.buf(alloc(es, nc, f"pso{i}", [128, 512], F32, psum=True)) for i in range(2)])
        ps_r_r = Ring([cx.buf(alloc(es, nc, f"psr{i}", [128, 512], F32, psum=True)) for i in range(2)])
        blk = es.enter_context(nc.Block())

        cx.dma('pool', lambda e: e.dma_start(out=ones_b.t[:], in_=consts[:, C_ONES:C_ONES + 128]), writes=[ones_b], sembuf=ones_b)
        cx.dma('pool', lambda e: e.dma_start(out=id_b.t[:], in_=consts[:, C_ID:C_ID + 128]), writes=[id_b], sembuf=id_b)
        cx.dma('pool', lambda e: e.dma_start(out=mask.t[:], in_=maskd[:, :, :]), writes=[mask], sembuf=mask)
        cx.dma('sp', lambda e: e.dma_start(out=pr.t[:], in_=prm[:, :]), writes=[pr], sembuf=pr)
        cx.dma('pool', lambda e: e.dma_start(out=wqb.t[:], in_=wq.rearrange("(kc p) n -> p kc n", p=128)), writes=[wqb], sembuf=wqb)
        cx.dma('pool', lambda e: e.dma_start(out=wkvb.t[:], in_=wkv.rearrange("(kc p) n -> p kc n", p=128)), writes=[wkvb], sembuf=wkvb)

        def norm_into(dst, row0, nch, gcol0, t0):
            for half in range(S // T):
                tt0 = t0 + half * T
                ld = lambda kc, hb, tt0=tt0: (lambda e: e.dma_start(out=hb.t[:, :T], in_=pT[row0 + kc * 128:row0 + (kc + 1) * 128, tt0:tt0 + T]))
                rms_stats(cx, nc, ld, nch, T, hring, sqring, ps_ss, ones_f, nch * 128, eps_t, rstd, rstd)
                for kc in range(nch):
                    hb = hring.next()
                    cx.dma('sp', ld(kc, hb), writes=[hb], sembuf=hb)
                    cx.op('dve', lambda e, hb=hb, kc=kc, half=half: nc.vector.scalar_tensor_tensor(
                        out=dst.t[:, kc, half * T:(half + 1) * T], in0=hb.t[:, :T], scalar=pr.t[:, gcol0 + kc:gcol0 + kc + 1],
                        in1=rstd.t[:, :T], op0=ALU.mult, op1=ALU.mult), reads=[hb, pr, rstd], writes=[dst])

        def rope(dst, x_ap, xs_ap, sl, xbufs):
            cx.op('dve', lambda e: nc.vector.tensor_tensor(out=xt.t[:, :sl.stop - sl.start], in0=xs_ap, in1=ss.t[:, sl], op=ALU.mult),
                  reads=xbufs + [ss], writes=[xt])
            cx.op('dve', lambda e: nc.vector.tensor_tensor(out=yv.t[0:64, :sl.stop - sl.start], in0=x_ap, in1=cc.t[:, sl], op=ALU.mult),
                  reads=xbufs + [cc], writes=[yv])
            cx.op('dve', lambda e: nc.vector.tensor_tensor(out=dst.t[0:64, sl], in0=yv.t[0:64, :sl.stop - sl.start],
                                                           in1=xt.t[:, :sl.stop - sl.start], op=ALU.add),
                  reads=[yv, xt], writes=[dst])

        def seq_body(b):
            t0 = b * S
            cx.dma('sp', lambda e: e.dma_start(out=cc.t[:], in_=cs[b, 0, :, :]), writes=[cc], sembuf=cc)
            cx.dma('sp', lambda e: e.dma_start(out=ss.t[:], in_=cs[b, 1, :, :]), writes=[ss], sembuf=ss)
            cx.dma('sp', lambda e: e.dma_start(out=kx.t[:], in_=pT[768:832, t0:t0 + S]), writes=[kx], sembuf=kx)
            cx.dma('sp', lambda e: e.dma_start(out=kxs.t[:], in_=pT[R_KRS:R_KRS + 64, t0:t0 + S]), writes=[kxs], sembuf=kxs)
            norm_into(cqn, 0, 4, 0, t0)
            norm_into(ckvn, 512, 2, 4, t0)
            for sub in range(4):
                sl = slice(sub * 512, (sub + 1) * 512)
                rope(k_r, kx.t[:, sl], kxs.t[:, sl], sl, [kx, kxs])
            for h in range(8):
                head_body(h, t0)

        def head_body(h, t0):
            if True:
                c0 = h * 256
                for sub in range(4):
                    sl = slice(sub * 512, (sub + 1) * 512)
                    p = ps_p.next()
                    mm_group(cx, p, [(wqb.t[:, kc, c0:c0 + 128], cqn.t[:, kc, sl]) for kc in range(4)], reads=[wqb, cqn])
                    cx.op('act', lambda e, p=p, sl=sl: nc.scalar.copy(q_n.t[:, sl], p.t[:]), reads=[p], writes=[q_n])
                    p = ps_p.next()
                    mm_group(cx, p, [(wkvb.t[:, kc, c0:c0 + 128], ckvn.t[:, kc, sl]) for kc in range(2)], reads=[wkvb, ckvn])
                    cx.op('act', lambda e, p=p, sl=sl: nc.scalar.copy(k_n.t[:, sl], p.t[:]), reads=[p], writes=[k_n])
                    p1 = ps_p.next()
                    mm_group(cx, p1, [(wqb.t[:, kc, c0 + 128:c0 + 192], cqn.t[:, kc, sl]) for kc in range(4)], reads=[wqb, cqn],
                             out_ap=p1.t[0:64, :])
                    p2 = ps_p.next()
                    mm_group(cx, p2, [(wqb.t[:, kc, c0 + 192:c0 + 256], cqn.t[:, kc, sl]) for kc in range(4)], reads=[wqb, cqn],
                             out_ap=p2.t[0:64, :])
                    rope(q_r, p1.t[0:64, :], p2.t[0:64, :], sl, [p1, p2])
                    p = ps_p.next()

                    def vfn(eng, p=p, sub=sub):
                        ins = None
                        for j in range(4):
                            tb = sub * 4 + j
                            for kc in range(2):
                                ins = nc.tensor.matmul(p.t[:, j * 128:(j + 1) * 128], ckvn.t[:, kc, tb * 128:(tb + 1) * 128],
                                                       wkvb.t[:, kc, c0 + 128:c0 + 256], start=(kc == 0), stop=(kc == 1))
                        return ins
                    cx.op('pe', vfn, reads=[ckvn, wkvb], writes=[p])
                    cx.op('act', lambda e, p=p, sub=sub: nc.scalar.copy(
                        V.t[:, sub * 4:(sub + 1) * 4, :], p.t[:].rearrange("p (a b) -> p a b", a=4)), reads=[p], writes=[V])
                for qt in range(4):
                    qs = slice(qt * 512, (qt + 1) * 512)
                    nkb = 4 * (qt + 1)
                    ps_o = ps_o_r.next()
                    ps_r = ps_r_r.next()
                    for kb in range(nkb):
                        ks = slice(kb * 128, (kb + 1) * 128)
                        st = ps_s.next()
                        pairs = [(k_n.t[:, ks], q_n.t[:, qs]), (k_r.t[0:64, ks], q_r.t[0:64, qs])]
                        rds = [k_n, q_n, k_r, q_r]
                        if kb >= 4 * qt:
                            pairs.append((id_b.t[:], mask.t[:, kb - 4 * qt, :]))
                            rds += [id_b, mask]
                        mm_group(cx, st, pairs, reads=rds)
                        pt = PT.next()
                        cx.op('act', lambda e, st=st, pt=pt: nc.scalar.activation(out=pt.t[:], in_=st.t[:], func=AF.Exp, scale=scale),
                              reads=[st], writes=[pt])

                        def pv(eng, pt=pt, kb=kb, nkb=nkb, ps_o=ps_o, ps_r=ps_r):
                            nc.tensor.matmul(ps_o.t[:], V.t[:, kb, :], pt.t[:], start=(kb == 0), stop=(kb == nkb - 1))
                            return nc.tensor.matmul(ps_r.t[:], ones_b.t[:], pt.t[:], start=(kb == 0), stop=(kb == nkb - 1))
                        cx.op('pe', pv, reads=[pt, V, ones_b], writes=[ps_o, ps_r])
                    cx.op('dve', lambda e, ps_r=ps_r: nc.vector.reciprocal(rinv.t[:], ps_r.t[:]), reads=[ps_r], writes=[rinv])
                    cx.op('dve', lambda e, ps_o=ps_o: nc.vector.tensor_tensor(out=yv.t[:], in0=ps_o.t[:], in1=rinv.t[:], op=ALU.mult),
                          reads=[ps_o, rinv], writes=[yv])
                    cx.op('act', lambda e: nc.scalar.activation(out=ysq.t[:], in_=yv.t[:], func=AF.Square), reads=[yv], writes=[ysq])
                    pm = ps_p.next()
                    cx.op('pe', lambda e, pm=pm: nc.tensor.matmul(pm.t[:], ones_f.t[:], ysq.t[:], start=True, stop=True),
                          reads=[ones_f, ysq], writes=[pm])
                    cx.op('act', lambda e, pm=pm: nc.scalar.activation(out=sd2.t[:], in_=pm.t[:], func=AF.Sqrt,
                                                                       bias=eps_t.t[:, 0:1], scale=1.0 / 128),
                          reads=[pm, eps_t], writes=[sd2])
                    cx.op('dve', lambda e: nc.vector.reciprocal(sd2.t[:], sd2.t[:]), reads=[sd2], writes=[sd2])
                    o = obr.next()
                    cx.op('dve', lambda e, o=o: nc.vector.scalar_tensor_tensor(
                        out=o.t[:], in0=yv.t[:], scalar=pr.t[:, 6 + h:7 + h], in1=sd2.t[:], op0=ALU.mult, op1=ALU.mult),
                        reads=[yv, pr, sd2], writes=[o])
                    cx.dma('sp', lambda e, o=o, qt=qt: e.dma_start(
                        out=yT[h * 128:(h + 1) * 128, t0 + qt * 512:t0 + (qt + 1) * 512], in_=o.t[:]), reads=[o], sembuf=o)
        for b in range(NSEQ):
            seq_body(b)
        cx.end_stage(blk)


def stage_rwkv(cx, pT, rp, dup, iup, gup, consts, yT):
    nc = cx.nc
    TB = 128
    c1 = -0.6065306597126334
    with contextlib.ExitStack() as es:
        cx.begin_stage()

        def sb(name, shape, dt=F32, dma=False):
            return cx.buf(alloc(es, nc, name, shape, dt), dma=dma)
        cst = sb("cst", [64, 1024], dma=True)
        rpt = sb("rpt", [128, 96], dma=True)
        dupt = sb("dupt", [32, 512], dma=True)
        iupt = sb("iupt", [32, 512], dma=True)
        gupt = sb("gupt", [96, 512], dma=True)
        omka = sb("omka", [64, 8])
        cur3 = sb("cur3", [64, 3, 8, TB], dma=True)
        prv3 = sb("prv3", [64, 3, 8, TB], dma=True)
        loc = sb("loc", [96, 3, TB], dma=True)
        lop = sb("lop", [96, 3, TB], dma=True)
        names = ["sw", "a", "g", "cum", "E1", "E2", "E3", "E4", "kk", "nr", "tmp", "kp", "bv", "aT", "bb", "bT", "bhT", "kT", "khT", "rT"]
        Tt = {n: sb(n, [64, 8, TB]) for n in names}
        cn = ["V", "Bh", "Kh", "N", "L", "AKu", "BRu", "KRu", "P", "Q", "Na", "La", "Nb", "Lb"]
        Cs = [{n: sb(n + str(c), [64, 8, 64]) for n in cn} for c in range(TB // 64)]
        Ct = {n: sb(n, [64, 8, 64]) for n in ["Zs", "Us", "H", "tH", "ysb", "yc", "sq", "sdv"]}
        obr = Ring([sb(f"ob{i}", [64, 8, 64], BF16, dma=True) for i in range(2)])
        pp = Ring([cx.buf(alloc(es, nc, f"pp{i}", [128, 512], F32, psum=True)) for i in range(8)])
        blk = es.enter_context(nc.Block())

        cx.dma('sp', lambda e: e.dma_start(out=cst.t[:], in_=consts[0:64, :]), writes=[cst], sembuf=cst)
        cx.dma('sp', lambda e: e.dma_start(out=rpt.t[:], in_=rp[:, :]), writes=[rpt], sembuf=rpt)
        cx.dma('sp', lambda e: e.dma_start(out=dupt.t[:], in_=dup[:, :]), writes=[dupt], sembuf=dupt)
        cx.dma('sp', lambda e: e.dma_start(out=iupt.t[:], in_=iup[:, :]), writes=[iupt], sembuf=iupt)
        cx.dma('sp', lambda e: e.dma_start(out=gupt.t[:], in_=gup[:, :]), writes=[gupt], sembuf=gupt)
        ones64 = cst.t[:, C_ONES:C_ONES + 64]
        id64 = cst.t[:, C_ID:C_ID + 64]
        eps64 = cst.t[:, C_EPS + 1:C_EPS + 2]

        def mk(c):
            return cst.t[:, c:c + 64].unsqueeze(1).broadcast_to([64, 8, 64])
        SUb, SLb, IUb, IDb = mk(C_SU), mk(C_SL), mk(C_IU), mk(C_ID)
        rmask = cst.t[:, C_RM:C_RM + TB]

        def pb(col):
            return rpt.t[0:64, col:col + 8]

        def bc(ap2, n):
            return ap2.unsqueeze(2).broadcast_to([64, 8, n])

        def TTo(o, oap, ins, op, eng='dve'):
            bufs = [x[0] for x in ins]
            aps = [x[1] for x in ins]
            cx.op(eng, lambda e: nc.vector.tensor_tensor(out=oap, in0=aps[0], in1=aps[1], op=op),
                  reads=[b for b in bufs if b is not None], writes=[o])

        def ACTo(o, oap, i, iap, func, scale=1.0, bias=None, extra=()):
            def fn(e):
                if bias is None:
                    return nc.scalar.activation(out=oap, in_=iap, func=func, scale=scale)
                return nc.scalar.activation(out=oap, in_=iap, func=func, scale=scale, bias=bias)
            cx.op('act', fn, reads=[i] + list(extra), writes=[o])

        for bt in (loc, lop):
            cx.op('dve', lambda e, bt=bt: nc.vector.memset(bt.t[:], 0.0), writes=[bt])
        cx.op('dve', lambda e: nc.vector.tensor_scalar(out=omka.t[:], in0=pb(48), scalar1=-1.0, scalar2=1.0,
                                                       op0=ALU.mult, op1=ALU.add), reads=[rpt], writes=[omka])

        def rows3(t_lo, t_hi):
            return [pT[B0 + kd * 512:B0 + (kd + 1) * 512, t_lo:t_hi].rearrange("(h f) t -> f h t", f=64) for kd in range(3)]

        def block_body(b, k):
            t0 = b * S + k * TB
            T = Tt
            for kd, src in enumerate(rows3(t0, t0 + TB)):
                cx.dma('sp', lambda e, kd=kd, src=src: e.dma_start(out=cur3.t[:, kd], in_=src), writes=[cur3], sembuf=cur3)
            cx.dma('sp', lambda e: e.dma_start(out=loc.t[0:32, 0, :], in_=pT[B0 + 1536:B0 + 1568, t0:t0 + TB]), writes=[loc], sembuf=loc)
            cx.dma('sp', lambda e: e.dma_start(out=loc.t[0:32, 1, :], in_=pT[B0 + 1568:B0 + 1600, t0:t0 + TB]), writes=[loc], sembuf=loc)
            cx.dma('sp', lambda e: e.dma_start(out=loc.t[0:96, 2, :], in_=pT[B0 + 1600:B0 + 1696, t0:t0 + TB]), writes=[loc], sembuf=loc)
            if k == 0:
                cx.op('dve', lambda e: nc.vector.memset(prv3.t[:, :, :, 0:1], 0.0), writes=[prv3])
                cx.op('dve', lambda e: nc.vector.memset(lop.t[:, :, 0:1], 0.0), writes=[lop])
                cx.op('dve', lambda e: nc.vector.memset(Ct["H"].t[:], 0.0), writes=[Ct["H"]])
                for kd, src in enumerate(rows3(t0, t0 + TB - 1)):
                    cx.dma('sp', lambda e, kd=kd, src=src: e.dma_start(out=prv3.t[:, kd, :, 1:TB], in_=src), writes=[prv3], sembuf=prv3)
                o1, lo_, hi_ = 1, t0, t0 + TB - 1
            else:
                for kd, src in enumerate(rows3(t0 - 1, t0 + TB - 1)):
                    cx.dma('sp', lambda e, kd=kd, src=src: e.dma_start(out=prv3.t[:, kd], in_=src), writes=[prv3], sembuf=prv3)
                o1, lo_, hi_ = 0, t0 - 1, t0 + TB - 1
            cx.dma('sp', lambda e: e.dma_start(out=lop.t[0:32, 0, o1:TB], in_=pT[B0 + 1536:B0 + 1568, lo_:hi_]), writes=[lop], sembuf=lop)
            cx.dma('sp', lambda e: e.dma_start(out=lop.t[0:32, 1, o1:TB], in_=pT[B0 + 1568:B0 + 1600, lo_:hi_]), writes=[lop], sembuf=lop)
            cx.dma('sp', lambda e: e.dma_start(out=lop.t[0:96, 2, o1:TB], in_=pT[B0 + 1600:B0 + 1696, lo_:hi_]), writes=[lop], sembuf=lop)
            c3 = cur3.t[:].rearrange("p a h t -> p (a h) t")
            p3 = prv3.t[:].rearrange("p a h t -> p (a h) t")
            mu3 = rpt.t[0:64, 0:24].unsqueeze(2).broadcast_to([64, 24, TB])
            TTo(prv3, p3, [(prv3, p3), (cur3, c3)], ALU.subtract)
            TTo(prv3, p3, [(prv3, p3), (rpt, mu3)], ALU.mult)
            TTo(prv3, p3, [(prv3, p3), (cur3, c3)], ALU.add)
            mul = rpt.t[0:96, 80:83].unsqueeze(2).broadcast_to([96, 3, TB])
            TTo(lop, lop.t[:], [(lop, lop.t[:]), (loc, loc.t[:])], ALU.subtract)
            TTo(lop, lop.t[:], [(lop, lop.t[:]), (rpt, mul)], ALU.mult)
            TTo(lop, lop.t[:], [(lop, lop.t[:]), (loc, loc.t[:])], ALU.add)
            rs_, ks_, vs_ = prv3.t[:, 0], prv3.t[:, 1], prv3.t[:, 2]
            ACTo(lop, lop.t[0:32, 0, :], lop, lop.t[0:32, 0, :], AF.Tanh)
            ACTo(lop, lop.t[0:96, 2, :], lop, lop.t[0:96, 2, :], AF.Sigmoid)
            for (dst, wt, kdim, li, bcol) in ((T["sw"], dupt, 32, 0, 24), (T["a"], iupt, 32, 1, 32), (T["g"], gupt, 96, 2, None)):
                for half in range(2):
                    p = pp.next()

                    def fn(e, p=p, wt=wt, kdim=kdim, li=li, half=half):
                        ins = None
                        for hh in range(4):
                            h = half * 4 + hh
                            ins = nc.tensor.matmul(p.t[0:64, hh * TB:(hh + 1) * TB], wt.t[0:kdim, h * 64:(h + 1) * 64],
                                                   lop.t[0:kdim, li, :], start=True, stop=True)
                        return ins
                    cx.op('pe', fn, reads=[wt, lop], writes=[p])
                    dap = dst.t[:, half * 4:(half + 1) * 4, :]
                    pap = p.t[0:64, :].rearrange("p (h t) -> p h t", h=4)
                    if bcol is None:
                        ACTo(dst, dap, p, pap, AF.Copy)
                    else:
                        bb_ = rpt.t[0:64, bcol + half * 4:bcol + half * 4 + 4].unsqueeze(2).broadcast_to([64, 4, TB])
                        TTo(dst, dap, [(p, pap), (rpt, bb_)], ALU.add)
                if bcol is not None:
                    ACTo(dst, dst.t[:], dst, dst.t[:], AF.Sigmoid)
            for h in range(8):
                cx.op('dve', lambda e, h=h: nc.vector.tensor_tensor_scan(
                    out=T["cum"].t[:, h, :], data0=rmask, data1=T["sw"].t[:, h, :], initial=0.0, op0=ALU.mult, op1=ALU.add),
                    reads=[cst, T["sw"]], writes=[T["cum"]])
            ACTo(T["E1"], T["E1"].t[:], T["cum"], T["cum"].t[:], AF.Exp, scale=c1)
            ACTo(T["E2"], T["E2"].t[:], T["cum"], T["cum"].t[:], AF.Exp, scale=-c1)
            TTo(T["E3"], T["E3"].t[:], [(T["cum"], T["cum"].t[:]), (T["sw"], T["sw"].t[:])], ALU.subtract)
            ACTo(T["E3"], T["E3"].t[:], T["E3"], T["E3"].t[:], AF.Exp, scale=c1)
            cum4 = T["cum"].t[:].rearrange("p h (c t) -> p h c t", t=64)
            cumC = cum4[:, :, :, 63:64].broadcast_to([64, 8, TB // 64, 64])
            e44 = T["E4"].t[:].rearrange("p h (c t) -> p h c t", t=64)
            TTo(T["E4"], e44, [(T["cum"], cumC), (T["cum"], cum4)], ALU.subtract)
            ACTo(T["E4"], T["E4"].t[:], T["E4"], T["E4"].t[:], AF.Exp, scale=c1)
            TTo(T["kk"], T["kk"].t[:], [(prv3, ks_), (rpt, bc(pb(40), TB))], ALU.mult)
            ACTo(T["nr"], T["nr"].t[:], T["kk"], T["kk"].t[:], AF.Square)
            for half in range(2):
                p = pp.next()
                hs = slice(half * 4, (half + 1) * 4)
                cx.op('pe', lambda e, p=p, hs=hs: nc.tensor.matmul(p.t[0:64, :], ones64, T["nr"].t[:, hs, :], start=True, stop=True),
                      reads=[cst, T["nr"]], writes=[p])
                ACTo(T["tmp"], T["tmp"].t[:, hs, :], p, p.t[0:64, :].rearrange("p (h t) -> p h t", h=4), AF.Sqrt)
            cx.op('dve', lambda e: nc.vector.tensor_scalar(out=T["tmp"].t[:], in0=T["tmp"].t[:], scalar1=1e-12, scalar2=None, op0=ALU.max),
                  reads=[T["tmp"]], writes=[T["tmp"]])
            cx.op('dve', lambda e: nc.vector.reciprocal(T["tmp"].t[:], T["tmp"].t[:]), reads=[T["tmp"]], writes=[T["tmp"]])
            TTo(T["kk"], T["kk"].t[:], [(T["kk"], T["kk"].t[:]), (T["tmp"], T["tmp"].t[:])], ALU.mult)
            TTo(T["tmp"], T["tmp"].t[:], [(T["a"], T["a"].t[:]), (rpt, bc(pb(48), TB))], ALU.mult)
            TTo(T["tmp"], T["tmp"].t[:], [(T["tmp"], T["tmp"].t[:]), (omka, bc(omka.t[:], TB))], ALU.add)
            TTo(T["kp"], T["kp"].t[:], [(prv3, ks_), (T["tmp"], T["tmp"].t[:])], ALU.mult)
            TTo(T["tmp"], T["tmp"].t[:], [(prv3, rs_), (T["kp"], T["kp"].t[:])], ALU.mult)
            TTo(T["tmp"], T["tmp"].t[:], [(T["tmp"], T["tmp"].t[:]), (rpt, bc(pb(56), TB))], ALU.mult)
            for half in range(2):
                p = pp.next()
                hs = slice(half * 4, (half + 1) * 4)
                cx.op('pe', lambda e, p=p, hs=hs: nc.tensor.matmul(p.t[0:64, :], ones64, T["tmp"].t[:, hs, :], start=True, stop=True),
                      reads=[cst, T["tmp"]], writes=[p])
                TTo(T["bv"], T["bv"].t[:, hs, :], [(p, p.t[0:64, :].rearrange("p (h t) -> p h t", h=4)), (prv3, prv3.t[:, 2, hs, :])], ALU.mult)
            cx.op('dve', lambda e: nc.vector.scalar_tensor_tensor(out=T["aT"].t[:], in0=T["kk"].t[:], scalar=-1.0, in1=T["E3"].t[:],
                                                                  op0=ALU.mult, op1=ALU.mult), reads=[T["kk"], T["E3"]], writes=[T["aT"]])
            TTo(T["bb"], T["bb"].t[:], [(T["kk"], T["kk"].t[:]), (T["a"], T["a"].t[:])], ALU.mult)
            TTo(T["bT"], T["bT"].t[:], [(T["bb"], T["bb"].t[:]), (T["E2"], T["E2"].t[:])], ALU.mult)
            TTo(T["bhT"], T["bhT"].t[:], [(T["bb"], T["bb"].t[:]), (T["E4"], T["E4"].t[:])], ALU.mult)
            TTo(T["kT"], T["kT"].t[:], [(T["kp"], T["kp"].t[:]), (T["E2"], T["E2"].t[:])], ALU.mult)
            TTo(T["khT"], T["khT"].t[:], [(T["kp"], T["kp"].t[:]), (T["E4"], T["E4"].t[:])], ALU.mult)
            TTo(T["rT"], T["rT"].t[:], [(prv3, rs_), (T["E1"], T["E1"].t[:])], ALU.mult)
            run_chunks(b, k)

        def pgroup(builder, reads):
            p = pp.next()

            def fn(e, p=p):
                ins = None
                for h in range(8):
                    ins = builder(p.t[0:64, h * 64:(h + 1) * 64], h)
                return ins
            cx.op('pe', fn, reads=reads, writes=[p])
            return p, p.t[0:64, :].rearrange("p (h t) -> p h t", h=8)

        def phase_a(b, k, c):
            T, C = Tt, Cs[c]
            cs_ = slice(c * 64, (c + 1) * 64)
            for dst, src_b, src_ap in ((C["V"], prv3, prv3.t[:, 2]), (C["Bh"], T["bhT"], T["bhT"].t[:]), (C["Kh"], T["khT"], T["khT"].t[:])):
                p, pv = pgroup(lambda o, h, src_ap=src_ap: nc.tensor.transpose(o, src_ap[:, h, cs_], id64), [src_b, cst])
                ACTo(dst, dst.t[:], p, pv, AF.Copy)
            yield

            def pw(dst, lt, rt, mb):
                p, pv = pgroup(lambda o, h: nc.tensor.matmul(o, lt.t[:, h, cs_], rt.t[:, h, cs_], start=True, stop=True), [lt, rt])
                TTo(dst, dst.t[:], [(p, pv), (cst, mb)], ALU.mult)
            pw(C["N"], T["bT"], T["aT"], SUb)
            pw(C["L"], T["aT"], T["bT"], SLb)
            yield
            pw(C["AKu"], T["kT"], T["aT"], SUb)
            pw(C["BRu"], T["bT"], T["rT"], IUb)
            pw(C["KRu"], T["kT"], T["rT"], IUb)
            TTo(C["P"], C["P"].t[:], [(C["N"], C["N"].t[:]), (cst, IDb)], ALU.add)
            TTo(C["Q"], C["Q"].t[:], [(C["L"], C["L"].t[:]), (cst, IDb)], ALU.add)
            yield
            Nc, Lc = C["N"], C["L"]
            nxt = [(C["Na"], C["La"]), (C["Nb"], C["Lb"])]
            for lvl in range(5):
                Nn, Ln = nxt[lvl % 2]
                p, pv = pgroup(lambda o, h, Nc=Nc, Lc=Lc: nc.tensor.matmul(o, Lc.t[:, h, :], Nc.t[:, h, :], start=True, stop=True), [Nc, Lc])
                if lvl < 4:
                    p2, pv2 = pgroup(lambda o, h, Nc=Nc, Lc=Lc: nc.tensor.matmul(o, Nc.t[:, h, :], Lc.t[:, h, :], start=True, stop=True), [Nc, Lc])
                ACTo(Nn, Nn.t[:], p, pv, AF.Copy)
                if lvl < 4:
                    cx.op('dve', lambda e, Ln=Ln, pv2=pv2: nc.vector.tensor_copy(Ln.t[:], pv2), reads=[p2], writes=[Ln])
                yield
                p, pv = pgroup(lambda o, h, Nn=Nn: nc.tensor.matmul(o, C["Q"].t[:, h, :], Nn.t[:, h, :], start=True, stop=True), [C["Q"], Nn])
                if lvl < 4:
                    p2, pv2 = pgroup(lambda o, h, Ln=Ln: nc.tensor.matmul(o, C["P"].t[:, h, :], Ln.t[:, h, :], start=True, stop=True), [C["P"], Ln])
                TTo(C["P"], C["P"].t[:], [(C["P"], C["P"].t[:]), (p, pv)], ALU.add)
                if lvl < 4:
                    TTo(C["Q"], C["Q"].t[:], [(C["Q"], C["Q"].t[:]), (p2, pv2)], ALU.add)
                Nc, Lc = Nn, Ln
                yield

        def phase_b(b, k, c):
            T, C = Tt, Cs[c]
            cs_ = slice(c * 64, (c + 1) * 64)
            tc0 = b * S + k * TB + c * 64
            H = Ct["H"]

            def zb(o, h):
                nc.tensor.matmul(o, T["aT"].t[:, h, cs_], H.t[:, h, :], start=True, stop=False)
                return nc.tensor.matmul(o, C["AKu"].t[:, h, :], C["V"].t[:, h, :], start=False, stop=True)
            p, pv = pgroup(zb, [T["aT"], H, C["AKu"], C["V"]])
            ACTo(Ct["Zs"], Ct["Zs"].t[:], p, pv, AF.Copy)
            p, pv = pgroup(lambda o, h: nc.tensor.matmul(o, C["P"].t[:, h, :], Ct["Zs"].t[:, h, :], start=True, stop=True), [C["P"], Ct["Zs"]])
            cx.op('dve', lambda e, pv=pv: nc.vector.tensor_copy(Ct["Us"].t[:], pv), reads=[p], writes=[Ct["Us"]])

            def yb_(o, h):
                nc.tensor.matmul(o, H.t[:, h, :], T["rT"].t[:, h, cs_], start=True, stop=False)
                nc.tensor.matmul(o, Ct["Us"].t[:, h, :], C["BRu"].t[:, h, :], start=False, stop=False)
                return nc.tensor.matmul(o, C["V"].t[:, h, :], C["KRu"].t[:, h, :], start=False, stop=True)

            def hb_(o, h):
                nc.tensor.matmul(o, C["Bh"].t[:, h, :], Ct["Us"].t[:, h, :], start=True, stop=False)
                return nc.tensor.matmul(o, C["Kh"].t[:, h, :], C["V"].t[:, h, :], start=False, stop=True)
            ph, phv = pgroup(hb_, [C["Bh"], Ct["Us"], C["Kh"], C["V"]])
            py, pyv = pgroup(yb_, [H, T["rT"], Ct["Us"], C["BRu"], C["V"], C["KRu"]])
            gC = T["E1"].t[:, :, c * 64 + 63:c * 64 + 64].broadcast_to([64, 8, 64])
            TTo(Ct["tH"], Ct["tH"].t[:], [(H, H.t[:]), (T["E1"], gC)], ALU.mult)
            TTo(H, H.t[:], [(Ct["tH"], Ct["tH"].t[:]), (ph, phv)], ALU.add)
            Cc = Ct
            ACTo(Cc["ysb"], Cc["ysb"].t[:], py, pyv, AF.Copy)
            ysf = Cc["ysb"].t[:].rearrange("p h t -> p (h t)")
            pm = pp.next()
            cx.op('pe', lambda e, pm=pm: nc.tensor.matmul(pm.t[0:64, :], ones64, ysf, start=True, stop=True), reads=[cst, Cc["ysb"]], writes=[pm])
            cx.op('dve', lambda e, pm=pm: nc.vector.scalar_tensor_tensor(
                out=Cc["yc"].t[:].rearrange("p h t -> p (h t)"), in0=pm.t[0:64, :], scalar=-1.0 / 64, in1=ysf, op0=ALU.mult, op1=ALU.add),
                reads=[pm, Cc["ysb"]], writes=[Cc["yc"]])
            ACTo(Cc["sq"], Cc["sq"].t[:], Cc["yc"], Cc["yc"].t[:], AF.Square)
            pv_ = pp.next()
            cx.op('pe', lambda e, pv_=pv_: nc.tensor.matmul(pv_.t[0:64, :], ones64, Cc["sq"].t[:].rearrange("p h t -> p (h t)"), start=True, stop=True),
                  reads=[cst, Cc["sq"]], writes=[pv_])
            ACTo(Cc["sdv"], Cc["sdv"].t[:].rearrange("p h t -> p (h t)"), pv_, pv_.t[0:64, :], AF.Sqrt, scale=1.0 / 64, bias=eps64, extra=[cst])
            cx.op('dve', lambda e: nc.vector.reciprocal(Cc["sdv"].t[:], Cc["sdv"].t[:]), reads=[Cc["sdv"]], writes=[Cc["sdv"]])
            yc = Cc["yc"]
            TTo(yc, yc.t[:], [(yc, yc.t[:]), (Cc["sdv"], Cc["sdv"].t[:])], ALU.mult)
            TTo(yc, yc.t[:], [(yc, yc.t[:]), (rpt, bc(pb(64), 64))], ALU.mult)
            TTo(yc, yc.t[:], [(yc, yc.t[:]), (rpt, bc(pb(72), 64))], ALU.add)
            TTo(yc, yc.t[:], [(yc, yc.t[:]), (T["bv"], T["bv"].t[:, :, cs_])], ALU.add)
            o = obr.next()
            TTo(o, o.t[:], [(yc, yc.t[:]), (T["g"], T["g"].t[:, :, cs_])], ALU.mult)
            cx.dma('sp', lambda e, o=o: e.dma_start(
                out=yT[1024:1536, tc0:tc0 + 64].rearrange("(h f) t -> f h t", f=64), in_=o.t[:]), reads=[o], sembuf=o)

        def run_chunks(b, k):
            gens = [phase_a(b, k, c) for c in range(TB // 64)]
            while gens:
                for g_ in list(gens):
                    try:
                        next(g_)
                    except StopIteration:
                        gens.remove(g_)
            for c in range(TB // 64):
                phase_b(b, k, c)

        for b in range(NSEQ):
            for k in range(S // TB):
                block_body(b, k)
        cx.end_stage(blk)


def stage_rope(cx, pos, invf, cs):
    nc = cx.nc
    PI = float(np.pi)
    with contextlib.ExitStack() as es:
        cx.begin_stage()
        pi_ = cx.buf(alloc(es, nc, "pi_", [32, S], I32), dma=True)
        fr = cx.buf(alloc(es, nc, "fr", [32, 1], F32), dma=True)
        ang = cx.buf(alloc(es, nc, "ang", [32, S], F32))
        tf = cx.buf(alloc(es, nc, "tf", [32, S], F32))
        ki = cx.buf(alloc(es, nc, "ki", [32, S], I32))
        r = cx.buf(alloc(es, nc, "r", [32, S], F32))
        m = cx.buf(alloc(es, nc, "m", [32, S], F32))
        outs = [cx.buf(alloc(es, nc, f"o{i}", [32, S], F32), dma=True) for i in range(3)]
        blk = es.enter_context(nc.Block())
        cx.dma('sp', lambda e: e.dma_start(out=fr.t[:], in_=invf[:, :]), writes=[fr], sembuf=fr)

        def wrap(buf):
            cx.op('dve', lambda e: nc.vector.tensor_scalar(out=m.t[:], in0=buf.t[:], scalar1=PI, scalar2=-2 * PI, op0=ALU.is_gt, op1=ALU.mult),
                  reads=[buf], writes=[m])
            cx.op('dve', lambda e: nc.vector.tensor_tensor(out=buf.t[:], in0=buf.t[:], in1=m.t[:], op=ALU.add), reads=[m], writes=[buf])
            cx.op('dve', lambda e: nc.vector.tensor_scalar(out=m.t[:], in0=buf.t[:], scalar1=-PI, scalar2=2 * PI, op0=ALU.is_lt, op1=ALU.mult),
                  reads=[buf], writes=[m])
            cx.op('dve', lambda e: nc.vector.tensor_tensor(out=buf.t[:], in0=buf.t[:], in1=m.t[:], op=ALU.add), reads=[m], writes=[buf])

        def body(b):
            cx.dma('sp', lambda e: e.dma_start(out=pi_.t[:], in_=pos[b:b + 1, :].broadcast_to([32, S])), writes=[pi_], sembuf=pi_)
            cx.op('dve', lambda e: nc.vector.tensor_copy(ang.t[:], pi_.t[:]), reads=[pi_], writes=[ang])
            cx.op('dve', lambda e: nc.vector.tensor_scalar(out=ang.t[:], in0=ang.t[:], scalar1=fr.t[:, 0:1], scalar2=None, op0=ALU.mult),
                  reads=[fr], writes=[ang])
            cx.op('dve', lambda e: nc.vector.tensor_scalar(out=tf.t[:], in0=ang.t[:], scalar1=1.0 / (2 * PI), scalar2=None, op0=ALU.mult),
                  reads=[ang], writes=[tf])
            cx.op('dve', lambda e: nc.vector.tensor_copy(ki.t[:], tf.t[:]), reads=[tf], writes=[ki])
            cx.op('dve', lambda e: nc.vector.tensor_copy(tf.t[:], ki.t[:]), reads=[ki], writes=[tf])
            cx.op('dve', lambda e: nc.vector.scalar_tensor_tensor(out=r.t[:], in0=tf.t[:], scalar=-2 * PI, in1=ang.t[:], op0=ALU.mult, op1=ALU.add),
                  reads=[tf, ang], writes=[r])
            wrap(r)
            cx.op('act', lambda e: nc.scalar.activation(out=outs[1].t[:], in_=r.t[:], func=AF.Sin), reads=[r], writes=[outs[1]])
            cx.op('act', lambda e: nc.scalar.activation(out=outs[2].t[:], in_=r.t[:], func=AF.Sin, scale=-1.0), reads=[r], writes=[outs[2]])
            cx.op('dve', lambda e: nc.vector.tensor_scalar(out=r.t[:], in0=r.t[:], scalar1=PI / 2, scalar2=None, op0=ALU.add),
                  reads=[outs[1], outs[2]], writes=[r])
            wrap(r)
            cx.op('act', lambda e: nc.scalar.activation(out=outs[0].t[:], in_=r.t[:], func=AF.Sin), reads=[r], writes=[outs[0]])
            for (src, j, r0) in ((outs[0], 0, 0), (outs[0], 0, 32), (outs[2], 1, 0), (outs[1], 1, 32)):
                cx.dma('sp', lambda e, src=src, j=j, r0=r0: e.dma_start(out=cs[b, j, r0:r0 + 32, :], in_=src.t[:]), reads=[src], sembuf=src)
        for b in range(NSEQ):
            body(b)
        cx.end_stage(blk)


NPAD = 4224
BIGW = ["ffn1_gate", "ffn1_up", "ffn1_down", "ffn2_gate", "ffn2_up", "ffn2_down", "w_out", "w_ukv",
        "decay_up", "iclr_up", "gate_up"]
BIGW_SHAPES = {"ffn1_gate": [D, DFF], "ffn1_up": [D, DFF], "ffn1_down": [DFF, D], "ffn2_gate": [D, DFF],
               "ffn2_up": [D, DFF], "ffn2_down": [DFF, D], "w_out": [D, D], "w_ukv": [256, 2048],
               "decay_up": [32, 512], "iclr_up": [32, 512], "gate_up": [96, 512]}


def build_program(nlayers=DEPTH, dbg=False):
    nc = bass.Bass("TRN2", target_bir_lowering=False)
    dt = lambda n, s, d=F32, kind="ExternalInput": nc.dram_tensor(n, s, d, kind=kind).ap()
    x = dt("x", [NT, D])
    pos = dt("pos", [NSEQ, S], I32)
    W = {n: dt(n, [DEPTH] + BIGW_SHAPES[n]) for n in BIGW}
    w_inx = dt("w_inx", [DEPTH, D, NPAD])
    wq = dt("wq", [DEPTH, 512, 2048])
    spk = dt("spk", [DEPTH, 128, 192])
    gfin = dt("gfin", [128, KC])
    consts = dt("consts", [128, 1024])
    maskd = dt("maskd", [128, 4, 512])
    invf = dt("invf", [32, 1])
    out = dt("out", [NT, D], kind="ExternalOutput")
    sk = "ExternalOutput" if dbg else "Internal"
    hT = dt("hT", [KC, 128, NT], kind=sk)
    pT = dt("pT", [NPAD, NT], kind=sk)
    yT = dt("yT", [D, NT], BF16, kind=sk)
    cs = dt("cs", [NSEQ, 2, 64, S], kind=sk)
    with contextlib.ExitStack() as es:
        sems = [es.enter_context(nc.semaphore(f"s{i}")) for i in range(96)]
        cx = Ctx(nc, sems)
        stage_rope(cx, pos, invf, cs)
        stage_in(cx, x, hT, consts[:, C_ID:C_ID + 128])
        for l in range(nlayers):
            sp = spk[l]
            stage_ffn(cx, hT, W["ffn1_gate"][l], W["ffn1_up"][l], W["ffn1_down"][l], sp[:, 0:16], consts)
            stage_proj(cx, hT, w_inx[l], sp[:, 16:32], consts, pT, NPAD)
            stage_mla(cx, pT, wq[l], W["w_ukv"][l], sp[:, 48:64], cs, consts, maskd, yT)
            stage_rwkv(cx, pT, sp[:, 64:160], W["decay_up"][l], W["iclr_up"][l], W["gate_up"][l], consts, yT)
            stage_conv(cx, pT, sp[:, 160:176].rearrange("p (a b) -> p a b", a=4), consts, yT)
            stage_wout(cx, hT, yT, W["w_out"][l])
            stage_ffn(cx, hT, W["ffn2_gate"][l], W["ffn2_up"][l], W["ffn2_down"][l], sp[:, 32:48], consts)
        stage_out(cx, hT, gfin, consts, consts[:, C_ID:C_ID + 128], out)
    return nc


def _fm(v, nch):
    return np.ascontiguousarray(np.asarray(v, np.float32).reshape(nch, 128).T)


def _hd(v):
    return np.ascontiguousarray(np.asarray(v, np.float32).reshape(8, 64).T)


def host_layout(inp):
    f32 = np.float32
    w_in = np.asarray(inp["w_in"], f32)
    w_inx = np.zeros((DEPTH, D, NPAD), f32)
    w_inx[:, :, :NIN] = w_in
    w_inx[:, :, NIN:NIN + 32] = w_in[:, :, 800:832]
    w_inx[:, :, NIN + 32:NIN + 64] = w_in[:, :, 768:800]
    w_uq = np.asarray(inp["w_uq"], f32).reshape(DEPTH, 512, 8, 192)
    wq = np.concatenate([w_uq[..., :128], w_uq[..., 128:192], w_uq[..., 160:192], w_uq[..., 128:160]], axis=-1)
    wq = np.ascontiguousarray(wq.reshape(DEPTH, 512, 2048))
    spk = np.zeros((DEPTH, 128, 192), f32)
    for l in range(DEPTH):
        spk[l, :, 0:16] = _fm(inp["norm_ffn1"][l], 16)
        spk[l, :, 16:32] = _fm(inp["norm_mix"][l], 16)
        spk[l, :, 32:48] = _fm(inp["norm_ffn2"][l], 16)
        spk[l, :, 48:52] = _fm(inp["q_norm"][l], 4)
        spk[l, :, 52:54] = _fm(inp["kv_norm"][l], 2)
        spk[l, :, 54:62] = _fm(inp["attn_out_norm"][l], 8)
        rp = spk[l, :, 64:160]
        mu = np.asarray(inp["shift_mu"][l], f32)
        rp[0:64, 0:8] = _hd(mu[0:512])
        rp[0:64, 8:16] = _hd(mu[512:1024])
        rp[0:64, 16:24] = _hd(mu[1024:1536])
        rp[0:64, 24:32] = _hd(inp["decay_w0"][l])
        rp[0:64, 32:40] = _hd(inp["iclr_a0"][l])
        rp[0:64, 40:48] = _hd(inp["k_k"][l])
        rp[0:64, 48:56] = _hd(inp["k_a"][l])
        rp[0:64, 56:64] = _hd(np.asarray(inp["r_k"][l], f32).reshape(512))
        rp[0:64, 64:72] = _hd(inp["lnx_gain"][l])
        rp[0:64, 72:80] = _hd(inp["lnx_bias"][l])
        rp[0:32, 80] = mu[1536:1568]
        rp[0:32, 81] = mu[1568:1600]
        rp[0:96, 82] = mu[1600:1696]
        cw = np.asarray(inp["conv_w"][l], f32)
        cp = np.zeros((128, 4, 4), f32)
        for k in range(3):
            cp[:, :, k] = cw[k].reshape(4, 128).T
        cp[:, :, 3] = np.asarray(inp["conv_out_norm"][l], f32).reshape(4, 128).T
        spk[l, :, 160:176] = cp.reshape(128, 16)
    consts = np.zeros((128, 1024), f32)
    consts[:, C_ONES:C_ONES + 128] = 1.0
    consts[:, C_EPS] = 1e-6
    consts[:, C_EPS + 1] = 64e-5
    consts[0:64, C_BD64:C_BD64 + 64] = 1.0
    consts[64:128, C_BD64 + 64:C_BD64 + 128] = 1.0
    consts[:, C_ID:C_ID + 128] = np.eye(128, dtype=f32)
    i = np.arange(64)
    consts[0:64, C_SU:C_SU + 64] = (i[:, None] < i[None, :])
    consts[0:64, C_SL:C_SL + 64] = (i[:, None] > i[None, :])
    consts[0:64, C_IU:C_IU + 64] = (i[:, None] <= i[None, :])
    rm = np.ones(128, f32)
    rm[0::64] = 0.0
    consts[:, C_RM:C_RM + 128] = rm[None, :]
    kl = np.arange(128)[:, None]
    ql = np.arange(512)[None, :]
    maskd = np.zeros((128, 4, 512), f32)
    for j in range(4):
        maskd[:, j, :] = np.where(128 * j + kl > ql, -30000.0, 0.0)
    invf = (1.0 / (np.float32(10000.0) ** (np.arange(0, 64, 2, dtype=f32) / np.float32(64)))).astype(f32).reshape(32, 1)
    shared = {n: np.ascontiguousarray(np.asarray(inp[n], f32)) for n in BIGW}
    shared.update({"w_inx": w_inx, "wq": wq, "spk": spk, "gfin": _fm(inp["norm_final"], 16), "consts": consts,
                   "maskd": maskd, "invf": invf})
    return shared


_PROG = {}


def kernel(**inp):
    shared = host_layout(inp)
    x = np.asarray(inp["x"], np.float32)
    pos = np.asarray(inp["positions"], np.int32)
    if "nc" not in _PROG:
        _PROG["nc"] = build_program()
    nc = _PROG["nc"]
    in_maps = []
    for c in range(8):
        m = dict(shared)
        m["x"] = np.ascontiguousarray(x[c * NSEQ:(c + 1) * NSEQ].reshape(NT, D))
        m["pos"] = np.ascontiguousarray(pos[c * NSEQ:(c + 1) * NSEQ])
        in_maps.append(m)
    res = run_bass_kernel_spmd(nc, in_maps, core_ids=list(range(8)))
    out = np.stack([np.asarray(r["out"]).reshape(NSEQ, S, D) for r in res.results], axis=0)
    return out.reshape(16, S, D).astype(np.float32)
```

```python
import contextlib
import numpy as np
import concourse.bass as bass
import concourse.mybir as mybir
from concourse.bass_utils import run_bass_kernel_spmd

F32 = mybir.dt.float32
BF16 = mybir.dt.bfloat16
I32 = mybir.dt.int32
AF = mybir.ActivationFunctionType
ALU = mybir.AluOpType
AX = mybir.AxisListType

D = 2048
DFF = 5632
S = 2048
NSEQ = 2
NT = NSEQ * S
DEPTH = 4
KC = D // 128
FC = DFF // 128
EPS = 1e-6
NIN = 4064
NINX = 4128


class Sem:
    def __init__(self, h):
        self.h = h
        self.n = 0


class Buf:
    def __init__(self, cx, t, dma=False):
        self.t = t
        self.wr = None
        self.rd = []
        self.sem = cx.new_sem() if dma else None

    def __getitem__(self, k):
        return self.t[k]


class Ctx:
    ENG = ('pe', 'act', 'dve', 'pool', 'sp')

    def __init__(self, nc, sem_handles):
        self.nc = nc
        self.sems = [Sem(h) for h in sem_handles]
        self.free = list(self.sems)
        self.esem = {e: self.new_sem() for e in ('pe', 'act', 'dve', 'pool')}
        self.q = {e: [] for e in self.ENG}
        self.seen = {e: {} for e in self.ENG}
        self.stage_sems = []
        self.pending_dma = {e: [] for e in self.ENG}

    def new_sem(self):
        return self.free.pop()

    def begin_stage(self):
        self.mark = len(self.free)
        self.taken = []

    def buf(self, t, dma=False):
        b = Buf.__new__(Buf)
        b.t = t
        b.wr = None
        b.rd = []
        b.sem = None
        if dma:
            b.sem = self.free.pop()
            self.taken.append(b.sem)
        return b

    def end_stage(self, block):
        for e in self.ENG:
            for tok in self.pending_dma[e]:
                self._wait(e, tok)
            self.pending_dma[e] = []
        self.flush(block)
        self.free.extend(self.taken)
        self.taken = []

    def _wait(self, eng, tok):
        if tok is None:
            return
        sem, val, src = tok
        if src == eng and src in ('pe', 'act', 'dve'):
            return
        if self.seen[eng].get(id(sem), 0) >= val:
            return
        self.seen[eng][id(sem)] = val
        self.q[eng].append(('wait', sem, val))

    def _deps(self, eng, reads, writes):
        best = {}
        toks = [b.wr for b in reads] + [b.wr for b in writes]
        for b in writes:
            toks.extend(b.rd)
        for t in toks:
            if t is None:
                continue
            k = id(t[0])
            if k not in best or best[k][1] < t[1]:
                best[k] = t
        for t in best.values():
            self._wait(eng, t)

    def op(self, eng, fn, reads=(), writes=(), sig=True):
        self._deps(eng, reads, writes)
        tok = None
        if sig:
            s = self.esem[eng]
            s.n += 1
            tok = (s, s.n, eng)
            self.q[eng].append(('op', fn, s, 1))
        else:
            self.q[eng].append(('op', fn, None, 0))
        for b in writes:
            b.wr = tok
            b.rd = []
        for b in reads:
            if tok is not None:
                b.rd = [t for t in b.rd if t[0] is not tok[0]] + [tok]
        return tok

    def dma(self, eng, fn, reads=(), writes=(), sembuf=None):
        s = sembuf.sem
        saved = []
        for b in writes:
            if b.wr is not None and b.wr[2] == 'dma' and b.wr[0] is s and not b.rd:
                saved.append((b, b.wr))
                b.wr = None
        self._deps(eng, reads, writes)
        for b, w in saved:
            b.wr = w
        s.n += 16
        tok = (s, s.n, 'dma')
        self.q[eng].append(('op', fn, s, 16))
        for b in writes:
            b.wr = tok
            b.rd = []
        for b in reads:
            b.rd = [t for t in b.rd if t[0] is not tok[0]] + [tok]
        if not writes:
            self.pending_dma[eng].append(tok)
        return tok

    def flush(self, block):
        m = {'pe': block.tensor, 'act': block.scalar, 'dve': block.vector,
             'pool': block.gpsimd, 'sp': block.sync}
        for e in self.ENG:
            lst = self.q[e]
            if not lst:
                continue

            def body(eng, lst=lst):
                for it in lst:
                    if it[0] == 'wait':
                        eng.wait_ge(it[1].h, it[2])
                    else:
                        ins = it[1](eng)
                        if it[2] is not None:
                            ins.then_inc(it[2].h, it[3])
            m[e](body)
            self.q[e] = []


class Ring:
    def __init__(self, bufs):
        self.bufs = bufs
        self.i = 0

    def next(self):
        b = self.bufs[self.i % len(self.bufs)]
        self.i += 1
        return b


def mm_group(cx, out_buf, pairs, reads, out_ap=None):
    nc = cx.nc
    oap = out_buf.t[:] if out_ap is None else out_ap
    n = len(pairs)

    def fn(eng):
        ins = None
        for i, (l, r) in enumerate(pairs):
            ins = nc.tensor.matmul(oap, l, r, start=(i == 0), stop=(i == n - 1))
        return ins
    return cx.op('pe', fn, reads=reads, writes=[out_buf])


_UID = [0]


def alloc(es, nc, name, shape, dt, psum=False):
    _UID[0] += 1
    name = f"{name}_{_UID[0]}"
    if psum:
        return es.enter_context(nc.psum_tensor(name, shape, dt))
    return es.enter_context(nc.sbuf_tensor(name, shape, dt))


def stage_in(cx, x, hT, ident_dram):
    nc = cx.nc
    with contextlib.ExitStack() as es:
        cx.begin_stage()
        ident = cx.buf(alloc(es, nc, "ident", [128, 128], F32), dma=True)
        xin = Ring([cx.buf(alloc(es, nc, f"xin{i}", [128, D], F32), dma=True) for i in range(3)])
        xo = Ring([cx.buf(alloc(es, nc, f"xo{i}", [128, KC, 128], F32), dma=True) for i in range(3)])
        ps = Ring([cx.buf(alloc(es, nc, f"ps{i}", [128, 512], F32, psum=True)) for i in range(4)])
        blk = es.enter_context(nc.Block())
        cx.dma('sp', lambda e: e.dma_start(out=ident.t[:], in_=ident_dram[:, :]), writes=[ident], sembuf=ident)
        for tt in range(NT // 128):
            xb = xin.next()
            cx.dma('sp', lambda e, xb=xb, tt=tt: e.dma_start(out=xb.t[:], in_=x[tt * 128:(tt + 1) * 128, :]),
                   writes=[xb], sembuf=xb)
            ob = xo.next()
            for g in range(KC // 4):
                p = ps.next()

                def fn(eng, p=p, xb=xb, g=g):
                    ins = None
                    for j in range(4):
                        kc = g * 4 + j
                        ins = nc.tensor.transpose(p.t[:, j * 128:(j + 1) * 128], xb.t[:, kc * 128:(kc + 1) * 128], ident.t[:])
                    return ins
                cx.op('pe', fn, reads=[xb, ident], writes=[p])
                eng = 'dve' if g % 2 == 0 else 'act'
                if eng == 'dve':
                    cx.op('dve', lambda e, p=p, ob=ob, g=g: nc.vector.tensor_copy(
                        ob.t[:, g * 4:(g + 1) * 4, :], p.t[:].rearrange("p (a b) -> p a b", a=4)), reads=[p], writes=[ob])
                else:
                    cx.op('act', lambda e, p=p, ob=ob, g=g: nc.scalar.copy(
                        ob.t[:, g * 4:(g + 1) * 4, :], p.t[:].rearrange("p (a b) -> p a b", a=4)), reads=[p], writes=[ob])
            cx.dma('sp', lambda e, ob=ob, tt=tt: e.dma_start(
                out=hT[:, :, tt * 128:(tt + 1) * 128].rearrange("k p t -> p k t"), in_=ob.t[:]),
                reads=[ob], sembuf=ob)
        cx.end_stage(blk)


def rms_stats(cx, nc, chunks_fn, nchunks, T, hring, sqring, ps_ss, ones_f, nfeat, eps_t, sd, rstd, src_rows=128):
    nsub = T // 512
    for kc in range(nchunks):
        hb = hring.next()
        cx.dma('sp', chunks_fn(kc, hb), writes=[hb], sembuf=hb)
        sq = sqring.next()
        cx.op('act', lambda e, hb=hb, sq=sq: nc.scalar.activation(out=sq.t[:src_rows, :T], in_=hb.t[:src_rows, :T], func=AF.Square),
              reads=[hb], writes=[sq])
        for sub in range(nsub):
            p = ps_ss[sub]

            def fn(eng, p=p, sq=sq, sub=sub, kc=kc):
                return nc.tensor.matmul(p.t[:], ones_f.t[:src_rows, :], sq.t[:src_rows, sub * 512:(sub + 1) * 512],
                                        start=(kc == 0), stop=(kc == nchunks - 1))
            cx.op('pe', fn, reads=[sq, ones_f], writes=[p] if kc == 0 else [], sig=True)
            if kc != 0:
                pass
        if kc == nchunks - 1:
            last_tok = (cx.esem['pe'], cx.esem['pe'].n, 'pe')
            for sub in range(nsub):
                ps_ss[sub].wr = last_tok
    for sub in range(nsub):
        p = ps_ss[sub]
        cx.op('act', lambda e, p=p, sub=sub: nc.scalar.activation(
            out=sd.t[:, sub * 512:(sub + 1) * 512], in_=p.t[:], func=AF.Sqrt, bias=eps_t.t[:, 0:1], scale=1.0 / nfeat),
            reads=[p, eps_t], writes=[sd] if sub == 0 else [])
    sd.wr = (cx.esem['act'], cx.esem['act'].n, 'act')
    cx.op('dve', lambda e: nc.vector.reciprocal(rstd.t[:, :T], sd.t[:, :T]), reads=[sd], writes=[rstd])


def stage_ffn(cx, hT, wg, wu, wd, gvec, consts, T=1024, tiles=None):
    nc = cx.nc
    nsub = T // 512
    NJ = 256
    with contextlib.ExitStack() as es:
        cx.begin_stage()
        ones_f = cx.buf(alloc(es, nc, "ones_f", [128, 128], F32), dma=True)
        eps_t = cx.buf(alloc(es, nc, "eps_t", [128, 1], F32), dma=True)
        g_t = cx.buf(alloc(es, nc, "g_t", [128, KC], F32), dma=True)
        xn = cx.buf(alloc(es, nc, "xn", [128, KC, T], BF16))
        act = [cx.buf(alloc(es, nc, f"act{j}", [128, T], BF16)) for j in range(FC)]
        wring = Ring([cx.buf(alloc(es, nc, f"w{i}", [128, 16, NJ], BF16), dma=True) for i in range(6)])
        hring = Ring([cx.buf(alloc(es, nc, f"hb{i}", [128, T], F32), dma=True) for i in range(2)])
        sqring = Ring([cx.buf(alloc(es, nc, f"sq{i}", [128, T], F32)) for i in range(2)])
        rstd = cx.buf(alloc(es, nc, "rstd", [128, T], F32))
        sd = rstd
        sgr = Ring([cx.buf(alloc(es, nc, f"sg{i}", [128, 512], F32)) for i in range(2)])
        outr = Ring([cx.buf(alloc(es, nc, f"ob{i}", [128, 512], F32), dma=True) for i in range(2)])
        hres = Ring([cx.buf(alloc(es, nc, f"hr{i}", [128, 512], F32), dma=True) for i in range(2)])
        ps_g = Ring([cx.buf(alloc(es, nc, f"psg{i}", [128, 512], F32, psum=True)) for i in range(2)])
        ps_u = Ring([cx.buf(alloc(es, nc, f"psu{i}", [128, 512], F32, psum=True)) for i in range(2)])
        ps_d = Ring([cx.buf(alloc(es, nc, f"psd{i}", [128, 512], F32, psum=True)) for i in range(2)])
        ps_ss = [cx.buf(alloc(es, nc, f"pss{i}", [128, 512], F32, psum=True)) for i in range(nsub)]
        blk = es.enter_context(nc.Block())

        cx.dma('sp', lambda e: e.dma_start(out=ones_f.t[:], in_=consts[:, 0:128]), writes=[ones_f], sembuf=ones_f)
        cx.dma('sp', lambda e: e.dma_start(out=eps_t.t[:], in_=consts[:, 128:129], allow_slow_non_contiguous=True), writes=[eps_t], sembuf=eps_t)
        cx.dma('sp', lambda e: e.dma_start(out=g_t.t[:], in_=gvec[:, :]), writes=[g_t], sembuf=g_t)
        wgv = wg.rearrange("(kc p) n -> p kc n", p=128)
        wuv = wu.rearrange("(kc p) n -> p kc n", p=128)
        wdv = wd.rearrange("(j p) n -> p j n", p=128)
        tl = list(range(NT // T)) if tiles is None else tiles
        def tile_body(t0):
            rms_stats(cx, nc, lambda kc, hb: (lambda e: e.dma_start(out=hb.t[:, :T], in_=hT[kc, :, t0:t0 + T])),
                      KC, T, hring, sqring, ps_ss, ones_f, D, eps_t, sd, rstd)
            for kc in range(KC):
                hb = hring.next()
                cx.dma('sp', lambda e, hb=hb, kc=kc: e.dma_start(out=hb.t[:, :T], in_=hT[kc, :, t0:t0 + T]),
                       writes=[hb], sembuf=hb)
                cx.op('dve', lambda e, hb=hb, kc=kc: nc.vector.scalar_tensor_tensor(
                    out=xn.t[:, kc, :], in0=hb.t[:, :T], scalar=g_t.t[:, kc:kc + 1], in1=rstd.t[:, :T],
                    op0=ALU.mult, op1=ALU.mult), reads=[hb, g_t, rstd], writes=[xn] if kc == 0 else [])
            xn.wr = (cx.esem['dve'], cx.esem['dve'].n, 'dve')
            for jt in range(DFF // NJ):
                wgb = wring.next()
                cx.dma('pool', lambda e, b=wgb, jt=jt: e.dma_start(out=b.t[:], in_=wgv[:, :, jt * NJ:(jt + 1) * NJ]),
                       writes=[wgb], sembuf=wgb)
                wub = wring.next()
                cx.dma('pool', lambda e, b=wub, jt=jt: e.dma_start(out=b.t[:], in_=wuv[:, :, jt * NJ:(jt + 1) * NJ]),
                       writes=[wub], sembuf=wub)
                for jj in range(NJ // 128):
                    j = jt * (NJ // 128) + jj
                    for sub in range(nsub):
                        pg = ps_g.next()
                        pu = ps_u.next()
                        mm_group(cx, pg, [(wgb.t[:, kc, jj * 128:(jj + 1) * 128], xn.t[:, kc, sub * 512:(sub + 1) * 512])
                                          for kc in range(KC)], reads=[wgb, xn])
                        mm_group(cx, pu, [(wub.t[:, kc, jj * 128:(jj + 1) * 128], xn.t[:, kc, sub * 512:(sub + 1) * 512])
                                          for kc in range(KC)], reads=[wub, xn])
                        sg = sgr.next()
                        cx.op('act', lambda e, pg=pg, sg=sg: nc.scalar.activation(out=sg.t[:], in_=pg.t[:], func=AF.Silu),
                              reads=[pg], writes=[sg])
                        cx.op('dve', lambda e, sg=sg, pu=pu, j=j, sub=sub: nc.vector.tensor_tensor(
                            out=act[j].t[:, sub * 512:(sub + 1) * 512], in0=sg.t[:], in1=pu.t[:], op=ALU.mult),
                            reads=[sg, pu], writes=[act[j]])
            JD = 11
            for ct in range(D // NJ):
                wds = []
                for q4 in range(FC // JD):
                    wb = wring.next()
                    cx.dma('pool', lambda e, b=wb, q4=q4, ct=ct: e.dma_start(
                        out=b.t[:, 0:JD, :], in_=wdv[:, q4 * JD:(q4 + 1) * JD, ct * NJ:(ct + 1) * NJ]),
                        writes=[wb], sembuf=wb)
                    wds.append(wb)
                for cc in range(NJ // 128):
                    c = ct * (NJ // 128) + cc
                    for sub in range(nsub):
                        hr = hres.next()
                        cx.dma('sp', lambda e, hr=hr, c=c, sub=sub: e.dma_start(
                            out=hr.t[:], in_=hT[c, :, t0 + sub * 512:t0 + (sub + 1) * 512]), writes=[hr], sembuf=hr)
                        pd = ps_d.next()
                        mm_group(cx, pd, [(wds[j // JD].t[:, j % JD, cc * 128:(cc + 1) * 128], act[j].t[:, sub * 512:(sub + 1) * 512])
                                          for j in range(FC)], reads=wds + act)
                        ob = outr.next()
                        cx.op('dve', lambda e, pd=pd, hr=hr, ob=ob: nc.vector.scalar_tensor_tensor(
                            out=ob.t[:], in0=pd.t[:], scalar=0.5, in1=hr.t[:], op0=ALU.mult, op1=ALU.add),
                            reads=[pd, hr], writes=[ob])
                        cx.dma('sp', lambda e, ob=ob, c=c, sub=sub: e.dma_start(
                            out=hT[c, :, t0 + sub * 512:t0 + (sub + 1) * 512], in_=ob.t[:]), reads=[ob], sembuf=ob)
        for ti in tl:
            tile_body(ti * T)
        cx.end_stage(blk)


def stage_out(cx, hT, gvec, consts, ident_dram, out):
    nc = cx.nc
    T = 512
    with contextlib.ExitStack() as es:
        cx.begin_stage()
        ones_f = cx.buf(alloc(es, nc, "ones_f", [128, 128], F32), dma=True)
        ident = cx.buf(alloc(es, nc, "ident", [128, 128], F32), dma=True)
        eps_t = cx.buf(alloc(es, nc, "eps_t", [128, 1], F32), dma=True)
        g_t = cx.buf(alloc(es, nc, "g_t", [128, KC], F32), dma=True)
        hring = Ring([cx.buf(alloc(es, nc, f"hb{i}", [128, T], F32), dma=True) for i in range(3)])
        sqring = Ring([cx.buf(alloc(es, nc, f"sq{i}", [128, T], F32)) for i in range(2)])
        sd = cx.buf(alloc(es, nc, "sd", [128, T], F32))
        rstd = cx.buf(alloc(es, nc, "rstd", [128, T], F32))
        xn = cx.buf(alloc(es, nc, "xnf", [128, KC, T], F32))
        ps_ss = [cx.buf(alloc(es, nc, "pss0", [128, 512], F32, psum=True))]
        ps = Ring([cx.buf(alloc(es, nc, f"ps{i}", [128, 512], F32, psum=True)) for i in range(4)])
        orow = Ring([cx.buf(alloc(es, nc, f"orow{i}", [128, D], F32), dma=True) for i in range(3)])
        blk = es.enter_context(nc.Block())
        cx.dma('sp', lambda e: e.dma_start(out=ones_f.t[:], in_=consts[:, 0:128]), writes=[ones_f], sembuf=ones_f)
        cx.dma('sp', lambda e: e.dma_start(out=eps_t.t[:], in_=consts[:, 128:129], allow_slow_non_contiguous=True), writes=[eps_t], sembuf=eps_t)
        cx.dma('sp', lambda e: e.dma_start(out=g_t.t[:], in_=gvec[:, :]), writes=[g_t], sembuf=g_t)
        cx.dma('sp', lambda e: e.dma_start(out=ident.t[:], in_=ident_dram[:, :]), writes=[ident], sembuf=ident)
        def tile_body(t0):
            rms_stats(cx, nc, lambda kc, hb: (lambda e: e.dma_start(out=hb.t[:, :T], in_=hT[kc, :, t0:t0 + T])),
                      KC, T, hring, sqring, ps_ss, ones_f, D, eps_t, sd, rstd)
            for kc in range(KC):
                hb = hring.next()
                cx.dma('sp', lambda e, hb=hb, kc=kc: e.dma_start(out=hb.t[:, :T], in_=hT[kc, :, t0:t0 + T]),
                       writes=[hb], sembuf=hb)
                cx.op('dve', lambda e, hb=hb, kc=kc: nc.vector.scalar_tensor_tensor(
                    out=xn.t[:, kc, :], in0=hb.t[:, :T], scalar=g_t.t[:, kc:kc + 1], in1=rstd.t[:, :T],
                    op0=ALU.mult, op1=ALU.mult), reads=[hb, g_t, rstd], writes=[xn] if kc == 0 else [])
            xn.wr = (cx.esem['dve'], cx.esem['dve'].n, 'dve')
            for tb in range(T // 128):
                ob = orow.next()
                for g in range(KC // 4):
                    p = ps.next()

                    def fn(eng, p=p, g=g, tb=tb):
                        ins = None
                        for j in range(4):
                            kc = g * 4 + j
                            ins = nc.tensor.transpose(p.t[:, j * 128:(j + 1) * 128], xn.t[:, kc, tb * 128:(tb + 1) * 128], ident.t[:])
                        return ins
                    cx.op('pe', fn, reads=[xn, ident], writes=[p])
                    if g % 2 == 0:
                        cx.op('dve', lambda e, p=p, ob=ob, g=g: nc.vector.tensor_copy(ob.t[:, g * 512:(g + 1) * 512], p.t[:]),
                              reads=[p], writes=[ob])
                    else:
                        cx.op('act', lambda e, p=p, ob=ob, g=g: nc.scalar.copy(ob.t[:, g * 512:(g + 1) * 512], p.t[:]),
                              reads=[p], writes=[ob])
                cx.dma('sp', lambda e, ob=ob, tb=tb: e.dma_start(out=out[t0 + tb * 128:t0 + (tb + 1) * 128, :], in_=ob.t[:]),
                       reads=[ob], sembuf=ob)
        for ti in range(NT // T):
            tile_body(ti * T)
        cx.end_stage(blk)


def load_consts(cx, es, nc, consts, ident_dram=None):
    ones_f = cx.buf(alloc(es, nc, "ones_f", [128, 128], F32), dma=True)
    eps_t = cx.buf(alloc(es, nc, "eps_t", [128, 1], F32), dma=True)
    cx.dma('sp', lambda e: e.dma_start(out=ones_f.t[:], in_=consts[:, 0:128]), writes=[ones_f], sembuf=ones_f)
    cx.dma('sp', lambda e: e.dma_start(out=eps_t.t[:], in_=consts[:, 128:129], allow_slow_non_contiguous=True),
           writes=[eps_t], sembuf=eps_t)
    return ones_f, eps_t


def stage_proj(cx, hT, w, gvec, consts, pT, ncols, T=1024):
    nc = cx.nc
    nsub = T // 512
    NJ = 384
    with contextlib.ExitStack() as es:
        cx.begin_stage()
        ones_f, eps_t = load_consts(cx, es, nc, consts)
        g_t = cx.buf(alloc(es, nc, "g_t", [128, KC], F32), dma=True)
        xn = cx.buf(alloc(es, nc, "xn", [128, KC, T], BF16))
        wring = Ring([cx.buf(alloc(es, nc, f"w{i}", [128, 16, NJ], BF16), dma=True) for i in range(3)])
        hring = Ring([cx.buf(alloc(es, nc, f"hb{i}", [128, T], F32), dma=True) for i in range(2)])
        sqring = Ring([cx.buf(alloc(es, nc, f"sq{i}", [128, T], F32)) for i in range(2)])
        rstd = cx.buf(alloc(es, nc, "rstd", [128, T], F32))
        outr = Ring([cx.buf(alloc(es, nc, f"ob{i}", [128, 512], F32), dma=True) for i in range(4)])
        ps_o = Ring([cx.buf(alloc(es, nc, f"pso{i}", [128, 512], F32, psum=True)) for i in range(4)])
        ps_ss = [cx.buf(alloc(es, nc, f"pss{i}", [128, 512], F32, psum=True)) for i in range(nsub)]
        blk = es.enter_context(nc.Block())
        cx.dma('sp', lambda e: e.dma_start(out=g_t.t[:], in_=gvec[:, :]), writes=[g_t], sembuf=g_t)
        wv = w.rearrange("(kc p) n -> p kc n", p=128)

        def tile_body(t0):
            rms_stats(cx, nc, lambda kc, hb: (lambda e: e.dma_start(out=hb.t[:, :T], in_=hT[kc, :, t0:t0 + T])),
                      KC, T, hring, sqring, ps_ss, ones_f, D, eps_t, rstd, rstd)
            for kc in range(KC):
                hb = hring.next()
                cx.dma('sp', lambda e, hb=hb, kc=kc: e.dma_start(out=hb.t[:, :T], in_=hT[kc, :, t0:t0 + T]),
                       writes=[hb], sembuf=hb)
                cx.op('dve', lambda e, hb=hb, kc=kc: nc.vector.scalar_tensor_tensor(
                    out=xn.t[:, kc, :], in0=hb.t[:, :T], scalar=g_t.t[:, kc:kc + 1], in1=rstd.t[:, :T],
                    op0=ALU.mult, op1=ALU.mult), reads=[hb, g_t, rstd], writes=[xn] if kc == 0 else [])
            xn.wr = (cx.esem['dve'], cx.esem['dve'].n, 'dve')
            for jt in range(ncols // NJ):
                wb = wring.next()
                cx.dma('pool', lambda e, b=wb, jt=jt: e.dma_start(out=b.t[:], in_=wv[:, :, jt * NJ:(jt + 1) * NJ]),
                       writes=[wb], sembuf=wb)
                for jj in range(NJ // 128):
                    r0 = jt * NJ + jj * 128
                    for sub in range(nsub):
                        po = ps_o.next()
                        mm_group(cx, po, [(wb.t[:, kc, jj * 128:(jj + 1) * 128], xn.t[:, kc, sub * 512:(sub + 1) * 512])
                                          for kc in range(KC)], reads=[wb, xn])
                        ob = outr.next()
                        if (jj + sub) % 2 == 0:
                            cx.op('act', lambda e, po=po, ob=ob: nc.scalar.copy(ob.t[:], po.t[:]), reads=[po], writes=[ob])
                        else:
                            cx.op('dve', lambda e, po=po, ob=ob: nc.vector.tensor_copy(ob.t[:], po.t[:]), reads=[po], writes=[ob])
                        cx.dma('sp', lambda e, ob=ob, r0=r0, sub=sub: e.dma_start(
                            out=pT[r0:r0 + 128, t0 + sub * 512:t0 + (sub + 1) * 512], in_=ob.t[:]), reads=[ob], sembuf=ob)
        for ti in range(NT // T):
            tile_body(ti * T)
        cx.end_stage(blk)


def stage_wout(cx, hT, yT, w, T=1024):
    nc = cx.nc
    nsub = T // 512
    NJ = 256
    with contextlib.ExitStack() as es:
        cx.begin_stage()
        yb = cx.buf(alloc(es, nc, "yb", [128, KC, T], BF16), dma=True)
        wring = Ring([cx.buf(alloc(es, nc, f"w{i}", [128, 16, NJ], BF16), dma=True) for i in range(3)])
        outr = Ring([cx.buf(alloc(es, nc, f"ob{i}", [128, 512], F32), dma=True) for i in range(3)])
        hres = Ring([cx.buf(alloc(es, nc, f"hr{i}", [128, 512], F32), dma=True) for i in range(3)])
        ps_o = Ring([cx.buf(alloc(es, nc, f"pso{i}", [128, 512], F32, psum=True)) for i in range(4)])
        blk = es.enter_context(nc.Block())
        wv = w.rearrange("(kc p) n -> p kc n", p=128)
        yv = yT.rearrange("(kc p) t -> p kc t", p=128)

        def tile_body(t0):
            cx.dma('sp', lambda e: e.dma_start(out=yb.t[:], in_=yv[:, :, t0:t0 + T]), writes=[yb], sembuf=yb)
            for jt in range(D // NJ):
                wb = wring.next()
                cx.dma('pool', lambda e, b=wb, jt=jt: e.dma_start(out=b.t[:], in_=wv[:, :, jt * NJ:(jt + 1) * NJ]),
                       writes=[wb], sembuf=wb)
                for jj in range(NJ // 128):
                    c = jt * (NJ // 128) + jj
                    for sub in range(nsub):
                        hr = hres.next()
                        cx.dma('sp', lambda e, hr=hr, c=c, sub=sub: e.dma_start(
                            out=hr.t[:], in_=hT[c, :, t0 + sub * 512:t0 + (sub + 1) * 512]), writes=[hr], sembuf=hr)
                        po = ps_o.next()
                        mm_group(cx, po, [(wb.t[:, kc, jj * 128:(jj + 1) * 128], yb.t[:, kc, sub * 512:(sub + 1) * 512])
                                          for kc in range(KC)], reads=[wb, yb])
                        ob = outr.next()
                        cx.op('dve', lambda e, po=po, hr=hr, ob=ob: nc.vector.tensor_tensor(
                            out=ob.t[:], in0=po.t[:], in1=hr.t[:], op=ALU.add), reads=[po, hr], writes=[ob])
                        cx.dma('sp', lambda e, ob=ob, c=c, sub=sub: e.dma_start(
                            out=hT[c, :, t0 + sub * 512:t0 + (sub + 1) * 512], in_=ob.t[:]), reads=[ob], sembuf=ob)
        for ti in range(NT // T):
            tile_body(ti * T)
        cx.end_stage(blk)


C_ONES, C_EPS, C_BD64, C_ID, C_SU, C_SL, C_IU, C_RM = 0, 128, 256, 384, 512, 576, 640, 704
B0 = 832
C0 = 2528
R_KRS = 4064


def stage_conv(cx, pT, prm, consts, yT):
    nc = cx.nc
    with contextlib.ExitStack() as es:
        cx.begin_stage()
        bd = cx.buf(alloc(es, nc, "bd", [128, 128], F32), dma=True)
        eps_t = cx.buf(alloc(es, nc, "eps_t", [128, 1], F32), dma=True)
        pr = cx.buf(alloc(es, nc, "pr", [128, 4, 4], F32), dma=True)
        bg = Ring([cx.buf(alloc(es, nc, f"bg{i}", [128, S], F32), dma=True) for i in range(2)])
        cg = Ring([cx.buf(alloc(es, nc, f"cg{i}", [128, S], F32), dma=True) for i in range(2)])
        hh = Ring([cx.buf(alloc(es, nc, f"hh{i}", [128, S], F32), dma=True) for i in range(2)])
        u = cx.buf(alloc(es, nc, "u", [128, S + 2], F32))
        y = cx.buf(alloc(es, nc, "y", [128, S], F32))
        z = cx.buf(alloc(es, nc, "z", [128, S], F32))
        zsq = cx.buf(alloc(es, nc, "zsq", [128, S], F32))
        rs = cx.buf(alloc(es, nc, "rs", [128, S], F32))
        ob = Ring([cx.buf(alloc(es, nc, f"ob{i}", [128, S], BF16), dma=True) for i in range(2)])
        ps = Ring([cx.buf(alloc(es, nc, f"ps{i}", [128, 512], F32, psum=True)) for i in range(4)])
        blk = es.enter_context(nc.Block())
        cx.dma('sp', lambda e: e.dma_start(out=bd.t[:], in_=consts[:, C_BD64:C_BD64 + 128]), writes=[bd], sembuf=bd)
        cx.dma('sp', lambda e: e.dma_start(out=eps_t.t[:], in_=consts[:, C_EPS:C_EPS + 1], allow_slow_non_contiguous=True),
               writes=[eps_t], sembuf=eps_t)
        cx.dma('sp', lambda e: e.dma_start(out=pr.t[:], in_=prm[:, :, :]), writes=[pr], sembuf=pr)
        cx.op('dve', lambda e: nc.vector.memset(u.t[:, 0:2], 0.0), writes=[u])

        def body(b, ch):
            t0 = b * S
            bgb, cgb, hhb = bg.next(), cg.next(), hh.next()
            for (buf, r0) in ((bgb, C0 + ch * 128), (cgb, C0 + 512 + ch * 128), (hhb, C0 + 1024 + ch * 128)):
                cx.dma('sp', lambda e, buf=buf, r0=r0: e.dma_start(out=buf.t[:], in_=pT[r0:r0 + 128, t0:t0 + S]),
                       writes=[buf], sembuf=buf)
            cx.op('dve', lambda e: nc.vector.tensor_tensor(out=u.t[:, 2:S + 2], in0=cgb.t[:], in1=hhb.t[:], op=ALU.mult),
                  reads=[cgb, hhb], writes=[u])
            cx.op('act', lambda e: nc.scalar.activation(out=y.t[:], in_=u.t[:, 2:S + 2], func=AF.Copy, scale=pr.t[:, ch, 2:3]),
                  reads=[u, pr], writes=[y])
            cx.op('dve', lambda e: nc.vector.scalar_tensor_tensor(out=y.t[:], in0=u.t[:, 1:S + 1], scalar=pr.t[:, ch, 1:2],
                                                                  in1=y.t[:], op0=ALU.mult, op1=ALU.add), reads=[u, pr], writes=[y])
            cx.op('dve', lambda e: nc.vector.scalar_tensor_tensor(out=y.t[:], in0=u.t[:, 0:S], scalar=pr.t[:, ch, 0:1],
                                                                  in1=y.t[:], op0=ALU.mult, op1=ALU.add), reads=[u, pr], writes=[y])
            cx.op('dve', lambda e: nc.vector.tensor_tensor(out=z.t[:], in0=bgb.t[:], in1=y.t[:], op=ALU.mult),
                  reads=[bgb, y], writes=[z])
            cx.op('act', lambda e: nc.scalar.activation(out=zsq.t[:], in_=z.t[:], func=AF.Square), reads=[z], writes=[zsq])
            for sub in range(S // 512):
                p = ps.next()
                sl = slice(sub * 512, (sub + 1) * 512)
                cx.op('pe', lambda e, p=p, sl=sl: nc.tensor.matmul(p.t[:], bd.t[:], zsq.t[:, sl], start=True, stop=True),
                      reads=[bd, zsq], writes=[p])
                cx.op('act', lambda e, p=p, sl=sl: nc.scalar.activation(out=rs.t[:, sl], in_=p.t[:], func=AF.Sqrt,
                                                                        bias=eps_t.t[:, 0:1], scale=1.0 / 64),
                      reads=[p, eps_t], writes=[rs])
            cx.op('dve', lambda e: nc.vector.reciprocal(rs.t[:], rs.t[:]), reads=[rs], writes=[rs])
            o = ob.next()
            cx.op('dve', lambda e, o=o: nc.vector.scalar_tensor_tensor(out=o.t[:], in0=z.t[:], scalar=pr.t[:, ch, 3:4],
                                                                       in1=rs.t[:], op0=ALU.mult, op1=ALU.mult),
                  reads=[z, pr, rs], writes=[o])
            cx.dma('sp', lambda e, o=o: e.dma_start(out=yT[1536 + ch * 128:1536 + (ch + 1) * 128, t0:t0 + S], in_=o.t[:]),
                   reads=[o], sembuf=o)
        for b in range(NSEQ):
            for ch in range(4):
                body(b, ch)
        cx.end_stage(blk)


def stage_mla(cx, pT, wq, wkv, prm, cs, consts, maskd, yT):
    nc = cx.nc
    T = 1024
    scale = float((128 + 64) ** -0.5)
    with contextlib.ExitStack() as es:
        cx.begin_stage()
        ones_f, eps_t = load_consts(cx, es, nc, consts)
        ones_b = cx.buf(alloc(es, nc, "ones_b", [128, 128], BF16), dma=True)
        id_b = cx.buf(alloc(es, nc, "id_b", [128, 128], BF16), dma=True)
        mask = cx.buf(alloc(es, nc, "mask", [128, 4, 512], BF16), dma=True)
        pr = cx.buf(alloc(es, nc, "pr", [128, 16], F32), dma=True)
        wqb = cx.buf(alloc(es, nc, "wqb", [128, 4, 2048], BF16), dma=True)
        wkvb = cx.buf(alloc(es, nc, "wkvb", [128, 2, 2048], BF16), dma=True)
        cqn = cx.buf(alloc(es, nc, "cqn", [128, 4, S], BF16))
        ckvn = cx.buf(alloc(es, nc, "ckvn", [128, 2, S], BF16))
        cc = cx.buf(alloc(es, nc, "cc", [64, S], F32), dma=True)
        ss = cx.buf(alloc(es, nc, "ss", [64, S], F32), dma=True)
        kx = cx.buf(alloc(es, nc, "kx", [64, S], F32), dma=True)
        kxs = cx.buf(alloc(es, nc, "kxs", [64, S], F32), dma=True)
        k_r = cx.buf(alloc(es, nc, "k_r", [64, S], BF16))
        q_n = cx.buf(alloc(es, nc, "q_n", [128, S], BF16))
        q_r = cx.buf(alloc(es, nc, "q_r", [64, S], BF16))
        k_n = cx.buf(alloc(es, nc, "k_n", [128, S], BF16))
        V = cx.buf(alloc(es, nc, "V", [128, 16, 128], BF16))
        xt = cx.buf(alloc(es, nc, "xt", [64, 512], F32))
        PT = Ring([cx.buf(alloc(es, nc, f"PT{i}", [128, 512], BF16)) for i in range(3)])
        hring = Ring([cx.buf(alloc(es, nc, f"hb{i}", [128, T], F32), dma=True) for i in range(2)])
        sqring = Ring([cx.buf(alloc(es, nc, f"sq{i}", [128, T], F32)) for i in range(2)])
        rstd = cx.buf(alloc(es, nc, "rstd", [128, T], F32))
        rinv = cx.buf(alloc(es, nc, "rinv", [128, 512], F32))
        yv = cx.buf(alloc(es, nc, "yv", [128, 512], F32))
        ysq = cx.buf(alloc(es, nc, "ysq", [128, 512], F32))
        sd2 = cx.buf(alloc(es, nc, "sd2", [128, 512], F32))
        obr = Ring([cx.buf(alloc(es, nc, f"ob{i}", [128, 512], BF16), dma=True) for i in range(2)])
        ps_p = Ring([cx.buf(alloc(es, nc, f"psp{i}", [128, 512], F32, psum=True)) for i in range(2)])
        ps_ss = ps_p.bufs
        ps_s = Ring([cx.buf(alloc(es, nc, f"pst{i}", [128, 512], F32, psum=True)) for i in range(2)])
        ps_o_r = Ring([cx.buf(alloc(es, nc, f"pso{i}", [128, 512], F32, psum=True)) for i in range(2)])
        ps_r_r = Ring([cx.buf(alloc(es, nc, f"psr{i}", [128, 512], F32, psum=True)) for i in range(2)])
        blk = es.enter_context(nc.Block())

        cx.dma('pool', lambda e: e.dma_start(out=ones_b.t[:], in_=consts[:, C_ONES:C_ONES + 128]), writes=[ones_b], sembuf=ones_b)
        cx.dma('pool', lambda e: e.dma_start(out=id_b.t[:], in_=consts[:, C_ID:C_ID + 128]), writes=[id_b], sembuf=id_b)
        cx.dma('pool', lambda e: e.dma_start(out=mask.t[:], in_=maskd[:, :, :]), writes=[mask], sembuf=mask)
        cx.dma('sp', lambda e: e.dma_start(out=pr.t[:], in_=prm[:, :]), writes=[pr], sembuf=pr)
        cx.dma('pool', lambda e: e.dma_start(out=wqb.t[:], in_=wq.rearrange("(kc p) n -> p kc n", p=128)), writes=[wqb], sembuf=wqb)
        cx.dma('pool', lambda e: e.dma_start(out=wkvb.t[:], in_=wkv.rearrange("(kc p) n -> p kc n", p=128)), writes=[wkvb], sembuf=wkvb)

        def norm_into(dst, row0, nch, gcol0, t0):
            for half in range(S // T):
                tt0 = t0 + half * T
                ld = lambda kc, hb, tt0=tt0: (lambda e: e.dma_start(out=hb.t[:, :T], in_=pT[row0 + kc * 128:row0 + (kc + 1) * 128, tt0:tt0 + T]))
                rms_stats(cx, nc, ld, nch, T, hring, sqring, ps_ss, ones_f, nch * 128, eps_t, rstd, rstd)
                for kc in range(nch):
                    hb = hring.next()
                    cx.dma('sp', ld(kc, hb), writes=[hb], sembuf=hb)
                    cx.op('dve', lambda e, hb=hb, kc=kc, half=half: nc.vector.scalar_tensor_tensor(
                        out=dst.t[:, kc, half * T:(half + 1) * T], in0=hb.t[:, :T], scalar=pr.t[:, gcol0 + kc:gcol0 + kc + 1],
                        in1=rstd.t[:, :T], op0=ALU.mult, op1=ALU.mult), reads=[hb, pr, rstd], writes=[dst])

        def rope(dst, x_ap, xs_ap, sl, xbufs):
            cx.op('dve', lambda e: nc.vector.tensor_tensor(out=xt.t[:, :sl.stop - sl.start], in0=xs_ap, in1=ss.t[:, sl], op=ALU.mult),
                  reads=xbufs + [ss], writes=[xt])
            cx.op('dve', lambda e: nc.vector.tensor_tensor(out=yv.t[0:64, :sl.stop - sl.start], in0=x_ap, in1=cc.t[:, sl], op=ALU.mult),
                  reads=xbufs + [cc], writes=[yv])
            cx.op('dve', lambda e: nc.vector.tensor_tensor(out=dst.t[0:64, sl], in0=yv.t[0:64, :sl.stop - sl.start],
                                                           in1=xt.t[:, :sl.stop - sl.start], op=ALU.add),
                  reads=[yv, xt], writes=[dst])

        def seq_body(b):
            t0 = b * S
            cx.dma('sp', lambda e: e.dma_start(out=cc.t[:], in_=cs[b, 0, :, :]), writes=[cc], sembuf=cc)
            cx.dma('sp', lambda e: e.dma_start(out=ss.t[:], in_=cs[b, 1, :, :]), writes=[ss], sembuf=ss)
            cx.dma('sp', lambda e: e.dma_start(out=kx.t[:], in_=pT[768:832, t0:t0 + S]), writes=[kx], sembuf=kx)
            cx.dma('sp', lambda e: e.dma_start(out=kxs.t[:], in_=pT[R_KRS:R_KRS + 64, t0:t0 + S]), writes=[kxs], sembuf=kxs)
            norm_into(cqn, 0, 4, 0, t0)
            norm_into(ckvn, 512, 2, 4, t0)
            for sub in range(4):
                sl = slice(sub * 512, (sub + 1) * 512)
                rope(k_r, kx.t[:, sl], kxs.t[:, sl], sl, [kx, kxs])
            for h in range(8):
                head_body(h, t0)

        def head_body(h, t0):
            if True:
                c0 = h * 256
                for sub in range(4):
                    sl = slice(sub * 512, (sub + 1) * 512)
                    p = ps_p.next()
                    mm_group(cx, p, [(wqb.t[:, kc, c0:c0 + 128], cqn.t[:, kc, sl]) for kc in range(4)], reads=[wqb, cqn])
                    cx.op('act', lambda e, p=p, sl=sl: nc.scalar.copy(q_n.t[:, sl], p.t[:]), reads=[p], writes=[q_n])
                    p = ps_p.next()
                    mm_group(cx, p, [(wkvb.t[:, kc, c0:c0 + 128], ckvn.t[:, kc, sl]) for kc in range(2)], reads=[wkvb, ckvn])
                    cx.op('act', lambda e, p=p, sl=sl: nc.scalar.copy(k_n.t[:, sl], p.t[:]), reads=[p], writes=[k_n])
                    p1 = ps_p.next()
                    mm_group(cx, p1, [(wqb.t[:, kc, c0 + 128:c0 + 192], cqn.t[:, kc, sl]) for kc in range(4)], reads=[wqb, cqn],
                             out_ap=p1.t[0:64, :])
                    p2 = ps_p.next()
                    mm_group(cx, p2, [(wqb.t[:, kc, c0 + 192:c0 + 256], cqn.t[:, kc, sl]) for kc in range(4)], reads=[wqb, cqn],
                             out_ap=p2.t[0:64, :])
                    rope(q_r, p1.t[0:64, :], p2.t[0:64, :], sl, [p1, p2])
                    p = ps_p.next()

                    def vfn(eng, p=p, sub=sub):
                        ins = None
                        for j in range(4):
                            tb = sub * 4 + j
                            for kc in range(2):
                                ins = nc.tensor.matmul(p.t[:, j * 128:(j + 1) * 128], ckvn.t[:, kc, tb * 128:(tb + 1) * 128],
                                                       wkvb.t[:, kc, c0 + 128:c0 + 256], start=(kc == 0), stop=(kc == 1))
                        return ins
                    cx.op('pe', vfn, reads=[ckvn, wkvb], writes=[p])
                    cx.op('act', lambda e, p=p, sub=sub: nc.scalar.copy(
                        V.t[:, sub * 4:(sub + 1) * 4, :], p.t[:].rearrange("p (a b) -> p a b", a=4)), reads=[p], writes=[V])
                items = [(qt, kb) for qt in range(4) for kb in range(4 * (qt + 1))]
                acc = {}

                def emit_st(qt, kb):
                    qs = slice(qt * 512, (qt + 1) * 512)
                    ks = slice(kb * 128, (kb + 1) * 128)
                    st = ps_s.next()
                    pairs = [(k_n.t[:, ks], q_n.t[:, qs]), (k_r.t[0:64, ks], q_r.t[0:64, qs])]
                    rds = [k_n, q_n, k_r, q_r]
                    if kb >= 4 * qt:
                        pairs.append((id_b.t[:], mask.t[:, kb - 4 * qt, :]))
                        rds += [id_b, mask]
                    mm_group(cx, st, pairs, reads=rds)
                    pt = PT.next()
                    cx.op('act', lambda e, st=st, pt=pt: nc.scalar.activation(out=pt.t[:], in_=st.t[:], func=AF.Exp, scale=scale),
                          reads=[st], writes=[pt])
                    return pt

                def emit_pv(qt, kb, pt):
                    nkb = 4 * (qt + 1)
                    if kb == 0:
                        acc[qt] = (ps_o_r.next(), ps_r_r.next())
                    ps_o, ps_r = acc[qt]

                    def pv(eng):
                        nc.tensor.matmul(ps_o.t[:], V.t[:, kb, :], pt.t[:], start=(kb == 0), stop=(kb == nkb - 1))
                        return nc.tensor.matmul(ps_r.t[:], ones_b.t[:], pt.t[:], start=(kb == 0), stop=(kb == nkb - 1))
                    cx.op('pe', pv, reads=[pt, V, ones_b], writes=[ps_o, ps_r])
                    if kb == nkb - 1:
                        finalize(qt, ps_o, ps_r)

                def finalize(qt, ps_o, ps_r):
                    cx.op('dve', lambda e: nc.vector.reciprocal(rinv.t[:], ps_r.t[:]), reads=[ps_r], writes=[rinv])
                    cx.op('dve', lambda e: nc.vector.tensor_tensor(out=yv.t[:], in0=ps_o.t[:], in1=rinv.t[:], op=ALU.mult),
                          reads=[ps_o, rinv], writes=[yv])
                    cx.op('act', lambda e: nc.scalar.activation(out=ysq.t[:], in_=yv.t[:], func=AF.Square), reads=[yv], writes=[ysq])
                    pm = ps_p.next()
                    cx.op('pe', lambda e: nc.tensor.matmul(pm.t[:], ones_f.t[:], ysq.t[:], start=True, stop=True),
                          reads=[ones_f, ysq], writes=[pm])
                    cx.op('act', lambda e: nc.scalar.activation(out=sd2.t[:], in_=pm.t[:], func=AF.Sqrt,
                                                                bias=eps_t.t[:, 0:1], scale=1.0 / 128),
                          reads=[pm, eps_t], writes=[sd2])
                    cx.op('dve', lambda e: nc.vector.reciprocal(sd2.t[:], sd2.t[:]), reads=[sd2], writes=[sd2])
                    o = obr.next()
                    cx.op('dve', lambda e: nc.vector.scalar_tensor_tensor(
                        out=o.t[:], in0=yv.t[:], scalar=pr.t[:, 6 + h:7 + h], in1=sd2.t[:], op0=ALU.mult, op1=ALU.mult),
                        reads=[yv, pr, sd2], writes=[o])
                    cx.dma('sp', lambda e: e.dma_start(
                        out=yT[h * 128:(h + 1) * 128, t0 + qt * 512:t0 + (qt + 1) * 512], in_=o.t[:]), reads=[o], sembuf=o)

                pending = None
                for (qt, kb) in items:
                    pt = emit_st(qt, kb)
                    if pending is not None:
                        emit_pv(*pending)
                    pending = (qt, kb, pt)
                emit_pv(*pending)
        for b in range(NSEQ):
            seq_body(b)
        cx.end_stage(blk)


def stage_rwkv(cx, pT, rp, dup, iup, gup, consts, yT):
    nc = cx.nc
    TB = 128
    c1 = -0.6065306597126334
    with contextlib.ExitStack() as es:
        cx.begin_stage()

        def sb(name, shape, dt=F32, dma=False):
            return cx.buf(alloc(es, nc, name, shape, dt), dma=dma)
        cst = sb("cst", [64, 1024], dma=True)
        rpt = sb("rpt", [128, 96], dma=True)
        dupt = sb("dupt", [32, 512], dma=True)
        iupt = sb("iupt", [32, 512], dma=True)
        gupt = sb("gupt", [96, 512], dma=True)
        omka = sb("omka", [64, 8])
        cur3 = sb("cur3", [64, 3, 8, TB], dma=True)
        prv3 = sb("prv3", [64, 3, 8, TB], dma=True)
        loc = sb("loc", [96, 3, TB], dma=True)
        lop = sb("lop", [96, 3, TB], dma=True)
        names = ["sw", "a", "g", "cum", "E1", "E2", "E3", "E4", "kk", "nr", "tmp", "kp", "bv", "aT", "bb", "bT", "bhT", "kT", "khT", "rT"]
        Tt = {n: sb(n, [64, 8, TB]) for n in names}
        cn = ["V", "Bh", "Kh", "N", "L", "AKu", "BRu", "KRu", "P", "Q", "Na", "La", "Nb", "Lb"]
        Cs = [{n: sb(n + str(c), [64, 8, 64]) for n in cn} for c in range(TB // 64)]
        Ct = {n: sb(n, [64, 8, 64]) for n in ["Zs", "Us", "H", "tH", "ysb", "yc", "sq", "sdv"]}
        obr = Ring([sb(f"ob{i}", [64, 8, 64], BF16, dma=True) for i in range(2)])
        pp = Ring([cx.buf(alloc(es, nc, f"pp{i}", [128, 512], F32, psum=True)) for i in range(8)])
        blk = es.enter_context(nc.Block())

        cx.dma('sp', lambda e: e.dma_start(out=cst.t[:], in_=consts[0:64, :]), writes=[cst], sembuf=cst)
        cx.dma('sp', lambda e: e.dma_start(out=rpt.t[:], in_=rp[:, :]), writes=[rpt], sembuf=rpt)
        cx.dma('sp', lambda e: e.dma_start(out=dupt.t[:], in_=dup[:, :]), writes=[dupt], sembuf=dupt)
        cx.dma('sp', lambda e: e.dma_start(out=iupt.t[:], in_=iup[:, :]), writes=[iupt], sembuf=iupt)
        cx.dma('sp', lambda e: e.dma_start(out=gupt.t[:], in_=gup[:, :]), writes=[gupt], sembuf=gupt)
        ones64 = cst.t[:, C_ONES:C_ONES + 64]
        id64 = cst.t[:, C_ID:C_ID + 64]
        eps64 = cst.t[:, C_EPS + 1:C_EPS + 2]

        def mk(c):
            return cst.t[:, c:c + 64].unsqueeze(1).broadcast_to([64, 8, 64])
        SUb, SLb, IUb, IDb = mk(C_SU), mk(C_SL), mk(C_IU), mk(C_ID)
        rmask = cst.t[:, C_RM:C_RM + TB]

        def pb(col):
            return rpt.t[0:64, col:col + 8]

        def bc(ap2, n):
            return ap2.unsqueeze(2).broadcast_to([64, 8, n])

        def TTo(o, oap, ins, op, eng='dve'):
            bufs = [x[0] for x in ins]
            aps = [x[1] for x in ins]
            cx.op(eng, lambda e: nc.vector.tensor_tensor(out=oap, in0=aps[0], in1=aps[1], op=op),
                  reads=[b for b in bufs if b is not None], writes=[o])

        def ACTo(o, oap, i, iap, func, scale=1.0, bias=None, extra=()):
            def fn(e):
                if bias is None:
                    return nc.scalar.activation(out=oap, in_=iap, func=func, scale=scale)
                return nc.scalar.activation(out=oap, in_=iap, func=func, scale=scale, bias=bias)
            cx.op('act', fn, reads=[i] + list(extra), writes=[o])

        for bt in (loc, lop):
            cx.op('dve', lambda e, bt=bt: nc.vector.memset(bt.t[:], 0.0), writes=[bt])
        cx.op('dve', lambda e: nc.vector.tensor_scalar(out=omka.t[:], in0=pb(48), scalar1=-1.0, scalar2=1.0,
                                                       op0=ALU.mult, op1=ALU.add), reads=[rpt], writes=[omka])

        def rows3(t_lo, t_hi):
            return [pT[B0 + kd * 512:B0 + (kd + 1) * 512, t_lo:t_hi].rearrange("(h f) t -> f h t", f=64) for kd in range(3)]

        def block_body(b, k):
            t0 = b * S + k * TB
            T = Tt
            for kd, src in enumerate(rows3(t0, t0 + TB)):
                cx.dma('sp', lambda e, kd=kd, src=src: e.dma_start(out=cur3.t[:, kd], in_=src), writes=[cur3], sembuf=cur3)
            cx.dma('sp', lambda e: e.dma_start(out=loc.t[0:32, 0, :], in_=pT[B0 + 1536:B0 + 1568, t0:t0 + TB]), writes=[loc], sembuf=loc)
            cx.dma('sp', lambda e: e.dma_start(out=loc.t[0:32, 1, :], in_=pT[B0 + 1568:B0 + 1600, t0:t0 + TB]), writes=[loc], sembuf=loc)
            cx.dma('sp', lambda e: e.dma_start(out=loc.t[0:96, 2, :], in_=pT[B0 + 1600:B0 + 1696, t0:t0 + TB]), writes=[loc], sembuf=loc)
            if k == 0:
                cx.op('dve', lambda e: nc.vector.memset(prv3.t[:, :, :, 0:1], 0.0), writes=[prv3])
                cx.op('dve', lambda e: nc.vector.memset(lop.t[:, :, 0:1], 0.0), writes=[lop])
                cx.op('dve', lambda e: nc.vector.memset(Ct["H"].t[:], 0.0), writes=[Ct["H"]])
                for kd, src in enumerate(rows3(t0, t0 + TB - 1)):
                    cx.dma('sp', lambda e, kd=kd, src=src: e.dma_start(out=prv3.t[:, kd, :, 1:TB], in_=src), writes=[prv3], sembuf=prv3)
                o1, lo_, hi_ = 1, t0, t0 + TB - 1
            else:
                for kd, src in enumerate(rows3(t0 - 1, t0 + TB - 1)):
                    cx.dma('sp', lambda e, kd=kd, src=src: e.dma_start(out=prv3.t[:, kd], in_=src), writes=[prv3], sembuf=prv3)
                o1, lo_, hi_ = 0, t0 - 1, t0 + TB - 1
            cx.dma('sp', lambda e: e.dma_start(out=lop.t[0:32, 0, o1:TB], in_=pT[B0 + 1536:B0 + 1568, lo_:hi_]), writes=[lop], sembuf=lop)
            cx.dma('sp', lambda e: e.dma_start(out=lop.t[0:32, 1, o1:TB], in_=pT[B0 + 1568:B0 + 1600, lo_:hi_]), writes=[lop], sembuf=lop)
            cx.dma('sp', lambda e: e.dma_start(out=lop.t[0:96, 2, o1:TB], in_=pT[B0 + 1600:B0 + 1696, lo_:hi_]), writes=[lop], sembuf=lop)
            c3 = cur3.t[:].rearrange("p a h t -> p (a h) t")
            p3 = prv3.t[:].rearrange("p a h t -> p (a h) t")
            mu3 = rpt.t[0:64, 0:24].unsqueeze(2).broadcast_to([64, 24, TB])
            TTo(prv3, p3, [(prv3, p3), (cur3, c3)], ALU.subtract)
            TTo(prv3, p3, [(prv3, p3), (rpt, mu3)], ALU.mult)
            TTo(prv3, p3, [(prv3, p3), (cur3, c3)], ALU.add)
            mul = rpt.t[0:96, 80:83].unsqueeze(2).broadcast_to([96, 3, TB])
            TTo(lop, lop.t[:], [(lop, lop.t[:]), (loc, loc.t[:])], ALU.subtract)
            TTo(lop, lop.t[:], [(lop, lop.t[:]), (rpt, mul)], ALU.mult)
            TTo(lop, lop.t[:], [(lop, lop.t[:]), (loc, loc.t[:])], ALU.add)
            rs_, ks_, vs_ = prv3.t[:, 0], prv3.t[:, 1], prv3.t[:, 2]
            ACTo(lop, lop.t[0:32, 0, :], lop, lop.t[0:32, 0, :], AF.Tanh)
            ACTo(lop, lop.t[0:96, 2, :], lop, lop.t[0:96, 2, :], AF.Sigmoid)
            for (dst, wt, kdim, li, bcol) in ((T["sw"], dupt, 32, 0, 24), (T["a"], iupt, 32, 1, 32), (T["g"], gupt, 96, 2, None)):
                for half in range(2):
                    p = pp.next()

                    def fn(e, p=p, wt=wt, kdim=kdim, li=li, half=half):
                        ins = None
                        for hh in range(4):
                            h = half * 4 + hh
                            ins = nc.tensor.matmul(p.t[0:64, hh * TB:(hh + 1) * TB], wt.t[0:kdim, h * 64:(h + 1) * 64],
                                                   lop.t[0:kdim, li, :], start=True, stop=True)
                        return ins
                    cx.op('pe', fn, reads=[wt, lop], writes=[p])
                    dap = dst.t[:, half * 4:(half + 1) * 4, :]
                    pap = p.t[0:64, :].rearrange("p (h t) -> p h t", h=4)
                    if bcol is None:
                        ACTo(dst, dap, p, pap, AF.Copy)
                    else:
                        bb_ = rpt.t[0:64, bcol + half * 4:bcol + half * 4 + 4].unsqueeze(2).broadcast_to([64, 4, TB])
                        TTo(dst, dap, [(p, pap), (rpt, bb_)], ALU.add)
                if bcol is not None:
                    ACTo(dst, dst.t[:], dst, dst.t[:], AF.Sigmoid)
            for h in range(8):
                cx.op('dve', lambda e, h=h: nc.vector.tensor_tensor_scan(
                    out=T["cum"].t[:, h, :], data0=rmask, data1=T["sw"].t[:, h, :], initial=0.0, op0=ALU.mult, op1=ALU.add),
                    reads=[cst, T["sw"]], writes=[T["cum"]])
            ACTo(T["E1"], T["E1"].t[:], T["cum"], T["cum"].t[:], AF.Exp, scale=c1)
            ACTo(T["E2"], T["E2"].t[:], T["cum"], T["cum"].t[:], AF.Exp, scale=-c1)
            TTo(T["E3"], T["E3"].t[:], [(T["cum"], T["cum"].t[:]), (T["sw"], T["sw"].t[:])], ALU.subtract)
            ACTo(T["E3"], T["E3"].t[:], T["E3"], T["E3"].t[:], AF.Exp, scale=c1)
            cum4 = T["cum"].t[:].rearrange("p h (c t) -> p h c t", t=64)
            cumC = cum4[:, :, :, 63:64].broadcast_to([64, 8, TB // 64, 64])
            e44 = T["E4"].t[:].rearrange("p h (c t) -> p h c t", t=64)
            TTo(T["E4"], e44, [(T["cum"], cumC), (T["cum"], cum4)], ALU.subtract)
            ACTo(T["E4"], T["E4"].t[:], T["E4"], T["E4"].t[:], AF.Exp, scale=c1)
            TTo(T["kk"], T["kk"].t[:], [(prv3, ks_), (rpt, bc(pb(40), TB))], ALU.mult)
            ACTo(T["nr"], T["nr"].t[:], T["kk"], T["kk"].t[:], AF.Square)
            for half in range(2):
                p = pp.next()
                hs = slice(half * 4, (half + 1) * 4)
                cx.op('pe', lambda e, p=p, hs=hs: nc.tensor.matmul(p.t[0:64, :], ones64, T["nr"].t[:, hs, :], start=True, stop=True),
                      reads=[cst, T["nr"]], writes=[p])
                ACTo(T["tmp"], T["tmp"].t[:, hs, :], p, p.t[0:64, :].rearrange("p (h t) -> p h t", h=4), AF.Sqrt)
            cx.op('dve', lambda e: nc.vector.tensor_scalar(out=T["tmp"].t[:], in0=T["tmp"].t[:], scalar1=1e-12, scalar2=None, op0=ALU.max),
                  reads=[T["tmp"]], writes=[T["tmp"]])
            cx.op('dve', lambda e: nc.vector.reciprocal(T["tmp"].t[:], T["tmp"].t[:]), reads=[T["tmp"]], writes=[T["tmp"]])
            TTo(T["kk"], T["kk"].t[:], [(T["kk"], T["kk"].t[:]), (T["tmp"], T["tmp"].t[:])], ALU.mult)
            TTo(T["tmp"], T["tmp"].t[:], [(T["a"], T["a"].t[:]), (rpt, bc(pb(48), TB))], ALU.mult)
            TTo(T["tmp"], T["tmp"].t[:], [(T["tmp"], T["tmp"].t[:]), (omka, bc(omka.t[:], TB))], ALU.add)
            TTo(T["kp"], T["kp"].t[:], [(prv3, ks_), (T["tmp"], T["tmp"].t[:])], ALU.mult)
            TTo(T["tmp"], T["tmp"].t[:], [(prv3, rs_), (T["kp"], T["kp"].t[:])], ALU.mult)
            TTo(T["tmp"], T["tmp"].t[:], [(T["tmp"], T["tmp"].t[:]), (rpt, bc(pb(56), TB))], ALU.mult)
            for half in range(2):
                p = pp.next()
                hs = slice(half * 4, (half + 1) * 4)
                cx.op('pe', lambda e, p=p, hs=hs: nc.tensor.matmul(p.t[0:64, :], ones64, T["tmp"].t[:, hs, :], start=True, stop=True),
                      reads=[cst, T["tmp"]], writes=[p])
                TTo(T["bv"], T["bv"].t[:, hs, :], [(p, p.t[0:64, :].rearrange("p (h t) -> p h t", h=4)), (prv3, prv3.t[:, 2, hs, :])], ALU.mult)
            cx.op('dve', lambda e: nc.vector.scalar_tensor_tensor(out=T["aT"].t[:], in0=T["kk"].t[:], scalar=-1.0, in1=T["E3"].t[:],
                                                                  op0=ALU.mult, op1=ALU.mult), reads=[T["kk"], T["E3"]], writes=[T["aT"]])
            TTo(T["bb"], T["bb"].t[:], [(T["kk"], T["kk"].t[:]), (T["a"], T["a"].t[:])], ALU.mult)
            TTo(T["bT"], T["bT"].t[:], [(T["bb"], T["bb"].t[:]), (T["E2"], T["E2"].t[:])], ALU.mult)
            TTo(T["bhT"], T["bhT"].t[:], [(T["bb"], T["bb"].t[:]), (T["E4"], T["E4"].t[:])], ALU.mult)
            TTo(T["kT"], T["kT"].t[:], [(T["kp"], T["kp"].t[:]), (T["E2"], T["E2"].t[:])], ALU.mult)
            TTo(T["khT"], T["khT"].t[:], [(T["kp"], T["kp"].t[:]), (T["E4"], T["E4"].t[:])], ALU.mult)
            TTo(T["rT"], T["rT"].t[:], [(prv3, rs_), (T["E1"], T["E1"].t[:])], ALU.mult)
            run_chunks(b, k)

        def pgroup(builder, reads):
            p = pp.next()

            def fn(e, p=p):
                ins = None
                for h in range(8):
                    ins = builder(p.t[0:64, h * 64:(h + 1) * 64], h)
                return ins
            cx.op('pe', fn, reads=reads, writes=[p])
            return p, p.t[0:64, :].rearrange("p (h t) -> p h t", h=8)

        def phase_a(b, k, c):
            T, C = Tt, Cs[c]
            cs_ = slice(c * 64, (c + 1) * 64)
            for dst, src_b, src_ap in ((C["V"], prv3, prv3.t[:, 2]), (C["Bh"], T["bhT"], T["bhT"].t[:]), (C["Kh"], T["khT"], T["khT"].t[:])):
                p, pv = pgroup(lambda o, h, src_ap=src_ap: nc.tensor.transpose(o, src_ap[:, h, cs_], id64), [src_b, cst])
                ACTo(dst, dst.t[:], p, pv, AF.Copy)
            yield

            def pw(dst, lt, rt, mb):
                p, pv = pgroup(lambda o, h: nc.tensor.matmul(o, lt.t[:, h, cs_], rt.t[:, h, cs_], start=True, stop=True), [lt, rt])
                TTo(dst, dst.t[:], [(p, pv), (cst, mb)], ALU.mult)
            pw(C["N"], T["bT"], T["aT"], SUb)
            pw(C["L"], T["aT"], T["bT"], SLb)
            yield
            pw(C["AKu"], T["kT"], T["aT"], SUb)
            pw(C["BRu"], T["bT"], T["rT"], IUb)
            pw(C["KRu"], T["kT"], T["rT"], IUb)
            TTo(C["P"], C["P"].t[:], [(C["N"], C["N"].t[:]), (cst, IDb)], ALU.add)
            TTo(C["Q"], C["Q"].t[:], [(C["L"], C["L"].t[:]), (cst, IDb)], ALU.add)
            yield
            Nc, Lc = C["N"], C["L"]
            nxt = [(C["Na"], C["La"]), (C["Nb"], C["Lb"])]
            for lvl in range(5):
                Nn, Ln = nxt[lvl % 2]
                p, pv = pgroup(lambda o, h, Nc=Nc, Lc=Lc: nc.tensor.matmul(o, Lc.t[:, h, :], Nc.t[:, h, :], start=True, stop=True), [Nc, Lc])
                if lvl < 4:
                    p2, pv2 = pgroup(lambda o, h, Nc=Nc, Lc=Lc: nc.tensor.matmul(o, Nc.t[:, h, :], Lc.t[:, h, :], start=True, stop=True), [Nc, Lc])
                ACTo(Nn, Nn.t[:], p, pv, AF.Copy)
                if lvl < 4:
                    cx.op('dve', lambda e, Ln=Ln, pv2=pv2: nc.vector.tensor_copy(Ln.t[:], pv2), reads=[p2], writes=[Ln])
                yield
                p, pv = pgroup(lambda o, h, Nn=Nn: nc.tensor.matmul(o, C["Q"].t[:, h, :], Nn.t[:, h, :], start=True, stop=True), [C["Q"], Nn])
                if lvl < 4:
                    p2, pv2 = pgroup(lambda o, h, Ln=Ln: nc.tensor.matmul(o, C["P"].t[:, h, :], Ln.t[:, h, :], start=True, stop=True), [C["P"], Ln])
                TTo(C["P"], C["P"].t[:], [(C["P"], C["P"].t[:]), (p, pv)], ALU.add)
                if lvl < 4:
                    TTo(C["Q"], C["Q"].t[:], [(C["Q"], C["Q"].t[:]), (p2, pv2)], ALU.add)
                Nc, Lc = Nn, Ln
                yield

        def phase_b(b, k, c):
            T, C = Tt, Cs[c]
            cs_ = slice(c * 64, (c + 1) * 64)
            tc0 = b * S + k * TB + c * 64
            H = Ct["H"]

            def zb(o, h):
                nc.tensor.matmul(o, T["aT"].t[:, h, cs_], H.t[:, h, :], start=True, stop=False)
                return nc.tensor.matmul(o, C["AKu"].t[:, h, :], C["V"].t[:, h, :], start=False, stop=True)
            p, pv = pgroup(zb, [T["aT"], H, C["AKu"], C["V"]])
            ACTo(Ct["Zs"], Ct["Zs"].t[:], p, pv, AF.Copy)
            p, pv = pgroup(lambda o, h: nc.tensor.matmul(o, C["P"].t[:, h, :], Ct["Zs"].t[:, h, :], start=True, stop=True), [C["P"], Ct["Zs"]])
            cx.op('dve', lambda e, pv=pv: nc.vector.tensor_copy(Ct["Us"].t[:], pv), reads=[p], writes=[Ct["Us"]])

            def yb_(o, h):
                nc.tensor.matmul(o, H.t[:, h, :], T["rT"].t[:, h, cs_], start=True, stop=False)
                nc.tensor.matmul(o, Ct["Us"].t[:, h, :], C["BRu"].t[:, h, :], start=False, stop=False)
                return nc.tensor.matmul(o, C["V"].t[:, h, :], C["KRu"].t[:, h, :], start=False, stop=True)

            def hb_(o, h):
                nc.tensor.matmul(o, C["Bh"].t[:, h, :], Ct["Us"].t[:, h, :], start=True, stop=False)
                return nc.tensor.matmul(o, C["Kh"].t[:, h, :], C["V"].t[:, h, :], start=False, stop=True)
            ph, phv = pgroup(hb_, [C["Bh"], Ct["Us"], C["Kh"], C["V"]])
            py, pyv = pgroup(yb_, [H, T["rT"], Ct["Us"], C["BRu"], C["V"], C["KRu"]])
            gC = T["E1"].t[:, :, c * 64 + 63:c * 64 + 64].broadcast_to([64, 8, 64])
            TTo(Ct["tH"], Ct["tH"].t[:], [(H, H.t[:]), (T["E1"], gC)], ALU.mult)
            TTo(H, H.t[:], [(Ct["tH"], Ct["tH"].t[:]), (ph, phv)], ALU.add)
            Cc = Ct
            ACTo(Cc["ysb"], Cc["ysb"].t[:], py, pyv, AF.Copy)
            ysf = Cc["ysb"].t[:].rearrange("p h t -> p (h t)")
            pm = pp.next()
            cx.op('pe', lambda e, pm=pm: nc.tensor.matmul(pm.t[0:64, :], ones64, ysf, start=True, stop=True), reads=[cst, Cc["ysb"]], writes=[pm])
            cx.op('dve', lambda e, pm=pm: nc.vector.scalar_tensor_tensor(
                out=Cc["yc"].t[:].rearrange("p h t -> p (h t)"), in0=pm.t[0:64, :], scalar=-1.0 / 64, in1=ysf, op0=ALU.mult, op1=ALU.add),
                reads=[pm, Cc["ysb"]], writes=[Cc["yc"]])
            ACTo(Cc["sq"], Cc["sq"].t[:], Cc["yc"], Cc["yc"].t[:], AF.Square)
            pv_ = pp.next()
            cx.op('pe', lambda e, pv_=pv_: nc.tensor.matmul(pv_.t[0:64, :], ones64, Cc["sq"].t[:].rearrange("p h t -> p (h t)"), start=True, stop=True),
                  reads=[cst, Cc["sq"]], writes=[pv_])
            ACTo(Cc["sdv"], Cc["sdv"].t[:].rearrange("p h t -> p (h t)"), pv_, pv_.t[0:64, :], AF.Sqrt, scale=1.0 / 64, bias=eps64, extra=[cst])
            cx.op('dve', lambda e: nc.vector.reciprocal(Cc["sdv"].t[:], Cc["sdv"].t[:]), reads=[Cc["sdv"]], writes=[Cc["sdv"]])
            yc = Cc["yc"]
            TTo(yc, yc.t[:], [(yc, yc.t[:]), (Cc["sdv"], Cc["sdv"].t[:])], ALU.mult)
            TTo(yc, yc.t[:], [(yc, yc.t[:]), (rpt, bc(pb(64), 64))], ALU.mult)
            TTo(yc, yc.t[:], [(yc, yc.t[:]), (rpt, bc(pb(72), 64))], ALU.add)
            TTo(yc, yc.t[:], [(yc, yc.t[:]), (T["bv"], T["bv"].t[:, :, cs_])], ALU.add)
            o = obr.next()
            TTo(o, o.t[:], [(yc, yc.t[:]), (T["g"], T["g"].t[:, :, cs_])], ALU.mult)
            cx.dma('sp', lambda e, o=o: e.dma_start(
                out=yT[1024:1536, tc0:tc0 + 64].rearrange("(h f) t -> f h t", f=64), in_=o.t[:]), reads=[o], sembuf=o)

        def run_chunks(b, k):
            gens = [phase_a(b, k, c) for c in range(TB // 64)]
            while gens:
                for g_ in list(gens):
                    try:
                        next(g_)
                    except StopIteration:
                        gens.remove(g_)
            for c in range(TB // 64):
                phase_b(b, k, c)

        for b in range(NSEQ):
            for k in range(S // TB):
                block_body(b, k)
        cx.end_stage(blk)


def stage_rope(cx, pos, invf, cs):
    nc = cx.nc
    PI = float(np.pi)
    with contextlib.ExitStack() as es:
        cx.begin_stage()
        pi_ = cx.buf(alloc(es, nc, "pi_", [32, S], I32), dma=True)
        fr = cx.buf(alloc(es, nc, "fr", [32, 1], F32), dma=True)
        ang = cx.buf(alloc(es, nc, "ang", [32, S], F32))
        tf = cx.buf(alloc(es, nc, "tf", [32, S], F32))
        ki = cx.buf(alloc(es, nc, "ki", [32, S], I32))
        r = cx.buf(alloc(es, nc, "r", [32, S], F32))
        m = cx.buf(alloc(es, nc, "m", [32, S], F32))
        outs = [cx.buf(alloc(es, nc, f"o{i}", [32, S], F32), dma=True) for i in range(3)]
        blk = es.enter_context(nc.Block())
        cx.dma('sp', lambda e: e.dma_start(out=fr.t[:], in_=invf[:, :]), writes=[fr], sembuf=fr)

        def wrap(buf):
            cx.op('dve', lambda e: nc.vector.tensor_scalar(out=m.t[:], in0=buf.t[:], scalar1=PI, scalar2=-2 * PI, op0=ALU.is_gt, op1=ALU.mult),
                  reads=[buf], writes=[m])
            cx.op('dve', lambda e: nc.vector.tensor_tensor(out=buf.t[:], in0=buf.t[:], in1=m.t[:], op=ALU.add), reads=[m], writes=[buf])
            cx.op('dve', lambda e: nc.vector.tensor_scalar(out=m.t[:], in0=buf.t[:], scalar1=-PI, scalar2=2 * PI, op0=ALU.is_lt, op1=ALU.mult),
                  reads=[buf], writes=[m])
            cx.op('dve', lambda e: nc.vector.tensor_tensor(out=buf.t[:], in0=buf.t[:], in1=m.t[:], op=ALU.add), reads=[m], writes=[buf])

        def body(b):
            cx.dma('sp', lambda e: e.dma_start(out=pi_.t[:], in_=pos[b:b + 1, :].broadcast_to([32, S])), writes=[pi_], sembuf=pi_)
            cx.op('dve', lambda e: nc.vector.tensor_copy(ang.t[:], pi_.t[:]), reads=[pi_], writes=[ang])
            cx.op('dve', lambda e: nc.vector.tensor_scalar(out=ang.t[:], in0=ang.t[:], scalar1=fr.t[:, 0:1], scalar2=None, op0=ALU.mult),
                  reads=[fr], writes=[ang])
            cx.op('dve', lambda e: nc.vector.tensor_scalar(out=tf.t[:], in0=ang.t[:], scalar1=1.0 / (2 * PI), scalar2=None, op0=ALU.mult),
                  reads=[ang], writes=[tf])
            cx.op('dve', lambda e: nc.vector.tensor_copy(ki.t[:], tf.t[:]), reads=[tf], writes=[ki])
            cx.op('dve', lambda e: nc.vector.tensor_copy(tf.t[:], ki.t[:]), reads=[ki], writes=[tf])
            cx.op('dve', lambda e: nc.vector.scalar_tensor_tensor(out=r.t[:], in0=tf.t[:], scalar=-2 * PI, in1=ang.t[:], op0=ALU.mult, op1=ALU.add),
                  reads=[tf, ang], writes=[r])
            wrap(r)
            cx.op('act', lambda e: nc.scalar.activation(out=outs[1].t[:], in_=r.t[:], func=AF.Sin), reads=[r], writes=[outs[1]])
            cx.op('act', lambda e: nc.scalar.activation(out=outs[2].t[:], in_=r.t[:], func=AF.Sin, scale=-1.0), reads=[r], writes=[outs[2]])
            cx.op('dve', lambda e: nc.vector.tensor_scalar(out=r.t[:], in0=r.t[:], scalar1=PI / 2, scalar2=None, op0=ALU.add),
                  reads=[outs[1], outs[2]], writes=[r])
            wrap(r)
            cx.op('act', lambda e: nc.scalar.activation(out=outs[0].t[:], in_=r.t[:], func=AF.Sin), reads=[r], writes=[outs[0]])
            for (src, j, r0) in ((outs[0], 0, 0), (outs[0], 0, 32), (outs[2], 1, 0), (outs[1], 1, 32)):
                cx.dma('sp', lambda e, src=src, j=j, r0=r0: e.dma_start(out=cs[b, j, r0:r0 + 32, :], in_=src.t[:]), reads=[src], sembuf=src)
        for b in range(NSEQ):
            body(b)
        cx.end_stage(blk)


NPAD = 4224
BIGW = ["ffn1_gate", "ffn1_up", "ffn1_down", "ffn2_gate", "ffn2_up", "ffn2_down", "w_out", "w_ukv",
        "decay_up", "iclr_up", "gate_up"]
BIGW_SHAPES = {"ffn1_gate": [D, DFF], "ffn1_up": [D, DFF], "ffn1_down": [DFF, D], "ffn2_gate": [D, DFF],
               "ffn2_up": [D, DFF], "ffn2_down": [DFF, D], "w_out": [D, D], "w_ukv": [256, 2048],
               "decay_up": [32, 512], "iclr_up": [32, 512], "gate_up": [96, 512]}


def build_program(nlayers=DEPTH, dbg=False):
    nc = bass.Bass("TRN2", target_bir_lowering=False)
    dt = lambda n, s, d=F32, kind="ExternalInput": nc.dram_tensor(n, s, d, kind=kind).ap()
    x = dt("x", [NT, D])
    pos = dt("pos", [NSEQ, S], I32)
    W = {n: dt(n, [DEPTH] + BIGW_SHAPES[n]) for n in BIGW}
    w_inx = dt("w_inx", [DEPTH, D, NPAD])
    wq = dt("wq", [DEPTH, 512, 2048])
    spk = dt("spk", [DEPTH, 128, 192])
    gfin = dt("gfin", [128, KC])
    consts = dt("consts", [128, 1024])
    maskd = dt("maskd", [128, 4, 512])
    invf = dt("invf", [32, 1])
    out = dt("out", [NT, D], kind="ExternalOutput")
    sk = "ExternalOutput" if dbg else "Internal"
    hT = dt("hT", [KC, 128, NT], kind=sk)
    pT = dt("pT", [NPAD, NT], kind=sk)
    yT = dt("yT", [D, NT], BF16, kind=sk)
    cs = dt("cs", [NSEQ, 2, 64, S], kind=sk)
    with contextlib.ExitStack() as es:
        sems = [es.enter_context(nc.semaphore(f"s{i}")) for i in range(96)]
        cx = Ctx(nc, sems)
        stage_rope(cx, pos, invf, cs)
        stage_in(cx, x, hT, consts[:, C_ID:C_ID + 128])
        for l in range(nlayers):
            sp = spk[l]
            stage_ffn(cx, hT, W["ffn1_gate"][l], W["ffn1_up"][l], W["ffn1_down"][l], sp[:, 0:16], consts)
            stage_proj(cx, hT, w_inx[l], sp[:, 16:32], consts, pT, NPAD)
            stage_mla(cx, pT, wq[l], W["w_ukv"][l], sp[:, 48:64], cs, consts, maskd, yT)
            stage_rwkv(cx, pT, sp[:, 64:160], W["decay_up"][l], W["iclr_up"][l], W["gate_up"][l], consts, yT)
            stage_conv(cx, pT, sp[:, 160:176].rearrange("p (a b) -> p a b", a=4), consts, yT)
            stage_wout(cx, hT, yT, W["w_out"][l])
            stage_ffn(cx, hT, W["ffn2_gate"][l], W["ffn2_up"][l], W["ffn2_down"][l], sp[:, 32:48], consts)
        stage_out(cx, hT, gfin, consts, consts[:, C_ID:C_ID + 128], out)
    return nc


def _fm(v, nch):
    return np.ascontiguousarray(np.asarray(v, np.float32).reshape(nch, 128).T)


def _hd(v):
    return np.ascontiguousarray(np.asarray(v, np.float32).reshape(8, 64).T)


def host_layout(inp):
    f32 = np.float32
    w_in = np.asarray(inp["w_in"], f32)
    w_inx = np.zeros((DEPTH, D, NPAD), f32)
    w_inx[:, :, :NIN] = w_in
    w_inx[:, :, NIN:NIN + 32] = w_in[:, :, 800:832]
    w_inx[:, :, NIN + 32:NIN + 64] = w_in[:, :, 768:800]
    w_uq = np.asarray(inp["w_uq"], f32).reshape(DEPTH, 512, 8, 192)
    wq = np.concatenate([w_uq[..., :128], w_uq[..., 128:192], w_uq[..., 160:192], w_uq[..., 128:160]], axis=-1)
    wq = np.ascontiguousarray(wq.reshape(DEPTH, 512, 2048))
    spk = np.zeros((DEPTH, 128, 192), f32)
    for l in range(DEPTH):
        spk[l, :, 0:16] = _fm(inp["norm_ffn1"][l], 16)
        spk[l, :, 16:32] = _fm(inp["norm_mix"][l], 16)
        spk[l, :, 32:48] = _fm(inp["norm_ffn2"][l], 16)
        spk[l, :, 48:52] = _fm(inp["q_norm"][l], 4)
        spk[l, :, 52:54] = _fm(inp["kv_norm"][l], 2)
        spk[l, :, 54:62] = _fm(inp["attn_out_norm"][l], 8)
        rp = spk[l, :, 64:160]
        mu = np.asarray(inp["shift_mu"][l], f32)
        rp[0:64, 0:8] = _hd(mu[0:512])
        rp[0:64, 8:16] = _hd(mu[512:1024])
        rp[0:64, 16:24] = _hd(mu[1024:1536])
        rp[0:64, 24:32] = _hd(inp["decay_w0"][l])
        rp[0:64, 32:40] = _hd(inp["iclr_a0"][l])
        rp[0:64, 40:48] = _hd(inp["k_k"][l])
        rp[0:64, 48:56] = _hd(inp["k_a"][l])
        rp[0:64, 56:64] = _hd(np.asarray(inp["r_k"][l], f32).reshape(512))
        rp[0:64, 64:72] = _hd(inp["lnx_gain"][l])
        rp[0:64, 72:80] = _hd(inp["lnx_bias"][l])
        rp[0:32, 80] = mu[1536:1568]
        rp[0:32, 81] = mu[1568:1600]
        rp[0:96, 82] = mu[1600:1696]
        cw = np.asarray(inp["conv_w"][l], f32)
        cp = np.zeros((128, 4, 4), f32)
        for k in range(3):
            cp[:, :, k] = cw[k].reshape(4, 128).T
        cp[:, :, 3] = np.asarray(inp["conv_out_norm"][l], f32).reshape(4, 128).T
        spk[l, :, 160:176] = cp.reshape(128, 16)
    consts = np.zeros((128, 1024), f32)
    consts[:, C_ONES:C_ONES + 128] = 1.0
    consts[:, C_EPS] = 1e-6
    consts[:, C_EPS + 1] = 64e-5
    consts[0:64, C_BD64:C_BD64 + 64] = 1.0
    consts[64:128, C_BD64 + 64:C_BD64 + 128] = 1.0
    consts[:, C_ID:C_ID + 128] = np.eye(128, dtype=f32)
    i = np.arange(64)
    consts[0:64, C_SU:C_SU + 64] = (i[:, None] < i[None, :])
    consts[0:64, C_SL:C_SL + 64] = (i[:, None] > i[None, :])
    consts[0:64, C_IU:C_IU + 64] = (i[:, None] <= i[None, :])
    rm = np.ones(128, f32)
    rm[0::64] = 0.0
    consts[:, C_RM:C_RM + 128] = rm[None, :]
    kl = np.arange(128)[:, None]
    ql = np.arange(512)[None, :]
    maskd = np.zeros((128, 4, 512), f32)
    for j in range(4):
        maskd[:, j, :] = np.where(128 * j + kl > ql, -30000.0, 0.0)
    invf = (1.0 / (np.float32(10000.0) ** (np.arange(0, 64, 2, dtype=f32) / np.float32(64)))).astype(f32).reshape(32, 1)
    shared = {n: np.ascontiguousarray(np.asarray(inp[n], f32)) for n in BIGW}
    shared.update({"w_inx": w_inx, "wq": wq, "spk": spk, "gfin": _fm(inp["norm_final"], 16), "consts": consts,
                   "maskd": maskd, "invf": invf})
    return shared


_PROG = {}


def kernel(**inp):
    shared = host_layout(inp)
    x = np.asarray(inp["x"], np.float32)
    pos = np.asarray(inp["positions"], np.int32)
    if "nc" not in _PROG:
        _PROG["nc"] = build_program()
    nc = _PROG["nc"]
    in_maps = []
    for c in range(8):
        m = dict(shared)
        m["x"] = np.ascontiguousarray(x[c * NSEQ:(c + 1) * NSEQ].reshape(NT, D))
        m["pos"] = np.ascontiguousarray(pos[c * NSEQ:(c + 1) * NSEQ])
        in_maps.append(m)
    res = run_bass_kernel_spmd(nc, in_maps, core_ids=list(range(8)))
    out = np.stack([np.asarray(r["out"]).reshape(NSEQ, S, D) for r in res.results], axis=0)
    return out.reshape(16, S, D).astype(np.float32)
```

```python
import contextlib
import numpy as np
import concourse.bass as bass
import concourse.mybir as mybir
from concourse.bass_utils import run_bass_kernel_spmd

F32 = mybir.dt.float32
BF16 = mybir.dt.bfloat16
I32 = mybir.dt.int32
AF = mybir.ActivationFunctionType
ALU = mybir.AluOpType
AX = mybir.AxisListType

D = 2048
DFF = 5632
S = 2048
NSEQ = 2
NT = NSEQ * S
DEPTH = 4
KC = D // 128
FC = DFF // 128
EPS = 1e-6
NIN = 4064
NINX = 4128


class Sem:
    def __init__(self, h):
        self.h = h
        self.n = 0


class Buf:
    def __init__(self, cx, t, dma=False):
        self.t = t
        self.wr = None
        self.rd = []
        self.sem = cx.new_sem() if dma else None

    def __getitem__(self, k):
        return self.t[k]


class Ctx:
    ENG = ('pe', 'act', 'dve', 'pool', 'sp')

    def __init__(self, nc, sem_handles):
        self.nc = nc
        self.sems = [Sem(h) for h in sem_handles]
        self.free = list(self.sems)
        self.esem = {e: self.new_sem() for e in ('pe', 'act', 'dve', 'pool')}
        self.q = {e: [] for e in self.ENG}
        self.seen = {e: {} for e in self.ENG}
        self.stage_sems = []
        self.pending_dma = {e: [] for e in self.ENG}

    def new_sem(self):
        return self.free.pop()

    def begin_stage(self):
        self.mark = len(self.free)
        self.taken = []

    def buf(self, t, dma=False):
        b = Buf.__new__(Buf)
        b.t = t
        b.wr = None
        b.rd = []
        b.sem = None
        if dma:
            b.sem = self.free.pop()
            self.taken.append(b.sem)
        return b

    def end_stage(self, block):
        for e in self.ENG:
            for tok in self.pending_dma[e]:
                self._wait(e, tok)
            self.pending_dma[e] = []
        self.flush(block)
        self.free.extend(self.taken)
        self.taken = []

    def _wait(self, eng, tok):
        if tok is None:
            return
        sem, val, src = tok
        if src == eng and src in ('pe', 'act', 'dve'):
            return
        if self.seen[eng].get(id(sem), 0) >= val:
            return
        self.seen[eng][id(sem)] = val
        self.q[eng].append(('wait', sem, val))

    def _deps(self, eng, reads, writes):
        best = {}
        toks = [b.wr for b in reads] + [b.wr for b in writes]
        for b in writes:
            toks.extend(b.rd)
        for t in toks:
            if t is None:
                continue
            k = id(t[0])
            if k not in best or best[k][1] < t[1]:
                best[k] = t
        for t in best.values():
            self._wait(eng, t)

    def op(self, eng, fn, reads=(), writes=(), sig=True):
        self._deps(eng, reads, writes)
        tok = None
        if sig:
            s = self.esem[eng]
            s.n += 1
            tok = (s, s.n, eng)
            self.q[eng].append(('op', fn, s, 1))
        else:
            self.q[eng].append(('op', fn, None, 0))
        for b in writes:
            b.wr = tok
            b.rd = []
        for b in reads:
            if tok is not None:
                b.rd = [t for t in b.rd if t[0] is not tok[0]] + [tok]
        return tok

    def dma(self, eng, fn, reads=(), writes=(), sembuf=None):
        s = sembuf.sem
        saved = []
        for b in writes:
            if b.wr is not None and b.wr[2] == 'dma' and b.wr[0] is s and not b.rd:
                saved.append((b, b.wr))
                b.wr = None
        self._deps(eng, reads, writes)
        for b, w in saved:
            b.wr = w
        s.n += 16
        tok = (s, s.n, 'dma')
        self.q[eng].append(('op', fn, s, 16))
        for b in writes:
            b.wr = tok
            b.rd = []
        for b in reads:
            b.rd = [t for t in b.rd if t[0] is not tok[0]] + [tok]
        if not writes:
            self.pending_dma[eng].append(tok)
        return tok

    def flush(self, block):
        m = {'pe': block.tensor, 'act': block.scalar, 'dve': block.vector,
             'pool': block.gpsimd, 'sp': block.sync}
        for e in self.ENG:
            lst = self.q[e]
            if not lst:
                continue

            def body(eng, lst=lst):
                for it in lst:
                    if it[0] == 'wait':
                        eng.wait_ge(it[1].h, it[2])
                    else:
                        ins = it[1](eng)
                        if it[2] is not None:
                            ins.then_inc(it[2].h, it[3])
            m[e](body)
            self.q[e] = []


class Ring:
    def __init__(self, bufs):
        self.bufs = bufs
        self.i = 0

    def next(self):
        b = self.bufs[self.i % len(self.bufs)]
        self.i += 1
        return b


def mm_group(cx, out_buf, pairs, reads, out_ap=None):
    nc = cx.nc
    oap = out_buf.t[:] if out_ap is None else out_ap
    n = len(pairs)

    def fn(eng):
        ins = None
        for i, (l, r) in enumerate(pairs):
            ins = nc.tensor.matmul(oap, l, r, start=(i == 0), stop=(i == n - 1))
        return ins
    return cx.op('pe', fn, reads=reads, writes=[out_buf])


_UID = [0]


def alloc(es, nc, name, shape, dt, psum=False):
    _UID[0] += 1
    name = f"{name}_{_UID[0]}"
    if psum:
        return es.enter_context(nc.psum_tensor(name, shape, dt))
    return es.enter_context(nc.sbuf_tensor(name, shape, dt))


def stage_in(cx, x, hT, ident_dram):
    nc = cx.nc
    with contextlib.ExitStack() as es:
        cx.begin_stage()
        ident = cx.buf(alloc(es, nc, "ident", [128, 128], F32), dma=True)
        xin = Ring([cx.buf(alloc(es, nc, f"xin{i}", [128, D], F32), dma=True) for i in range(3)])
        xo = Ring([cx.buf(alloc(es, nc, f"xo{i}", [128, KC, 128], F32), dma=True) for i in range(3)])
        ps = Ring([cx.buf(alloc(es, nc, f"ps{i}", [128, 512], F32, psum=True)) for i in range(4)])
        blk = es.enter_context(nc.Block())
        cx.dma('sp', lambda e: e.dma_start(out=ident.t[:], in_=ident_dram[:, :]), writes=[ident], sembuf=ident)
        for tt in range(NT // 128):
            xb = xin.next()
            cx.dma('sp', lambda e, xb=xb, tt=tt: e.dma_start(out=xb.t[:], in_=x[tt * 128:(tt + 1) * 128, :]),
                   writes=[xb], sembuf=xb)
            ob = xo.next()
            for g in range(KC // 4):
                p = ps.next()

                def fn(eng, p=p, xb=xb, g=g):
                    ins = None
                    for j in range(4):
                        kc = g * 4 + j
                        ins = nc.tensor.transpose(p.t[:, j * 128:(j + 1) * 128], xb.t[:, kc * 128:(kc + 1) * 128], ident.t[:])
                    return ins
                cx.op('pe', fn, reads=[xb, ident], writes=[p])
                eng = 'dve' if g % 2 == 0 else 'act'
                if eng == 'dve':
                    cx.op('dve', lambda e, p=p, ob=ob, g=g: nc.vector.tensor_copy(
                        ob.t[:, g * 4:(g + 1) * 4, :], p.t[:].rearrange("p (a b) -> p a b", a=4)), reads=[p], writes=[ob])
                else:
                    cx.op('act', lambda e, p=p, ob=ob, g=g: nc.scalar.copy(
                        ob.t[:, g * 4:(g + 1) * 4, :], p.t[:].rearrange("p (a b) -> p a b", a=4)), reads=[p], writes=[ob])
            cx.dma('sp', lambda e, ob=ob, tt=tt: e.dma_start(
                out=hT[:, :, tt * 128:(tt + 1) * 128].rearrange("k p t -> p k t"), in_=ob.t[:]),
                reads=[ob], sembuf=ob)
        cx.end_stage(blk)


def rms_stats_gen(cx, nc, chunks_fn, nchunks, T, hring, sqring, ps_ss, ones_f, nfeat, eps_t, sd, rstd, src_rows=128):
    nsub = T // 512
    for kc in range(nchunks):
        hb = hring.next()
        cx.dma('sp', chunks_fn(kc, hb), writes=[hb], sembuf=hb)
        sq = sqring.next()
        cx.op('act', lambda e, hb=hb, sq=sq: nc.scalar.activation(out=sq.t[:src_rows, :T], in_=hb.t[:src_rows, :T], func=AF.Square),
              reads=[hb], writes=[sq])
        for sub in range(nsub):
            p = ps_ss[sub]

            def fn(eng, p=p, sq=sq, sub=sub, kc=kc):
                return nc.tensor.matmul(p.t[:], ones_f.t[:src_rows, :], sq.t[:src_rows, sub * 512:(sub + 1) * 512],
                                        start=(kc == 0), stop=(kc == nchunks - 1))
            cx.op('pe', fn, reads=[sq, ones_f], writes=[p] if kc == 0 else [], sig=True)
            if kc != 0:
                pass
        if kc == nchunks - 1:
            last_tok = (cx.esem['pe'], cx.esem['pe'].n, 'pe')
            for sub in range(nsub):
                ps_ss[sub].wr = last_tok
        yield
    for sub in range(nsub):
        p = ps_ss[sub]
        cx.op('act', lambda e, p=p, sub=sub: nc.scalar.activation(
            out=sd.t[:, sub * 512:(sub + 1) * 512], in_=p.t[:], func=AF.Sqrt, bias=eps_t.t[:, 0:1], scale=1.0 / nfeat),
            reads=[p, eps_t], writes=[sd] if sub == 0 else [])
    sd.wr = (cx.esem['act'], cx.esem['act'].n, 'act')
    cx.op('dve', lambda e: nc.vector.reciprocal(rstd.t[:, :T], sd.t[:, :T]), reads=[sd], writes=[rstd])


def rms_stats(*a, **k):
    for _ in rms_stats_gen(*a, **k):
        pass


def stage_ffn(cx, hT, wg, wu, wd, gvec, consts, T=1024, tiles=None):
    nc = cx.nc
    nsub = T // 512
    NJ = 256
    with contextlib.ExitStack() as es:
        cx.begin_stage()
        ones_f = cx.buf(alloc(es, nc, "ones_f", [128, 128], F32), dma=True)
        eps_t = cx.buf(alloc(es, nc, "eps_t", [128, 1], F32), dma=True)
        g_t = cx.buf(alloc(es, nc, "g_t", [128, KC], F32), dma=True)
        xn = cx.buf(alloc(es, nc, "xn", [128, KC, T], BF16))
        act = [cx.buf(alloc(es, nc, f"act{j}", [128, T], BF16)) for j in range(FC)]
        wring = Ring([cx.buf(alloc(es, nc, f"w{i}", [128, 16, NJ], BF16), dma=True) for i in range(6)])
        hring = Ring([cx.buf(alloc(es, nc, f"hb{i}", [128, T], F32), dma=True) for i in range(2)])
        sqring = Ring([cx.buf(alloc(es, nc, f"sq{i}", [128, T], F32)) for i in range(2)])
        rstd = cx.buf(alloc(es, nc, "rstd", [128, T], F32))
        sd = rstd
        sgr = Ring([cx.buf(alloc(es, nc, f"sg{i}", [128, 512], F32)) for i in range(2)])
        outr = Ring([cx.buf(alloc(es, nc, f"ob{i}", [128, 512], F32), dma=True) for i in range(2)])
        hres = Ring([cx.buf(alloc(es, nc, f"hr{i}", [128, 512], F32), dma=True) for i in range(2)])
        ps_g = Ring([cx.buf(alloc(es, nc, f"psg{i}", [128, 512], F32, psum=True)) for i in range(2)])
        ps_u = Ring([cx.buf(alloc(es, nc, f"psu{i}", [128, 512], F32, psum=True)) for i in range(2)])
        ps_d = Ring([cx.buf(alloc(es, nc, f"psd{i}", [128, 512], F32, psum=True)) for i in range(2)])
        ps_ss = [cx.buf(alloc(es, nc, f"pss{i}", [128, 512], F32, psum=True)) for i in range(nsub)]
        blk = es.enter_context(nc.Block())

        cx.dma('sp', lambda e: e.dma_start(out=ones_f.t[:], in_=consts[:, 0:128]), writes=[ones_f], sembuf=ones_f)
        cx.dma('sp', lambda e: e.dma_start(out=eps_t.t[:], in_=consts[:, 128:129], allow_slow_non_contiguous=True), writes=[eps_t], sembuf=eps_t)
        cx.dma('sp', lambda e: e.dma_start(out=g_t.t[:], in_=gvec[:, :]), writes=[g_t], sembuf=g_t)
        wgv = wg.rearrange("(kc p) n -> p kc n", p=128)
        wuv = wu.rearrange("(kc p) n -> p kc n", p=128)
        wdv = wd.rearrange("(j p) n -> p j n", p=128)
        tl = list(range(NT // T)) if tiles is None else tiles
        def norm_gen(t0):
            yield from rms_stats_gen(cx, nc, lambda kc, hb: (lambda e: e.dma_start(out=hb.t[:, :T], in_=hT[kc, :, t0:t0 + T])),
                                     KC, T, hring, sqring, ps_ss, ones_f, D, eps_t, sd, rstd)
            for kc in range(KC):
                hb = hring.next()
                cx.dma('sp', lambda e, hb=hb, kc=kc: e.dma_start(out=hb.t[:, :T], in_=hT[kc, :, t0:t0 + T]),
                       writes=[hb], sembuf=hb)
                cx.op('dve', lambda e, hb=hb, kc=kc: nc.vector.scalar_tensor_tensor(
                    out=xn.t[:, kc, :], in0=hb.t[:, :T], scalar=g_t.t[:, kc:kc + 1], in1=rstd.t[:, :T],
                    op0=ALU.mult, op1=ALU.mult), reads=[hb, g_t, rstd], writes=[xn] if kc == 0 else [])
                xn.wr = (cx.esem['dve'], cx.esem['dve'].n, 'dve')
                yield

        def gateup(t0):
            for jt in range(DFF // NJ):
                wgb = wring.next()
                cx.dma('pool', lambda e, b=wgb, jt=jt: e.dma_start(out=b.t[:], in_=wgv[:, :, jt * NJ:(jt + 1) * NJ]),
                       writes=[wgb], sembuf=wgb)
                wub = wring.next()
                cx.dma('pool', lambda e, b=wub, jt=jt: e.dma_start(out=b.t[:], in_=wuv[:, :, jt * NJ:(jt + 1) * NJ]),
                       writes=[wub], sembuf=wub)
                for jj in range(NJ // 128):
                    j = jt * (NJ // 128) + jj
                    for sub in range(nsub):
                        pg = ps_g.next()
                        pu = ps_u.next()
                        mm_group(cx, pg, [(wgb.t[:, kc, jj * 128:(jj + 1) * 128], xn.t[:, kc, sub * 512:(sub + 1) * 512])
                                          for kc in range(KC)], reads=[wgb, xn])
                        mm_group(cx, pu, [(wub.t[:, kc, jj * 128:(jj + 1) * 128], xn.t[:, kc, sub * 512:(sub + 1) * 512])
                                          for kc in range(KC)], reads=[wub, xn])
                        sg = sgr.next()
                        cx.op('act', lambda e, pg=pg, sg=sg: nc.scalar.activation(out=sg.t[:], in_=pg.t[:], func=AF.Silu),
                              reads=[pg], writes=[sg])
                        cx.op('dve', lambda e, sg=sg, pu=pu, j=j, sub=sub: nc.vector.tensor_tensor(
                            out=act[j].t[:, sub * 512:(sub + 1) * 512], in0=sg.t[:], in1=pu.t[:], op=ALU.mult),
                            reads=[sg, pu], writes=[act[j]])
        def down(t0, nxt):
            JD = 11
            for ct in range(D // NJ):
                wds = []
                for q4 in range(FC // JD):
                    wb = wring.next()
                    cx.dma('pool', lambda e, b=wb, q4=q4, ct=ct: e.dma_start(
                        out=b.t[:, 0:JD, :], in_=wdv[:, q4 * JD:(q4 + 1) * JD, ct * NJ:(ct + 1) * NJ]),
                        writes=[wb], sembuf=wb)
                    wds.append(wb)
                for cc in range(NJ // 128):
                    c = ct * (NJ // 128) + cc
                    for sub in range(nsub):
                        hr = hres.next()
                        cx.dma('sp', lambda e, hr=hr, c=c, sub=sub: e.dma_start(
                            out=hr.t[:], in_=hT[c, :, t0 + sub * 512:t0 + (sub + 1) * 512]), writes=[hr], sembuf=hr)
                        pd = ps_d.next()
                        mm_group(cx, pd, [(wds[j // JD].t[:, j % JD, cc * 128:(cc + 1) * 128], act[j].t[:, sub * 512:(sub + 1) * 512])
                                          for j in range(FC)], reads=wds + act)
                        ob = outr.next()
                        cx.op('dve', lambda e, pd=pd, hr=hr, ob=ob: nc.vector.scalar_tensor_tensor(
                            out=ob.t[:], in0=pd.t[:], scalar=0.5, in1=hr.t[:], op0=ALU.mult, op1=ALU.add),
                            reads=[pd, hr], writes=[ob])
                        cx.dma('sp', lambda e, ob=ob, c=c, sub=sub: e.dma_start(
                            out=hT[c, :, t0 + sub * 512:t0 + (sub + 1) * 512], in_=ob.t[:]), reads=[ob], sembuf=ob)
                        if nxt is not None:
                            next(nxt, None)
                            next(nxt, None)
            if nxt is not None:
                for _ in nxt:
                    pass
        first = norm_gen(tl[0] * T)
        for _ in first:
            pass
        for idx, ti in enumerate(tl):
            gateup(ti * T)
            nxt = norm_gen(tl[idx + 1] * T) if idx + 1 < len(tl) else None
            down(ti * T, nxt)
        cx.end_stage(blk)


def stage_out(cx, hT, gvec, consts, ident_dram, out):
    nc = cx.nc
    T = 512
    with contextlib.ExitStack() as es:
        cx.begin_stage()
        ones_f = cx.buf(alloc(es, nc, "ones_f", [128, 128], F32), dma=True)
        ident = cx.buf(alloc(es, nc, "ident", [128, 128], F32), dma=True)
        eps_t = cx.buf(alloc(es, nc, "eps_t", [128, 1], F32), dma=True)
        g_t = cx.buf(alloc(es, nc, "g_t", [128, KC], F32), dma=True)
        hring = Ring([cx.buf(alloc(es, nc, f"hb{i}", [128, T], F32), dma=True) for i in range(3)])
        sqring = Ring([cx.buf(alloc(es, nc, f"sq{i}", [128, T], F32)) for i in range(2)])
        sd = cx.buf(alloc(es, nc, "sd", [128, T], F32))
        rstd = cx.buf(alloc(es, nc, "rstd", [128, T], F32))
        xn = cx.buf(alloc(es, nc, "xnf", [128, KC, T], F32))
        ps_ss = [cx.buf(alloc(es, nc, "pss0", [128, 512], F32, psum=True))]
        ps = Ring([cx.buf(alloc(es, nc, f"ps{i}", [128, 512], F32, psum=True)) for i in range(4)])
        orow = Ring([cx.buf(alloc(es, nc, f"orow{i}", [128, D], F32), dma=True) for i in range(3)])
        blk = es.enter_context(nc.Block())
        cx.dma('sp', lambda e: e.dma_start(out=ones_f.t[:], in_=consts[:, 0:128]), writes=[ones_f], sembuf=ones_f)
        cx.dma('sp', lambda e: e.dma_start(out=eps_t.t[:], in_=consts[:, 128:129], allow_slow_non_contiguous=True), writes=[eps_t], sembuf=eps_t)
        cx.dma('sp', lambda e: e.dma_start(out=g_t.t[:], in_=gvec[:, :]), writes=[g_t], sembuf=g_t)
        cx.dma('sp', lambda e: e.dma_start(out=ident.t[:], in_=ident_dram[:, :]), writes=[ident], sembuf=ident)
        def tile_body(t0):
            rms_stats(cx, nc, lambda kc, hb: (lambda e: e.dma_start(out=hb.t[:, :T], in_=hT[kc, :, t0:t0 + T])),
                      KC, T, hring, sqring, ps_ss, ones_f, D, eps_t, sd, rstd)
            for kc in range(KC):
                hb = hring.next()
                cx.dma('sp', lambda e, hb=hb, kc=kc: e.dma_start(out=hb.t[:, :T], in_=hT[kc, :, t0:t0 + T]),
                       writes=[hb], sembuf=hb)
                cx.op('dve', lambda e, hb=hb, kc=kc: nc.vector.scalar_tensor_tensor(
                    out=xn.t[:, kc, :], in0=hb.t[:, :T], scalar=g_t.t[:, kc:kc + 1], in1=rstd.t[:, :T],
                    op0=ALU.mult, op1=ALU.mult), reads=[hb, g_t, rstd], writes=[xn] if kc == 0 else [])
            xn.wr = (cx.esem['dve'], cx.esem['dve'].n, 'dve')
            for tb in range(T // 128):
                ob = orow.next()
                for g in range(KC // 4):
                    p = ps.next()

                    def fn(eng, p=p, g=g, tb=tb):
                        ins = None
                        for j in range(4):
                            kc = g * 4 + j
                            ins = nc.tensor.transpose(p.t[:, j * 128:(j + 1) * 128], xn.t[:, kc, tb * 128:(tb + 1) * 128], ident.t[:])
                        return ins
                    cx.op('pe', fn, reads=[xn, ident], writes=[p])
                    if g % 2 == 0:
                        cx.op('dve', lambda e, p=p, ob=ob, g=g: nc.vector.tensor_copy(ob.t[:, g * 512:(g + 1) * 512], p.t[:]),
                              reads=[p], writes=[ob])
                    else:
                        cx.op('act', lambda e, p=p, ob=ob, g=g: nc.scalar.copy(ob.t[:, g * 512:(g + 1) * 512], p.t[:]),
                              reads=[p], writes=[ob])
                cx.dma('sp', lambda e, ob=ob, tb=tb: e.dma_start(out=out[t0 + tb * 128:t0 + (tb + 1) * 128, :], in_=ob.t[:]),
                       reads=[ob], sembuf=ob)
        for ti in range(NT // T):
            tile_body(ti * T)
        cx.end_stage(blk)


def load_consts(cx, es, nc, consts, ident_dram=None):
    ones_f = cx.buf(alloc(es, nc, "ones_f", [128, 128], F32), dma=True)
    eps_t = cx.buf(alloc(es, nc, "eps_t", [128, 1], F32), dma=True)
    cx.dma('sp', lambda e: e.dma_start(out=ones_f.t[:], in_=consts[:, 0:128]), writes=[ones_f], sembuf=ones_f)
    cx.dma('sp', lambda e: e.dma_start(out=eps_t.t[:], in_=consts[:, 128:129], allow_slow_non_contiguous=True),
           writes=[eps_t], sembuf=eps_t)
    return ones_f, eps_t


def stage_proj(cx, hT, w, gvec, consts, pT, ncols, T=1024):
    nc = cx.nc
    nsub = T // 512
    NJ = 384
    with contextlib.ExitStack() as es:
        cx.begin_stage()
        ones_f, eps_t = load_consts(cx, es, nc, consts)
        g_t = cx.buf(alloc(es, nc, "g_t", [128, KC], F32), dma=True)
        xn = cx.buf(alloc(es, nc, "xn", [128, KC, T], BF16))
        wring = Ring([cx.buf(alloc(es, nc, f"w{i}", [128, 16, NJ], BF16), dma=True) for i in range(3)])
        hring = Ring([cx.buf(alloc(es, nc, f"hb{i}", [128, T], F32), dma=True) for i in range(2)])
        sqring = Ring([cx.buf(alloc(es, nc, f"sq{i}", [128, T], F32)) for i in range(2)])
        rstd = cx.buf(alloc(es, nc, "rstd", [128, T], F32))
        outr = Ring([cx.buf(alloc(es, nc, f"ob{i}", [128, 512], F32), dma=True) for i in range(4)])
        ps_o = Ring([cx.buf(alloc(es, nc, f"pso{i}", [128, 512], F32, psum=True)) for i in range(4)])
        ps_ss = [cx.buf(alloc(es, nc, f"pss{i}", [128, 512], F32, psum=True)) for i in range(nsub)]
        blk = es.enter_context(nc.Block())
        cx.dma('sp', lambda e: e.dma_start(out=g_t.t[:], in_=gvec[:, :]), writes=[g_t], sembuf=g_t)
        wv = w.rearrange("(kc p) n -> p kc n", p=128)

        def tile_body(t0):
            rms_stats(cx, nc, lambda kc, hb: (lambda e: e.dma_start(out=hb.t[:, :T], in_=hT[kc, :, t0:t0 + T])),
                      KC, T, hring, sqring, ps_ss, ones_f, D, eps_t, rstd, rstd)
            for kc in range(KC):
                hb = hring.next()
                cx.dma('sp', lambda e, hb=hb, kc=kc: e.dma_start(out=hb.t[:, :T], in_=hT[kc, :, t0:t0 + T]),
                       writes=[hb], sembuf=hb)
                cx.op('dve', lambda e, hb=hb, kc=kc: nc.vector.scalar_tensor_tensor(
                    out=xn.t[:, kc, :], in0=hb.t[:, :T], scalar=g_t.t[:, kc:kc + 1], in1=rstd.t[:, :T],
                    op0=ALU.mult, op1=ALU.mult), reads=[hb, g_t, rstd], writes=[xn] if kc == 0 else [])
            xn.wr = (cx.esem['dve'], cx.esem['dve'].n, 'dve')
            for jt in range(ncols // NJ):
                wb = wring.next()
                cx.dma('pool', lambda e, b=wb, jt=jt: e.dma_start(out=b.t[:], in_=wv[:, :, jt * NJ:(jt + 1) * NJ]),
                       writes=[wb], sembuf=wb)
                for jj in range(NJ // 128):
                    r0 = jt * NJ + jj * 128
                    for sub in range(nsub):
                        po = ps_o.next()
                        mm_group(cx, po, [(wb.t[:, kc, jj * 128:(jj + 1) * 128], xn.t[:, kc, sub * 512:(sub + 1) * 512])
                                          for kc in range(KC)], reads=[wb, xn])
                        ob = outr.next()
                        if (jj + sub) % 2 == 0:
                            cx.op('act', lambda e, po=po, ob=ob: nc.scalar.copy(ob.t[:], po.t[:]), reads=[po], writes=[ob])
                        else:
                            cx.op('dve', lambda e, po=po, ob=ob: nc.vector.tensor_copy(ob.t[:], po.t[:]), reads=[po], writes=[ob])
                        cx.dma('sp', lambda e, ob=ob, r0=r0, sub=sub: e.dma_start(
                            out=pT[r0:r0 + 128, t0 + sub * 512:t0 + (sub + 1) * 512], in_=ob.t[:]), reads=[ob], sembuf=ob)
        for ti in range(NT // T):
            tile_body(ti * T)
        cx.end_stage(blk)


def stage_wout(cx, hT, yT, w, T=1024):
    nc = cx.nc
    nsub = T // 512
    NJ = 256
    with contextlib.ExitStack() as es:
        cx.begin_stage()
        yb = cx.buf(alloc(es, nc, "yb", [128, KC, T], BF16), dma=True)
        wring = Ring([cx.buf(alloc(es, nc, f"w{i}", [128, 16, NJ], BF16), dma=True) for i in range(3)])
        outr = Ring([cx.buf(alloc(es, nc, f"ob{i}", [128, 512], F32), dma=True) for i in range(3)])
        hres = Ring([cx.buf(alloc(es, nc, f"hr{i}", [128, 512], F32), dma=True) for i in range(3)])
        ps_o = Ring([cx.buf(alloc(es, nc, f"pso{i}", [128, 512], F32, psum=True)) for i in range(4)])
        blk = es.enter_context(nc.Block())
        wv = w.rearrange("(kc p) n -> p kc n", p=128)
        yv = yT.rearrange("(kc p) t -> p kc t", p=128)

        def tile_body(t0):
            cx.dma('sp', lambda e: e.dma_start(out=yb.t[:], in_=yv[:, :, t0:t0 + T]), writes=[yb], sembuf=yb)
            for jt in range(D // NJ):
                wb = wring.next()
                cx.dma('pool', lambda e, b=wb, jt=jt: e.dma_start(out=b.t[:], in_=wv[:, :, jt * NJ:(jt + 1) * NJ]),
                       writes=[wb], sembuf=wb)
                for jj in range(NJ // 128):
                    c = jt * (NJ // 128) + jj
                    for sub in range(nsub):
                        hr = hres.next()
                        cx.dma('sp', lambda e, hr=hr, c=c, sub=sub: e.dma_start(
                            out=hr.t[:], in_=hT[c, :, t0 + sub * 512:t0 + (sub + 1) * 512]), writes=[hr], sembuf=hr)
                        po = ps_o.next()
                        mm_group(cx, po, [(wb.t[:, kc, jj * 128:(jj + 1) * 128], yb.t[:, kc, sub * 512:(sub + 1) * 512])
                                          for kc in range(KC)], reads=[wb, yb])
                        ob = outr.next()
                        cx.op('dve', lambda e, po=po, hr=hr, ob=ob: nc.vector.tensor_tensor(
                            out=ob.t[:], in0=po.t[:], in1=hr.t[:], op=ALU.add), reads=[po, hr], writes=[ob])
                        cx.dma('sp', lambda e, ob=ob, c=c, sub=sub: e.dma_start(
                            out=hT[c, :, t0 + sub * 512:t0 + (sub + 1) * 512], in_=ob.t[:]), reads=[ob], sembuf=ob)
        for ti in range(NT // T):
            tile_body(ti * T)
        cx.end_stage(blk)


C_ONES, C_EPS, C_BD64, C_ID, C_SU, C_SL, C_IU, C_RM = 0, 128, 256, 384, 512, 576, 640, 704
B0 = 832
C0 = 2528
R_KRS = 4064


def stage_conv(cx, pT, prm, consts, yT):
    nc = cx.nc
    with contextlib.ExitStack() as es:
        cx.begin_stage()
        bd = cx.buf(alloc(es, nc, "bd", [128, 128], F32), dma=True)
        eps_t = cx.buf(alloc(es, nc, "eps_t", [128, 1], F32), dma=True)
        pr = cx.buf(alloc(es, nc, "pr", [128, 4, 4], F32), dma=True)
        bg = Ring([cx.buf(alloc(es, nc, f"bg{i}", [128, S], F32), dma=True) for i in range(2)])
        cg = Ring([cx.buf(alloc(es, nc, f"cg{i}", [128, S], F32), dma=True) for i in range(2)])
        hh = Ring([cx.buf(alloc(es, nc, f"hh{i}", [128, S], F32), dma=True) for i in range(2)])
        u = cx.buf(alloc(es, nc, "u", [128, S + 2], F32))
        y = cx.buf(alloc(es, nc, "y", [128, S], F32))
        z = cx.buf(alloc(es, nc, "z", [128, S], F32))
        zsq = cx.buf(alloc(es, nc, "zsq", [128, S], F32))
        rs = cx.buf(alloc(es, nc, "rs", [128, S], F32))
        ob = Ring([cx.buf(alloc(es, nc, f"ob{i}", [128, S], BF16), dma=True) for i in range(2)])
        ps = Ring([cx.buf(alloc(es, nc, f"ps{i}", [128, 512], F32, psum=True)) for i in range(4)])
        blk = es.enter_context(nc.Block())
        cx.dma('sp', lambda e: e.dma_start(out=bd.t[:], in_=consts[:, C_BD64:C_BD64 + 128]), writes=[bd], sembuf=bd)
        cx.dma('sp', lambda e: e.dma_start(out=eps_t.t[:], in_=consts[:, C_EPS:C_EPS + 1], allow_slow_non_contiguous=True),
               writes=[eps_t], sembuf=eps_t)
        cx.dma('sp', lambda e: e.dma_start(out=pr.t[:], in_=prm[:, :, :]), writes=[pr], sembuf=pr)
        cx.op('dve', lambda e: nc.vector.memset(u.t[:, 0:2], 0.0), writes=[u])

        def body(b, ch):
            t0 = b * S
            bgb, cgb, hhb = bg.next(), cg.next(), hh.next()
            for (buf, r0) in ((bgb, C0 + ch * 128), (cgb, C0 + 512 + ch * 128), (hhb, C0 + 1024 + ch * 128)):
                cx.dma('sp', lambda e, buf=buf, r0=r0: e.dma_start(out=buf.t[:], in_=pT[r0:r0 + 128, t0:t0 + S]),
                       writes=[buf], sembuf=buf)
            cx.op('dve', lambda e: nc.vector.tensor_tensor(out=u.t[:, 2:S + 2], in0=cgb.t[:], in1=hhb.t[:], op=ALU.mult),
                  reads=[cgb, hhb], writes=[u])
            cx.op('act', lambda e: nc.scalar.activation(out=y.t[:], in_=u.t[:, 2:S + 2], func=AF.Copy, scale=pr.t[:, ch, 2:3]),
                  reads=[u, pr], writes=[y])
            cx.op('dve', lambda e: nc.vector.scalar_tensor_tensor(out=y.t[:], in0=u.t[:, 1:S + 1], scalar=pr.t[:, ch, 1:2],
                                                                  in1=y.t[:], op0=ALU.mult, op1=ALU.add), reads=[u, pr], writes=[y])
            cx.op('dve', lambda e: nc.vector.scalar_tensor_tensor(out=y.t[:], in0=u.t[:, 0:S], scalar=pr.t[:, ch, 0:1],
                                                                  in1=y.t[:], op0=ALU.mult, op1=ALU.add), reads=[u, pr], writes=[y])
            cx.op('dve', lambda e: nc.vector.tensor_tensor(out=z.t[:], in0=bgb.t[:], in1=y.t[:], op=ALU.mult),
                  reads=[bgb, y], writes=[z])
            cx.op('act', lambda e: nc.scalar.activation(out=zsq.t[:], in_=z.t[:], func=AF.Square), reads=[z], writes=[zsq])
            for sub in range(S // 512):
                p = ps.next()
                sl = slice(sub * 512, (sub + 1) * 512)
                cx.op('pe', lambda e, p=p, sl=sl: nc.tensor.matmul(p.t[:], bd.t[:], zsq.t[:, sl], start=True, stop=True),
                      reads=[bd, zsq], writes=[p])
                cx.op('act', lambda e, p=p, sl=sl: nc.scalar.activation(out=rs.t[:, sl], in_=p.t[:], func=AF.Sqrt,
                                                                        bias=eps_t.t[:, 0:1], scale=1.0 / 64),
                      reads=[p, eps_t], writes=[rs])
            cx.op('dve', lambda e: nc.vector.reciprocal(rs.t[:], rs.t[:]), reads=[rs], writes=[rs])
            o = ob.next()
            cx.op('dve', lambda e, o=o: nc.vector.scalar_tensor_tensor(out=o.t[:], in0=z.t[:], scalar=pr.t[:, ch, 3:4],
                                                                       in1=rs.t[:], op0=ALU.mult, op1=ALU.mult),
                  reads=[z, pr, rs], writes=[o])
            cx.dma('sp', lambda e, o=o: e.dma_start(out=yT[1536 + ch * 128:1536 + (ch + 1) * 128, t0:t0 + S], in_=o.t[:]),
                   reads=[o], sembuf=o)
        for b in range(NSEQ):
            for ch in range(4):
                body(b, ch)
        cx.end_stage(blk)


def stage_mla(cx, pT, wq, wkv, prm, cs, consts, maskd, yT):
    nc = cx.nc
    T = 1024
    scale = float((128 + 64) ** -0.5)
    with contextlib.ExitStack() as es:
        cx.begin_stage()
        ones_f, eps_t = load_consts(cx, es, nc, consts)
        ones_b = cx.buf(alloc(es, nc, "ones_b", [128, 128], BF16), dma=True)
        id_b = cx.buf(alloc(es, nc, "id_b", [128, 128], BF16), dma=True)
        mask = cx.buf(alloc(es, nc, "mask", [128, 4, 512], BF16), dma=True)
        pr = cx.buf(alloc(es, nc, "pr", [128, 16], F32), dma=True)
        wqb = cx.buf(alloc(es, nc, "wqb", [128, 4, 2048], BF16), dma=True)
        wkvb = cx.buf(alloc(es, nc, "wkvb", [128, 2, 2048], BF16), dma=True)
        cqn = cx.buf(alloc(es, nc, "cqn", [128, 4, S], BF16))
        ckvn = cx.buf(alloc(es, nc, "ckvn", [128, 2, S], BF16))
        cc = cx.buf(alloc(es, nc, "cc", [64, S], F32), dma=True)
        ss = cx.buf(alloc(es, nc, "ss", [64, S], F32), dma=True)
        kx = cx.buf(alloc(es, nc, "kx", [64, S], F32), dma=True)
        kxs = cx.buf(alloc(es, nc, "kxs", [64, S], F32), dma=True)
        k_r = cx.buf(alloc(es, nc, "k_r", [64, S], BF16))
        hsets = [dict(q_n=cx.buf(alloc(es, nc, "q_n", [128, S], BF16)), q_r=cx.buf(alloc(es, nc, "q_r", [64, S], BF16)),
                      k_n=cx.buf(alloc(es, nc, "k_n", [128, S], BF16)), V=cx.buf(alloc(es, nc, "V", [128, 16, 128], BF16)))
                 for _ in range(2)]
        xt = cx.buf(alloc(es, nc, "xt", [64, 512], F32))
        rt2 = cx.buf(alloc(es, nc, "rt2", [64, 512], F32))
        PT = Ring([cx.buf(alloc(es, nc, f"PT{i}", [128, 512], BF16)) for i in range(3)])
        hring = Ring([cx.buf(alloc(es, nc, f"hb{i}", [128, T], F32), dma=True) for i in range(2)])
        sqring = Ring([cx.buf(alloc(es, nc, f"sq{i}", [128, T], F32)) for i in range(2)])
        rstd = cx.buf(alloc(es, nc, "rstd", [128, T], F32))
        rinv = cx.buf(alloc(es, nc, "rinv", [128, 512], F32))
        yv = cx.buf(alloc(es, nc, "yv", [128, 512], F32))
        ysq = cx.buf(alloc(es, nc, "ysq", [128, 512], F32))
        sd2 = cx.buf(alloc(es, nc, "sd2", [128, 512], F32))
        obr = Ring([cx.buf(alloc(es, nc, f"ob{i}", [128, 512], BF16), dma=True) for i in range(2)])
        ps_p = Ring([cx.buf(alloc(es, nc, f"psp{i}", [128, 512], F32, psum=True)) for i in range(2)])
        ps_ss = ps_p.bufs
        ps_s = Ring([cx.buf(alloc(es, nc, f"pst{i}", [128, 512], F32, psum=True)) for i in range(2)])
        ps_o_r = Ring([cx.buf(alloc(es, nc, f"pso{i}", [128, 512], F32, psum=True)) for i in range(2)])
        ps_r_r = Ring([cx.buf(alloc(es, nc, f"psr{i}", [128, 512], F32, psum=True)) for i in range(2)])
        blk = es.enter_context(nc.Block())

        cx.dma('pool', lambda e: e.dma_start(out=ones_b.t[:], in_=consts[:, C_ONES:C_ONES + 128]), writes=[ones_b], sembuf=ones_b)
        cx.dma('pool', lambda e: e.dma_start(out=id_b.t[:], in_=consts[:, C_ID:C_ID + 128]), writes=[id_b], sembuf=id_b)
        cx.dma('pool', lambda e: e.dma_start(out=mask.t[:], in_=maskd[:, :, :]), writes=[mask], sembuf=mask)
        cx.dma('sp', lambda e: e.dma_start(out=pr.t[:], in_=prm[:, :]), writes=[pr], sembuf=pr)
        cx.dma('pool', lambda e: e.dma_start(out=wqb.t[:], in_=wq.rearrange("(kc p) n -> p kc n", p=128)), writes=[wqb], sembuf=wqb)
        cx.dma('pool', lambda e: e.dma_start(out=wkvb.t[:], in_=wkv.rearrange("(kc p) n -> p kc n", p=128)), writes=[wkvb], sembuf=wkvb)

        def norm_into(dst, row0, nch, gcol0, t0):
            for half in range(S // T):
                tt0 = t0 + half * T
                ld = lambda kc, hb, tt0=tt0: (lambda e: e.dma_start(out=hb.t[:, :T], in_=pT[row0 + kc * 128:row0 + (kc + 1) * 128, tt0:tt0 + T]))
                rms_stats(cx, nc, ld, nch, T, hring, sqring, ps_ss, ones_f, nch * 128, eps_t, rstd, rstd)
                for kc in range(nch):
                    hb = hring.next()
                    cx.dma('sp', ld(kc, hb), writes=[hb], sembuf=hb)
                    cx.op('dve', lambda e, hb=hb, kc=kc, half=half: nc.vector.scalar_tensor_tensor(
                        out=dst.t[:, kc, half * T:(half + 1) * T], in0=hb.t[:, :T], scalar=pr.t[:, gcol0 + kc:gcol0 + kc + 1],
                        in1=rstd.t[:, :T], op0=ALU.mult, op1=ALU.mult), reads=[hb, pr, rstd], writes=[dst])

        def rope(dst, x_ap, xs_ap, sl, xbufs):
            cx.op('dve', lambda e: nc.vector.tensor_tensor(out=xt.t[:, :sl.stop - sl.start], in0=xs_ap, in1=ss.t[:, sl], op=ALU.mult),
                  reads=xbufs + [ss], writes=[xt])
            cx.op('dve', lambda e: nc.vector.tensor_tensor(out=rt2.t[0:64, :sl.stop - sl.start], in0=x_ap, in1=cc.t[:, sl], op=ALU.mult),
                  reads=xbufs + [cc], writes=[rt2])
            cx.op('dve', lambda e: nc.vector.tensor_tensor(out=dst.t[0:64, sl], in0=rt2.t[0:64, :sl.stop - sl.start],
                                                           in1=xt.t[:, :sl.stop - sl.start], op=ALU.add),
                  reads=[rt2, xt], writes=[dst])

        def seq_body(b):
            t0 = b * S
            cx.dma('sp', lambda e: e.dma_start(out=cc.t[:], in_=cs[b, 0, :, :]), writes=[cc], sembuf=cc)
            cx.dma('sp', lambda e: e.dma_start(out=ss.t[:], in_=cs[b, 1, :, :]), writes=[ss], sembuf=ss)
            cx.dma('sp', lambda e: e.dma_start(out=kx.t[:], in_=pT[768:832, t0:t0 + S]), writes=[kx], sembuf=kx)
            cx.dma('sp', lambda e: e.dma_start(out=kxs.t[:], in_=pT[R_KRS:R_KRS + 64, t0:t0 + S]), writes=[kxs], sembuf=kxs)
            norm_into(cqn, 0, 4, 0, t0)
            norm_into(ckvn, 512, 2, 4, t0)
            for sub in range(4):
                sl = slice(sub * 512, (sub + 1) * 512)
                rope(k_r, kx.t[:, sl], kxs.t[:, sl], sl, [kx, kxs])
            first = proj_gen(0, t0, hsets[0])
            for _ in first:
                pass
            for h in range(8):
                nxt = proj_gen(h + 1, t0, hsets[(h + 1) % 2]) if h < 7 else None
                attn(h, t0, hsets[h % 2], nxt)
                if nxt is not None:
                    for _ in nxt:
                        pass

        def proj_gen(h, t0, hs):
            q_n, q_r, k_n, V = hs["q_n"], hs["q_r"], hs["k_n"], hs["V"]
            if True:
                c0 = h * 256
                for sub in range(4):
                    sl = slice(sub * 512, (sub + 1) * 512)
                    p = ps_p.next()
                    mm_group(cx, p, [(wqb.t[:, kc, c0:c0 + 128], cqn.t[:, kc, sl]) for kc in range(4)], reads=[wqb, cqn])
                    cx.op('act', lambda e, p=p, sl=sl: nc.scalar.copy(q_n.t[:, sl], p.t[:]), reads=[p], writes=[q_n])
                    yield
                    p = ps_p.next()
                    mm_group(cx, p, [(wkvb.t[:, kc, c0:c0 + 128], ckvn.t[:, kc, sl]) for kc in range(2)], reads=[wkvb, ckvn])
                    cx.op('act', lambda e, p=p, sl=sl: nc.scalar.copy(k_n.t[:, sl], p.t[:]), reads=[p], writes=[k_n])
                    yield
                    p1 = ps_p.next()
                    mm_group(cx, p1, [(wqb.t[:, kc, c0 + 128:c0 + 192], cqn.t[:, kc, sl]) for kc in range(4)], reads=[wqb, cqn],
                             out_ap=p1.t[0:64, :])
                    p2 = ps_p.next()
                    mm_group(cx, p2, [(wqb.t[:, kc, c0 + 192:c0 + 256], cqn.t[:, kc, sl]) for kc in range(4)], reads=[wqb, cqn],
                             out_ap=p2.t[0:64, :])
                    rope(q_r, p1.t[0:64, :], p2.t[0:64, :], sl, [p1, p2])
                    yield
                    p = ps_p.next()

                    def vfn(eng, p=p, sub=sub):
                        ins = None
                        for j in range(4):
                            tb = sub * 4 + j
                            for kc in range(2):
                                ins = nc.tensor.matmul(p.t[:, j * 128:(j + 1) * 128], ckvn.t[:, kc, tb * 128:(tb + 1) * 128],
                                                       wkvb.t[:, kc, c0 + 128:c0 + 256], start=(kc == 0), stop=(kc == 1))
                        return ins
                    cx.op('pe', vfn, reads=[ckvn, wkvb], writes=[p])
                    cx.op('act', lambda e, p=p, sub=sub: nc.scalar.copy(
                        V.t[:, sub * 4:(sub + 1) * 4, :], p.t[:].rearrange("p (a b) -> p a b", a=4)), reads=[p], writes=[V])
                    yield

        def attn(h, t0, hs, nxt):
            q_n, q_r, k_n, V = hs["q_n"], hs["q_r"], hs["k_n"], hs["V"]
            if True:
                items = [(qt, kb) for qt in range(4) for kb in range(4 * (qt + 1))]
                acc = {}

                def emit_st(qt, kb):
                    qs = slice(qt * 512, (qt + 1) * 512)
                    ks = slice(kb * 128, (kb + 1) * 128)
                    st = ps_s.next()
                    pairs = [(k_n.t[:, ks], q_n.t[:, qs]), (k_r.t[0:64, ks], q_r.t[0:64, qs])]
                    rds = [k_n, q_n, k_r, q_r]
                    if kb >= 4 * qt:
                        pairs.append((id_b.t[:], mask.t[:, kb - 4 * qt, :]))
                        rds += [id_b, mask]
                    mm_group(cx, st, pairs, reads=rds)
                    pt = PT.next()
                    cx.op('act', lambda e, st=st, pt=pt: nc.scalar.activation(out=pt.t[:], in_=st.t[:], func=AF.Exp, scale=scale),
                          reads=[st], writes=[pt])
                    return pt

                def emit_pv(qt, kb, pt):
                    nkb = 4 * (qt + 1)
                    if kb == 0:
                        acc[qt] = (ps_o_r.next(), ps_r_r.next())
                    ps_o, ps_r = acc[qt]

                    def pv(eng):
                        nc.tensor.matmul(ps_o.t[:], V.t[:, kb, :], pt.t[:], start=(kb == 0), stop=(kb == nkb - 1))
                        return nc.tensor.matmul(ps_r.t[:], ones_b.t[:], pt.t[:], start=(kb == 0), stop=(kb == nkb - 1))
                    cx.op('pe', pv, reads=[pt, V, ones_b], writes=[ps_o, ps_r])
                    if kb == nkb - 1:
                        finalize(qt, ps_o, ps_r)

                def finalize(qt, ps_o, ps_r):
                    cx.op('dve', lambda e: nc.vector.reciprocal(rinv.t[:], ps_r.t[:]), reads=[ps_r], writes=[rinv])
                    cx.op('dve', lambda e: nc.vector.tensor_tensor(out=yv.t[:], in0=ps_o.t[:], in1=rinv.t[:], op=ALU.mult),
                          reads=[ps_o, rinv], writes=[yv])
                    cx.op('act', lambda e: nc.scalar.activation(out=ysq.t[:], in_=yv.t[:], func=AF.Square), reads=[yv], writes=[ysq])
                    pm = ps_p.next()
                    cx.op('pe', lambda e: nc.tensor.matmul(pm.t[:], ones_f.t[:], ysq.t[:], start=True, stop=True),
                          reads=[ones_f, ysq], writes=[pm])
                    cx.op('act', lambda e: nc.scalar.activation(out=sd2.t[:], in_=pm.t[:], func=AF.Sqrt,
                                                                bias=eps_t.t[:, 0:1], scale=1.0 / 128),
                          reads=[pm, eps_t], writes=[sd2])
                    cx.op('dve', lambda e: nc.vector.reciprocal(sd2.t[:], sd2.t[:]), reads=[sd2], writes=[sd2])
                    o = obr.next()
                    cx.op('dve', lambda e: nc.vector.scalar_tensor_tensor(
                        out=o.t[:], in0=yv.t[:], scalar=pr.t[:, 6 + h:7 + h], in1=sd2.t[:], op0=ALU.mult, op1=ALU.mult),
                        reads=[yv, pr, sd2], writes=[o])
                    cx.dma('sp', lambda e: e.dma_start(
                        out=yT[h * 128:(h + 1) * 128, t0 + qt * 512:t0 + (qt + 1) * 512], in_=o.t[:]), reads=[o], sembuf=o)

                pending = None
                for ii, (qt, kb) in enumerate(items):
                    pt = emit_st(qt, kb)
                    if pending is not None:
                        emit_pv(*pending)
                    pending = (qt, kb, pt)
                    if nxt is not None and ii % 2 == 1:
                        next(nxt, None)
                emit_pv(*pending)
        for b in range(NSEQ):
            seq_body(b)
        cx.end_stage(blk)


def stage_rwkv(cx, pT, rp, dup, iup, gup, consts, yT):
    nc = cx.nc
    TB = 128
    c1 = -0.6065306597126334
    with contextlib.ExitStack() as es:
        cx.begin_stage()

        def sb(name, shape, dt=F32, dma=False):
            return cx.buf(alloc(es, nc, name, shape, dt), dma=dma)
        cst = sb("cst", [64, 1024], dma=True)
        rpt = sb("rpt", [128, 96], dma=True)
        dupt = sb("dupt", [32, 512], dma=True)
        iupt = sb("iupt", [32, 512], dma=True)
        gupt = sb("gupt", [96, 512], dma=True)
        omka = sb("omka", [64, 8])
        cur3 = sb("cur3", [64, 3, 8, TB], dma=True)
        prv3 = sb("prv3", [64, 3, 8, TB], dma=True)
        loc = sb("loc", [96, 3, TB], dma=True)
        lop = sb("lop", [96, 3, TB], dma=True)
        names = ["sw", "a", "g", "cum", "E1", "E2", "E3", "E4", "kk", "nr", "tmp", "kp", "bv", "aT", "bb", "bT", "bhT", "kT", "khT", "rT"]
        Tt = {n: sb(n, [64, 8, TB]) for n in names}
        cn = ["V", "Bh", "Kh", "N", "L", "AKu", "BRu", "KRu", "P", "Q", "Na", "La", "Nb", "Lb"]
        Cs = [{n: sb(n + str(c), [64, 8, 64]) for n in cn} for c in range(TB // 64)]
        Ct = {n: sb(n, [64, 8, 64]) for n in ["Zs", "Us", "H", "tH", "ysb", "yc", "sq", "sdv"]}
        obr = Ring([sb(f"ob{i}", [64, 8, 64], BF16, dma=True) for i in range(2)])
        pp = Ring([cx.buf(alloc(es, nc, f"pp{i}", [128, 512], F32, psum=True)) for i in range(8)])
        blk = es.enter_context(nc.Block())

        cx.dma('sp', lambda e: e.dma_start(out=cst.t[:], in_=consts[0:64, :]), writes=[cst], sembuf=cst)
        cx.dma('sp', lambda e: e.dma_start(out=rpt.t[:], in_=rp[:, :]), writes=[rpt], sembuf=rpt)
        cx.dma('sp', lambda e: e.dma_start(out=dupt.t[:], in_=dup[:, :]), writes=[dupt], sembuf=dupt)
        cx.dma('sp', lambda e: e.dma_start(out=iupt.t[:], in_=iup[:, :]), writes=[iupt], sembuf=iupt)
        cx.dma('sp', lambda e: e.dma_start(out=gupt.t[:], in_=gup[:, :]), writes=[gupt], sembuf=gupt)
        ones64 = cst.t[:, C_ONES:C_ONES + 64]
        id64 = cst.t[:, C_ID:C_ID + 64]
        eps64 = cst.t[:, C_EPS + 1:C_EPS + 2]

        def mk(c):
            return cst.t[:, c:c + 64].unsqueeze(1).broadcast_to([64, 8, 64])
        SUb, SLb, IUb, IDb = mk(C_SU), mk(C_SL), mk(C_IU), mk(C_ID)
        rmask = cst.t[:, C_RM:C_RM + TB]

        def pb(col):
            return rpt.t[0:64, col:col + 8]

        def bc(ap2, n):
            return ap2.unsqueeze(2).broadcast_to([64, 8, n])

        def TTo(o, oap, ins, op, eng='dve'):
            bufs = [x[0] for x in ins]
            aps = [x[1] for x in ins]
            cx.op(eng, lambda e: nc.vector.tensor_tensor(out=oap, in0=aps[0], in1=aps[1], op=op),
                  reads=[b for b in bufs if b is not None], writes=[o])

        def ACTo(o, oap, i, iap, func, scale=1.0, bias=None, extra=()):
            def fn(e):
                if bias is None:
                    return nc.scalar.activation(out=oap, in_=iap, func=func, scale=scale)
                return nc.scalar.activation(out=oap, in_=iap, func=func, scale=scale, bias=bias)
            cx.op('act', fn, reads=[i] + list(extra), writes=[o])

        for bt in (loc, lop):
            cx.op('dve', lambda e, bt=bt: nc.vector.memset(bt.t[:], 0.0), writes=[bt])
        cx.op('dve', lambda e: nc.vector.tensor_scalar(out=omka.t[:], in0=pb(48), scalar1=-1.0, scalar2=1.0,
                                                       op0=ALU.mult, op1=ALU.add), reads=[rpt], writes=[omka])

        def rows3(t_lo, t_hi):
            return [pT[B0 + kd * 512:B0 + (kd + 1) * 512, t_lo:t_hi].rearrange("(h f) t -> f h t", f=64) for kd in range(3)]

        def block_body(b, k):
            t0 = b * S + k * TB
            T = Tt
            for kd, src in enumerate(rows3(t0, t0 + TB)):
                cx.dma('sp', lambda e, kd=kd, src=src: e.dma_start(out=cur3.t[:, kd], in_=src), writes=[cur3], sembuf=cur3)
            cx.dma('sp', lambda e: e.dma_start(out=loc.t[0:32, 0, :], in_=pT[B0 + 1536:B0 + 1568, t0:t0 + TB]), writes=[loc], sembuf=loc)
            cx.dma('sp', lambda e: e.dma_start(out=loc.t[0:32, 1, :], in_=pT[B0 + 1568:B0 + 1600, t0:t0 + TB]), writes=[loc], sembuf=loc)
            cx.dma('sp', lambda e: e.dma_start(out=loc.t[0:96, 2, :], in_=pT[B0 + 1600:B0 + 1696, t0:t0 + TB]), writes=[loc], sembuf=loc)
            if k == 0:
                cx.op('dve', lambda e: nc.vector.memset(prv3.t[:, :, :, 0:1], 0.0), writes=[prv3])
                cx.op('dve', lambda e: nc.vector.memset(lop.t[:, :, 0:1], 0.0), writes=[lop])
                cx.op('dve', lambda e: nc.vector.memset(Ct["H"].t[:], 0.0), writes=[Ct["H"]])
                for kd, src in enumerate(rows3(t0, t0 + TB - 1)):
                    cx.dma('sp', lambda e, kd=kd, src=src: e.dma_start(out=prv3.t[:, kd, :, 1:TB], in_=src), writes=[prv3], sembuf=prv3)
                o1, lo_, hi_ = 1, t0, t0 + TB - 1
            else:
                for kd, src in enumerate(rows3(t0 - 1, t0 + TB - 1)):
                    cx.dma('sp', lambda e, kd=kd, src=src: e.dma_start(out=prv3.t[:, kd], in_=src), writes=[prv3], sembuf=prv3)
                o1, lo_, hi_ = 0, t0 - 1, t0 + TB - 1
            cx.dma('sp', lambda e: e.dma_start(out=lop.t[0:32, 0, o1:TB], in_=pT[B0 + 1536:B0 + 1568, lo_:hi_]), writes=[lop], sembuf=lop)
            cx.dma('sp', lambda e: e.dma_start(out=lop.t[0:32, 1, o1:TB], in_=pT[B0 + 1568:B0 + 1600, lo_:hi_]), writes=[lop], sembuf=lop)
            cx.dma('sp', lambda e: e.dma_start(out=lop.t[0:96, 2, o1:TB], in_=pT[B0 + 1600:B0 + 1696, lo_:hi_]), writes=[lop], sembuf=lop)
            c3 = cur3.t[:].rearrange("p a h t -> p (a h) t")
            p3 = prv3.t[:].rearrange("p a h t -> p (a h) t")
            mu3 = rpt.t[0:64, 0:24].unsqueeze(2).broadcast_to([64, 24, TB])
            TTo(prv3, p3, [(prv3, p3), (cur3, c3)], ALU.subtract)
            TTo(prv3, p3, [(prv3, p3), (rpt, mu3)], ALU.mult)
            TTo(prv3, p3, [(prv3, p3), (cur3, c3)], ALU.add)
            mul = rpt.t[0:96, 80:83].unsqueeze(2).broadcast_to([96, 3, TB])
            TTo(lop, lop.t[:], [(lop, lop.t[:]), (loc, loc.t[:])], ALU.subtract)
            TTo(lop, lop.t[:], [(lop, lop.t[:]), (rpt, mul)], ALU.mult)
            TTo(lop, lop.t[:], [(lop, lop.t[:]), (loc, loc.t[:])], ALU.add)
            rs_, ks_, vs_ = prv3.t[:, 0], prv3.t[:, 1], prv3.t[:, 2]
            ACTo(lop, lop.t[0:32, 0, :], lop, lop.t[0:32, 0, :], AF.Tanh)
            ACTo(lop, lop.t[0:96, 2, :], lop, lop.t[0:96, 2, :], AF.Sigmoid)
            for (dst, wt, kdim, li, bcol) in ((T["sw"], dupt, 32, 0, 24), (T["a"], iupt, 32, 1, 32), (T["g"], gupt, 96, 2, None)):
                for half in range(2):
                    p = pp.next()

                    def fn(e, p=p, wt=wt, kdim=kdim, li=li, half=half):
                        ins = None
                        for hh in range(4):
                            h = half * 4 + hh
                            ins = nc.tensor.matmul(p.t[0:64, hh * TB:(hh + 1) * TB], wt.t[0:kdim, h * 64:(h + 1) * 64],
                                                   lop.t[0:kdim, li, :], start=True, stop=True)
                        return ins
                    cx.op('pe', fn, reads=[wt, lop], writes=[p])
                    dap = dst.t[:, half * 4:(half + 1) * 4, :]
                    pap = p.t[0:64, :].rearrange("p (h t) -> p h t", h=4)
                    if bcol is None:
                        ACTo(dst, dap, p, pap, AF.Copy)
                    else:
                        bb_ = rpt.t[0:64, bcol + half * 4:bcol + half * 4 + 4].unsqueeze(2).broadcast_to([64, 4, TB])
                        TTo(dst, dap, [(p, pap), (rpt, bb_)], ALU.add)
                if bcol is not None:
                    ACTo(dst, dst.t[:], dst, dst.t[:], AF.Sigmoid)
            for h in range(8):
                cx.op('dve', lambda e, h=h: nc.vector.tensor_tensor_scan(
                    out=T["cum"].t[:, h, :], data0=rmask, data1=T["sw"].t[:, h, :], initial=0.0, op0=ALU.mult, op1=ALU.add),
                    reads=[cst, T["sw"]], writes=[T["cum"]])
            ACTo(T["E1"], T["E1"].t[:], T["cum"], T["cum"].t[:], AF.Exp, scale=c1)
            ACTo(T["E2"], T["E2"].t[:], T["cum"], T["cum"].t[:], AF.Exp, scale=-c1)
            TTo(T["E3"], T["E3"].t[:], [(T["cum"], T["cum"].t[:]), (T["sw"], T["sw"].t[:])], ALU.subtract)
            ACTo(T["E3"], T["E3"].t[:], T["E3"], T["E3"].t[:], AF.Exp, scale=c1)
            cum4 = T["cum"].t[:].rearrange("p h (c t) -> p h c t", t=64)
            cumC = cum4[:, :, :, 63:64].broadcast_to([64, 8, TB // 64, 64])
            e44 = T["E4"].t[:].rearrange("p h (c t) -> p h c t", t=64)
            TTo(T["E4"], e44, [(T["cum"], cumC), (T["cum"], cum4)], ALU.subtract)
            ACTo(T["E4"], T["E4"].t[:], T["E4"], T["E4"].t[:], AF.Exp, scale=c1)
            TTo(T["kk"], T["kk"].t[:], [(prv3, ks_), (rpt, bc(pb(40), TB))], ALU.mult)
            ACTo(T["nr"], T["nr"].t[:], T["kk"], T["kk"].t[:], AF.Square)
            for half in range(2):
                p = pp.next()
                hs = slice(half * 4, (half + 1) * 4)
                cx.op('pe', lambda e, p=p, hs=hs: nc.tensor.matmul(p.t[0:64, :], ones64, T["nr"].t[:, hs, :], start=True, stop=True),
                      reads=[cst, T["nr"]], writes=[p])
                ACTo(T["tmp"], T["tmp"].t[:, hs, :], p, p.t[0:64, :].rearrange("p (h t) -> p h t", h=4), AF.Sqrt)
            cx.op('dve', lambda e: nc.vector.tensor_scalar(out=T["tmp"].t[:], in0=T["tmp"].t[:], scalar1=1e-12, scalar2=None, op0=ALU.max),
                  reads=[T["tmp"]], writes=[T["tmp"]])
            cx.op('dve', lambda e: nc.vector.reciprocal(T["tmp"].t[:], T["tmp"].t[:]), reads=[T["tmp"]], writes=[T["tmp"]])
            TTo(T["kk"], T["kk"].t[:], [(T["kk"], T["kk"].t[:]), (T["tmp"], T["tmp"].t[:])], ALU.mult)
            TTo(T["tmp"], T["tmp"].t[:], [(T["a"], T["a"].t[:]), (rpt, bc(pb(48), TB))], ALU.mult)
            TTo(T["tmp"], T["tmp"].t[:], [(T["tmp"], T["tmp"].t[:]), (omka, bc(omka.t[:], TB))], ALU.add)
            TTo(T["kp"], T["kp"].t[:], [(prv3, ks_), (T["tmp"], T["tmp"].t[:])], ALU.mult)
            TTo(T["tmp"], T["tmp"].t[:], [(prv3, rs_), (T["kp"], T["kp"].t[:])], ALU.mult)
            TTo(T["tmp"], T["tmp"].t[:], [(T["tmp"], T["tmp"].t[:]), (rpt, bc(pb(56), TB))], ALU.mult)
            for half in range(2):
                p = pp.next()
                hs = slice(half * 4, (half + 1) * 4)
                cx.op('pe', lambda e, p=p, hs=hs: nc.tensor.matmul(p.t[0:64, :], ones64, T["tmp"].t[:, hs, :], start=True, stop=True),
                      reads=[cst, T["tmp"]], writes=[p])
                TTo(T["bv"], T["bv"].t[:, hs, :], [(p, p.t[0:64, :].rearrange("p (h t) -> p h t", h=4)), (prv3, prv3.t[:, 2, hs, :])], ALU.mult)
            cx.op('dve', lambda e: nc.vector.scalar_tensor_tensor(out=T["aT"].t[:], in0=T["kk"].t[:], scalar=-1.0, in1=T["E3"].t[:],
                                                                  op0=ALU.mult, op1=ALU.mult), reads=[T["kk"], T["E3"]], writes=[T["aT"]])
            TTo(T["bb"], T["bb"].t[:], [(T["kk"], T["kk"].t[:]), (T["a"], T["a"].t[:])], ALU.mult)
            TTo(T["bT"], T["bT"].t[:], [(T["bb"], T["bb"].t[:]), (T["E2"], T["E2"].t[:])], ALU.mult)
            TTo(T["bhT"], T["bhT"].t[:], [(T["bb"], T["bb"].t[:]), (T["E4"], T["E4"].t[:])], ALU.mult)
            TTo(T["kT"], T["kT"].t[:], [(T["kp"], T["kp"].t[:]), (T["E2"], T["E2"].t[:])], ALU.mult)
            TTo(T["khT"], T["khT"].t[:], [(T["kp"], T["kp"].t[:]), (T["E4"], T["E4"].t[:])], ALU.mult)
            TTo(T["rT"], T["rT"].t[:], [(prv3, rs_), (T["E1"], T["E1"].t[:])], ALU.mult)
            run_chunks(b, k)

        def pgroup(builder, reads):
            p = pp.next()

            def fn(e, p=p):
                ins = None
                for h in range(8):
                    ins = builder(p.t[0:64, h * 64:(h + 1) * 64], h)
                return ins
            cx.op('pe', fn, reads=reads, writes=[p])
            return p, p.t[0:64, :].rearrange("p (h t) -> p h t", h=8)

        def phase_a(b, k, c):
            T, C = Tt, Cs[c]
            cs_ = slice(c * 64, (c + 1) * 64)
            for dst, src_b, src_ap in ((C["V"], prv3, prv3.t[:, 2]), (C["Bh"], T["bhT"], T["bhT"].t[:]), (C["Kh"], T["khT"], T["khT"].t[:])):
                p, pv = pgroup(lambda o, h, src_ap=src_ap: nc.tensor.transpose(o, src_ap[:, h, cs_], id64), [src_b, cst])
                ACTo(dst, dst.t[:], p, pv, AF.Copy)
            yield

            def pw(dst, lt, rt, mb):
                p, pv = pgroup(lambda o, h: nc.tensor.matmul(o, lt.t[:, h, cs_], rt.t[:, h, cs_], start=True, stop=True), [lt, rt])
                TTo(dst, dst.t[:], [(p, pv), (cst, mb)], ALU.mult)
            pw(C["N"], T["bT"], T["aT"], SUb)
            pw(C["L"], T["aT"], T["bT"], SLb)
            yield
            pw(C["AKu"], T["kT"], T["aT"], SUb)
            pw(C["BRu"], T["bT"], T["rT"], IUb)
            pw(C["KRu"], T["kT"], T["rT"], IUb)
            TTo(C["P"], C["P"].t[:], [(C["N"], C["N"].t[:]), (cst, IDb)], ALU.add)
            TTo(C["Q"], C["Q"].t[:], [(C["L"], C["L"].t[:]), (cst, IDb)], ALU.add)
            yield
            Nc, Lc = C["N"], C["L"]
            nxt = [(C["Na"], C["La"]), (C["Nb"], C["Lb"])]
            for lvl in range(5):
                Nn, Ln = nxt[lvl % 2]
                p, pv = pgroup(lambda o, h, Nc=Nc, Lc=Lc: nc.tensor.matmul(o, Lc.t[:, h, :], Nc.t[:, h, :], start=True, stop=True), [Nc, Lc])
                if lvl < 4:
                    p2, pv2 = pgroup(lambda o, h, Nc=Nc, Lc=Lc: nc.tensor.matmul(o, Nc.t[:, h, :], Lc.t[:, h, :], start=True, stop=True), [Nc, Lc])
                ACTo(Nn, Nn.t[:], p, pv, AF.Copy)
                if lvl < 4:
                    cx.op('dve', lambda e, Ln=Ln, pv2=pv2: nc.vector.tensor_copy(Ln.t[:], pv2), reads=[p2], writes=[Ln])
                yield
                p, pv = pgroup(lambda o, h, Nn=Nn: nc.tensor.matmul(o, C["Q"].t[:, h, :], Nn.t[:, h, :], start=True, stop=True), [C["Q"], Nn])
                if lvl < 4:
                    p2, pv2 = pgroup(lambda o, h, Ln=Ln: nc.tensor.matmul(o, C["P"].t[:, h, :], Ln.t[:, h, :], start=True, stop=True), [C["P"], Ln])
                TTo(C["P"], C["P"].t[:], [(C["P"], C["P"].t[:]), (p, pv)], ALU.add)
                if lvl < 4:
                    TTo(C["Q"], C["Q"].t[:], [(C["Q"], C["Q"].t[:]), (p2, pv2)], ALU.add)
                Nc, Lc = Nn, Ln
                yield

        def phase_b(b, k, c):
            T, C = Tt, Cs[c]
            cs_ = slice(c * 64, (c + 1) * 64)
            tc0 = b * S + k * TB + c * 64
            H = Ct["H"]

            def zb(o, h):
                nc.tensor.matmul(o, T["aT"].t[:, h, cs_], H.t[:, h, :], start=True, stop=False)
                return nc.tensor.matmul(o, C["AKu"].t[:, h, :], C["V"].t[:, h, :], start=False, stop=True)
            p, pv = pgroup(zb, [T["aT"], H, C["AKu"], C["V"]])
            ACTo(Ct["Zs"], Ct["Zs"].t[:], p, pv, AF.Copy)
            p, pv = pgroup(lambda o, h: nc.tensor.matmul(o, C["P"].t[:, h, :], Ct["Zs"].t[:, h, :], start=True, stop=True), [C["P"], Ct["Zs"]])
            cx.op('dve', lambda e, pv=pv: nc.vector.tensor_copy(Ct["Us"].t[:], pv), reads=[p], writes=[Ct["Us"]])

            def yb_(o, h):
                nc.tensor.matmul(o, H.t[:, h, :], T["rT"].t[:, h, cs_], start=True, stop=False)
                nc.tensor.matmul(o, Ct["Us"].t[:, h, :], C["BRu"].t[:, h, :], start=False, stop=False)
                return nc.tensor.matmul(o, C["V"].t[:, h, :], C["KRu"].t[:, h, :], start=False, stop=True)

            def hb_(o, h):
                nc.tensor.matmul(o, C["Bh"].t[:, h, :], Ct["Us"].t[:, h, :], start=True, stop=False)
                return nc.tensor.matmul(o, C["Kh"].t[:, h, :], C["V"].t[:, h, :], start=False, stop=True)
            ph, phv = pgroup(hb_, [C["Bh"], Ct["Us"], C["Kh"], C["V"]])
            py, pyv = pgroup(yb_, [H, T["rT"], Ct["Us"], C["BRu"], C["V"], C["KRu"]])
            gC = T["E1"].t[:, :, c * 64 + 63:c * 64 + 64].broadcast_to([64, 8, 64])
            TTo(Ct["tH"], Ct["tH"].t[:], [(H, H.t[:]), (T["E1"], gC)], ALU.mult)
            TTo(H, H.t[:], [(Ct["tH"], Ct["tH"].t[:]), (ph, phv)], ALU.add)
            Cc = Ct
            ACTo(Cc["ysb"], Cc["ysb"].t[:], py, pyv, AF.Copy)
            ysf = Cc["ysb"].t[:].rearrange("p h t -> p (h t)")
            pm = pp.next()
            cx.op('pe', lambda e, pm=pm: nc.tensor.matmul(pm.t[0:64, :], ones64, ysf, start=True, stop=True), reads=[cst, Cc["ysb"]], writes=[pm])
            cx.op('dve', lambda e, pm=pm: nc.vector.scalar_tensor_tensor(
                out=Cc["yc"].t[:].rearrange("p h t -> p (h t)"), in0=pm.t[0:64, :], scalar=-1.0 / 64, in1=ysf, op0=ALU.mult, op1=ALU.add),
                reads=[pm, Cc["ysb"]], writes=[Cc["yc"]])
            ACTo(Cc["sq"], Cc["sq"].t[:], Cc["yc"], Cc["yc"].t[:], AF.Square)
            pv_ = pp.next()
            cx.op('pe', lambda e, pv_=pv_: nc.tensor.matmul(pv_.t[0:64, :], ones64, Cc["sq"].t[:].rearrange("p h t -> p (h t)"), start=True, stop=True),
                  reads=[cst, Cc["sq"]], writes=[pv_])
            ACTo(Cc["sdv"], Cc["sdv"].t[:].rearrange("p h t -> p (h t)"), pv_, pv_.t[0:64, :], AF.Sqrt, scale=1.0 / 64, bias=eps64, extra=[cst])
            cx.op('dve', lambda e: nc.vector.reciprocal(Cc["sdv"].t[:], Cc["sdv"].t[:]), reads=[Cc["sdv"]], writes=[Cc["sdv"]])
            yc = Cc["yc"]
            TTo(yc, yc.t[:], [(yc, yc.t[:]), (Cc["sdv"], Cc["sdv"].t[:])], ALU.mult)
            TTo(yc, yc.t[:], [(yc, yc.t[:]), (rpt, bc(pb(64), 64))], ALU.mult)
            TTo(yc, yc.t[:], [(yc, yc.t[:]), (rpt, bc(pb(72), 64))], ALU.add)
            TTo(yc, yc.t[:], [(yc, yc.t[:]), (T["bv"], T["bv"].t[:, :, cs_])], ALU.add)
            o = obr.next()
            TTo(o, o.t[:], [(yc, yc.t[:]), (T["g"], T["g"].t[:, :, cs_])], ALU.mult)
            cx.dma('sp', lambda e, o=o: e.dma_start(
                out=yT[1024:1536, tc0:tc0 + 64].rearrange("(h f) t -> f h t", f=64), in_=o.t[:]), reads=[o], sembuf=o)

        def run_chunks(b, k):
            gens = [phase_a(b, k, c) for c in range(TB // 64)]
            while gens:
                for g_ in list(gens):
                    try:
                        next(g_)
                    except StopIteration:
                        gens.remove(g_)
            for c in range(TB // 64):
                phase_b(b, k, c)

        for b in range(NSEQ):
            for k in range(S // TB):
                block_body(b, k)
        cx.end_stage(blk)


def stage_rope(cx, pos, invf, cs):
    nc = cx.nc
    PI = float(np.pi)
    with contextlib.ExitStack() as es:
        cx.begin_stage()
        pi_ = cx.buf(alloc(es, nc, "pi_", [32, S], I32), dma=True)
        fr = cx.buf(alloc(es, nc, "fr", [32, 1], F32), dma=True)
        ang = cx.buf(alloc(es, nc, "ang", [32, S], F32))
        tf = cx.buf(alloc(es, nc, "tf", [32, S], F32))
        ki = cx.buf(alloc(es, nc, "ki", [32, S], I32))
        r = cx.buf(alloc(es, nc, "r", [32, S], F32))
        m = cx.buf(alloc(es, nc, "m", [32, S], F32))
        outs = [cx.buf(alloc(es, nc, f"o{i}", [32, S], F32), dma=True) for i in range(3)]
        blk = es.enter_context(nc.Block())
        cx.dma('sp', lambda e: e.dma_start(out=fr.t[:], in_=invf[:, :]), writes=[fr], sembuf=fr)

        def wrap(buf):
            cx.op('dve', lambda e: nc.vector.tensor_scalar(out=m.t[:], in0=buf.t[:], scalar1=PI, scalar2=-2 * PI, op0=ALU.is_gt, op1=ALU.mult),
                  reads=[buf], writes=[m])
            cx.op('dve', lambda e: nc.vector.tensor_tensor(out=buf.t[:], in0=buf.t[:], in1=m.t[:], op=ALU.add), reads=[m], writes=[buf])
            cx.op('dve', lambda e: nc.vector.tensor_scalar(out=m.t[:], in0=buf.t[:], scalar1=-PI, scalar2=2 * PI, op0=ALU.is_lt, op1=ALU.mult),
                  reads=[buf], writes=[m])
            cx.op('dve', lambda e: nc.vector.tensor_tensor(out=buf.t[:], in0=buf.t[:], in1=m.t[:], op=ALU.add), reads=[m], writes=[buf])

        def body(b):
            cx.dma('sp', lambda e: e.dma_start(out=pi_.t[:], in_=pos[b:b + 1, :].broadcast_to([32, S])), writes=[pi_], sembuf=pi_)
            cx.op('dve', lambda e: nc.vector.tensor_copy(ang.t[:], pi_.t[:]), reads=[pi_], writes=[ang])
            cx.op('dve', lambda e: nc.vector.tensor_scalar(out=ang.t[:], in0=ang.t[:], scalar1=fr.t[:, 0:1], scalar2=None, op0=ALU.mult),
                  reads=[fr], writes=[ang])
            cx.op('dve', lambda e: nc.vector.tensor_scalar(out=tf.t[:], in0=ang.t[:], scalar1=1.0 / (2 * PI), scalar2=None, op0=ALU.mult),
                  reads=[ang], writes=[tf])
            cx.op('dve', lambda e: nc.vector.tensor_copy(ki.t[:], tf.t[:]), reads=[tf], writes=[ki])
            cx.op('dve', lambda e: nc.vector.tensor_copy(tf.t[:], ki.t[:]), reads=[ki], writes=[tf])
            cx.op('dve', lambda e: nc.vector.scalar_tensor_tensor(out=r.t[:], in0=tf.t[:], scalar=-2 * PI, in1=ang.t[:], op0=ALU.mult, op1=ALU.add),
                  reads=[tf, ang], writes=[r])
            wrap(r)
            cx.op('act', lambda e: nc.scalar.activation(out=outs[1].t[:], in_=r.t[:], func=AF.Sin), reads=[r], writes=[outs[1]])
            cx.op('act', lambda e: nc.scalar.activation(out=outs[2].t[:], in_=r.t[:], func=AF.Sin, scale=-1.0), reads=[r], writes=[outs[2]])
            cx.op('dve', lambda e: nc.vector.tensor_scalar(out=r.t[:], in0=r.t[:], scalar1=PI / 2, scalar2=None, op0=ALU.add),
                  reads=[outs[1], outs[2]], writes=[r])
            wrap(r)
            cx.op('act', lambda e: nc.scalar.activation(out=outs[0].t[:], in_=r.t[:], func=AF.Sin), reads=[r], writes=[outs[0]])
            for (src, j, r0) in ((outs[0], 0, 0), (outs[0], 0, 32), (outs[2], 1, 0), (outs[1], 1, 32)):
                cx.dma('sp', lambda e, src=src, j=j, r0=r0: e.dma_start(out=cs[b, j, r0:r0 + 32, :], in_=src.t[:]), reads=[src], sembuf=src)
        for b in range(NSEQ):
            body(b)
        cx.end_stage(blk)


NPAD = 4224
BIGW = ["ffn1_gate", "ffn1_up", "ffn1_down", "ffn2_gate", "ffn2_up", "ffn2_down", "w_out", "w_ukv",
        "decay_up", "iclr_up", "gate_up"]
BIGW_SHAPES = {"ffn1_gate": [D, DFF], "ffn1_up": [D, DFF], "ffn1_down": [DFF, D], "ffn2_gate": [D, DFF],
               "ffn2_up": [D, DFF], "ffn2_down": [DFF, D], "w_out": [D, D], "w_ukv": [256, 2048],
               "decay_up": [32, 512], "iclr_up": [32, 512], "gate_up": [96, 512]}


def build_program(nlayers=DEPTH, dbg=False):
    nc = bass.Bass("TRN2", target_bir_lowering=False)
    dt = lambda n, s, d=F32, kind="ExternalInput": nc.dram_tensor(n, s, d, kind=kind).ap()
    x = dt("x", [NT, D])
    pos = dt("pos", [NSEQ, S], I32)
    W = {n: dt(n, [DEPTH] + BIGW_SHAPES[n]) for n in BIGW}
    w_inx = dt("w_inx", [DEPTH, D, NPAD])
    wq = dt("wq", [DEPTH, 512, 2048])
    spk = dt("spk", [DEPTH, 128, 192])
    gfin = dt("gfin", [128, KC])
    consts = dt("consts", [128, 1024])
    maskd = dt("maskd", [128, 4, 512])
    invf = dt("invf", [32, 1])
    out = dt("out", [NT, D], kind="ExternalOutput")
    sk = "ExternalOutput" if dbg else "Internal"
    hT = dt("hT", [KC, 128, NT], kind=sk)
    pT = dt("pT", [NPAD, NT], kind=sk)
    yT = dt("yT", [D, NT], BF16, kind=sk)
    cs = dt("cs", [NSEQ, 2, 64, S], kind=sk)
    with contextlib.ExitStack() as es:
        sems = [es.enter_context(nc.semaphore(f"s{i}")) for i in range(96)]
        cx = Ctx(nc, sems)
        stage_rope(cx, pos, invf, cs)
        stage_in(cx, x, hT, consts[:, C_ID:C_ID + 128])
        for l in range(nlayers):
            sp = spk[l]
            stage_ffn(cx, hT, W["ffn1_gate"][l], W["ffn1_up"][l], W["ffn1_down"][l], sp[:, 0:16], consts)
            stage_proj(cx, hT, w_inx[l], sp[:, 16:32], consts, pT, NPAD)
            stage_mla(cx, pT, wq[l], W["w_ukv"][l], sp[:, 48:64], cs, consts, maskd, yT)
            stage_rwkv(cx, pT, sp[:, 64:160], W["decay_up"][l], W["iclr_up"][l], W["gate_up"][l], consts, yT)
            stage_conv(cx, pT, sp[:, 160:176].rearrange("p (a b) -> p a b", a=4), consts, yT)
            stage_wout(cx, hT, yT, W["w_out"][l])
            stage_ffn(cx, hT, W["ffn2_gate"][l], W["ffn2_up"][l], W["ffn2_down"][l], sp[:, 32:48], consts)
        stage_out(cx, hT, gfin, consts, consts[:, C_ID:C_ID + 128], out)
    return nc


def _fm(v, nch):
    return np.ascontiguousarray(np.asarray(v, np.float32).reshape(nch, 128).T)


def _hd(v):
    return np.ascontiguousarray(np.asarray(v, np.float32).reshape(8, 64).T)


def host_layout(inp):
    f32 = np.float32
    w_in = np.asarray(inp["w_in"], f32)
    w_inx = np.zeros((DEPTH, D, NPAD), f32)
    w_inx[:, :, :NIN] = w_in
    w_inx[:, :, NIN:NIN + 32] = w_in[:, :, 800:832]
    w_inx[:, :, NIN + 32:NIN + 64] = w_in[:, :, 768:800]
    w_uq = np.asarray(inp["w_uq"], f32).reshape(DEPTH, 512, 8, 192)
    wq = np.concatenate([w_uq[..., :128], w_uq[..., 128:192], w_uq[..., 160:192], w_uq[..., 128:160]], axis=-1)
    wq = np.ascontiguousarray(wq.reshape(DEPTH, 512, 2048))
    spk = np.zeros((DEPTH, 128, 192), f32)
    for l in range(DEPTH):
        spk[l, :, 0:16] = _fm(inp["norm_ffn1"][l], 16)
        spk[l, :, 16:32] = _fm(inp["norm_mix"][l], 16)
        spk[l, :, 32:48] = _fm(inp["norm_ffn2"][l], 16)
        spk[l, :, 48:52] = _fm(inp["q_norm"][l], 4)
        spk[l, :, 52:54] = _fm(inp["kv_norm"][l], 2)
        spk[l, :, 54:62] = _fm(inp["attn_out_norm"][l], 8)
        rp = spk[l, :, 64:160]
        mu = np.asarray(inp["shift_mu"][l], f32)
        rp[0:64, 0:8] = _hd(mu[0:512])
        rp[0:64, 8:16] = _hd(mu[512:1024])
        rp[0:64, 16:24] = _hd(mu[1024:1536])
        rp[0:64, 24:32] = _hd(inp["decay_w0"][l])
        rp[0:64, 32:40] = _hd(inp["iclr_a0"][l])
        rp[0:64, 40:48] = _hd(inp["k_k"][l])
        rp[0:64, 48:56] = _hd(inp["k_a"][l])
        rp[0:64, 56:64] = _hd(np.asarray(inp["r_k"][l], f32).reshape(512))
        rp[0:64, 64:72] = _hd(inp["lnx_gain"][l])
        rp[0:64, 72:80] = _hd(inp["lnx_bias"][l])
        rp[0:32, 80] = mu[1536:1568]
        rp[0:32, 81] = mu[1568:1600]
        rp[0:96, 82] = mu[1600:1696]
        cw = np.asarray(inp["conv_w"][l], f32)
        cp = np.zeros((128, 4, 4), f32)
        for k in range(3):
            cp[:, :, k] = cw[k].reshape(4, 128).T
        cp[:, :, 3] = np.asarray(inp["conv_out_norm"][l], f32).reshape(4, 128).T
        spk[l, :, 160:176] = cp.reshape(128, 16)
    consts = np.zeros((128, 1024), f32)
    consts[:, C_ONES:C_ONES + 128] = 1.0
    consts[:, C_EPS] = 1e-6
    consts[:, C_EPS + 1] = 64e-5
    consts[0:64, C_BD64:C_BD64 + 64] = 1.0
    consts[64:128, C_BD64 + 64:C_BD64 + 128] = 1.0
    consts[:, C_ID:C_ID + 128] = np.eye(128, dtype=f32)
    i = np.arange(64)
    consts[0:64, C_SU:C_SU + 64] = (i[:, None] < i[None, :])
    consts[0:64, C_SL:C_SL + 64] = (i[:, None] > i[None, :])
    consts[0:64, C_IU:C_IU + 64] = (i[:, None] <= i[None, :])
    rm = np.ones(128, f32)
    rm[0::64] = 0.0
    consts[:, C_RM:C_RM + 128] = rm[None, :]
    kl = np.arange(128)[:, None]
    ql = np.arange(512)[None, :]
    maskd = np.zeros((128, 4, 512), f32)
    for j in range(4):
        maskd[:, j, :] = np.where(128 * j + kl > ql, -30000.0, 0.0)
    invf = (1.0 / (np.float32(10000.0) ** (np.arange(0, 64, 2, dtype=f32) / np.float32(64)))).astype(f32).reshape(32, 1)
    shared = {n: np.ascontiguousarray(np.asarray(inp[n], f32)) for n in BIGW}
    shared.update({"w_inx": w_inx, "wq": wq, "spk": spk, "gfin": _fm(inp["norm_final"], 16), "consts": consts,
                   "maskd": maskd, "invf": invf})
    return shared


_PROG = {}


def kernel(**inp):
    shared = host_layout(inp)
    x = np.asarray(inp["x"], np.float32)
    pos = np.asarray(inp["positions"], np.int32)
    if "nc" not in _PROG:
        _PROG["nc"] = build_program()
    nc = _PROG["nc"]
    in_maps = []
    for c in range(8):
        m = dict(shared)
        m["x"] = np.ascontiguousarray(x[c * NSEQ:(c + 1) * NSEQ].reshape(NT, D))
        m["pos"] = np.ascontiguousarray(pos[c * NSEQ:(c + 1) * NSEQ])
        in_maps.append(m)
    res = run_bass_kernel_spmd(nc, in_maps, core_ids=list(range(8)))
    out = np.stack([np.asarray(r["out"]).reshape(NSEQ, S, D) for r in res.results], axis=0)
    return out.reshape(16, S, D).astype(np.float32)
```

```python
import contextlib
import numpy as np
import concourse.bass as bass
import concourse.mybir as mybir
from concourse.bass_utils import run_bass_kernel_spmd

F32 = mybir.dt.float32
BF16 = mybir.dt.bfloat16
I32 = mybir.dt.int32
AF = mybir.ActivationFunctionType
ALU = mybir.AluOpType
AX = mybir.AxisListType

D = 2048
DFF = 5632
S = 2048
NSEQ = 2
NT = NSEQ * S
DEPTH = 4
KC = D // 128
FC = DFF // 128
EPS = 1e-6
NIN = 4064
NINX = 4128


class Sem:
    def __init__(self, h):
        self.h = h
        self.n = 0


class Buf:
    def __init__(self, cx, t, dma=False):
        self.t = t
        self.wr = None
        self.rd = []
        self.sem = cx.new_sem() if dma else None

    def __getitem__(self, k):
        return self.t[k]


class Ctx:
    ENG = ('pe', 'act', 'dve', 'pool', 'sp')

    def __init__(self, nc, sem_handles):
        self.nc = nc
        self.sems = [Sem(h) for h in sem_handles]
        self.free = list(self.sems)
        self.esem = {e: self.new_sem() for e in ('pe', 'act', 'dve', 'pool')}
        self.q = {e: [] for e in self.ENG}
        self.seen = {e: {} for e in self.ENG}
        self.stage_sems = []
        self.pending_dma = {e: [] for e in self.ENG}

    def new_sem(self):
        return self.free.pop()

    def begin_stage(self):
        self.mark = len(self.free)
        self.taken = []

    def buf(self, t, dma=False):
        b = Buf.__new__(Buf)
        b.t = t
        b.wr = None
        b.rd = []
        b.sem = None
        if dma:
            b.sem = self.free.pop()
            self.taken.append(b.sem)
        return b

    def end_stage(self, block):
        for e in self.ENG:
            for tok in self.pending_dma[e]:
                self._wait(e, tok)
            self.pending_dma[e] = []
        self.flush(block)
        self.free.extend(self.taken)
        self.taken = []

    def _wait(self, eng, tok):
        if tok is None:
            return
        sem, val, src = tok
        if src == eng and src in ('pe', 'act', 'dve'):
            return
        if self.seen[eng].get(id(sem), 0) >= val:
            return
        self.seen[eng][id(sem)] = val
        self.q[eng].append(('wait', sem, val))

    def _deps(self, eng, reads, writes):
        best = {}
        toks = [b.wr for b in reads] + [b.wr for b in writes]
        for b in writes:
            toks.extend(b.rd)
        for t in toks:
            if t is None:
                continue
            k = id(t[0])
            if k not in best or best[k][1] < t[1]:
                best[k] = t
        for t in best.values():
            self._wait(eng, t)

    def op(self, eng, fn, reads=(), writes=(), sig=True):
        self._deps(eng, reads, writes)
        tok = None
        if sig:
            s = self.esem[eng]
            s.n += 1
            tok = (s, s.n, eng)
            self.q[eng].append(('op', fn, s, 1))
        else:
            self.q[eng].append(('op', fn, None, 0))
        for b in writes:
            b.wr = tok
            b.rd = []
        for b in reads:
            if tok is not None:
                b.rd = [t for t in b.rd if t[0] is not tok[0]] + [tok]
        return tok

    def dma(self, eng, fn, reads=(), writes=(), sembuf=None):
        s = sembuf.sem
        saved = []
        for b in writes:
            if b.wr is not None and b.wr[2] == 'dma' and b.wr[0] is s and not b.rd:
                saved.append((b, b.wr))
                b.wr = None
        self._deps(eng, reads, writes)
        for b, w in saved:
            b.wr = w
        s.n += 16
        tok = (s, s.n, 'dma')
        self.q[eng].append(('op', fn, s, 16))
        for b in writes:
            b.wr = tok
            b.rd = []
        for b in reads:
            b.rd = [t for t in b.rd if t[0] is not tok[0]] + [tok]
        if not writes:
            self.pending_dma[eng].append(tok)
        return tok

    def flush(self, block):
        m = {'pe': block.tensor, 'act': block.scalar, 'dve': block.vector,
             'pool': block.gpsimd, 'sp': block.sync}
        for e in self.ENG:
            lst = self.q[e]
            if not lst:
                continue

            def body(eng, lst=lst):
                for it in lst:
                    if it[0] == 'wait':
                        eng.wait_ge(it[1].h, it[2])
                    else:
                        ins = it[1](eng)
                        if it[2] is not None:
                            ins.then_inc(it[2].h, it[3])
            m[e](body)
            self.q[e] = []


class Ring:
    def __init__(self, bufs):
        self.bufs = bufs
        self.i = 0

    def next(self):
        b = self.bufs[self.i % len(self.bufs)]
        self.i += 1
        return b


def mm_group(cx, out_buf, pairs, reads, out_ap=None):
    nc = cx.nc
    oap = out_buf.t[:] if out_ap is None else out_ap
    n = len(pairs)

    def fn(eng):
        ins = None
        for i, (l, r) in enumerate(pairs):
            ins = nc.tensor.matmul(oap, l, r, start=(i == 0), stop=(i == n - 1))
        return ins
    return cx.op('pe', fn, reads=reads, writes=[out_buf])


_UID = [0]


def alloc(es, nc, name, shape, dt, psum=False):
    _UID[0] += 1
    name = f"{name}_{_UID[0]}"
    if psum:
        return es.enter_context(nc.psum_tensor(name, shape, dt))
    return es.enter_context(nc.sbuf_tensor(name, shape, dt))


def stage_in(cx, x, hT, ident_dram):
    nc = cx.nc
    with contextlib.ExitStack() as es:
        cx.begin_stage()
        ident = cx.buf(alloc(es, nc, "ident", [128, 128], F32), dma=True)
        xin = Ring([cx.buf(alloc(es, nc, f"xin{i}", [128, D], F32), dma=True) for i in range(3)])
        xo = Ring([cx.buf(alloc(es, nc, f"xo{i}", [128, KC, 128], F32), dma=True) for i in range(3)])
        ps = Ring([cx.buf(alloc(es, nc, f"ps{i}", [128, 512], F32, psum=True)) for i in range(4)])
        blk = es.enter_context(nc.Block())
        cx.dma('sp', lambda e: e.dma_start(out=ident.t[:], in_=ident_dram[:, :]), writes=[ident], sembuf=ident)
        for tt in range(NT // 128):
            xb = xin.next()
            cx.dma('sp', lambda e, xb=xb, tt=tt: e.dma_start(out=xb.t[:], in_=x[tt * 128:(tt + 1) * 128, :]),
                   writes=[xb], sembuf=xb)
            ob = xo.next()
            for g in range(KC // 4):
                p = ps.next()

                def fn(eng, p=p, xb=xb, g=g):
                    ins = None
                    for j in range(4):
                        kc = g * 4 + j
                        ins = nc.tensor.transpose(p.t[:, j * 128:(j + 1) * 128], xb.t[:, kc * 128:(kc + 1) * 128], ident.t[:])
                    return ins
                cx.op('pe', fn, reads=[xb, ident], writes=[p])
                eng = 'dve' if g % 2 == 0 else 'act'
                if eng == 'dve':
                    cx.op('dve', lambda e, p=p, ob=ob, g=g: nc.vector.tensor_copy(
                        ob.t[:, g * 4:(g + 1) * 4, :], p.t[:].rearrange("p (a b) -> p a b", a=4)), reads=[p], writes=[ob])
                else:
                    cx.op('act', lambda e, p=p, ob=ob, g=g: nc.scalar.copy(
                        ob.t[:, g * 4:(g + 1) * 4, :], p.t[:].rearrange("p (a b) -> p a b", a=4)), reads=[p], writes=[ob])
            cx.dma('sp', lambda e, ob=ob, tt=tt: e.dma_start(
                out=hT[:, :, tt * 128:(tt + 1) * 128].rearrange("k p t -> p k t"), in_=ob.t[:]),
                reads=[ob], sembuf=ob)
        cx.end_stage(blk)


def rms_stats_gen(cx, nc, chunks_fn, nchunks, T, hring, sqring, ps_ss, ones_f, nfeat, eps_t, sd, rstd, src_rows=128, ldq='sp'):
    nsub = T // 512
    for kc in range(nchunks):
        hb = hring.next()
        cx.dma(ldq, chunks_fn(kc, hb), writes=[hb], sembuf=hb)
        sq = sqring.next()
        cx.op('act', lambda e, hb=hb, sq=sq: nc.scalar.activation(out=sq.t[:src_rows, :T], in_=hb.t[:src_rows, :T], func=AF.Square),
              reads=[hb], writes=[sq])
        for sub in range(nsub):
            p = ps_ss[sub]

            def fn(eng, p=p, sq=sq, sub=sub, kc=kc):
                return nc.tensor.matmul(p.t[:], ones_f.t[:src_rows, :], sq.t[:src_rows, sub * 512:(sub + 1) * 512],
                                        start=(kc == 0), stop=(kc == nchunks - 1))
            cx.op('pe', fn, reads=[sq, ones_f], writes=[p] if kc == 0 else [], sig=True)
            if kc != 0:
                pass
        if kc == nchunks - 1:
            last_tok = (cx.esem['pe'], cx.esem['pe'].n, 'pe')
            for sub in range(nsub):
                ps_ss[sub].wr = last_tok
        yield
    for sub in range(nsub):
        p = ps_ss[sub]
        cx.op('act', lambda e, p=p, sub=sub: nc.scalar.activation(
            out=sd.t[:, sub * 512:(sub + 1) * 512], in_=p.t[:], func=AF.Sqrt, bias=eps_t.t[:, 0:1], scale=1.0 / nfeat),
            reads=[p, eps_t], writes=[sd] if sub == 0 else [])
    sd.wr = (cx.esem['act'], cx.esem['act'].n, 'act')
    cx.op('dve', lambda e: nc.vector.reciprocal(rstd.t[:, :T], sd.t[:, :T]), reads=[sd], writes=[rstd])


def rms_stats(*a, **k):
    for _ in rms_stats_gen(*a, **k):
        pass


def stage_ffn(cx, hT, wg, wu, wd, gvec, consts, T=1024, tiles=None):
    nc = cx.nc
    nsub = T // 512
    NJ = 256
    with contextlib.ExitStack() as es:
        cx.begin_stage()
        ones_f = cx.buf(alloc(es, nc, "ones_f", [128, 128], F32), dma=True)
        eps_t = cx.buf(alloc(es, nc, "eps_t", [128, 1], F32), dma=True)
        g_t = cx.buf(alloc(es, nc, "g_t", [128, KC], F32), dma=True)
        xn = cx.buf(alloc(es, nc, "xn", [128, KC, T], BF16))
        act = [cx.buf(alloc(es, nc, f"act{j}", [128, T], BF16)) for j in range(FC)]
        wring = Ring([cx.buf(alloc(es, nc, f"w{i}", [128, 16, NJ], BF16), dma=True) for i in range(6)])
        hring = Ring([cx.buf(alloc(es, nc, f"hb{i}", [128, T], F32), dma=True) for i in range(2)])
        sqring = Ring([cx.buf(alloc(es, nc, f"sq{i}", [128, T], F32)) for i in range(2)])
        rstd = cx.buf(alloc(es, nc, "rstd", [128, T], F32))
        sd = rstd
        sgr = Ring([cx.buf(alloc(es, nc, f"sg{i}", [128, 512], F32)) for i in range(2)])
        outr = Ring([cx.buf(alloc(es, nc, f"ob{i}", [128, 512], F32), dma=True) for i in range(2)])
        hres = Ring([cx.buf(alloc(es, nc, f"hr{i}", [128, 512], F32), dma=True) for i in range(2)])
        ps_g = Ring([cx.buf(alloc(es, nc, f"psg{i}", [128, 512], F32, psum=True)) for i in range(2)])
        ps_u = Ring([cx.buf(alloc(es, nc, f"psu{i}", [128, 512], F32, psum=True)) for i in range(2)])
        ps_d = Ring([cx.buf(alloc(es, nc, f"psd{i}", [128, 512], F32, psum=True)) for i in range(2)])
        ps_ss = [cx.buf(alloc(es, nc, f"pss{i}", [128, 512], F32, psum=True)) for i in range(nsub)]
        blk = es.enter_context(nc.Block())

        cx.dma('sp', lambda e: e.dma_start(out=ones_f.t[:], in_=consts[:, 0:128]), writes=[ones_f], sembuf=ones_f)
        cx.dma('sp', lambda e: e.dma_start(out=eps_t.t[:], in_=consts[:, 128:129], allow_slow_non_contiguous=True), writes=[eps_t], sembuf=eps_t)
        cx.dma('sp', lambda e: e.dma_start(out=g_t.t[:], in_=gvec[:, :]), writes=[g_t], sembuf=g_t)
        wgv = wg.rearrange("(kc p) n -> p kc n", p=128)
        wuv = wu.rearrange("(kc p) n -> p kc n", p=128)
        wdv = wd.rearrange("(j p) n -> p j n", p=128)
        tl = list(range(NT // T)) if tiles is None else tiles
        def norm_gen(t0):
            yield from rms_stats_gen(cx, nc, lambda kc, hb: (lambda e: e.dma_start(out=hb.t[:, :T], in_=hT[kc, :, t0:t0 + T])),
                                     KC, T, hring, sqring, ps_ss, ones_f, D, eps_t, sd, rstd, ldq='act')
            for kc in range(KC):
                hb = hring.next()
                cx.dma('act', lambda e, hb=hb, kc=kc: e.dma_start(out=hb.t[:, :T], in_=hT[kc, :, t0:t0 + T]),
                       writes=[hb], sembuf=hb)
                cx.op('dve', lambda e, hb=hb, kc=kc: nc.vector.scalar_tensor_tensor(
                    out=xn.t[:, kc, :], in0=hb.t[:, :T], scalar=g_t.t[:, kc:kc + 1], in1=rstd.t[:, :T],
                    op0=ALU.mult, op1=ALU.mult), reads=[hb, g_t, rstd], writes=[xn] if kc == 0 else [])
                xn.wr = (cx.esem['dve'], cx.esem['dve'].n, 'dve')
                yield

        def gateup(t0):
            for jt in range(DFF // NJ):
                wgb = wring.next()
                cx.dma('pool', lambda e, b=wgb, jt=jt: e.dma_start(out=b.t[:], in_=wgv[:, :, jt * NJ:(jt + 1) * NJ]),
                       writes=[wgb], sembuf=wgb)
                wub = wring.next()
                cx.dma('pool', lambda e, b=wub, jt=jt: e.dma_start(out=b.t[:], in_=wuv[:, :, jt * NJ:(jt + 1) * NJ]),
                       writes=[wub], sembuf=wub)
                for jj in range(NJ // 128):
                    j = jt * (NJ // 128) + jj
                    for sub in range(nsub):
                        pg = ps_g.next()
                        pu = ps_u.next()
                        mm_group(cx, pg, [(wgb.t[:, kc, jj * 128:(jj + 1) * 128], xn.t[:, kc, sub * 512:(sub + 1) * 512])
                                          for kc in range(KC)], reads=[wgb, xn])
                        mm_group(cx, pu, [(wub.t[:, kc, jj * 128:(jj + 1) * 128], xn.t[:, kc, sub * 512:(sub + 1) * 512])
                                          for kc in range(KC)], reads=[wub, xn])
                        sg = sgr.next()
                        cx.op('act', lambda e, pg=pg, sg=sg: nc.scalar.activation(out=sg.t[:], in_=pg.t[:], func=AF.Silu),
                              reads=[pg], writes=[sg])
                        cx.op('dve', lambda e, sg=sg, pu=pu, j=j, sub=sub: nc.vector.tensor_tensor(
                            out=act[j].t[:, sub * 512:(sub + 1) * 512], in0=sg.t[:], in1=pu.t[:], op=ALU.mult),
                            reads=[sg, pu], writes=[act[j]])
        def down(t0, nxt):
            JD = 16
            for ct in range(D // NJ):
                wds = []
                for q4 in range((FC + JD - 1) // JD):
                    wb = wring.next()
                    nj_ = min(JD, FC - q4 * JD)
                    cx.dma('pool', lambda e, b=wb, q4=q4, ct=ct, nj_=nj_: e.dma_start(
                        out=b.t[:, 0:nj_, :], in_=wdv[:, q4 * JD:q4 * JD + nj_, ct * NJ:(ct + 1) * NJ]),
                        writes=[wb], sembuf=wb)
                    wds.append(wb)
                for cc in range(NJ // 128):
                    c = ct * (NJ // 128) + cc
                    for sub in range(nsub):
                        hr = hres.next()
                        cx.dma('act', lambda e, hr=hr, c=c, sub=sub: e.dma_start(
                            out=hr.t[:], in_=hT[c, :, t0 + sub * 512:t0 + (sub + 1) * 512]), writes=[hr], sembuf=hr)
                        pd = ps_d.next()
                        mm_group(cx, pd, [(wds[j // JD].t[:, j % JD, cc * 128:(cc + 1) * 128], act[j].t[:, sub * 512:(sub + 1) * 512])
                                          for j in range(FC)], reads=wds + act)
                        ob = outr.next()
                        cx.op('dve', lambda e, pd=pd, hr=hr, ob=ob: nc.vector.scalar_tensor_tensor(
                            out=ob.t[:], in0=pd.t[:], scalar=0.5, in1=hr.t[:], op0=ALU.mult, op1=ALU.add),
                            reads=[pd, hr], writes=[ob])
                        cx.dma('sp', lambda e, ob=ob, c=c, sub=sub: e.dma_start(
                            out=hT[c, :, t0 + sub * 512:t0 + (sub + 1) * 512], in_=ob.t[:]), reads=[ob], sembuf=ob)
                        if nxt is not None:
                            next(nxt, None)
                            next(nxt, None)
            if nxt is not None:
                for _ in nxt:
                    pass
        first = norm_gen(tl[0] * T)
        for _ in first:
            pass
        for idx, ti in enumerate(tl):
            gateup(ti * T)
            nxt = norm_gen(tl[idx + 1] * T) if idx + 1 < len(tl) else None
            down(ti * T, nxt)
        cx.end_stage(blk)


def stage_out(cx, hT, gvec, consts, ident_dram, out):
    nc = cx.nc
    T = 512
    with contextlib.ExitStack() as es:
        cx.begin_stage()
        ones_f = cx.buf(alloc(es, nc, "ones_f", [128, 128], F32), dma=True)
        ident = cx.buf(alloc(es, nc, "ident", [128, 128], F32), dma=True)
        eps_t = cx.buf(alloc(es, nc, "eps_t", [128, 1], F32), dma=True)
        g_t = cx.buf(alloc(es, nc, "g_t", [128, KC], F32), dma=True)
        hring = Ring([cx.buf(alloc(es, nc, f"hb{i}", [128, T], F32), dma=True) for i in range(3)])
        sqring = Ring([cx.buf(alloc(es, nc, f"sq{i}", [128, T], F32)) for i in range(2)])
        sd = cx.buf(alloc(es, nc, "sd", [128, T], F32))
        rstd = cx.buf(alloc(es, nc, "rstd", [128, T], F32))
        xn = cx.buf(alloc(es, nc, "xnf", [128, KC, T], F32))
        ps_ss = [cx.buf(alloc(es, nc, "pss0", [128, 512], F32, psum=True))]
        ps = Ring([cx.buf(alloc(es, nc, f"ps{i}", [128, 512], F32, psum=True)) for i in range(4)])
        orow = Ring([cx.buf(alloc(es, nc, f"orow{i}", [128, D], F32), dma=True) for i in range(3)])
        blk = es.enter_context(nc.Block())
        cx.dma('sp', lambda e: e.dma_start(out=ones_f.t[:], in_=consts[:, 0:128]), writes=[ones_f], sembuf=ones_f)
        cx.dma('sp', lambda e: e.dma_start(out=eps_t.t[:], in_=consts[:, 128:129], allow_slow_non_contiguous=True), writes=[eps_t], sembuf=eps_t)
        cx.dma('sp', lambda e: e.dma_start(out=g_t.t[:], in_=gvec[:, :]), writes=[g_t], sembuf=g_t)
        cx.dma('sp', lambda e: e.dma_start(out=ident.t[:], in_=ident_dram[:, :]), writes=[ident], sembuf=ident)
        def tile_body(t0):
            rms_stats(cx, nc, lambda kc, hb: (lambda e: e.dma_start(out=hb.t[:, :T], in_=hT[kc, :, t0:t0 + T])),
                      KC, T, hring, sqring, ps_ss, ones_f, D, eps_t, sd, rstd)
            for kc in range(KC):
                hb = hring.next()
                cx.dma('sp', lambda e, hb=hb, kc=kc: e.dma_start(out=hb.t[:, :T], in_=hT[kc, :, t0:t0 + T]),
                       writes=[hb], sembuf=hb)
                cx.op('dve', lambda e, hb=hb, kc=kc: nc.vector.scalar_tensor_tensor(
                    out=xn.t[:, kc, :], in0=hb.t[:, :T], scalar=g_t.t[:, kc:kc + 1], in1=rstd.t[:, :T],
                    op0=ALU.mult, op1=ALU.mult), reads=[hb, g_t, rstd], writes=[xn] if kc == 0 else [])
            xn.wr = (cx.esem['dve'], cx.esem['dve'].n, 'dve')
            for tb in range(T // 128):
                ob = orow.next()
                for g in range(KC // 4):
                    p = ps.next()

                    def fn(eng, p=p, g=g, tb=tb):
                        ins = None
                        for j in range(4):
                            kc = g * 4 + j
                            ins = nc.tensor.transpose(p.t[:, j * 128:(j + 1) * 128], xn.t[:, kc, tb * 128:(tb + 1) * 128], ident.t[:])
                        return ins
                    cx.op('pe', fn, reads=[xn, ident], writes=[p])
                    if g % 2 == 0:
                        cx.op('dve', lambda e, p=p, ob=ob, g=g: nc.vector.tensor_copy(ob.t[:, g * 512:(g + 1) * 512], p.t[:]),
                              reads=[p], writes=[ob])
                    else:
                        cx.op('act', lambda e, p=p, ob=ob, g=g: nc.scalar.copy(ob.t[:, g * 512:(g + 1) * 512], p.t[:]),
                              reads=[p], writes=[ob])
                cx.dma('sp', lambda e, ob=ob, tb=tb: e.dma_start(out=out[t0 + tb * 128:t0 + (tb + 1) * 128, :], in_=ob.t[:]),
                       reads=[ob], sembuf=ob)
        for ti in range(NT // T):
            tile_body(ti * T)
        cx.end_stage(blk)


def load_consts(cx, es, nc, consts, ident_dram=None):
    ones_f = cx.buf(alloc(es, nc, "ones_f", [128, 128], F32), dma=True)
    eps_t = cx.buf(alloc(es, nc, "eps_t", [128, 1], F32), dma=True)
    cx.dma('sp', lambda e: e.dma_start(out=ones_f.t[:], in_=consts[:, 0:128]), writes=[ones_f], sembuf=ones_f)
    cx.dma('sp', lambda e: e.dma_start(out=eps_t.t[:], in_=consts[:, 128:129], allow_slow_non_contiguous=True),
           writes=[eps_t], sembuf=eps_t)
    return ones_f, eps_t


def stage_proj(cx, hT, w, gvec, consts, pT, ncols, T=1024):
    nc = cx.nc
    nsub = T // 512
    NJ = 384
    with contextlib.ExitStack() as es:
        cx.begin_stage()
        ones_f, eps_t = load_consts(cx, es, nc, consts)
        g_t = cx.buf(alloc(es, nc, "g_t", [128, KC], F32), dma=True)
        xns = [cx.buf(alloc(es, nc, f"xn{i}", [128, KC, T], BF16)) for i in range(2)]
        wring = Ring([cx.buf(alloc(es, nc, f"w{i}", [128, 16, NJ], BF16), dma=True) for i in range(3)])
        hring = Ring([cx.buf(alloc(es, nc, f"hb{i}", [128, T], F32), dma=True) for i in range(2)])
        sqring = Ring([cx.buf(alloc(es, nc, f"sq{i}", [128, T], F32)) for i in range(2)])
        rstd = cx.buf(alloc(es, nc, "rstd", [128, T], F32))
        outr = Ring([cx.buf(alloc(es, nc, f"ob{i}", [128, 512], F32), dma=True) for i in range(4)])
        ps_o = Ring([cx.buf(alloc(es, nc, f"pso{i}", [128, 512], F32, psum=True)) for i in range(4)])
        ps_ss = [cx.buf(alloc(es, nc, f"pss{i}", [128, 512], F32, psum=True)) for i in range(nsub)]
        blk = es.enter_context(nc.Block())
        cx.dma('sp', lambda e: e.dma_start(out=g_t.t[:], in_=gvec[:, :]), writes=[g_t], sembuf=g_t)
        wv = w.rearrange("(kc p) n -> p kc n", p=128)

        def norm_gen(t0, xn):
            yield from rms_stats_gen(cx, nc, lambda kc, hb: (lambda e: e.dma_start(out=hb.t[:, :T], in_=hT[kc, :, t0:t0 + T])),
                                     KC, T, hring, sqring, ps_ss, ones_f, D, eps_t, rstd, rstd, ldq='act')
            for kc in range(KC):
                hb = hring.next()
                cx.dma('act', lambda e, hb=hb, kc=kc: e.dma_start(out=hb.t[:, :T], in_=hT[kc, :, t0:t0 + T]),
                       writes=[hb], sembuf=hb)
                cx.op('dve', lambda e, hb=hb, kc=kc: nc.vector.scalar_tensor_tensor(
                    out=xn.t[:, kc, :], in0=hb.t[:, :T], scalar=g_t.t[:, kc:kc + 1], in1=rstd.t[:, :T],
                    op0=ALU.mult, op1=ALU.mult), reads=[hb, g_t, rstd], writes=[xn] if kc == 0 else [])
                xn.wr = (cx.esem['dve'], cx.esem['dve'].n, 'dve')
                yield

        def main(t0, xn, nxt):
            gcount = 0
            for jt in range(ncols // NJ):
                wb = wring.next()
                cx.dma('pool', lambda e, b=wb, jt=jt: e.dma_start(out=b.t[:], in_=wv[:, :, jt * NJ:(jt + 1) * NJ]),
                       writes=[wb], sembuf=wb)
                for jj in range(NJ // 128):
                    r0 = jt * NJ + jj * 128
                    for sub in range(nsub):
                        po = ps_o.next()
                        mm_group(cx, po, [(wb.t[:, kc, jj * 128:(jj + 1) * 128], xn.t[:, kc, sub * 512:(sub + 1) * 512])
                                          for kc in range(KC)], reads=[wb, xn])
                        ob = outr.next()
                        if (jj + sub) % 2 == 0:
                            cx.op('act', lambda e, po=po, ob=ob: nc.scalar.copy(ob.t[:], po.t[:]), reads=[po], writes=[ob])
                        else:
                            cx.op('dve', lambda e, po=po, ob=ob: nc.vector.tensor_copy(ob.t[:], po.t[:]), reads=[po], writes=[ob])
                        cx.dma('sp', lambda e, ob=ob, r0=r0, sub=sub: e.dma_start(
                            out=pT[r0:r0 + 128, t0 + sub * 512:t0 + (sub + 1) * 512], in_=ob.t[:]), reads=[ob], sembuf=ob)
                        gcount += 1
                        if nxt is not None and gcount % 2 == 0:
                            next(nxt, None)
            if nxt is not None:
                for _ in nxt:
                    pass
        ntl = NT // T
        for _ in norm_gen(0, xns[0]):
            pass
        for ti in range(ntl):
            nxt = norm_gen((ti + 1) * T, xns[(ti + 1) % 2]) if ti + 1 < ntl else None
            main(ti * T, xns[ti % 2], nxt)
        cx.end_stage(blk)


def stage_wout(cx, hT, yT, w, T=1024):
    nc = cx.nc
    nsub = T // 512
    NJ = 256
    with contextlib.ExitStack() as es:
        cx.begin_stage()
        yb = cx.buf(alloc(es, nc, "yb", [128, KC, T], BF16), dma=True)
        wring = Ring([cx.buf(alloc(es, nc, f"w{i}", [128, 16, NJ], BF16), dma=True) for i in range(3)])
        outr = Ring([cx.buf(alloc(es, nc, f"ob{i}", [128, 512], F32), dma=True) for i in range(3)])
        hres = Ring([cx.buf(alloc(es, nc, f"hr{i}", [128, 512], F32), dma=True) for i in range(3)])
        ps_o = Ring([cx.buf(alloc(es, nc, f"pso{i}", [128, 512], F32, psum=True)) for i in range(4)])
        blk = es.enter_context(nc.Block())
        wv = w.rearrange("(kc p) n -> p kc n", p=128)
        yv = yT.rearrange("(kc p) t -> p kc t", p=128)

        def tile_body(t0):
            cx.dma('sp', lambda e: e.dma_start(out=yb.t[:], in_=yv[:, :, t0:t0 + T]), writes=[yb], sembuf=yb)
            for jt in range(D // NJ):
                wb = wring.next()
                cx.dma('pool', lambda e, b=wb, jt=jt: e.dma_start(out=b.t[:], in_=wv[:, :, jt * NJ:(jt + 1) * NJ]),
                       writes=[wb], sembuf=wb)
                for jj in range(NJ // 128):
                    c = jt * (NJ // 128) + jj
                    for sub in range(nsub):
                        hr = hres.next()
                        cx.dma('act', lambda e, hr=hr, c=c, sub=sub: e.dma_start(
                            out=hr.t[:], in_=hT[c, :, t0 + sub * 512:t0 + (sub + 1) * 512]), writes=[hr], sembuf=hr)
                        po = ps_o.next()
                        mm_group(cx, po, [(wb.t[:, kc, jj * 128:(jj + 1) * 128], yb.t[:, kc, sub * 512:(sub + 1) * 512])
                                          for kc in range(KC)], reads=[wb, yb])
                        ob = outr.next()
                        cx.op('dve', lambda e, po=po, hr=hr, ob=ob: nc.vector.tensor_tensor(
                            out=ob.t[:], in0=po.t[:], in1=hr.t[:], op=ALU.add), reads=[po, hr], writes=[ob])
                        cx.dma('sp', lambda e, ob=ob, c=c, sub=sub: e.dma_start(
                            out=hT[c, :, t0 + sub * 512:t0 + (sub + 1) * 512], in_=ob.t[:]), reads=[ob], sembuf=ob)
        for ti in range(NT // T):
            tile_body(ti * T)
        cx.end_stage(blk)


C_ONES, C_EPS, C_BD64, C_ID, C_SU, C_SL, C_IU, C_RM = 0, 128, 256, 384, 512, 576, 640, 704
B0 = 832
C0 = 2528
R_KRS = 4064


def stage_conv(cx, pT, prm, consts, yT):
    nc = cx.nc
    with contextlib.ExitStack() as es:
        cx.begin_stage()
        bd = cx.buf(alloc(es, nc, "bd", [128, 128], F32), dma=True)
        eps_t = cx.buf(alloc(es, nc, "eps_t", [128, 1], F32), dma=True)
        pr = cx.buf(alloc(es, nc, "pr", [128, 4, 4], F32), dma=True)
        bg = Ring([cx.buf(alloc(es, nc, f"bg{i}", [128, S], F32), dma=True) for i in range(2)])
        cg = Ring([cx.buf(alloc(es, nc, f"cg{i}", [128, S], F32), dma=True) for i in range(2)])
        hh = Ring([cx.buf(alloc(es, nc, f"hh{i}", [128, S], F32), dma=True) for i in range(2)])
        u = cx.buf(alloc(es, nc, "u", [128, S + 2], F32))
        y = cx.buf(alloc(es, nc, "y", [128, S], F32))
        z = cx.buf(alloc(es, nc, "z", [128, S], F32))
        zsq = cx.buf(alloc(es, nc, "zsq", [128, S], F32))
        rs = cx.buf(alloc(es, nc, "rs", [128, S], F32))
        ob = Ring([cx.buf(alloc(es, nc, f"ob{i}", [128, S], BF16), dma=True) for i in range(2)])
        ps = Ring([cx.buf(alloc(es, nc, f"ps{i}", [128, 512], F32, psum=True)) for i in range(4)])
        blk = es.enter_context(nc.Block())
        cx.dma('sp', lambda e: e.dma_start(out=bd.t[:], in_=consts[:, C_BD64:C_BD64 + 128]), writes=[bd], sembuf=bd)
        cx.dma('sp', lambda e: e.dma_start(out=eps_t.t[:], in_=consts[:, C_EPS:C_EPS + 1], allow_slow_non_contiguous=True),
               writes=[eps_t], sembuf=eps_t)
        cx.dma('sp', lambda e: e.dma_start(out=pr.t[:], in_=prm[:, :, :]), writes=[pr], sembuf=pr)
        cx.op('dve', lambda e: nc.vector.memset(u.t[:, 0:2], 0.0), writes=[u])

        def body(b, ch):
            t0 = b * S
            bgb, cgb, hhb = bg.next(), cg.next(), hh.next()
            for (buf, r0) in ((bgb, C0 + ch * 128), (cgb, C0 + 512 + ch * 128), (hhb, C0 + 1024 + ch * 128)):
                cx.dma('sp', lambda e, buf=buf, r0=r0: e.dma_start(out=buf.t[:], in_=pT[r0:r0 + 128, t0:t0 + S]),
                       writes=[buf], sembuf=buf)
            cx.op('dve', lambda e: nc.vector.tensor_tensor(out=u.t[:, 2:S + 2], in0=cgb.t[:], in1=hhb.t[:], op=ALU.mult),
                  reads=[cgb, hhb], writes=[u])
            cx.op('act', lambda e: nc.scalar.activation(out=y.t[:], in_=u.t[:, 2:S + 2], func=AF.Copy, scale=pr.t[:, ch, 2:3]),
                  reads=[u, pr], writes=[y])
            cx.op('dve', lambda e: nc.vector.scalar_tensor_tensor(out=y.t[:], in0=u.t[:, 1:S + 1], scalar=pr.t[:, ch, 1:2],
                                                                  in1=y.t[:], op0=ALU.mult, op1=ALU.add), reads=[u, pr], writes=[y])
            cx.op('dve', lambda e: nc.vector.scalar_tensor_tensor(out=y.t[:], in0=u.t[:, 0:S], scalar=pr.t[:, ch, 0:1],
                                                                  in1=y.t[:], op0=ALU.mult, op1=ALU.add), reads=[u, pr], writes=[y])
            cx.op('dve', lambda e: nc.vector.tensor_tensor(out=z.t[:], in0=bgb.t[:], in1=y.t[:], op=ALU.mult),
                  reads=[bgb, y], writes=[z])
            cx.op('act', lambda e: nc.scalar.activation(out=zsq.t[:], in_=z.t[:], func=AF.Square), reads=[z], writes=[zsq])
            for sub in range(S // 512):
                p = ps.next()
                sl = slice(sub * 512, (sub + 1) * 512)
                cx.op('pe', lambda e, p=p, sl=sl: nc.tensor.matmul(p.t[:], bd.t[:], zsq.t[:, sl], start=True, stop=True),
                      reads=[bd, zsq], writes=[p])
                cx.op('act', lambda e, p=p, sl=sl: nc.scalar.activation(out=rs.t[:, sl], in_=p.t[:], func=AF.Sqrt,
                                                                        bias=eps_t.t[:, 0:1], scale=1.0 / 64),
                      reads=[p, eps_t], writes=[rs])
            cx.op('dve', lambda e: nc.vector.reciprocal(rs.t[:], rs.t[:]), reads=[rs], writes=[rs])
            o = ob.next()
            cx.op('dve', lambda e, o=o: nc.vector.scalar_tensor_tensor(out=o.t[:], in0=z.t[:], scalar=pr.t[:, ch, 3:4],
                                                                       in1=rs.t[:], op0=ALU.mult, op1=ALU.mult),
                  reads=[z, pr, rs], writes=[o])
            cx.dma('sp', lambda e, o=o: e.dma_start(out=yT[1536 + ch * 128:1536 + (ch + 1) * 128, t0:t0 + S], in_=o.t[:]),
                   reads=[o], sembuf=o)
        for b in range(NSEQ):
            for ch in range(4):
                body(b, ch)
        cx.end_stage(blk)


def stage_mla(cx, pT, wq, wkv, prm, cs, consts, maskd, yT):
    nc = cx.nc
    T = 1024
    scale = float((128 + 64) ** -0.5)
    with contextlib.ExitStack() as es:
        cx.begin_stage()
        ones_f, eps_t = load_consts(cx, es, nc, consts)
        ones_b = cx.buf(alloc(es, nc, "ones_b", [128, 128], BF16), dma=True)
        id_b = cx.buf(alloc(es, nc, "id_b", [128, 128], BF16), dma=True)
        mask = cx.buf(alloc(es, nc, "mask", [128, 4, 512], BF16), dma=True)
        pr = cx.buf(alloc(es, nc, "pr", [128, 16], F32), dma=True)
        wqb = cx.buf(alloc(es, nc, "wqb", [128, 4, 2048], BF16), dma=True)
        wkvb = cx.buf(alloc(es, nc, "wkvb", [128, 2, 2048], BF16), dma=True)
        cqn = cx.buf(alloc(es, nc, "cqn", [128, 4, S], BF16))
        ckvn = cx.buf(alloc(es, nc, "ckvn", [128, 2, S], BF16))
        cc = cx.buf(alloc(es, nc, "cc", [64, S], F32), dma=True)
        ss = cx.buf(alloc(es, nc, "ss", [64, S], F32), dma=True)
        kx = cx.buf(alloc(es, nc, "kx", [64, S], F32), dma=True)
        kxs = cx.buf(alloc(es, nc, "kxs", [64, S], F32), dma=True)
        k_r = cx.buf(alloc(es, nc, "k_r", [64, S], BF16))
        hsets = [dict(q_n=cx.buf(alloc(es, nc, "q_n", [128, S], BF16)), q_r=cx.buf(alloc(es, nc, "q_r", [64, S], BF16)),
                      k_n=cx.buf(alloc(es, nc, "k_n", [128, S], BF16)), V=cx.buf(alloc(es, nc, "V", [128, 16, 128], BF16)))
                 for _ in range(2)]
        xt = cx.buf(alloc(es, nc, "xt", [64, 512], F32))
        rt2 = cx.buf(alloc(es, nc, "rt2", [64, 512], F32))
        PT = Ring([cx.buf(alloc(es, nc, f"PT{i}", [128, 512], BF16)) for i in range(3)])
        hring = Ring([cx.buf(alloc(es, nc, f"hb{i}", [128, T], F32), dma=True) for i in range(2)])
        sqring = Ring([cx.buf(alloc(es, nc, f"sq{i}", [128, T], F32)) for i in range(2)])
        rstd = cx.buf(alloc(es, nc, "rstd", [128, T], F32))
        rinv = cx.buf(alloc(es, nc, "rinv", [128, 512], F32))
        yv = cx.buf(alloc(es, nc, "yv", [128, 512], F32))
        ysq = cx.buf(alloc(es, nc, "ysq", [128, 512], F32))
        sd2 = cx.buf(alloc(es, nc, "sd2", [128, 512], F32))
        obr = Ring([cx.buf(alloc(es, nc, f"ob{i}", [128, 512], BF16), dma=True) for i in range(2)])
        ps_p = Ring([cx.buf(alloc(es, nc, f"psp{i}", [128, 512], F32, psum=True)) for i in range(2)])
        ps_ss = ps_p.bufs
        ps_s = Ring([cx.buf(alloc(es, nc, f"pst{i}", [128, 512], F32, psum=True)) for i in range(2)])
        ps_o_r = Ring([cx.buf(alloc(es, nc, f"pso{i}", [128, 512], F32, psum=True)) for i in range(2)])
        ps_r_r = Ring([cx.buf(alloc(es, nc, f"psr{i}", [128, 512], F32, psum=True)) for i in range(2)])
        blk = es.enter_context(nc.Block())

        cx.dma('pool', lambda e: e.dma_start(out=ones_b.t[:], in_=consts[:, C_ONES:C_ONES + 128]), writes=[ones_b], sembuf=ones_b)
        cx.dma('pool', lambda e: e.dma_start(out=id_b.t[:], in_=consts[:, C_ID:C_ID + 128]), writes=[id_b], sembuf=id_b)
        cx.dma('pool', lambda e: e.dma_start(out=mask.t[:], in_=maskd[:, :, :]), writes=[mask], sembuf=mask)
        cx.dma('sp', lambda e: e.dma_start(out=pr.t[:], in_=prm[:, :]), writes=[pr], sembuf=pr)
        cx.dma('pool', lambda e: e.dma_start(out=wqb.t[:], in_=wq.rearrange("(kc p) n -> p kc n", p=128)), writes=[wqb], sembuf=wqb)
        cx.dma('pool', lambda e: e.dma_start(out=wkvb.t[:], in_=wkv.rearrange("(kc p) n -> p kc n", p=128)), writes=[wkvb], sembuf=wkvb)

        def norm_into(dst, row0, nch, gcol0, t0):
            for half in range(S // T):
                tt0 = t0 + half * T
                ld = lambda kc, hb, tt0=tt0: (lambda e: e.dma_start(out=hb.t[:, :T], in_=pT[row0 + kc * 128:row0 + (kc + 1) * 128, tt0:tt0 + T]))
                rms_stats(cx, nc, ld, nch, T, hring, sqring, ps_ss, ones_f, nch * 128, eps_t, rstd, rstd)
                for kc in range(nch):
                    hb = hring.next()
                    cx.dma('sp', ld(kc, hb), writes=[hb], sembuf=hb)
                    cx.op('dve', lambda e, hb=hb, kc=kc, half=half: nc.vector.scalar_tensor_tensor(
                        out=dst.t[:, kc, half * T:(half + 1) * T], in0=hb.t[:, :T], scalar=pr.t[:, gcol0 + kc:gcol0 + kc + 1],
                        in1=rstd.t[:, :T], op0=ALU.mult, op1=ALU.mult), reads=[hb, pr, rstd], writes=[dst])

        def rope(dst, x_ap, xs_ap, sl, xbufs):
            cx.op('dve', lambda e: nc.vector.tensor_tensor(out=xt.t[:, :sl.stop - sl.start], in0=xs_ap, in1=ss.t[:, sl], op=ALU.mult),
                  reads=xbufs + [ss], writes=[xt])
            cx.op('dve', lambda e: nc.vector.tensor_tensor(out=rt2.t[0:64, :sl.stop - sl.start], in0=x_ap, in1=cc.t[:, sl], op=ALU.mult),
                  reads=xbufs + [cc], writes=[rt2])
            cx.op('dve', lambda e: nc.vector.tensor_tensor(out=dst.t[0:64, sl], in0=rt2.t[0:64, :sl.stop - sl.start],
                                                           in1=xt.t[:, :sl.stop - sl.start], op=ALU.add),
                  reads=[rt2, xt], writes=[dst])

        def seq_body(b):
            t0 = b * S
            cx.dma('sp', lambda e: e.dma_start(out=cc.t[:], in_=cs[b, 0, :, :]), writes=[cc], sembuf=cc)
            cx.dma('sp', lambda e: e.dma_start(out=ss.t[:], in_=cs[b, 1, :, :]), writes=[ss], sembuf=ss)
            cx.dma('sp', lambda e: e.dma_start(out=kx.t[:], in_=pT[768:832, t0:t0 + S]), writes=[kx], sembuf=kx)
            cx.dma('sp', lambda e: e.dma_start(out=kxs.t[:], in_=pT[R_KRS:R_KRS + 64, t0:t0 + S]), writes=[kxs], sembuf=kxs)
            norm_into(cqn, 0, 4, 0, t0)
            norm_into(ckvn, 512, 2, 4, t0)
            for sub in range(4):
                sl = slice(sub * 512, (sub + 1) * 512)
                rope(k_r, kx.t[:, sl], kxs.t[:, sl], sl, [kx, kxs])
            first = proj_gen(0, t0, hsets[0])
            for _ in first:
                pass
            for h in range(8):
                nxt = proj_gen(h + 1, t0, hsets[(h + 1) % 2]) if h < 7 else None
                attn(h, t0, hsets[h % 2], nxt)
                if nxt is not None:
                    for _ in nxt:
                        pass

        def proj_gen(h, t0, hs):
            q_n, q_r, k_n, V = hs["q_n"], hs["q_r"], hs["k_n"], hs["V"]
            if True:
                c0 = h * 256
                for sub in range(4):
                    sl = slice(sub * 512, (sub + 1) * 512)
                    p = ps_p.next()
                    mm_group(cx, p, [(wqb.t[:, kc, c0:c0 + 128], cqn.t[:, kc, sl]) for kc in range(4)], reads=[wqb, cqn])
                    cx.op('act', lambda e, p=p, sl=sl: nc.scalar.copy(q_n.t[:, sl], p.t[:]), reads=[p], writes=[q_n])
                    yield
                    p = ps_p.next()
                    mm_group(cx, p, [(wkvb.t[:, kc, c0:c0 + 128], ckvn.t[:, kc, sl]) for kc in range(2)], reads=[wkvb, ckvn])
                    cx.op('act', lambda e, p=p, sl=sl: nc.scalar.copy(k_n.t[:, sl], p.t[:]), reads=[p], writes=[k_n])
                    yield
                    p1 = ps_p.next()
                    mm_group(cx, p1, [(wqb.t[:, kc, c0 + 128:c0 + 192], cqn.t[:, kc, sl]) for kc in range(4)], reads=[wqb, cqn],
                             out_ap=p1.t[0:64, :])
                    p2 = ps_p.next()
                    mm_group(cx, p2, [(wqb.t[:, kc, c0 + 192:c0 + 256], cqn.t[:, kc, sl]) for kc in range(4)], reads=[wqb, cqn],
                             out_ap=p2.t[0:64, :])
                    rope(q_r, p1.t[0:64, :], p2.t[0:64, :], sl, [p1, p2])
                    yield
                    p = ps_p.next()

                    def vfn(eng, p=p, sub=sub):
                        ins = None
                        for j in range(4):
                            tb = sub * 4 + j
                            for kc in range(2):
                                ins = nc.tensor.matmul(p.t[:, j * 128:(j + 1) * 128], ckvn.t[:, kc, tb * 128:(tb + 1) * 128],
                                                       wkvb.t[:, kc, c0 + 128:c0 + 256], start=(kc == 0), stop=(kc == 1))
                        return ins
                    cx.op('pe', vfn, reads=[ckvn, wkvb], writes=[p])
                    cx.op('act', lambda e, p=p, sub=sub: nc.scalar.copy(
                        V.t[:, sub * 4:(sub + 1) * 4, :], p.t[:].rearrange("p (a b) -> p a b", a=4)), reads=[p], writes=[V])
                    yield

        def attn(h, t0, hs, nxt):
            q_n, q_r, k_n, V = hs["q_n"], hs["q_r"], hs["k_n"], hs["V"]
            if True:
                items = [(qt, kb) for qt in range(4) for kb in range(4 * (qt + 1))]
                acc = {}

                def emit_st(qt, kb):
                    qs = slice(qt * 512, (qt + 1) * 512)
                    ks = slice(kb * 128, (kb + 1) * 128)
                    st = ps_s.next()
                    pairs = [(k_n.t[:, ks], q_n.t[:, qs]), (k_r.t[0:64, ks], q_r.t[0:64, qs])]
                    rds = [k_n, q_n, k_r, q_r]
                    if kb >= 4 * qt:
                        pairs.append((id_b.t[:], mask.t[:, kb - 4 * qt, :]))
                        rds += [id_b, mask]
                    mm_group(cx, st, pairs, reads=rds)
                    pt = PT.next()
                    cx.op('act', lambda e, st=st, pt=pt: nc.scalar.activation(out=pt.t[:], in_=st.t[:], func=AF.Exp, scale=scale),
                          reads=[st], writes=[pt])
                    return pt

                def emit_pv(qt, kb, pt):
                    nkb = 4 * (qt + 1)
                    if kb == 0:
                        acc[qt] = (ps_o_r.next(), ps_r_r.next())
                    ps_o, ps_r = acc[qt]

                    def pv(eng):
                        nc.tensor.matmul(ps_o.t[:], V.t[:, kb, :], pt.t[:], start=(kb == 0), stop=(kb == nkb - 1))
                        return nc.tensor.matmul(ps_r.t[:], ones_b.t[:], pt.t[:], start=(kb == 0), stop=(kb == nkb - 1))
                    cx.op('pe', pv, reads=[pt, V, ones_b], writes=[ps_o, ps_r])
                    if kb == nkb - 1:
                        finalize(qt, ps_o, ps_r)

                def finalize(qt, ps_o, ps_r):
                    cx.op('dve', lambda e: nc.vector.reciprocal(rinv.t[:], ps_r.t[:]), reads=[ps_r], writes=[rinv])
                    cx.op('dve', lambda e: nc.vector.tensor_tensor(out=yv.t[:], in0=ps_o.t[:], in1=rinv.t[:], op=ALU.mult),
                          reads=[ps_o, rinv], writes=[yv])
                    cx.op('act', lambda e: nc.scalar.activation(out=ysq.t[:], in_=yv.t[:], func=AF.Square), reads=[yv], writes=[ysq])
                    pm = ps_p.next()
                    cx.op('pe', lambda e: nc.tensor.matmul(pm.t[:], ones_f.t[:], ysq.t[:], start=True, stop=True),
                          reads=[ones_f, ysq], writes=[pm])
                    cx.op('act', lambda e: nc.scalar.activation(out=sd2.t[:], in_=pm.t[:], func=AF.Sqrt,
                                                                bias=eps_t.t[:, 0:1], scale=1.0 / 128),
                          reads=[pm, eps_t], writes=[sd2])
                    cx.op('dve', lambda e: nc.vector.reciprocal(sd2.t[:], sd2.t[:]), reads=[sd2], writes=[sd2])
                    o = obr.next()
                    cx.op('dve', lambda e: nc.vector.scalar_tensor_tensor(
                        out=o.t[:], in0=yv.t[:], scalar=pr.t[:, 6 + h:7 + h], in1=sd2.t[:], op0=ALU.mult, op1=ALU.mult),
                        reads=[yv, pr, sd2], writes=[o])
                    cx.dma('sp', lambda e: e.dma_start(
                        out=yT[h * 128:(h + 1) * 128, t0 + qt * 512:t0 + (qt + 1) * 512], in_=o.t[:]), reads=[o], sembuf=o)

                pending = None
                for ii, (qt, kb) in enumerate(items):
                    pt = emit_st(qt, kb)
                    if pending is not None:
                        emit_pv(*pending)
                    pending = (qt, kb, pt)
                    if nxt is not None and ii % 2 == 1:
                        next(nxt, None)
                emit_pv(*pending)
        for b in range(NSEQ):
            seq_body(b)
        cx.end_stage(blk)


def stage_rwkv(cx, pT, rp, dup, iup, gup, consts, yT):
    nc = cx.nc
    TB = 128
    c1 = -0.6065306597126334
    with contextlib.ExitStack() as es:
        cx.begin_stage()

        def sb(name, shape, dt=F32, dma=False):
            return cx.buf(alloc(es, nc, name, shape, dt), dma=dma)
        cst = sb("cst", [64, 1024], dma=True)
        rpt = sb("rpt", [128, 96], dma=True)
        dupt = sb("dupt", [32, 512], dma=True)
        iupt = sb("iupt", [32, 512], dma=True)
        gupt = sb("gupt", [96, 512], dma=True)
        omka = sb("omka", [64, 8])
        cur3 = sb("cur3", [64, 3, 8, TB], dma=True)
        prv3 = sb("prv3", [64, 3, 8, TB], dma=True)
        loc = sb("loc", [96, 3, TB], dma=True)
        lop = sb("lop", [96, 3, TB], dma=True)
        names = ["sw", "a", "g", "cum", "E1", "E2", "E3", "E4", "kk", "nr", "tmp", "kp", "bv", "aT", "bb", "bT", "bhT", "kT", "khT", "rT"]
        Tt = {n: sb(n, [64, 8, TB]) for n in names}
        cn = ["V", "Bh", "Kh", "N", "L", "AKu", "BRu", "KRu", "P", "Q", "Na", "La", "Nb", "Lb"]
        Cs = [{n: sb(n + str(c), [64, 8, 64]) for n in cn} for c in range(TB // 64)]
        Ct = {n: sb(n, [64, 8, 64]) for n in ["Zs", "Us", "H", "tH", "ysb", "yc", "sq", "sdv"]}
        obr = Ring([sb(f"ob{i}", [64, 8, 64], BF16, dma=True) for i in range(2)])
        pp = Ring([cx.buf(alloc(es, nc, f"pp{i}", [128, 512], F32, psum=True)) for i in range(8)])
        blk = es.enter_context(nc.Block())

        cx.dma('sp', lambda e: e.dma_start(out=cst.t[:], in_=consts[0:64, :]), writes=[cst], sembuf=cst)
        cx.dma('sp', lambda e: e.dma_start(out=rpt.t[:], in_=rp[:, :]), writes=[rpt], sembuf=rpt)
        cx.dma('sp', lambda e: e.dma_start(out=dupt.t[:], in_=dup[:, :]), writes=[dupt], sembuf=dupt)
        cx.dma('sp', lambda e: e.dma_start(out=iupt.t[:], in_=iup[:, :]), writes=[iupt], sembuf=iupt)
        cx.dma('sp', lambda e: e.dma_start(out=gupt.t[:], in_=gup[:, :]), writes=[gupt], sembuf=gupt)
        ones64 = cst.t[:, C_ONES:C_ONES + 64]
        id64 = cst.t[:, C_ID:C_ID + 64]
        eps64 = cst.t[:, C_EPS + 1:C_EPS + 2]

        def mk(c):
            return cst.t[:, c:c + 64].unsqueeze(1).broadcast_to([64, 8, 64])
        SUb, SLb, IUb, IDb = mk(C_SU), mk(C_SL), mk(C_IU), mk(C_ID)
        rmask = cst.t[:, C_RM:C_RM + TB]

        def pb(col):
            return rpt.t[0:64, col:col + 8]

        def bc(ap2, n):
            return ap2.unsqueeze(2).broadcast_to([64, 8, n])

        def TTo(o, oap, ins, op, eng='dve'):
            bufs = [x[0] for x in ins]
            aps = [x[1] for x in ins]
            cx.op(eng, lambda e: nc.vector.tensor_tensor(out=oap, in0=aps[0], in1=aps[1], op=op),
                  reads=[b for b in bufs if b is not None], writes=[o])

        def ACTo(o, oap, i, iap, func, scale=1.0, bias=None, extra=()):
            def fn(e):
                if bias is None:
                    return nc.scalar.activation(out=oap, in_=iap, func=func, scale=scale)
                return nc.scalar.activation(out=oap, in_=iap, func=func, scale=scale, bias=bias)
            cx.op('act', fn, reads=[i] + list(extra), writes=[o])

        for bt in (loc, lop):
            cx.op('dve', lambda e, bt=bt: nc.vector.memset(bt.t[:], 0.0), writes=[bt])
        cx.op('dve', lambda e: nc.vector.tensor_scalar(out=omka.t[:], in0=pb(48), scalar1=-1.0, scalar2=1.0,
                                                       op0=ALU.mult, op1=ALU.add), reads=[rpt], writes=[omka])

        def rows3(t_lo, t_hi):
            return [pT[B0 + kd * 512:B0 + (kd + 1) * 512, t_lo:t_hi].rearrange("(h f) t -> f h t", f=64) for kd in range(3)]

        def block_body(b, k):
            t0 = b * S + k * TB
            T = Tt
            for kd, src in enumerate(rows3(t0, t0 + TB)):
                cx.dma('sp', lambda e, kd=kd, src=src: e.dma_start(out=cur3.t[:, kd], in_=src), writes=[cur3], sembuf=cur3)
            cx.dma('sp', lambda e: e.dma_start(out=loc.t[0:32, 0, :], in_=pT[B0 + 1536:B0 + 1568, t0:t0 + TB]), writes=[loc], sembuf=loc)
            cx.dma('sp', lambda e: e.dma_start(out=loc.t[0:32, 1, :], in_=pT[B0 + 1568:B0 + 1600, t0:t0 + TB]), writes=[loc], sembuf=loc)
            cx.dma('sp', lambda e: e.dma_start(out=loc.t[0:96, 2, :], in_=pT[B0 + 1600:B0 + 1696, t0:t0 + TB]), writes=[loc], sembuf=loc)
            if k == 0:
                cx.op('dve', lambda e: nc.vector.memset(prv3.t[:, :, :, 0:1], 0.0), writes=[prv3])
                cx.op('dve', lambda e: nc.vector.memset(lop.t[:, :, 0:1], 0.0), writes=[lop])
                cx.op('dve', lambda e: nc.vector.memset(Ct["H"].t[:], 0.0), writes=[Ct["H"]])
                for kd, src in enumerate(rows3(t0, t0 + TB - 1)):
                    cx.dma('sp', lambda e, kd=kd, src=src: e.dma_start(out=prv3.t[:, kd, :, 1:TB], in_=src), writes=[prv3], sembuf=prv3)
                o1, lo_, hi_ = 1, t0, t0 + TB - 1
            else:
                for kd, src in enumerate(rows3(t0 - 1, t0 + TB - 1)):
                    cx.dma('sp', lambda e, kd=kd, src=src: e.dma_start(out=prv3.t[:, kd], in_=src), writes=[prv3], sembuf=prv3)
                o1, lo_, hi_ = 0, t0 - 1, t0 + TB - 1
            cx.dma('sp', lambda e: e.dma_start(out=lop.t[0:32, 0, o1:TB], in_=pT[B0 + 1536:B0 + 1568, lo_:hi_]), writes=[lop], sembuf=lop)
            cx.dma('sp', lambda e: e.dma_start(out=lop.t[0:32, 1, o1:TB], in_=pT[B0 + 1568:B0 + 1600, lo_:hi_]), writes=[lop], sembuf=lop)
            cx.dma('sp', lambda e: e.dma_start(out=lop.t[0:96, 2, o1:TB], in_=pT[B0 + 1600:B0 + 1696, lo_:hi_]), writes=[lop], sembuf=lop)
            c3 = cur3.t[:].rearrange("p a h t -> p (a h) t")
            p3 = prv3.t[:].rearrange("p a h t -> p (a h) t")
            mu3 = rpt.t[0:64, 0:24].unsqueeze(2).broadcast_to([64, 24, TB])
            TTo(prv3, p3, [(prv3, p3), (cur3, c3)], ALU.subtract)
            TTo(prv3, p3, [(prv3, p3), (rpt, mu3)], ALU.mult)
            TTo(prv3, p3, [(prv3, p3), (cur3, c3)], ALU.add)
            mul = rpt.t[0:96, 80:83].unsqueeze(2).broadcast_to([96, 3, TB])
            TTo(lop, lop.t[:], [(lop, lop.t[:]), (loc, loc.t[:])], ALU.subtract)
            TTo(lop, lop.t[:], [(lop, lop.t[:]), (rpt, mul)], ALU.mult)
            TTo(lop, lop.t[:], [(lop, lop.t[:]), (loc, loc.t[:])], ALU.add)
            rs_, ks_, vs_ = prv3.t[:, 0], prv3.t[:, 1], prv3.t[:, 2]
            ACTo(lop, lop.t[0:32, 0, :], lop, lop.t[0:32, 0, :], AF.Tanh)
            ACTo(lop, lop.t[0:96, 2, :], lop, lop.t[0:96, 2, :], AF.Sigmoid)
            for (dst, wt, kdim, li, bcol) in ((T["sw"], dupt, 32, 0, 24), (T["a"], iupt, 32, 1, 32), (T["g"], gupt, 96, 2, None)):
                for half in range(2):
                    p = pp.next()

                    def fn(e, p=p, wt=wt, kdim=kdim, li=li, half=half):
                        ins = None
                        for hh in range(4):
                            h = half * 4 + hh
                            ins = nc.tensor.matmul(p.t[0:64, hh * TB:(hh + 1) * TB], wt.t[0:kdim, h * 64:(h + 1) * 64],
                                                   lop.t[0:kdim, li, :], start=True, stop=True)
                        return ins
                    cx.op('pe', fn, reads=[wt, lop], writes=[p])
                    dap = dst.t[:, half * 4:(half + 1) * 4, :]
                    pap = p.t[0:64, :].rearrange("p (h t) -> p h t", h=4)
                    if bcol is None:
                        ACTo(dst, dap, p, pap, AF.Copy)
                    else:
                        bb_ = rpt.t[0:64, bcol + half * 4:bcol + half * 4 + 4].unsqueeze(2).broadcast_to([64, 4, TB])
                        TTo(dst, dap, [(p, pap), (rpt, bb_)], ALU.add)
                if bcol is not None:
                    ACTo(dst, dst.t[:], dst, dst.t[:], AF.Sigmoid)
            for h in range(8):
                cx.op('dve', lambda e, h=h: nc.vector.tensor_tensor_scan(
                    out=T["cum"].t[:, h, :], data0=rmask, data1=T["sw"].t[:, h, :], initial=0.0, op0=ALU.mult, op1=ALU.add),
                    reads=[cst, T["sw"]], writes=[T["cum"]])
            ACTo(T["E1"], T["E1"].t[:], T["cum"], T["cum"].t[:], AF.Exp, scale=c1)
            ACTo(T["E2"], T["E2"].t[:], T["cum"], T["cum"].t[:], AF.Exp, scale=-c1)
            TTo(T["E3"], T["E3"].t[:], [(T["cum"], T["cum"].t[:]), (T["sw"], T["sw"].t[:])], ALU.subtract)
            ACTo(T["E3"], T["E3"].t[:], T["E3"], T["E3"].t[:], AF.Exp, scale=c1)
            cum4 = T["cum"].t[:].rearrange("p h (c t) -> p h c t", t=64)
            cumC = cum4[:, :, :, 63:64].broadcast_to([64, 8, TB // 64, 64])
            e44 = T["E4"].t[:].rearrange("p h (c t) -> p h c t", t=64)
            TTo(T["E4"], e44, [(T["cum"], cumC), (T["cum"], cum4)], ALU.subtract)
            ACTo(T["E4"], T["E4"].t[:], T["E4"], T["E4"].t[:], AF.Exp, scale=c1)
            TTo(T["kk"], T["kk"].t[:], [(prv3, ks_), (rpt, bc(pb(40), TB))], ALU.mult)
            ACTo(T["nr"], T["nr"].t[:], T["kk"], T["kk"].t[:], AF.Square)
            for half in range(2):
                p = pp.next()
                hs = slice(half * 4, (half + 1) * 4)
                cx.op('pe', lambda e, p=p, hs=hs: nc.tensor.matmul(p.t[0:64, :], ones64, T["nr"].t[:, hs, :], start=True, stop=True),
                      reads=[cst, T["nr"]], writes=[p])
                ACTo(T["tmp"], T["tmp"].t[:, hs, :], p, p.t[0:64, :].rearrange("p (h t) -> p h t", h=4), AF.Sqrt)
            cx.op('dve', lambda e: nc.vector.tensor_scalar(out=T["tmp"].t[:], in0=T["tmp"].t[:], scalar1=1e-12, scalar2=None, op0=ALU.max),
                  reads=[T["tmp"]], writes=[T["tmp"]])
            cx.op('dve', lambda e: nc.vector.reciprocal(T["tmp"].t[:], T["tmp"].t[:]), reads=[T["tmp"]], writes=[T["tmp"]])
            TTo(T["kk"], T["kk"].t[:], [(T["kk"], T["kk"].t[:]), (T["tmp"], T["tmp"].t[:])], ALU.mult)
            TTo(T["tmp"], T["tmp"].t[:], [(T["a"], T["a"].t[:]), (rpt, bc(pb(48), TB))], ALU.mult)
            TTo(T["tmp"], T["tmp"].t[:], [(T["tmp"], T["tmp"].t[:]), (omka, bc(omka.t[:], TB))], ALU.add)
            TTo(T["kp"], T["kp"].t[:], [(prv3, ks_), (T["tmp"], T["tmp"].t[:])], ALU.mult)
            TTo(T["tmp"], T["tmp"].t[:], [(prv3, rs_), (T["kp"], T["kp"].t[:])], ALU.mult)
            TTo(T["tmp"], T["tmp"].t[:], [(T["tmp"], T["tmp"].t[:]), (rpt, bc(pb(56), TB))], ALU.mult)
            for half in range(2):
                p = pp.next()
                hs = slice(half * 4, (half + 1) * 4)
                cx.op('pe', lambda e, p=p, hs=hs: nc.tensor.matmul(p.t[0:64, :], ones64, T["tmp"].t[:, hs, :], start=True, stop=True),
                      reads=[cst, T["tmp"]], writes=[p])
                TTo(T["bv"], T["bv"].t[:, hs, :], [(p, p.t[0:64, :].rearrange("p (h t) -> p h t", h=4)), (prv3, prv3.t[:, 2, hs, :])], ALU.mult)
            cx.op('dve', lambda e: nc.vector.scalar_tensor_tensor(out=T["aT"].t[:], in0=T["kk"].t[:], scalar=-1.0, in1=T["E3"].t[:],
                                                                  op0=ALU.mult, op1=ALU.mult), reads=[T["kk"], T["E3"]], writes=[T["aT"]])
            TTo(T["bb"], T["bb"].t[:], [(T["kk"], T["kk"].t[:]), (T["a"], T["a"].t[:])], ALU.mult)
            TTo(T["bT"], T["bT"].t[:], [(T["bb"], T["bb"].t[:]), (T["E2"], T["E2"].t[:])], ALU.mult)
            TTo(T["bhT"], T["bhT"].t[:], [(T["bb"], T["bb"].t[:]), (T["E4"], T["E4"].t[:])], ALU.mult)
            TTo(T["kT"], T["kT"].t[:], [(T["kp"], T["kp"].t[:]), (T["E2"], T["E2"].t[:])], ALU.mult)
            TTo(T["khT"], T["khT"].t[:], [(T["kp"], T["kp"].t[:]), (T["E4"], T["E4"].t[:])], ALU.mult)
            TTo(T["rT"], T["rT"].t[:], [(prv3, rs_), (T["E1"], T["E1"].t[:])], ALU.mult)
            run_chunks(b, k)

        def pgroup(builder, reads):
            p = pp.next()

            def fn(e, p=p):
                ins = None
                for h in range(8):
                    ins = builder(p.t[0:64, h * 64:(h + 1) * 64], h)
                return ins
            cx.op('pe', fn, reads=reads, writes=[p])
            return p, p.t[0:64, :].rearrange("p (h t) -> p h t", h=8)

        def phase_a(b, k, c):
            T, C = Tt, Cs[c]
            cs_ = slice(c * 64, (c + 1) * 64)
            for dst, src_b, src_ap in ((C["V"], prv3, prv3.t[:, 2]), (C["Bh"], T["bhT"], T["bhT"].t[:]), (C["Kh"], T["khT"], T["khT"].t[:])):
                p, pv = pgroup(lambda o, h, src_ap=src_ap: nc.tensor.transpose(o, src_ap[:, h, cs_], id64), [src_b, cst])
                ACTo(dst, dst.t[:], p, pv, AF.Copy)
            yield

            def pw(dst, lt, rt, mb):
                p, pv = pgroup(lambda o, h: nc.tensor.matmul(o, lt.t[:, h, cs_], rt.t[:, h, cs_], start=True, stop=True), [lt, rt])
                TTo(dst, dst.t[:], [(p, pv), (cst, mb)], ALU.mult)
            pw(C["N"], T["bT"], T["aT"], SUb)
            pw(C["L"], T["aT"], T["bT"], SLb)
            yield
            pw(C["AKu"], T["kT"], T["aT"], SUb)
            pw(C["BRu"], T["bT"], T["rT"], IUb)
            pw(C["KRu"], T["kT"], T["rT"], IUb)
            TTo(C["P"], C["P"].t[:], [(C["N"], C["N"].t[:]), (cst, IDb)], ALU.add)
            TTo(C["Q"], C["Q"].t[:], [(C["L"], C["L"].t[:]), (cst, IDb)], ALU.add)
            yield
            Nc, Lc = C["N"], C["L"]
            nxt = [(C["Na"], C["La"]), (C["Nb"], C["Lb"])]
            for lvl in range(5):
                Nn, Ln = nxt[lvl % 2]
                p, pv = pgroup(lambda o, h, Nc=Nc, Lc=Lc: nc.tensor.matmul(o, Lc.t[:, h, :], Nc.t[:, h, :], start=True, stop=True), [Nc, Lc])
                if lvl < 4:
                    p2, pv2 = pgroup(lambda o, h, Nc=Nc, Lc=Lc: nc.tensor.matmul(o, Nc.t[:, h, :], Lc.t[:, h, :], start=True, stop=True), [Nc, Lc])
                ACTo(Nn, Nn.t[:], p, pv, AF.Copy)
                if lvl < 4:
                    cx.op('dve', lambda e, Ln=Ln, pv2=pv2: nc.vector.tensor_copy(Ln.t[:], pv2), reads=[p2], writes=[Ln])
                yield
                p, pv = pgroup(lambda o, h, Nn=Nn: nc.tensor.matmul(o, C["Q"].t[:, h, :], Nn.t[:, h, :], start=True, stop=True), [C["Q"], Nn])
                if lvl < 4:
                    p2, pv2 = pgroup(lambda o, h, Ln=Ln: nc.tensor.matmul(o, C["P"].t[:, h, :], Ln.t[:, h, :], start=True, stop=True), [C["P"], Ln])
                TTo(C["P"], C["P"].t[:], [(C["P"], C["P"].t[:]), (p, pv)], ALU.add)
                if lvl < 4:
                    TTo(C["Q"], C["Q"].t[:], [(C["Q"], C["Q"].t[:]), (p2, pv2)], ALU.add)
                Nc, Lc = Nn, Ln
                yield

        def phase_b(b, k, c):
            T, C = Tt, Cs[c]
            cs_ = slice(c * 64, (c + 1) * 64)
            tc0 = b * S + k * TB + c * 64
            H = Ct["H"]

            def zb(o, h):
                nc.tensor.matmul(o, T["aT"].t[:, h, cs_], H.t[:, h, :], start=True, stop=False)
                return nc.tensor.matmul(o, C["AKu"].t[:, h, :], C["V"].t[:, h, :], start=False, stop=True)
            p, pv = pgroup(zb, [T["aT"], H, C["AKu"], C["V"]])
            ACTo(Ct["Zs"], Ct["Zs"].t[:], p, pv, AF.Copy)
            p, pv = pgroup(lambda o, h: nc.tensor.matmul(o, C["P"].t[:, h, :], Ct["Zs"].t[:, h, :], start=True, stop=True), [C["P"], Ct["Zs"]])
            cx.op('dve', lambda e, pv=pv: nc.vector.tensor_copy(Ct["Us"].t[:], pv), reads=[p], writes=[Ct["Us"]])

            def yb_(o, h):
                nc.tensor.matmul(o, H.t[:, h, :], T["rT"].t[:, h, cs_], start=True, stop=False)
                nc.tensor.matmul(o, Ct["Us"].t[:, h, :], C["BRu"].t[:, h, :], start=False, stop=False)
                return nc.tensor.matmul(o, C["V"].t[:, h, :], C["KRu"].t[:, h, :], start=False, stop=True)

            def hb_(o, h):
                nc.tensor.matmul(o, C["Bh"].t[:, h, :], Ct["Us"].t[:, h, :], start=True, stop=False)
                return nc.tensor.matmul(o, C["Kh"].t[:, h, :], C["V"].t[:, h, :], start=False, stop=True)
            ph, phv = pgroup(hb_, [C["Bh"], Ct["Us"], C["Kh"], C["V"]])
            py, pyv = pgroup(yb_, [H, T["rT"], Ct["Us"], C["BRu"], C["V"], C["KRu"]])
            gC = T["E1"].t[:, :, c * 64 + 63:c * 64 + 64].broadcast_to([64, 8, 64])
            TTo(Ct["tH"], Ct["tH"].t[:], [(H, H.t[:]), (T["E1"], gC)], ALU.mult)
            TTo(H, H.t[:], [(Ct["tH"], Ct["tH"].t[:]), (ph, phv)], ALU.add)
            Cc = Ct
            ACTo(Cc["ysb"], Cc["ysb"].t[:], py, pyv, AF.Copy)
            ysf = Cc["ysb"].t[:].rearrange("p h t -> p (h t)")
            pm = pp.next()
            cx.op('pe', lambda e, pm=pm: nc.tensor.matmul(pm.t[0:64, :], ones64, ysf, start=True, stop=True), reads=[cst, Cc["ysb"]], writes=[pm])
            cx.op('dve', lambda e, pm=pm: nc.vector.scalar_tensor_tensor(
                out=Cc["yc"].t[:].rearrange("p h t -> p (h t)"), in0=pm.t[0:64, :], scalar=-1.0 / 64, in1=ysf, op0=ALU.mult, op1=ALU.add),
                reads=[pm, Cc["ysb"]], writes=[Cc["yc"]])
            ACTo(Cc["sq"], Cc["sq"].t[:], Cc["yc"], Cc["yc"].t[:], AF.Square)
            pv_ = pp.next()
            cx.op('pe', lambda e, pv_=pv_: nc.tensor.matmul(pv_.t[0:64, :], ones64, Cc["sq"].t[:].rearrange("p h t -> p (h t)"), start=True, stop=True),
                  reads=[cst, Cc["sq"]], writes=[pv_])
            ACTo(Cc["sdv"], Cc["sdv"].t[:].rearrange("p h t -> p (h t)"), pv_, pv_.t[0:64, :], AF.Sqrt, scale=1.0 / 64, bias=eps64, extra=[cst])
            cx.op('dve', lambda e: nc.vector.reciprocal(Cc["sdv"].t[:], Cc["sdv"].t[:]), reads=[Cc["sdv"]], writes=[Cc["sdv"]])
            yc = Cc["yc"]
            TTo(yc, yc.t[:], [(yc, yc.t[:]), (Cc["sdv"], Cc["sdv"].t[:])], ALU.mult)
            TTo(yc, yc.t[:], [(yc, yc.t[:]), (rpt, bc(pb(64), 64))], ALU.mult)
            TTo(yc, yc.t[:], [(yc, yc.t[:]), (rpt, bc(pb(72), 64))], ALU.add)
            TTo(yc, yc.t[:], [(yc, yc.t[:]), (T["bv"], T["bv"].t[:, :, cs_])], ALU.add)
            o = obr.next()
            TTo(o, o.t[:], [(yc, yc.t[:]), (T["g"], T["g"].t[:, :, cs_])], ALU.mult)
            cx.dma('sp', lambda e, o=o: e.dma_start(
                out=yT[1024:1536, tc0:tc0 + 64].rearrange("(h f) t -> f h t", f=64), in_=o.t[:]), reads=[o], sembuf=o)

        def run_chunks(b, k):
            gens = [phase_a(b, k, c) for c in range(TB // 64)]
            while gens:
                for g_ in list(gens):
                    try:
                        next(g_)
                    except StopIteration:
                        gens.remove(g_)
            for c in range(TB // 64):
                phase_b(b, k, c)

        for b in range(NSEQ):
            for k in range(S // TB):
                block_body(b, k)
        cx.end_stage(blk)


def stage_rope(cx, pos, invf, cs):
    nc = cx.nc
    PI = float(np.pi)
    with contextlib.ExitStack() as es:
        cx.begin_stage()
        pi_ = cx.buf(alloc(es, nc, "pi_", [32, S], I32), dma=True)
        fr = cx.buf(alloc(es, nc, "fr", [32, 1], F32), dma=True)
        ang = cx.buf(alloc(es, nc, "ang", [32, S], F32))
        tf = cx.buf(alloc(es, nc, "tf", [32, S], F32))
        ki = cx.buf(alloc(es, nc, "ki", [32, S], I32))
        r = cx.buf(alloc(es, nc, "r", [32, S], F32))
        m = cx.buf(alloc(es, nc, "m", [32, S], F32))
        outs = [cx.buf(alloc(es, nc, f"o{i}", [32, S], F32), dma=True) for i in range(3)]
        blk = es.enter_context(nc.Block())
        cx.dma('sp', lambda e: e.dma_start(out=fr.t[:], in_=invf[:, :]), writes=[fr], sembuf=fr)

        def wrap(buf):
            cx.op('dve', lambda e: nc.vector.tensor_scalar(out=m.t[:], in0=buf.t[:], scalar1=PI, scalar2=-2 * PI, op0=ALU.is_gt, op1=ALU.mult),
                  reads=[buf], writes=[m])
            cx.op('dve', lambda e: nc.vector.tensor_tensor(out=buf.t[:], in0=buf.t[:], in1=m.t[:], op=ALU.add), reads=[m], writes=[buf])
            cx.op('dve', lambda e: nc.vector.tensor_scalar(out=m.t[:], in0=buf.t[:], scalar1=-PI, scalar2=2 * PI, op0=ALU.is_lt, op1=ALU.mult),
                  reads=[buf], writes=[m])
            cx.op('dve', lambda e: nc.vector.tensor_tensor(out=buf.t[:], in0=buf.t[:], in1=m.t[:], op=ALU.add), reads=[m], writes=[buf])

        def body(b):
            cx.dma('sp', lambda e: e.dma_start(out=pi_.t[:], in_=pos[b:b + 1, :].broadcast_to([32, S])), writes=[pi_], sembuf=pi_)
            cx.op('dve', lambda e: nc.vector.tensor_copy(ang.t[:], pi_.t[:]), reads=[pi_], writes=[ang])
            cx.op('dve', lambda e: nc.vector.tensor_scalar(out=ang.t[:], in0=ang.t[:], scalar1=fr.t[:, 0:1], scalar2=None, op0=ALU.mult),
                  reads=[fr], writes=[ang])
            cx.op('dve', lambda e: nc.vector.tensor_scalar(out=tf.t[:], in0=ang.t[:], scalar1=1.0 / (2 * PI), scalar2=None, op0=ALU.mult),
                  reads=[ang], writes=[tf])
            cx.op('dve', lambda e: nc.vector.tensor_copy(ki.t[:], tf.t[:]), reads=[tf], writes=[ki])
            cx.op('dve', lambda e: nc.vector.tensor_copy(tf.t[:], ki.t[:]), reads=[ki], writes=[tf])
            cx.op('dve', lambda e: nc.vector.scalar_tensor_tensor(out=r.t[:], in0=tf.t[:], scalar=-2 * PI, in1=ang.t[:], op0=ALU.mult, op1=ALU.add),
                  reads=[tf, ang], writes=[r])
            wrap(r)
            cx.op('act', lambda e: nc.scalar.activation(out=outs[1].t[:], in_=r.t[:], func=AF.Sin), reads=[r], writes=[outs[1]])
            cx.op('act', lambda e: nc.scalar.activation(out=outs[2].t[:], in_=r.t[:], func=AF.Sin, scale=-1.0), reads=[r], writes=[outs[2]])
            cx.op('dve', lambda e: nc.vector.tensor_scalar(out=r.t[:], in0=r.t[:], scalar1=PI / 2, scalar2=None, op0=ALU.add),
                  reads=[outs[1], outs[2]], writes=[r])
            wrap(r)
            cx.op('act', lambda e: nc.scalar.activation(out=outs[0].t[:], in_=r.t[:], func=AF.Sin), reads=[r], writes=[outs[0]])
            for (src, j, r0) in ((outs[0], 0, 0), (outs[0], 0, 32), (outs[2], 1, 0), (outs[1], 1, 32)):
                cx.dma('sp', lambda e, src=src, j=j, r0=r0: e.dma_start(out=cs[b, j, r0:r0 + 32, :], in_=src.t[:]), reads=[src], sembuf=src)
        for b in range(NSEQ):
            body(b)
        cx.end_stage(blk)


NPAD = 4224
BIGW = ["ffn1_gate", "ffn1_up", "ffn1_down", "ffn2_gate", "ffn2_up", "ffn2_down", "w_out", "w_ukv",
        "decay_up", "iclr_up", "gate_up"]
BIGW_SHAPES = {"ffn1_gate": [D, DFF], "ffn1_up": [D, DFF], "ffn1_down": [DFF, D], "ffn2_gate": [D, DFF],
               "ffn2_up": [D, DFF], "ffn2_down": [DFF, D], "w_out": [D, D], "w_ukv": [256, 2048],
               "decay_up": [32, 512], "iclr_up": [32, 512], "gate_up": [96, 512]}


def build_program(nlayers=DEPTH, dbg=False):
    nc = bass.Bass("TRN2", target_bir_lowering=False)
    dt = lambda n, s, d=F32, kind="ExternalInput": nc.dram_tensor(n, s, d, kind=kind).ap()
    x = dt("x", [NT, D])
    pos = dt("pos", [NSEQ, S], I32)
    W = {n: dt(n, [DEPTH] + BIGW_SHAPES[n]) for n in BIGW}
    w_inx = dt("w_inx", [DEPTH, D, NPAD])
    wq = dt("wq", [DEPTH, 512, 2048])
    spk = dt("spk", [DEPTH, 128, 192])
    gfin = dt("gfin", [128, KC])
    consts = dt("consts", [128, 1024])
    maskd = dt("maskd", [128, 4, 512])
    invf = dt("invf", [32, 1])
    out = dt("out", [NT, D], kind="ExternalOutput")
    sk = "ExternalOutput" if dbg else "Internal"
    hT = dt("hT", [KC, 128, NT], kind=sk)
    pT = dt("pT", [NPAD, NT], kind=sk)
    yT = dt("yT", [D, NT], BF16, kind=sk)
    cs = dt("cs", [NSEQ, 2, 64, S], kind=sk)
    with contextlib.ExitStack() as es:
        sems = [es.enter_context(nc.semaphore(f"s{i}")) for i in range(96)]
        cx = Ctx(nc, sems)
        stage_rope(cx, pos, invf, cs)
        stage_in(cx, x, hT, consts[:, C_ID:C_ID + 128])
        for l in range(nlayers):
            sp = spk[l]
            stage_ffn(cx, hT, W["ffn1_gate"][l], W["ffn1_up"][l], W["ffn1_down"][l], sp[:, 0:16], consts)
            stage_proj(cx, hT, w_inx[l], sp[:, 16:32], consts, pT, NPAD)
            stage_mla(cx, pT, wq[l], W["w_ukv"][l], sp[:, 48:64], cs, consts, maskd, yT)
            stage_rwkv(cx, pT, sp[:, 64:160], W["decay_up"][l], W["iclr_up"][l], W["gate_up"][l], consts, yT)
            stage_conv(cx, pT, sp[:, 160:176].rearrange("p (a b) -> p a b", a=4), consts, yT)
            stage_wout(cx, hT, yT, W["w_out"][l])
            stage_ffn(cx, hT, W["ffn2_gate"][l], W["ffn2_up"][l], W["ffn2_down"][l], sp[:, 32:48], consts)
        stage_out(cx, hT, gfin, consts, consts[:, C_ID:C_ID + 128], out)
    return nc


def _fm(v, nch):
    return np.ascontiguousarray(np.asarray(v, np.float32).reshape(nch, 128).T)


def _hd(v):
    return np.ascontiguousarray(np.asarray(v, np.float32).reshape(8, 64).T)


def host_layout(inp):
    f32 = np.float32
    w_in = np.asarray(inp["w_in"], f32)
    w_inx = np.zeros((DEPTH, D, NPAD), f32)
    w_inx[:, :, :NIN] = w_in
    w_inx[:, :, NIN:NIN + 32] = w_in[:, :, 800:832]
    w_inx[:, :, NIN + 32:NIN + 64] = w_in[:, :, 768:800]
    w_uq = np.asarray(inp["w_uq"], f32).reshape(DEPTH, 512, 8, 192)
    wq = np.concatenate([w_uq[..., :128], w_uq[..., 128:192], w_uq[..., 160:192], w_uq[..., 128:160]], axis=-1)
    wq = np.ascontiguousarray(wq.reshape(DEPTH, 512, 2048))
    spk = np.zeros((DEPTH, 128, 192), f32)
    for l in range(DEPTH):
        spk[l, :, 0:16] = _fm(inp["norm_ffn1"][l], 16)
        spk[l, :, 16:32] = _fm(inp["norm_mix"][l], 16)
        spk[l, :, 32:48] = _fm(inp["norm_ffn2"][l], 16)
        spk[l, :, 48:52] = _fm(inp["q_norm"][l], 4)
        spk[l, :, 52:54] = _fm(inp["kv_norm"][l], 2)
        spk[l, :, 54:62] = _fm(inp["attn_out_norm"][l], 8)
        rp = spk[l, :, 64:160]
        mu = np.asarray(inp["shift_mu"][l], f32)
        rp[0:64, 0:8] = _hd(mu[0:512])
        rp[0:64, 8:16] = _hd(mu[512:1024])
        rp[0:64, 16:24] = _hd(mu[1024:1536])
        rp[0:64, 24:32] = _hd(inp["decay_w0"][l])
        rp[0:64, 32:40] = _hd(inp["iclr_a0"][l])
        rp[0:64, 40:48] = _hd(inp["k_k"][l])
        rp[0:64, 48:56] = _hd(inp["k_a"][l])
        rp[0:64, 56:64] = _hd(np.asarray(inp["r_k"][l], f32).reshape(512))
        rp[0:64, 64:72] = _hd(inp["lnx_gain"][l])
        rp[0:64, 72:80] = _hd(inp["lnx_bias"][l])
        rp[0:32, 80] = mu[1536:1568]
        rp[0:32, 81] = mu[1568:1600]
        rp[0:96, 82] = mu[1600:1696]
        cw = np.asarray(inp["conv_w"][l], f32)
        cp = np.zeros((128, 4, 4), f32)
        for k in range(3):
            cp[:, :, k] = cw[k].reshape(4, 128).T
        cp[:, :, 3] = np.asarray(inp["conv_out_norm"][l], f32).reshape(4, 128).T
        spk[l, :, 160:176] = cp.reshape(128, 16)
    consts = np.zeros((128, 1024), f32)
    consts[:, C_ONES:C_ONES + 128] = 1.0
    consts[:, C_EPS] = 1e-6
    consts[:, C_EPS + 1] = 64e-5
    consts[0:64, C_BD64:C_BD64 + 64] = 1.0
    consts[64:128, C_BD64 + 64:C_BD64 + 128] = 1.0
    consts[:, C_ID:C_ID + 128] = np.eye(128, dtype=f32)
    i = np.arange(64)
    consts[0:64, C_SU:C_SU + 64] = (i[:, None] < i[None, :])
    consts[0:64, C_SL:C_SL + 64] = (i[:, None] > i[None, :])
    consts[0:64, C_IU:C_IU + 64] = (i[:, None] <= i[None, :])
    rm = np.ones(128, f32)
    rm[0::64] = 0.0
    consts[:, C_RM:C_RM + 128] = rm[None, :]
    kl = np.arange(128)[:, None]
    ql = np.arange(512)[None, :]
    maskd = np.zeros((128, 4, 512), f32)
    for j in range(4):
        maskd[:, j, :] = np.where(128 * j + kl > ql, -30000.0, 0.0)
    invf = (1.0 / (np.float32(10000.0) ** (np.arange(0, 64, 2, dtype=f32) / np.float32(64)))).astype(f32).reshape(32, 1)
    shared = {n: np.ascontiguousarray(np.asarray(inp[n], f32)) for n in BIGW}
    shared.update({"w_inx": w_inx, "wq": wq, "spk": spk, "gfin": _fm(inp["norm_final"], 16), "consts": consts,
                   "maskd": maskd, "invf": invf})
    return shared


_PROG = {}


def kernel(**inp):
    shared = host_layout(inp)
    x = np.asarray(inp["x"], np.float32)
    pos = np.asarray(inp["positions"], np.int32)
    if "nc" not in _PROG:
        _PROG["nc"] = build_program()
    nc = _PROG["nc"]
    in_maps = []
    for c in range(8):
        m = dict(shared)
        m["x"] = np.ascontiguousarray(x[c * NSEQ:(c + 1) * NSEQ].reshape(NT, D))
        m["pos"] = np.ascontiguousarray(pos[c * NSEQ:(c + 1) * NSEQ])
        in_maps.append(m)
    res = run_bass_kernel_spmd(nc, in_maps, core_ids=list(range(8)))
    out = np.stack([np.asarray(r["out"]).reshape(NSEQ, S, D) for r in res.results], axis=0)
    return out.reshape(16, S, D).astype(np.float32)
```

```python
import contextlib
import numpy as np
import concourse.bass as bass
import concourse.mybir as mybir
from concourse.bass_utils import run_bass_kernel_spmd

F32 = mybir.dt.float32
BF16 = mybir.dt.bfloat16
I32 = mybir.dt.int32
AF = mybir.ActivationFunctionType
ALU = mybir.AluOpType
AX = mybir.AxisListType

D = 2048
DFF = 5632
S = 2048
NSEQ = 2
NT = NSEQ * S
DEPTH = 4
KC = D // 128
FC = DFF // 128
EPS = 1e-6
NIN = 4064
NINX = 4128


class Sem:
    def __init__(self, h):
        self.h = h
        self.n = 0


class Buf:
    def __init__(self, cx, t, dma=False):
        self.t = t
        self.wr = None
        self.rd = []
        self.sem = cx.new_sem() if dma else None

    def __getitem__(self, k):
        return self.t[k]


class Ctx:
    ENG = ('pe', 'act', 'dve', 'pool', 'sp')

    def __init__(self, nc, sem_handles):
        self.nc = nc
        self.sems = [Sem(h) for h in sem_handles]
        self.free = list(self.sems)
        self.esem = {e: self.new_sem() for e in ('pe', 'act', 'dve', 'pool')}
        self.q = {e: [] for e in self.ENG}
        self.seen = {e: {} for e in self.ENG}
        self.stage_sems = []
        self.pending_dma = {e: [] for e in self.ENG}

    def new_sem(self):
        return self.free.pop()

    def begin_stage(self):
        self.mark = len(self.free)
        self.taken = []

    def buf(self, t, dma=False):
        b = Buf.__new__(Buf)
        b.t = t
        b.wr = None
        b.rd = []
        b.sem = None
        if dma:
            b.sem = self.free.pop()
            self.taken.append(b.sem)
        return b

    def end_stage(self, block):
        for e in self.ENG:
            for tok in self.pending_dma[e]:
                self._wait(e, tok)
            self.pending_dma[e] = []
        self.flush(block)
        self.free.extend(self.taken)
        self.taken = []

    def _wait(self, eng, tok):
        if tok is None:
            return
        sem, val, src = tok
        if src == eng and src in ('pe', 'act', 'dve'):
            return
        if self.seen[eng].get(id(sem), 0) >= val:
            return
        self.seen[eng][id(sem)] = val
        self.q[eng].append(('wait', sem, val))

    def _deps(self, eng, reads, writes):
        best = {}
        toks = [b.wr for b in reads] + [b.wr for b in writes]
        for b in writes:
            toks.extend(b.rd)
        for t in toks:
            if t is None:
                continue
            k = id(t[0])
            if k not in best or best[k][1] < t[1]:
                best[k] = t
        for t in best.values():
            self._wait(eng, t)

    def op(self, eng, fn, reads=(), writes=(), sig=True):
        self._deps(eng, reads, writes)
        tok = None
        if sig:
            s = self.esem[eng]
            s.n += 1
            tok = (s, s.n, eng)
            self.q[eng].append(('op', fn, s, 1))
        else:
            self.q[eng].append(('op', fn, None, 0))
        for b in writes:
            b.wr = tok
            b.rd = []
        for b in reads:
            if tok is not None:
                b.rd = [t for t in b.rd if t[0] is not tok[0]] + [tok]
        return tok

    def dma(self, eng, fn, reads=(), writes=(), sembuf=None):
        s = sembuf.sem
        saved = []
        for b in writes:
            if b.wr is not None and b.wr[2] == 'dma' and b.wr[0] is s and not b.rd:
                saved.append((b, b.wr))
                b.wr = None
        self._deps(eng, reads, writes)
        for b, w in saved:
            b.wr = w
        s.n += 16
        tok = (s, s.n, 'dma')
        self.q[eng].append(('op', fn, s, 16))
        for b in writes:
            b.wr = tok
            b.rd = []
        for b in reads:
            b.rd = [t for t in b.rd if t[0] is not tok[0]] + [tok]
        if not writes:
            self.pending_dma[eng].append(tok)
        return tok

    def flush(self, block):
        m = {'pe': block.tensor, 'act': block.scalar, 'dve': block.vector,
             'pool': block.gpsimd, 'sp': block.sync}
        for e in self.ENG:
            lst = self.q[e]
            if not lst:
                continue

            def body(eng, lst=lst):
                for it in lst:
                    if it[0] == 'wait':
                        eng.wait_ge(it[1].h, it[2])
                    else:
                        ins = it[1](eng)
                        if it[2] is not None:
                            ins.then_inc(it[2].h, it[3])
            m[e](body)
            self.q[e] = []


class Ring:
    def __init__(self, bufs):
        self.bufs = bufs
        self.i = 0

    def next(self):
        b = self.bufs[self.i % len(self.bufs)]
        self.i += 1
        return b


def mm_group(cx, out_buf, pairs, reads, out_ap=None):
    nc = cx.nc
    oap = out_buf.t[:] if out_ap is None else out_ap
    n = len(pairs)

    def fn(eng):
        ins = None
        for i, (l, r) in enumerate(pairs):
            ins = nc.tensor.matmul(oap, l, r, start=(i == 0), stop=(i == n - 1))
        return ins
    return cx.op('pe', fn, reads=reads, writes=[out_buf])


_UID = [0]


def alloc(es, nc, name, shape, dt, psum=False):
    _UID[0] += 1
    name = f"{name}_{_UID[0]}"
    if psum:
        return es.enter_context(nc.psum_tensor(name, shape, dt))
    return es.enter_context(nc.sbuf_tensor(name, shape, dt))


def stage_in(cx, x, hT, ident_dram):
    nc = cx.nc
    with contextlib.ExitStack() as es:
        cx.begin_stage()
        ident = cx.buf(alloc(es, nc, "ident", [128, 128], F32), dma=True)
        xin = Ring([cx.buf(alloc(es, nc, f"xin{i}", [128, D], F32), dma=True) for i in range(3)])
        xo = Ring([cx.buf(alloc(es, nc, f"xo{i}", [128, KC, 128], F32), dma=True) for i in range(3)])
        ps = Ring([cx.buf(alloc(es, nc, f"ps{i}", [128, 512], F32, psum=True)) for i in range(4)])
        blk = es.enter_context(nc.Block())
        cx.dma('sp', lambda e: e.dma_start(out=ident.t[:], in_=ident_dram[:, :]), writes=[ident], sembuf=ident)
        for tt in range(NT // 128):
            xb = xin.next()
            cx.dma('sp', lambda e, xb=xb, tt=tt: e.dma_start(out=xb.t[:], in_=x[tt * 128:(tt + 1) * 128, :]),
                   writes=[xb], sembuf=xb)
            ob = xo.next()
            for g in range(KC // 4):
                p = ps.next()

                def fn(eng, p=p, xb=xb, g=g):
                    ins = None
                    for j in range(4):
                        kc = g * 4 + j
                        ins = nc.tensor.transpose(p.t[:, j * 128:(j + 1) * 128], xb.t[:, kc * 128:(kc + 1) * 128], ident.t[:])
                    return ins
                cx.op('pe', fn, reads=[xb, ident], writes=[p])
                eng = 'dve' if g % 2 == 0 else 'act'
                if eng == 'dve':
                    cx.op('dve', lambda e, p=p, ob=ob, g=g: nc.vector.tensor_copy(
                        ob.t[:, g * 4:(g + 1) * 4, :], p.t[:].rearrange("p (a b) -> p a b", a=4)), reads=[p], writes=[ob])
                else:
                    cx.op('act', lambda e, p=p, ob=ob, g=g: nc.scalar.copy(
                        ob.t[:, g * 4:(g + 1) * 4, :], p.t[:].rearrange("p (a b) -> p a b", a=4)), reads=[p], writes=[ob])
            cx.dma('sp', lambda e, ob=ob, tt=tt: e.dma_start(
                out=hT[:, :, tt * 128:(tt + 1) * 128].rearrange("k p t -> p k t"), in_=ob.t[:]),
                reads=[ob], sembuf=ob)
        cx.end_stage(blk)


def rms_stats_gen(cx, nc, chunks_fn, nchunks, T, hring, sqring, ps_ss, ones_f, nfeat, eps_t, sd, rstd, src_rows=128, ldq='sp'):
    nsub = T // 512
    for kc in range(nchunks):
        hb = hring.next()
        cx.dma(ldq, chunks_fn(kc, hb), writes=[hb], sembuf=hb)
        sq = sqring.next()
        cx.op('act', lambda e, hb=hb, sq=sq: nc.scalar.activation(out=sq.t[:src_rows, :T], in_=hb.t[:src_rows, :T], func=AF.Square),
              reads=[hb], writes=[sq])
        for sub in range(nsub):
            p = ps_ss[sub]

            def fn(eng, p=p, sq=sq, sub=sub, kc=kc):
                return nc.tensor.matmul(p.t[:], ones_f.t[:src_rows, :], sq.t[:src_rows, sub * 512:(sub + 1) * 512],
                                        start=(kc == 0), stop=(kc == nchunks - 1))
            cx.op('pe', fn, reads=[sq, ones_f], writes=[p] if kc == 0 else [], sig=True)
            if kc != 0:
                pass
        if kc == nchunks - 1:
            last_tok = (cx.esem['pe'], cx.esem['pe'].n, 'pe')
            for sub in range(nsub):
                ps_ss[sub].wr = last_tok
        yield
    for sub in range(nsub):
        p = ps_ss[sub]
        cx.op('act', lambda e, p=p, sub=sub: nc.scalar.activation(
            out=sd.t[:, sub * 512:(sub + 1) * 512], in_=p.t[:], func=AF.Sqrt, bias=eps_t.t[:, 0:1], scale=1.0 / nfeat),
            reads=[p, eps_t], writes=[sd] if sub == 0 else [])
    sd.wr = (cx.esem['act'], cx.esem['act'].n, 'act')
    cx.op('dve', lambda e: nc.vector.reciprocal(rstd.t[:, :T], sd.t[:, :T]), reads=[sd], writes=[rstd])


def rms_stats(*a, **k):
    for _ in rms_stats_gen(*a, **k):
        pass


def stage_ffn(cx, hT, wg, wu, wd, gvec, consts, T=1024, tiles=None):
    nc = cx.nc
    nsub = T // 512
    NJ = 256
    with contextlib.ExitStack() as es:
        cx.begin_stage()
        ones_f = cx.buf(alloc(es, nc, "ones_f", [128, 128], F32), dma=True)
        eps_t = cx.buf(alloc(es, nc, "eps_t", [128, 1], F32), dma=True)
        g_t = cx.buf(alloc(es, nc, "g_t", [128, KC], F32), dma=True)
        xn = cx.buf(alloc(es, nc, "xn", [128, KC, T], BF16))
        act = [cx.buf(alloc(es, nc, f"act{j}", [128, T], BF16)) for j in range(FC)]
        wring = Ring([cx.buf(alloc(es, nc, f"w{i}", [128, 16, NJ], BF16), dma=True) for i in range(6)])
        hring = Ring([cx.buf(alloc(es, nc, f"hb{i}", [128, T], F32), dma=True) for i in range(2)])
        sqring = Ring([cx.buf(alloc(es, nc, f"sq{i}", [128, T], F32)) for i in range(2)])
        rstd = cx.buf(alloc(es, nc, "rstd", [128, T], F32))
        sd = rstd
        sgr = Ring([cx.buf(alloc(es, nc, f"sg{i}", [128, 512], F32)) for i in range(2)])
        outr = Ring([cx.buf(alloc(es, nc, f"ob{i}", [128, 512], F32), dma=True) for i in range(2)])
        hres = Ring([cx.buf(alloc(es, nc, f"hr{i}", [128, 512], F32), dma=True) for i in range(2)])
        ps_g = Ring([cx.buf(alloc(es, nc, f"psg{i}", [128, 512], F32, psum=True)) for i in range(2)])
        ps_u = Ring([cx.buf(alloc(es, nc, f"psu{i}", [128, 512], F32, psum=True)) for i in range(2)])
        ps_d = Ring([cx.buf(alloc(es, nc, f"psd{i}", [128, 512], F32, psum=True)) for i in range(2)])
        ps_ss = [cx.buf(alloc(es, nc, f"pss{i}", [128, 512], F32, psum=True)) for i in range(nsub)]
        blk = es.enter_context(nc.Block())

        cx.dma('sp', lambda e: e.dma_start(out=ones_f.t[:], in_=consts[:, 0:128]), writes=[ones_f], sembuf=ones_f)
        cx.dma('sp', lambda e: e.dma_start(out=eps_t.t[:], in_=consts[:, 128:129], allow_slow_non_contiguous=True), writes=[eps_t], sembuf=eps_t)
        cx.dma('sp', lambda e: e.dma_start(out=g_t.t[:], in_=gvec[:, :]), writes=[g_t], sembuf=g_t)
        wgv = wg.rearrange("(kc p) n -> p kc n", p=128)
        wuv = wu.rearrange("(kc p) n -> p kc n", p=128)
        wdv = wd.rearrange("(j p) n -> p j n", p=128)
        tl = list(range(NT // T)) if tiles is None else tiles
        def norm_gen(t0):
            yield from rms_stats_gen(cx, nc, lambda kc, hb: (lambda e: e.dma_start(out=hb.t[:, :T], in_=hT[kc, :, t0:t0 + T])),
                                     KC, T, hring, sqring, ps_ss, ones_f, D, eps_t, sd, rstd)
            for kc in range(KC):
                hb = hring.next()
                cx.dma('sp', lambda e, hb=hb, kc=kc: e.dma_start(out=hb.t[:, :T], in_=hT[kc, :, t0:t0 + T]),
                       writes=[hb], sembuf=hb)
                cx.op('dve', lambda e, hb=hb, kc=kc: nc.vector.scalar_tensor_tensor(
                    out=xn.t[:, kc, :], in0=hb.t[:, :T], scalar=g_t.t[:, kc:kc + 1], in1=rstd.t[:, :T],
                    op0=ALU.mult, op1=ALU.mult), reads=[hb, g_t, rstd], writes=[xn] if kc == 0 else [])
                xn.wr = (cx.esem['dve'], cx.esem['dve'].n, 'dve')
                yield

        def gateup(t0):
            for jt in range(DFF // NJ):
                wgb = wring.next()
                cx.dma('pool', lambda e, b=wgb, jt=jt: e.dma_start(out=b.t[:], in_=wgv[:, :, jt * NJ:(jt + 1) * NJ]),
                       writes=[wgb], sembuf=wgb)
                wub = wring.next()
                cx.dma('pool', lambda e, b=wub, jt=jt: e.dma_start(out=b.t[:], in_=wuv[:, :, jt * NJ:(jt + 1) * NJ]),
                       writes=[wub], sembuf=wub)
                for jj in range(NJ // 128):
                    j = jt * (NJ // 128) + jj
                    for sub in range(nsub):
                        pg = ps_g.next()
                        pu = ps_u.next()
                        mm_group(cx, pg, [(wgb.t[:, kc, jj * 128:(jj + 1) * 128], xn.t[:, kc, sub * 512:(sub + 1) * 512])
                                          for kc in range(KC)], reads=[wgb, xn])
                        mm_group(cx, pu, [(wub.t[:, kc, jj * 128:(jj + 1) * 128], xn.t[:, kc, sub * 512:(sub + 1) * 512])
                                          for kc in range(KC)], reads=[wub, xn])
                        sg = sgr.next()
                        cx.op('act', lambda e, pg=pg, sg=sg: nc.scalar.activation(out=sg.t[:], in_=pg.t[:], func=AF.Silu),
                              reads=[pg], writes=[sg])
                        cx.op('dve', lambda e, sg=sg, pu=pu, j=j, sub=sub: nc.vector.tensor_tensor(
                            out=act[j].t[:, sub * 512:(sub + 1) * 512], in0=sg.t[:], in1=pu.t[:], op=ALU.mult),
                            reads=[sg, pu], writes=[act[j]])
        def down(t0, nxt):
            JD = 16
            for ct in range(D // NJ):
                wds = []
                for q4 in range((FC + JD - 1) // JD):
                    wb = wring.next()
                    nj_ = min(JD, FC - q4 * JD)
                    cx.dma('pool', lambda e, b=wb, q4=q4, ct=ct, nj_=nj_: e.dma_start(
                        out=b.t[:, 0:nj_, :], in_=wdv[:, q4 * JD:q4 * JD + nj_, ct * NJ:(ct + 1) * NJ]),
                        writes=[wb], sembuf=wb)
                    wds.append(wb)
                for cc in range(NJ // 128):
                    c = ct * (NJ // 128) + cc
                    for sub in range(nsub):
                        hr = hres.next()
                        cx.dma('sp', lambda e, hr=hr, c=c, sub=sub: e.dma_start(
                            out=hr.t[:], in_=hT[c, :, t0 + sub * 512:t0 + (sub + 1) * 512]), writes=[hr], sembuf=hr)
                        pd = ps_d.next()
                        mm_group(cx, pd, [(wds[j // JD].t[:, j % JD, cc * 128:(cc + 1) * 128], act[j].t[:, sub * 512:(sub + 1) * 512])
                                          for j in range(FC)], reads=wds + act)
                        if nxt is not None:
                            next(nxt, None)
                            next(nxt, None)
                        ob = outr.next()
                        cx.op('dve', lambda e, pd=pd, hr=hr, ob=ob: nc.vector.scalar_tensor_tensor(
                            out=ob.t[:], in0=pd.t[:], scalar=0.5, in1=hr.t[:], op0=ALU.mult, op1=ALU.add),
                            reads=[pd, hr], writes=[ob])
                        cx.dma('sp', lambda e, ob=ob, c=c, sub=sub: e.dma_start(
                            out=hT[c, :, t0 + sub * 512:t0 + (sub + 1) * 512], in_=ob.t[:]), reads=[ob], sembuf=ob)
            if nxt is not None:
                for _ in nxt:
                    pass
        first = norm_gen(tl[0] * T)
        for _ in first:
            pass
        for idx, ti in enumerate(tl):
            gateup(ti * T)
            nxt = norm_gen(tl[idx + 1] * T) if idx + 1 < len(tl) else None
            down(ti * T, nxt)
        cx.end_stage(blk)


def stage_out(cx, hT, gvec, consts, ident_dram, out):
    nc = cx.nc
    T = 512
    with contextlib.ExitStack() as es:
        cx.begin_stage()
        ones_f = cx.buf(alloc(es, nc, "ones_f", [128, 128], F32), dma=True)
        ident = cx.buf(alloc(es, nc, "ident", [128, 128], F32), dma=True)
        eps_t = cx.buf(alloc(es, nc, "eps_t", [128, 1], F32), dma=True)
        g_t = cx.buf(alloc(es, nc, "g_t", [128, KC], F32), dma=True)
        hring = Ring([cx.buf(alloc(es, nc, f"hb{i}", [128, T], F32), dma=True) for i in range(3)])
        sqring = Ring([cx.buf(alloc(es, nc, f"sq{i}", [128, T], F32)) for i in range(2)])
        sd = cx.buf(alloc(es, nc, "sd", [128, T], F32))
        rstd = cx.buf(alloc(es, nc, "rstd", [128, T], F32))
        xn = cx.buf(alloc(es, nc, "xnf", [128, KC, T], F32))
        ps_ss = [cx.buf(alloc(es, nc, "pss0", [128, 512], F32, psum=True))]
        ps = Ring([cx.buf(alloc(es, nc, f"ps{i}", [128, 512], F32, psum=True)) for i in range(4)])
        orow = Ring([cx.buf(alloc(es, nc, f"orow{i}", [128, D], F32), dma=True) for i in range(3)])
        blk = es.enter_context(nc.Block())
        cx.dma('sp', lambda e: e.dma_start(out=ones_f.t[:], in_=consts[:, 0:128]), writes=[ones_f], sembuf=ones_f)
        cx.dma('sp', lambda e: e.dma_start(out=eps_t.t[:], in_=consts[:, 128:129], allow_slow_non_contiguous=True), writes=[eps_t], sembuf=eps_t)
        cx.dma('sp', lambda e: e.dma_start(out=g_t.t[:], in_=gvec[:, :]), writes=[g_t], sembuf=g_t)
        cx.dma('sp', lambda e: e.dma_start(out=ident.t[:], in_=ident_dram[:, :]), writes=[ident], sembuf=ident)
        def tile_body(t0):
            rms_stats(cx, nc, lambda kc, hb: (lambda e: e.dma_start(out=hb.t[:, :T], in_=hT[kc, :, t0:t0 + T])),
                      KC, T, hring, sqring, ps_ss, ones_f, D, eps_t, sd, rstd)
            for kc in range(KC):
                hb = hring.next()
                cx.dma('sp', lambda e, hb=hb, kc=kc: e.dma_start(out=hb.t[:, :T], in_=hT[kc, :, t0:t0 + T]),
                       writes=[hb], sembuf=hb)
                cx.op('dve', lambda e, hb=hb, kc=kc: nc.vector.scalar_tensor_tensor(
                    out=xn.t[:, kc, :], in0=hb.t[:, :T], scalar=g_t.t[:, kc:kc + 1], in1=rstd.t[:, :T],
                    op0=ALU.mult, op1=ALU.mult), reads=[hb, g_t, rstd], writes=[xn] if kc == 0 else [])
            xn.wr = (cx.esem['dve'], cx.esem['dve'].n, 'dve')
            for tb in range(T // 128):
                ob = orow.next()
                for g in range(KC // 4):
                    p = ps.next()

                    def fn(eng, p=p, g=g, tb=tb):
                        ins = None
                        for j in range(4):
                            kc = g * 4 + j
                            ins = nc.tensor.transpose(p.t[:, j * 128:(j + 1) * 128], xn.t[:, kc, tb * 128:(tb + 1) * 128], ident.t[:])
                        return ins
                    cx.op('pe', fn, reads=[xn, ident], writes=[p])
                    if g % 2 == 0:
                        cx.op('dve', lambda e, p=p, ob=ob, g=g: nc.vector.tensor_copy(ob.t[:, g * 512:(g + 1) * 512], p.t[:]),
                              reads=[p], writes=[ob])
                    else:
                        cx.op('act', lambda e, p=p, ob=ob, g=g: nc.scalar.copy(ob.t[:, g * 512:(g + 1) * 512], p.t[:]),
                              reads=[p], writes=[ob])
                cx.dma('sp', lambda e, ob=ob, tb=tb: e.dma_start(out=out[t0 + tb * 128:t0 + (tb + 1) * 128, :], in_=ob.t[:]),
                       reads=[ob], sembuf=ob)
        for ti in range(NT // T):
            tile_body(ti * T)
        cx.end_stage(blk)


def load_consts(cx, es, nc, consts, ident_dram=None):
    ones_f = cx.buf(alloc(es, nc, "ones_f", [128, 128], F32), dma=True)
    eps_t = cx.buf(alloc(es, nc, "eps_t", [128, 1], F32), dma=True)
    cx.dma('sp', lambda e: e.dma_start(out=ones_f.t[:], in_=consts[:, 0:128]), writes=[ones_f], sembuf=ones_f)
    cx.dma('sp', lambda e: e.dma_start(out=eps_t.t[:], in_=consts[:, 128:129], allow_slow_non_contiguous=True),
           writes=[eps_t], sembuf=eps_t)
    return ones_f, eps_t


def stage_proj(cx, hT, w, gvec, consts, pT, ncols, T=1024):
    nc = cx.nc
    nsub = T // 512
    NJ = 384
    with contextlib.ExitStack() as es:
        cx.begin_stage()
        ones_f, eps_t = load_consts(cx, es, nc, consts)
        g_t = cx.buf(alloc(es, nc, "g_t", [128, KC], F32), dma=True)
        xns = [cx.buf(alloc(es, nc, f"xn{i}", [128, KC, T], BF16)) for i in range(2)]
        wring = Ring([cx.buf(alloc(es, nc, f"w{i}", [128, 16, NJ], BF16), dma=True) for i in range(3)])
        hring = Ring([cx.buf(alloc(es, nc, f"hb{i}", [128, T], F32), dma=True) for i in range(2)])
        sqring = Ring([cx.buf(alloc(es, nc, f"sq{i}", [128, T], F32)) for i in range(2)])
        rstd = cx.buf(alloc(es, nc, "rstd", [128, T], F32))
        outr = Ring([cx.buf(alloc(es, nc, f"ob{i}", [128, 512], F32), dma=True) for i in range(4)])
        ps_o = Ring([cx.buf(alloc(es, nc, f"pso{i}", [128, 512], F32, psum=True)) for i in range(4)])
        ps_ss = [cx.buf(alloc(es, nc, f"pss{i}", [128, 512], F32, psum=True)) for i in range(nsub)]
        blk = es.enter_context(nc.Block())
        cx.dma('sp', lambda e: e.dma_start(out=g_t.t[:], in_=gvec[:, :]), writes=[g_t], sembuf=g_t)
        wv = w.rearrange("(kc p) n -> p kc n", p=128)

        def norm_gen(t0, xn):
            yield from rms_stats_gen(cx, nc, lambda kc, hb: (lambda e: e.dma_start(out=hb.t[:, :T], in_=hT[kc, :, t0:t0 + T])),
                                     KC, T, hring, sqring, ps_ss, ones_f, D, eps_t, rstd, rstd)
            for kc in range(KC):
                hb = hring.next()
                cx.dma('sp', lambda e, hb=hb, kc=kc: e.dma_start(out=hb.t[:, :T], in_=hT[kc, :, t0:t0 + T]),
                       writes=[hb], sembuf=hb)
                cx.op('dve', lambda e, hb=hb, kc=kc: nc.vector.scalar_tensor_tensor(
                    out=xn.t[:, kc, :], in0=hb.t[:, :T], scalar=g_t.t[:, kc:kc + 1], in1=rstd.t[:, :T],
                    op0=ALU.mult, op1=ALU.mult), reads=[hb, g_t, rstd], writes=[xn] if kc == 0 else [])
                xn.wr = (cx.esem['dve'], cx.esem['dve'].n, 'dve')
                yield

        def main(t0, xn, nxt):
            gcount = 0
            for jt in range(ncols // NJ):
                wb = wring.next()
                cx.dma('pool', lambda e, b=wb, jt=jt: e.dma_start(out=b.t[:], in_=wv[:, :, jt * NJ:(jt + 1) * NJ]),
                       writes=[wb], sembuf=wb)
                for jj in range(NJ // 128):
                    r0 = jt * NJ + jj * 128
                    for sub in range(nsub):
                        po = ps_o.next()
                        mm_group(cx, po, [(wb.t[:, kc, jj * 128:(jj + 1) * 128], xn.t[:, kc, sub * 512:(sub + 1) * 512])
                                          for kc in range(KC)], reads=[wb, xn])
                        gcount += 1
                        if nxt is not None and gcount % 2 == 0:
                            next(nxt, None)
                        ob = outr.next()
                        if (jj + sub) % 2 == 0:
                            cx.op('act', lambda e, po=po, ob=ob: nc.scalar.copy(ob.t[:], po.t[:]), reads=[po], writes=[ob])
                        else:
                            cx.op('dve', lambda e, po=po, ob=ob: nc.vector.tensor_copy(ob.t[:], po.t[:]), reads=[po], writes=[ob])
                        cx.dma('sp', lambda e, ob=ob, r0=r0, sub=sub: e.dma_start(
                            out=pT[r0:r0 + 128, t0 + sub * 512:t0 + (sub + 1) * 512], in_=ob.t[:]), reads=[ob], sembuf=ob)
            if nxt is not None:
                for _ in nxt:
                    pass
        ntl = NT // T
        for _ in norm_gen(0, xns[0]):
            pass
        for ti in range(ntl):
            nxt = norm_gen((ti + 1) * T, xns[(ti + 1) % 2]) if ti + 1 < ntl else None
            main(ti * T, xns[ti % 2], nxt)
        cx.end_stage(blk)


def stage_wout(cx, hT, yT, w, T=1024):
    nc = cx.nc
    nsub = T // 512
    NJ = 256
    with contextlib.ExitStack() as es:
        cx.begin_stage()
        yb = cx.buf(alloc(es, nc, "yb", [128, KC, T], BF16), dma=True)
        wring = Ring([cx.buf(alloc(es, nc, f"w{i}", [128, 16, NJ], BF16), dma=True) for i in range(3)])
        outr = Ring([cx.buf(alloc(es, nc, f"ob{i}", [128, 512], F32), dma=True) for i in range(3)])
        hres = Ring([cx.buf(alloc(es, nc, f"hr{i}", [128, 512], F32), dma=True) for i in range(3)])
        ps_o = Ring([cx.buf(alloc(es, nc, f"pso{i}", [128, 512], F32, psum=True)) for i in range(4)])
        blk = es.enter_context(nc.Block())
        wv = w.rearrange("(kc p) n -> p kc n", p=128)
        yv = yT.rearrange("(kc p) t -> p kc t", p=128)

        def tile_body(t0):
            cx.dma('sp', lambda e: e.dma_start(out=yb.t[:], in_=yv[:, :, t0:t0 + T]), writes=[yb], sembuf=yb)
            for jt in range(D // NJ):
                wb = wring.next()
                cx.dma('pool', lambda e, b=wb, jt=jt: e.dma_start(out=b.t[:], in_=wv[:, :, jt * NJ:(jt + 1) * NJ]),
                       writes=[wb], sembuf=wb)
                for jj in range(NJ // 128):
                    c = jt * (NJ // 128) + jj
                    for sub in range(nsub):
                        hr = hres.next()
                        cx.dma('sp', lambda e, hr=hr, c=c, sub=sub: e.dma_start(
                            out=hr.t[:], in_=hT[c, :, t0 + sub * 512:t0 + (sub + 1) * 512]), writes=[hr], sembuf=hr)
                        po = ps_o.next()
                        mm_group(cx, po, [(wb.t[:, kc, jj * 128:(jj + 1) * 128], yb.t[:, kc, sub * 512:(sub + 1) * 512])
                                          for kc in range(KC)], reads=[wb, yb])
                        ob = outr.next()
                        cx.op('dve', lambda e, po=po, hr=hr, ob=ob: nc.vector.tensor_tensor(
                            out=ob.t[:], in0=po.t[:], in1=hr.t[:], op=ALU.add), reads=[po, hr], writes=[ob])
                        cx.dma('sp', lambda e, ob=ob, c=c, sub=sub: e.dma_start(
                            out=hT[c, :, t0 + sub * 512:t0 + (sub + 1) * 512], in_=ob.t[:]), reads=[ob], sembuf=ob)
        for ti in range(NT // T):
            tile_body(ti * T)
        cx.end_stage(blk)


C_ONES, C_EPS, C_BD64, C_ID, C_SU, C_SL, C_IU, C_RM = 0, 128, 256, 384, 512, 576, 640, 704
B0 = 832
C0 = 2528
R_KRS = 4064


def stage_conv(cx, pT, prm, consts, yT):
    nc = cx.nc
    with contextlib.ExitStack() as es:
        cx.begin_stage()
        bd = cx.buf(alloc(es, nc, "bd", [128, 128], F32), dma=True)
        eps_t = cx.buf(alloc(es, nc, "eps_t", [128, 1], F32), dma=True)
        pr = cx.buf(alloc(es, nc, "pr", [128, 4, 4], F32), dma=True)
        bg = Ring([cx.buf(alloc(es, nc, f"bg{i}", [128, S], F32), dma=True) for i in range(2)])
        cg = Ring([cx.buf(alloc(es, nc, f"cg{i}", [128, S], F32), dma=True) for i in range(2)])
        hh = Ring([cx.buf(alloc(es, nc, f"hh{i}", [128, S], F32), dma=True) for i in range(2)])
        u = cx.buf(alloc(es, nc, "u", [128, S + 2], F32))
        y = cx.buf(alloc(es, nc, "y", [128, S], F32))
        z = cx.buf(alloc(es, nc, "z", [128, S], F32))
        zsq = cx.buf(alloc(es, nc, "zsq", [128, S], F32))
        rs = cx.buf(alloc(es, nc, "rs", [128, S], F32))
        ob = Ring([cx.buf(alloc(es, nc, f"ob{i}", [128, S], BF16), dma=True) for i in range(2)])
        ps = Ring([cx.buf(alloc(es, nc, f"ps{i}", [128, 512], F32, psum=True)) for i in range(4)])
        blk = es.enter_context(nc.Block())
        cx.dma('sp', lambda e: e.dma_start(out=bd.t[:], in_=consts[:, C_BD64:C_BD64 + 128]), writes=[bd], sembuf=bd)
        cx.dma('sp', lambda e: e.dma_start(out=eps_t.t[:], in_=consts[:, C_EPS:C_EPS + 1], allow_slow_non_contiguous=True),
               writes=[eps_t], sembuf=eps_t)
        cx.dma('sp', lambda e: e.dma_start(out=pr.t[:], in_=prm[:, :, :]), writes=[pr], sembuf=pr)
        cx.op('dve', lambda e: nc.vector.memset(u.t[:, 0:2], 0.0), writes=[u])

        def body(b, ch):
            t0 = b * S
            bgb, cgb, hhb = bg.next(), cg.next(), hh.next()
            for (buf, r0) in ((bgb, C0 + ch * 128), (cgb, C0 + 512 + ch * 128), (hhb, C0 + 1024 + ch * 128)):
                cx.dma('sp', lambda e, buf=buf, r0=r0: e.dma_start(out=buf.t[:], in_=pT[r0:r0 + 128, t0:t0 + S]),
                       writes=[buf], sembuf=buf)
            cx.op('dve', lambda e: nc.vector.tensor_tensor(out=u.t[:, 2:S + 2], in0=cgb.t[:], in1=hhb.t[:], op=ALU.mult),
                  reads=[cgb, hhb], writes=[u])
            cx.op('act', lambda e: nc.scalar.activation(out=y.t[:], in_=u.t[:, 2:S + 2], func=AF.Copy, scale=pr.t[:, ch, 2:3]),
                  reads=[u, pr], writes=[y])
            cx.op('dve', lambda e: nc.vector.scalar_tensor_tensor(out=y.t[:], in0=u.t[:, 1:S + 1], scalar=pr.t[:, ch, 1:2],
                                                                  in1=y.t[:], op0=ALU.mult, op1=ALU.add), reads=[u, pr], writes=[y])
            cx.op('dve', lambda e: nc.vector.scalar_tensor_tensor(out=y.t[:], in0=u.t[:, 0:S], scalar=pr.t[:, ch, 0:1],
                                                                  in1=y.t[:], op0=ALU.mult, op1=ALU.add), reads=[u, pr], writes=[y])
            cx.op('dve', lambda e: nc.vector.tensor_tensor(out=z.t[:], in0=bgb.t[:], in1=y.t[:], op=ALU.mult),
                  reads=[bgb, y], writes=[z])
            cx.op('act', lambda e: nc.scalar.activation(out=zsq.t[:], in_=z.t[:], func=AF.Square), reads=[z], writes=[zsq])
            for sub in range(S // 512):
                p = ps.next()
                sl = slice(sub * 512, (sub + 1) * 512)
                cx.op('pe', lambda e, p=p, sl=sl: nc.tensor.matmul(p.t[:], bd.t[:], zsq.t[:, sl], start=True, stop=True),
                      reads=[bd, zsq], writes=[p])
                cx.op('act', lambda e, p=p, sl=sl: nc.scalar.activation(out=rs.t[:, sl], in_=p.t[:], func=AF.Sqrt,
                                                                        bias=eps_t.t[:, 0:1], scale=1.0 / 64),
                      reads=[p, eps_t], writes=[rs])
            cx.op('dve', lambda e: nc.vector.reciprocal(rs.t[:], rs.t[:]), reads=[rs], writes=[rs])
            o = ob.next()
            cx.op('dve', lambda e, o=o: nc.vector.scalar_tensor_tensor(out=o.t[:], in0=z.t[:], scalar=pr.t[:, ch, 3:4],
                                                                       in1=rs.t[:], op0=ALU.mult, op1=ALU.mult),
                  reads=[z, pr, rs], writes=[o])
            cx.dma('sp', lambda e, o=o: e.dma_start(out=yT[1536 + ch * 128:1536 + (ch + 1) * 128, t0:t0 + S], in_=o.t[:]),
                   reads=[o], sembuf=o)
        for b in range(NSEQ):
            for ch in range(4):
                body(b, ch)
        cx.end_stage(blk)


def stage_mla(cx, pT, wq, wkv, prm, cs, consts, maskd, yT):
    nc = cx.nc
    T = 1024
    scale = float((128 + 64) ** -0.5)
    with contextlib.ExitStack() as es:
        cx.begin_stage()
        ones_f, eps_t = load_consts(cx, es, nc, consts)
        ones_b = cx.buf(alloc(es, nc, "ones_b", [128, 128], BF16), dma=True)
        id_b = cx.buf(alloc(es, nc, "id_b", [128, 128], BF16), dma=True)
        mask = cx.buf(alloc(es, nc, "mask", [128, 4, 512], BF16), dma=True)
        pr = cx.buf(alloc(es, nc, "pr", [128, 16], F32), dma=True)
        wqb = cx.buf(alloc(es, nc, "wqb", [128, 4, 2048], BF16), dma=True)
        wkvb = cx.buf(alloc(es, nc, "wkvb", [128, 2, 2048], BF16), dma=True)
        cqn = cx.buf(alloc(es, nc, "cqn", [128, 4, S], BF16))
        ckvn = cx.buf(alloc(es, nc, "ckvn", [128, 2, S], BF16))
        cc = cx.buf(alloc(es, nc, "cc", [64, S], F32), dma=True)
        ss = cx.buf(alloc(es, nc, "ss", [64, S], F32), dma=True)
        kx = cx.buf(alloc(es, nc, "kx", [64, S], F32), dma=True)
        kxs = cx.buf(alloc(es, nc, "kxs", [64, S], F32), dma=True)
        k_r = cx.buf(alloc(es, nc, "k_r", [64, S], BF16))
        hsets = [dict(q_n=cx.buf(alloc(es, nc, "q_n", [128, S], BF16)), q_r=cx.buf(alloc(es, nc, "q_r", [64, S], BF16)),
                      k_n=cx.buf(alloc(es, nc, "k_n", [128, S], BF16)), V=cx.buf(alloc(es, nc, "V", [128, 16, 128], BF16)))
                 for _ in range(2)]
        xt = cx.buf(alloc(es, nc, "xt", [64, 512], F32))
        rt2 = cx.buf(alloc(es, nc, "rt2", [64, 512], F32))
        PT = Ring([cx.buf(alloc(es, nc, f"PT{i}", [128, 512], BF16)) for i in range(3)])
        hring = Ring([cx.buf(alloc(es, nc, f"hb{i}", [128, T], F32), dma=True) for i in range(2)])
        sqring = Ring([cx.buf(alloc(es, nc, f"sq{i}", [128, T], F32)) for i in range(2)])
        rstd = cx.buf(alloc(es, nc, "rstd", [128, T], F32))
        rinv = cx.buf(alloc(es, nc, "rinv", [128, 512], F32))
        yv = cx.buf(alloc(es, nc, "yv", [128, 512], F32))
        ysq = cx.buf(alloc(es, nc, "ysq", [128, 512], F32))
        sd2 = cx.buf(alloc(es, nc, "sd2", [128, 512], F32))
        obr = Ring([cx.buf(alloc(es, nc, f"ob{i}", [128, 512], BF16), dma=True) for i in range(2)])
        ps_p = Ring([cx.buf(alloc(es, nc, f"psp{i}", [128, 512], F32, psum=True)) for i in range(2)])
        ps_ss = ps_p.bufs
        ps_s = Ring([cx.buf(alloc(es, nc, f"pst{i}", [128, 512], F32, psum=True)) for i in range(2)])
        ps_o_r = Ring([cx.buf(alloc(es, nc, f"pso{i}", [128, 512], F32, psum=True)) for i in range(2)])
        ps_r_r = Ring([cx.buf(alloc(es, nc, f"psr{i}", [128, 512], F32, psum=True)) for i in range(2)])
        blk = es.enter_context(nc.Block())

        cx.dma('pool', lambda e: e.dma_start(out=ones_b.t[:], in_=consts[:, C_ONES:C_ONES + 128]), writes=[ones_b], sembuf=ones_b)
        cx.dma('pool', lambda e: e.dma_start(out=id_b.t[:], in_=consts[:, C_ID:C_ID + 128]), writes=[id_b], sembuf=id_b)
        cx.dma('pool', lambda e: e.dma_start(out=mask.t[:], in_=maskd[:, :, :]), writes=[mask], sembuf=mask)
        cx.dma('sp', lambda e: e.dma_start(out=pr.t[:], in_=prm[:, :]), writes=[pr], sembuf=pr)
        cx.dma('pool', lambda e: e.dma_start(out=wqb.t[:], in_=wq.rearrange("(kc p) n -> p kc n", p=128)), writes=[wqb], sembuf=wqb)
        cx.dma('pool', lambda e: e.dma_start(out=wkvb.t[:], in_=wkv.rearrange("(kc p) n -> p kc n", p=128)), writes=[wkvb], sembuf=wkvb)

        def norm_into(dst, row0, nch, gcol0, t0):
            for half in range(S // T):
                tt0 = t0 + half * T
                ld = lambda kc, hb, tt0=tt0: (lambda e: e.dma_start(out=hb.t[:, :T], in_=pT[row0 + kc * 128:row0 + (kc + 1) * 128, tt0:tt0 + T]))
                rms_stats(cx, nc, ld, nch, T, hring, sqring, ps_ss, ones_f, nch * 128, eps_t, rstd, rstd)
                for kc in range(nch):
                    hb = hring.next()
                    cx.dma('sp', ld(kc, hb), writes=[hb], sembuf=hb)
                    cx.op('dve', lambda e, hb=hb, kc=kc, half=half: nc.vector.scalar_tensor_tensor(
                        out=dst.t[:, kc, half * T:(half + 1) * T], in0=hb.t[:, :T], scalar=pr.t[:, gcol0 + kc:gcol0 + kc + 1],
                        in1=rstd.t[:, :T], op0=ALU.mult, op1=ALU.mult), reads=[hb, pr, rstd], writes=[dst])

        def rope(dst, x_ap, xs_ap, sl, xbufs):
            cx.op('dve', lambda e: nc.vector.tensor_tensor(out=xt.t[:, :sl.stop - sl.start], in0=xs_ap, in1=ss.t[:, sl], op=ALU.mult),
                  reads=xbufs + [ss], writes=[xt])
            cx.op('dve', lambda e: nc.vector.tensor_tensor(out=rt2.t[0:64, :sl.stop - sl.start], in0=x_ap, in1=cc.t[:, sl], op=ALU.mult),
                  reads=xbufs + [cc], writes=[rt2])
            cx.op('dve', lambda e: nc.vector.tensor_tensor(out=dst.t[0:64, sl], in0=rt2.t[0:64, :sl.stop - sl.start],
                                                           in1=xt.t[:, :sl.stop - sl.start], op=ALU.add),
                  reads=[rt2, xt], writes=[dst])

        def seq_body(b):
            t0 = b * S
            cx.dma('sp', lambda e: e.dma_start(out=cc.t[:], in_=cs[b, 0, :, :]), writes=[cc], sembuf=cc)
            cx.dma('sp', lambda e: e.dma_start(out=ss.t[:], in_=cs[b, 1, :, :]), writes=[ss], sembuf=ss)
            cx.dma('sp', lambda e: e.dma_start(out=kx.t[:], in_=pT[768:832, t0:t0 + S]), writes=[kx], sembuf=kx)
            cx.dma('sp', lambda e: e.dma_start(out=kxs.t[:], in_=pT[R_KRS:R_KRS + 64, t0:t0 + S]), writes=[kxs], sembuf=kxs)
            norm_into(cqn, 0, 4, 0, t0)
            norm_into(ckvn, 512, 2, 4, t0)
            for sub in range(4):
                sl = slice(sub * 512, (sub + 1) * 512)
                rope(k_r, kx.t[:, sl], kxs.t[:, sl], sl, [kx, kxs])
            first = proj_gen(0, t0, hsets[0])
            for _ in first:
                pass
            for h in range(8):
                nxt = proj_gen(h + 1, t0, hsets[(h + 1) % 2]) if h < 7 else None
                attn(h, t0, hsets[h % 2], nxt)
                if nxt is not None:
                    for _ in nxt:
                        pass

        def proj_gen(h, t0, hs):
            q_n, q_r, k_n, V = hs["q_n"], hs["q_r"], hs["k_n"], hs["V"]
            if True:
                c0 = h * 256
                for sub in range(4):
                    sl = slice(sub * 512, (sub + 1) * 512)
                    p = ps_p.next()
                    mm_group(cx, p, [(wqb.t[:, kc, c0:c0 + 128], cqn.t[:, kc, sl]) for kc in range(4)], reads=[wqb, cqn])
                    cx.op('act', lambda e, p=p, sl=sl: nc.scalar.copy(q_n.t[:, sl], p.t[:]), reads=[p], writes=[q_n])
                    yield
                    p = ps_p.next()
                    mm_group(cx, p, [(wkvb.t[:, kc, c0:c0 + 128], ckvn.t[:, kc, sl]) for kc in range(2)], reads=[wkvb, ckvn])
                    cx.op('act', lambda e, p=p, sl=sl: nc.scalar.copy(k_n.t[:, sl], p.t[:]), reads=[p], writes=[k_n])
                    yield
                    p1 = ps_p.next()
                    mm_group(cx, p1, [(wqb.t[:, kc, c0 + 128:c0 + 192], cqn.t[:, kc, sl]) for kc in range(4)], reads=[wqb, cqn],
                             out_ap=p1.t[0:64, :])
                    p2 = ps_p.next()
                    mm_group(cx, p2, [(wqb.t[:, kc, c0 + 192:c0 + 256], cqn.t[:, kc, sl]) for kc in range(4)], reads=[wqb, cqn],
                             out_ap=p2.t[0:64, :])
                    rope(q_r, p1.t[0:64, :], p2.t[0:64, :], sl, [p1, p2])
                    yield
                    p = ps_p.next()

                    def vfn(eng, p=p, sub=sub):
                        ins = None
                        for j in range(4):
                            tb = sub * 4 + j
                            for kc in range(2):
                                ins = nc.tensor.matmul(p.t[:, j * 128:(j + 1) * 128], ckvn.t[:, kc, tb * 128:(tb + 1) * 128],
                                                       wkvb.t[:, kc, c0 + 128:c0 + 256], start=(kc == 0), stop=(kc == 1))
                        return ins
                    cx.op('pe', vfn, reads=[ckvn, wkvb], writes=[p])
                    cx.op('act', lambda e, p=p, sub=sub: nc.scalar.copy(
                        V.t[:, sub * 4:(sub + 1) * 4, :], p.t[:].rearrange("p (a b) -> p a b", a=4)), reads=[p], writes=[V])
                    yield

        def attn(h, t0, hs, nxt):
            q_n, q_r, k_n, V = hs["q_n"], hs["q_r"], hs["k_n"], hs["V"]
            if True:
                items = [(qt, kb) for qt in range(4) for kb in range(4 * (qt + 1))]
                acc = {}

                def emit_st(qt, kb):
                    qs = slice(qt * 512, (qt + 1) * 512)
                    ks = slice(kb * 128, (kb + 1) * 128)
                    st = ps_s.next()
                    pairs = [(k_n.t[:, ks], q_n.t[:, qs]), (k_r.t[0:64, ks], q_r.t[0:64, qs])]
                    rds = [k_n, q_n, k_r, q_r]
                    if kb >= 4 * qt:
                        pairs.append((id_b.t[:], mask.t[:, kb - 4 * qt, :]))
                        rds += [id_b, mask]
                    mm_group(cx, st, pairs, reads=rds)
                    pt = PT.next()
                    cx.op('act', lambda e, st=st, pt=pt: nc.scalar.activation(out=pt.t[:], in_=st.t[:], func=AF.Exp, scale=scale),
                          reads=[st], writes=[pt])
                    return pt

                def emit_pv(qt, kb, pt):
                    nkb = 4 * (qt + 1)
                    if kb == 0:
                        acc[qt] = (ps_o_r.next(), ps_r_r.next())
                    ps_o, ps_r = acc[qt]

                    def pv(eng):
                        nc.tensor.matmul(ps_o.t[:], V.t[:, kb, :], pt.t[:], start=(kb == 0), stop=(kb == nkb - 1))
                        return nc.tensor.matmul(ps_r.t[:], ones_b.t[:], pt.t[:], start=(kb == 0), stop=(kb == nkb - 1))
                    cx.op('pe', pv, reads=[pt, V, ones_b], writes=[ps_o, ps_r])
                    if kb == nkb - 1:
                        finalize(qt, ps_o, ps_r)

                def finalize(qt, ps_o, ps_r):
                    cx.op('dve', lambda e: nc.vector.reciprocal(rinv.t[:], ps_r.t[:]), reads=[ps_r], writes=[rinv])
                    cx.op('dve', lambda e: nc.vector.tensor_tensor(out=yv.t[:], in0=ps_o.t[:], in1=rinv.t[:], op=ALU.mult),
                          reads=[ps_o, rinv], writes=[yv])
                    cx.op('act', lambda e: nc.scalar.activation(out=ysq.t[:], in_=yv.t[:], func=AF.Square), reads=[yv], writes=[ysq])
                    pm = ps_p.next()
                    cx.op('pe', lambda e: nc.tensor.matmul(pm.t[:], ones_f.t[:], ysq.t[:], start=True, stop=True),
                          reads=[ones_f, ysq], writes=[pm])
                    cx.op('act', lambda e: nc.scalar.activation(out=sd2.t[:], in_=pm.t[:], func=AF.Sqrt,
                                                                bias=eps_t.t[:, 0:1], scale=1.0 / 128),
                          reads=[pm, eps_t], writes=[sd2])
                    cx.op('dve', lambda e: nc.vector.reciprocal(sd2.t[:], sd2.t[:]), reads=[sd2], writes=[sd2])
                    o = obr.next()
                    cx.op('dve', lambda e: nc.vector.scalar_tensor_tensor(
                        out=o.t[:], in0=yv.t[:], scalar=pr.t[:, 6 + h:7 + h], in1=sd2.t[:], op0=ALU.mult, op1=ALU.mult),
                        reads=[yv, pr, sd2], writes=[o])
                    cx.dma('sp', lambda e: e.dma_start(
                        out=yT[h * 128:(h + 1) * 128, t0 + qt * 512:t0 + (qt + 1) * 512], in_=o.t[:]), reads=[o], sembuf=o)

                pending = None
                for ii, (qt, kb) in enumerate(items):
                    pt = emit_st(qt, kb)
                    if pending is not None:
                        emit_pv(*pending)
                    pending = (qt, kb, pt)
                    if nxt is not None and ii % 2 == 1:
                        next(nxt, None)
                emit_pv(*pending)
        for b in range(NSEQ):
            seq_body(b)
        cx.end_stage(blk)


def stage_rwkv(cx, pT, rp, dup, iup, gup, consts, yT):
    nc = cx.nc
    TB = 128
    c1 = -0.6065306597126334
    with contextlib.ExitStack() as es:
        cx.begin_stage()

        def sb(name, shape, dt=F32, dma=False):
            return cx.buf(alloc(es, nc, name, shape, dt), dma=dma)
        cst = sb("cst", [64, 1024], dma=True)
        rpt = sb("rpt", [128, 96], dma=True)
        dupt = sb("dupt", [32, 512], dma=True)
        iupt = sb("iupt", [32, 512], dma=True)
        gupt = sb("gupt", [96, 512], dma=True)
        omka = sb("omka", [64, 8])
        cur3 = sb("cur3", [64, 3, 8, TB], dma=True)
        prv3 = sb("prv3", [64, 3, 8, TB], dma=True)
        loc = sb("loc", [96, 3, TB], dma=True)
        lop = sb("lop", [96, 3, TB], dma=True)
        names = ["sw", "a", "g", "cum", "E1", "E2", "E3", "E4", "kk", "nr", "tmp", "kp", "bv", "aT", "bb", "bT", "bhT", "kT", "khT", "rT"]
        Tt = {n: sb(n, [64, 8, TB]) for n in names}
        cn = ["V", "Bh", "Kh", "N", "L", "AKu", "BRu", "KRu", "P", "Q", "Na", "La", "Nb", "Lb"]
        Cs = [{n: sb(n + str(c), [64, 8, 64]) for n in cn} for c in range(TB // 64)]
        Ct = {n: sb(n, [64, 8, 64]) for n in ["Zs", "Us", "H", "tH", "ysb", "yc", "sq", "sdv"]}
        obr = Ring([sb(f"ob{i}", [64, 8, 64], BF16, dma=True) for i in range(2)])
        pp = Ring([cx.buf(alloc(es, nc, f"pp{i}", [128, 512], F32, psum=True)) for i in range(8)])
        blk = es.enter_context(nc.Block())

        cx.dma('sp', lambda e: e.dma_start(out=cst.t[:], in_=consts[0:64, :]), writes=[cst], sembuf=cst)
        cx.dma('sp', lambda e: e.dma_start(out=rpt.t[:], in_=rp[:, :]), writes=[rpt], sembuf=rpt)
        cx.dma('sp', lambda e: e.dma_start(out=dupt.t[:], in_=dup[:, :]), writes=[dupt], sembuf=dupt)
        cx.dma('sp', lambda e: e.dma_start(out=iupt.t[:], in_=iup[:, :]), writes=[iupt], sembuf=iupt)
        cx.dma('sp', lambda e: e.dma_start(out=gupt.t[:], in_=gup[:, :]), writes=[gupt], sembuf=gupt)
        ones64 = cst.t[:, C_ONES:C_ONES + 64]
        id64 = cst.t[:, C_ID:C_ID + 64]
        eps64 = cst.t[:, C_EPS + 1:C_EPS + 2]

        def mk(c):
            return cst.t[:, c:c + 64].unsqueeze(1).broadcast_to([64, 8, 64])
        SUb, SLb, IUb, IDb = mk(C_SU), mk(C_SL), mk(C_IU), mk(C_ID)
        rmask = cst.t[:, C_RM:C_RM + TB]

        def pb(col):
            return rpt.t[0:64, col:col + 8]

        def bc(ap2, n):
            return ap2.unsqueeze(2).broadcast_to([64, 8, n])

        def TTo(o, oap, ins, op, eng='dve'):
            bufs = [x[0] for x in ins]
            aps = [x[1] for x in ins]
            cx.op(eng, lambda e: nc.vector.tensor_tensor(out=oap, in0=aps[0], in1=aps[1], op=op),
                  reads=[b for b in bufs if b is not None], writes=[o])

        def ACTo(o, oap, i, iap, func, scale=1.0, bias=None, extra=()):
            def fn(e):
                if bias is None:
                    return nc.scalar.activation(out=oap, in_=iap, func=func, scale=scale)
                return nc.scalar.activation(out=oap, in_=iap, func=func, scale=scale, bias=bias)
            cx.op('act', fn, reads=[i] + list(extra), writes=[o])

        for bt in (loc, lop):
            cx.op('dve', lambda e, bt=bt: nc.vector.memset(bt.t[:], 0.0), writes=[bt])
        cx.op('dve', lambda e: nc.vector.tensor_scalar(out=omka.t[:], in0=pb(48), scalar1=-1.0, scalar2=1.0,
                                                       op0=ALU.mult, op1=ALU.add), reads=[rpt], writes=[omka])

        def rows3(t_lo, t_hi):
            return [pT[B0 + kd * 512:B0 + (kd + 1) * 512, t_lo:t_hi].rearrange("(h f) t -> f h t", f=64) for kd in range(3)]

        def block_body(b, k):
            t0 = b * S + k * TB
            T = Tt
            for kd, src in enumerate(rows3(t0, t0 + TB)):
                cx.dma('sp', lambda e, kd=kd, src=src: e.dma_start(out=cur3.t[:, kd], in_=src), writes=[cur3], sembuf=cur3)
            cx.dma('sp', lambda e: e.dma_start(out=loc.t[0:32, 0, :], in_=pT[B0 + 1536:B0 + 1568, t0:t0 + TB]), writes=[loc], sembuf=loc)
            cx.dma('sp', lambda e: e.dma_start(out=loc.t[0:32, 1, :], in_=pT[B0 + 1568:B0 + 1600, t0:t0 + TB]), writes=[loc], sembuf=loc)
            cx.dma('sp', lambda e: e.dma_start(out=loc.t[0:96, 2, :], in_=pT[B0 + 1600:B0 + 1696, t0:t0 + TB]), writes=[loc], sembuf=loc)
            if k == 0:
                cx.op('dve', lambda e: nc.vector.memset(prv3.t[:, :, :, 0:1], 0.0), writes=[prv3])
                cx.op('dve', lambda e: nc.vector.memset(lop.t[:, :, 0:1], 0.0), writes=[lop])
                cx.op('dve', lambda e: nc.vector.memset(Ct["H"].t[:], 0.0), writes=[Ct["H"]])
                for kd, src in enumerate(rows3(t0, t0 + TB - 1)):
                    cx.dma('sp', lambda e, kd=kd, src=src: e.dma_start(out=prv3.t[:, kd, :, 1:TB], in_=src), writes=[prv3], sembuf=prv3)
                o1, lo_, hi_ = 1, t0, t0 + TB - 1
            else:
                for kd, src in enumerate(rows3(t0 - 1, t0 + TB - 1)):
                    cx.dma('sp', lambda e, kd=kd, src=src: e.dma_start(out=prv3.t[:, kd], in_=src), writes=[prv3], sembuf=prv3)
                o1, lo_, hi_ = 0, t0 - 1, t0 + TB - 1
            cx.dma('sp', lambda e: e.dma_start(out=lop.t[0:32, 0, o1:TB], in_=pT[B0 + 1536:B0 + 1568, lo_:hi_]), writes=[lop], sembuf=lop)
            cx.dma('sp', lambda e: e.dma_start(out=lop.t[0:32, 1, o1:TB], in_=pT[B0 + 1568:B0 + 1600, lo_:hi_]), writes=[lop], sembuf=lop)
            cx.dma('sp', lambda e: e.dma_start(out=lop.t[0:96, 2, o1:TB], in_=pT[B0 + 1600:B0 + 1696, lo_:hi_]), writes=[lop], sembuf=lop)
            c3 = cur3.t[:].rearrange("p a h t -> p (a h) t")
            p3 = prv3.t[:].rearrange("p a h t -> p (a h) t")
            mu3 = rpt.t[0:64, 0:24].unsqueeze(2).broadcast_to([64, 24, TB])
            TTo(prv3, p3, [(prv3, p3), (cur3, c3)], ALU.subtract)
            TTo(prv3, p3, [(prv3, p3), (rpt, mu3)], ALU.mult)
            TTo(prv3, p3, [(prv3, p3), (cur3, c3)], ALU.add)
            mul = rpt.t[0:96, 80:83].unsqueeze(2).broadcast_to([96, 3, TB])
            TTo(lop, lop.t[:], [(lop, lop.t[:]), (loc, loc.t[:])], ALU.subtract)
            TTo(lop, lop.t[:], [(lop, lop.t[:]), (rpt, mul)], ALU.mult)
            TTo(lop, lop.t[:], [(lop, lop.t[:]), (loc, loc.t[:])], ALU.add)
            rs_, ks_, vs_ = prv3.t[:, 0], prv3.t[:, 1], prv3.t[:, 2]
            ACTo(lop, lop.t[0:32, 0, :], lop, lop.t[0:32, 0, :], AF.Tanh)
            ACTo(lop, lop.t[0:96, 2, :], lop, lop.t[0:96, 2, :], AF.Sigmoid)
            for (dst, wt, kdim, li, bcol) in ((T["sw"], dupt, 32, 0, 24), (T["a"], iupt, 32, 1, 32), (T["g"], gupt, 96, 2, None)):
                for half in range(2):
                    p = pp.next()

                    def fn(e, p=p, wt=wt, kdim=kdim, li=li, half=half):
                        ins = None
                        for hh in range(4):
                            h = half * 4 + hh
                            ins = nc.tensor.matmul(p.t[0:64, hh * TB:(hh + 1) * TB], wt.t[0:kdim, h * 64:(h + 1) * 64],
                                                   lop.t[0:kdim, li, :], start=True, stop=True)
                        return ins
                    cx.op('pe', fn, reads=[wt, lop], writes=[p])
                    dap = dst.t[:, half * 4:(half + 1) * 4, :]
                    pap = p.t[0:64, :].rearrange("p (h t) -> p h t", h=4)
                    if bcol is None:
                        ACTo(dst, dap, p, pap, AF.Copy)
                    else:
                        bb_ = rpt.t[0:64, bcol + half * 4:bcol + half * 4 + 4].unsqueeze(2).broadcast_to([64, 4, TB])
                        TTo(dst, dap, [(p, pap), (rpt, bb_)], ALU.add)
                if bcol is not None:
                    ACTo(dst, dst.t[:], dst, dst.t[:], AF.Sigmoid)
            for h in range(8):
                cx.op('dve', lambda e, h=h: nc.vector.tensor_tensor_scan(
                    out=T["cum"].t[:, h, :], data0=rmask, data1=T["sw"].t[:, h, :], initial=0.0, op0=ALU.mult, op1=ALU.add),
                    reads=[cst, T["sw"]], writes=[T["cum"]])
            ACTo(T["E1"], T["E1"].t[:], T["cum"], T["cum"].t[:], AF.Exp, scale=c1)
            ACTo(T["E2"], T["E2"].t[:], T["cum"], T["cum"].t[:], AF.Exp, scale=-c1)
            TTo(T["E3"], T["E3"].t[:], [(T["cum"], T["cum"].t[:]), (T["sw"], T["sw"].t[:])], ALU.subtract)
            ACTo(T["E3"], T["E3"].t[:], T["E3"], T["E3"].t[:], AF.Exp, scale=c1)
            cum4 = T["cum"].t[:].rearrange("p h (c t) -> p h c t", t=64)
            cumC = cum4[:, :, :, 63:64].broadcast_to([64, 8, TB // 64, 64])
            e44 = T["E4"].t[:].rearrange("p h (c t) -> p h c t", t=64)
            TTo(T["E4"], e44, [(T["cum"], cumC), (T["cum"], cum4)], ALU.subtract)
            ACTo(T["E4"], T["E4"].t[:], T["E4"], T["E4"].t[:], AF.Exp, scale=c1)
            TTo(T["kk"], T["kk"].t[:], [(prv3, ks_), (rpt, bc(pb(40), TB))], ALU.mult)
            ACTo(T["nr"], T["nr"].t[:], T["kk"], T["kk"].t[:], AF.Square)
            for half in range(2):
                p = pp.next()
                hs = slice(half * 4, (half + 1) * 4)
                cx.op('pe', lambda e, p=p, hs=hs: nc.tensor.matmul(p.t[0:64, :], ones64, T["nr"].t[:, hs, :], start=True, stop=True),
                      reads=[cst, T["nr"]], writes=[p])
                ACTo(T["tmp"], T["tmp"].t[:, hs, :], p, p.t[0:64, :].rearrange("p (h t) -> p h t", h=4), AF.Sqrt)
            cx.op('dve', lambda e: nc.vector.tensor_scalar(out=T["tmp"].t[:], in0=T["tmp"].t[:], scalar1=1e-12, scalar2=None, op0=ALU.max),
                  reads=[T["tmp"]], writes=[T["tmp"]])
            cx.op('dve', lambda e: nc.vector.reciprocal(T["tmp"].t[:], T["tmp"].t[:]), reads=[T["tmp"]], writes=[T["tmp"]])
            TTo(T["kk"], T["kk"].t[:], [(T["kk"], T["kk"].t[:]), (T["tmp"], T["tmp"].t[:])], ALU.mult)
            TTo(T["tmp"], T["tmp"].t[:], [(T["a"], T["a"].t[:]), (rpt, bc(pb(48), TB))], ALU.mult)
            TTo(T["tmp"], T["tmp"].t[:], [(T["tmp"], T["tmp"].t[:]), (omka, bc(omka.t[:], TB))], ALU.add)
            TTo(T["kp"], T["kp"].t[:], [(prv3, ks_), (T["tmp"], T["tmp"].t[:])], ALU.mult)
            TTo(T["tmp"], T["tmp"].t[:], [(prv3, rs_), (T["kp"], T["kp"].t[:])], ALU.mult)
            TTo(T["tmp"], T["tmp"].t[:], [(T["tmp"], T["tmp"].t[:]), (rpt, bc(pb(56), TB))], ALU.mult)
            for half in range(2):
                p = pp.next()
                hs = slice(half * 4, (half + 1) * 4)
                cx.op('pe', lambda e, p=p, hs=hs: nc.tensor.matmul(p.t[0:64, :], ones64, T["tmp"].t[:, hs, :], start=True, stop=True),
                      reads=[cst, T["tmp"]], writes=[p])
                TTo(T["bv"], T["bv"].t[:, hs, :], [(p, p.t[0:64, :].rearrange("p (h t) -> p h t", h=4)), (prv3, prv3.t[:, 2, hs, :])], ALU.mult)
            cx.op('dve', lambda e: nc.vector.scalar_tensor_tensor(out=T["aT"].t[:], in0=T["kk"].t[:], scalar=-1.0, in1=T["E3"].t[:],
                                                                  op0=ALU.mult, op1=ALU.mult), reads=[T["kk"], T["E3"]], writes=[T["aT"]])
            TTo(T["bb"], T["bb"].t[:], [(T["kk"], T["kk"].t[:]), (T["a"], T["a"].t[:])], ALU.mult)
            TTo(T["bT"], T["bT"].t[:], [(T["bb"], T["bb"].t[:]), (T["E2"], T["E2"].t[:])], ALU.mult)
            TTo(T["bhT"], T["bhT"].t[:], [(T["bb"], T["bb"].t[:]), (T["E4"], T["E4"].t[:])], ALU.mult)
            TTo(T["kT"], T["kT"].t[:], [(T["kp"], T["kp"].t[:]), (T["E2"], T["E2"].t[:])], ALU.mult)
            TTo(T["khT"], T["khT"].t[:], [(T["kp"], T["kp"].t[:]), (T["E4"], T["E4"].t[:])], ALU.mult)
            TTo(T["rT"], T["rT"].t[:], [(prv3, rs_), (T["E1"], T["E1"].t[:])], ALU.mult)
            run_chunks(b, k)

        def pgroup(builder, reads):
            p = pp.next()

            def fn(e, p=p):
                ins = None
                for h in range(8):
                    ins = builder(p.t[0:64, h * 64:(h + 1) * 64], h)
                return ins
            cx.op('pe', fn, reads=reads, writes=[p])
            return p, p.t[0:64, :].rearrange("p (h t) -> p h t", h=8)

        def phase_a(b, k, c):
            T, C = Tt, Cs[c]
            cs_ = slice(c * 64, (c + 1) * 64)
            for dst, src_b, src_ap in ((C["V"], prv3, prv3.t[:, 2]), (C["Bh"], T["bhT"], T["bhT"].t[:]), (C["Kh"], T["khT"], T["khT"].t[:])):
                p, pv = pgroup(lambda o, h, src_ap=src_ap: nc.tensor.transpose(o, src_ap[:, h, cs_], id64), [src_b, cst])
                ACTo(dst, dst.t[:], p, pv, AF.Copy)
            yield

            def pw(dst, lt, rt, mb):
                p, pv = pgroup(lambda o, h: nc.tensor.matmul(o, lt.t[:, h, cs_], rt.t[:, h, cs_], start=True, stop=True), [lt, rt])
                TTo(dst, dst.t[:], [(p, pv), (cst, mb)], ALU.mult)
            pw(C["N"], T["bT"], T["aT"], SUb)
            pw(C["L"], T["aT"], T["bT"], SLb)
            yield
            pw(C["AKu"], T["kT"], T["aT"], SUb)
            pw(C["BRu"], T["bT"], T["rT"], IUb)
            pw(C["KRu"], T["kT"], T["rT"], IUb)
            TTo(C["P"], C["P"].t[:], [(C["N"], C["N"].t[:]), (cst, IDb)], ALU.add)
            TTo(C["Q"], C["Q"].t[:], [(C["L"], C["L"].t[:]), (cst, IDb)], ALU.add)
            yield
            Nc, Lc = C["N"], C["L"]
            nxt = [(C["Na"], C["La"]), (C["Nb"], C["Lb"])]
            for lvl in range(5):
                Nn, Ln = nxt[lvl % 2]
                p, pv = pgroup(lambda o, h, Nc=Nc, Lc=Lc: nc.tensor.matmul(o, Lc.t[:, h, :], Nc.t[:, h, :], start=True, stop=True), [Nc, Lc])
                if lvl < 4:
                    p2, pv2 = pgroup(lambda o, h, Nc=Nc, Lc=Lc: nc.tensor.matmul(o, Nc.t[:, h, :], Lc.t[:, h, :], start=True, stop=True), [Nc, Lc])
                ACTo(Nn, Nn.t[:], p, pv, AF.Copy)
                if lvl < 4:
                    cx.op('dve', lambda e, Ln=Ln, pv2=pv2: nc.vector.tensor_copy(Ln.t[:], pv2), reads=[p2], writes=[Ln])
                yield
                p, pv = pgroup(lambda o, h, Nn=Nn: nc.tensor.matmul(o, C["Q"].t[:, h, :], Nn.t[:, h, :], start=True, stop=True), [C["Q"], Nn])
                if lvl < 4:
                    p2, pv2 = pgroup(lambda o, h, Ln=Ln: nc.tensor.matmul(o, C["P"].t[:, h, :], Ln.t[:, h, :], start=True, stop=True), [C["P"], Ln])
                TTo(C["P"], C["P"].t[:], [(C["P"], C["P"].t[:]), (p, pv)], ALU.add)
                if lvl < 4:
                    TTo(C["Q"], C["Q"].t[:], [(C["Q"], C["Q"].t[:]), (p2, pv2)], ALU.add)
                Nc, Lc = Nn, Ln
                yield

        def phase_b(b, k, c):
            T, C = Tt, Cs[c]
            cs_ = slice(c * 64, (c + 1) * 64)
            tc0 = b * S + k * TB + c * 64
            H = Ct["H"]

            def zb(o, h):
                nc.tensor.matmul(o, T["aT"].t[:, h, cs_], H.t[:, h, :], start=True, stop=False)
                return nc.tensor.matmul(o, C["AKu"].t[:, h, :], C["V"].t[:, h, :], start=False, stop=True)
            p, pv = pgroup(zb, [T["aT"], H, C["AKu"], C["V"]])
            ACTo(Ct["Zs"], Ct["Zs"].t[:], p, pv, AF.Copy)
            p, pv = pgroup(lambda o, h: nc.tensor.matmul(o, C["P"].t[:, h, :], Ct["Zs"].t[:, h, :], start=True, stop=True), [C["P"], Ct["Zs"]])
            cx.op('dve', lambda e, pv=pv: nc.vector.tensor_copy(Ct["Us"].t[:], pv), reads=[p], writes=[Ct["Us"]])

            def yb_(o, h):
                nc.tensor.matmul(o, H.t[:, h, :], T["rT"].t[:, h, cs_], start=True, stop=False)
                nc.tensor.matmul(o, Ct["Us"].t[:, h, :], C["BRu"].t[:, h, :], start=False, stop=False)
                return nc.tensor.matmul(o, C["V"].t[:, h, :], C["KRu"].t[:, h, :], start=False, stop=True)

            def hb_(o, h):
                nc.tensor.matmul(o, C["Bh"].t[:, h, :], Ct["Us"].t[:, h, :], start=True, stop=False)
                return nc.tensor.matmul(o, C["Kh"].t[:, h, :], C["V"].t[:, h, :], start=False, stop=True)
            ph, phv = pgroup(hb_, [C["Bh"], Ct["Us"], C["Kh"], C["V"]])
            py, pyv = pgroup(yb_, [H, T["rT"], Ct["Us"], C["BRu"], C["V"], C["KRu"]])
            gC = T["E1"].t[:, :, c * 64 + 63:c * 64 + 64].broadcast_to([64, 8, 64])
            TTo(Ct["tH"], Ct["tH"].t[:], [(H, H.t[:]), (T["E1"], gC)], ALU.mult)
            TTo(H, H.t[:], [(Ct["tH"], Ct["tH"].t[:]), (ph, phv)], ALU.add)
            Cc = Ct
            ACTo(Cc["ysb"], Cc["ysb"].t[:], py, pyv, AF.Copy)
            ysf = Cc["ysb"].t[:].rearrange("p h t -> p (h t)")
            pm = pp.next()
            cx.op('pe', lambda e, pm=pm: nc.tensor.matmul(pm.t[0:64, :], ones64, ysf, start=True, stop=True), reads=[cst, Cc["ysb"]], writes=[pm])
            cx.op('dve', lambda e, pm=pm: nc.vector.scalar_tensor_tensor(
                out=Cc["yc"].t[:].rearrange("p h t -> p (h t)"), in0=pm.t[0:64, :], scalar=-1.0 / 64, in1=ysf, op0=ALU.mult, op1=ALU.add),
                reads=[pm, Cc["ysb"]], writes=[Cc["yc"]])
            ACTo(Cc["sq"], Cc["sq"].t[:], Cc["yc"], Cc["yc"].t[:], AF.Square)
            pv_ = pp.next()
            cx.op('pe', lambda e, pv_=pv_: nc.tensor.matmul(pv_.t[0:64, :], ones64, Cc["sq"].t[:].rearrange("p h t -> p (h t)"), start=True, stop=True),
                  reads=[cst, Cc["sq"]], writes=[pv_])
            ACTo(Cc["sdv"], Cc["sdv"].t[:].rearrange("p h t -> p (h t)"), pv_, pv_.t[0:64, :], AF.Sqrt, scale=1.0 / 64, bias=eps64, extra=[cst])
            cx.op('dve', lambda e: nc.vector.reciprocal(Cc["sdv"].t[:], Cc["sdv"].t[:]), reads=[Cc["sdv"]], writes=[Cc["sdv"]])
            yc = Cc["yc"]
            TTo(yc, yc.t[:], [(yc, yc.t[:]), (Cc["sdv"], Cc["sdv"].t[:])], ALU.mult)
            TTo(yc, yc.t[:], [(yc, yc.t[:]), (rpt, bc(pb(64), 64))], ALU.mult)
            TTo(yc, yc.t[:], [(yc, yc.t[:]), (rpt, bc(pb(72), 64))], ALU.add)
            TTo(yc, yc.t[:], [(yc, yc.t[:]), (T["bv"], T["bv"].t[:, :, cs_])], ALU.add)
            o = obr.next()
            TTo(o, o.t[:], [(yc, yc.t[:]), (T["g"], T["g"].t[:, :, cs_])], ALU.mult)
            cx.dma('sp', lambda e, o=o: e.dma_start(
                out=yT[1024:1536, tc0:tc0 + 64].rearrange("(h f) t -> f h t", f=64), in_=o.t[:]), reads=[o], sembuf=o)

        def run_chunks(b, k):
            gens = [phase_a(b, k, c) for c in range(TB // 64)]
            while gens:
                for g_ in list(gens):
                    try:
                        next(g_)
                    except StopIteration:
                        gens.remove(g_)
            for c in range(TB // 64):
                phase_b(b, k, c)

        for b in range(NSEQ):
            for k in range(S // TB):
                block_body(b, k)
        cx.end_stage(blk)


def stage_rope(cx, pos, invf, cs):
    nc = cx.nc
    PI = float(np.pi)
    with contextlib.ExitStack() as es:
        cx.begin_stage()
        pi_ = cx.buf(alloc(es, nc, "pi_", [32, S], I32), dma=True)
        fr = cx.buf(alloc(es, nc, "fr", [32, 1], F32), dma=True)
        ang = cx.buf(alloc(es, nc, "ang", [32, S], F32))
        tf = cx.buf(alloc(es, nc, "tf", [32, S], F32))
        ki = cx.buf(alloc(es, nc, "ki", [32, S], I32))
        r = cx.buf(alloc(es, nc, "r", [32, S], F32))
        m = cx.buf(alloc(es, nc, "m", [32, S], F32))
        outs = [cx.buf(alloc(es, nc, f"o{i}", [32, S], F32), dma=True) for i in range(3)]
        blk = es.enter_context(nc.Block())
        cx.dma('sp', lambda e: e.dma_start(out=fr.t[:], in_=invf[:, :]), writes=[fr], sembuf=fr)

        def wrap(buf):
            cx.op('dve', lambda e: nc.vector.tensor_scalar(out=m.t[:], in0=buf.t[:], scalar1=PI, scalar2=-2 * PI, op0=ALU.is_gt, op1=ALU.mult),
                  reads=[buf], writes=[m])
            cx.op('dve', lambda e: nc.vector.tensor_tensor(out=buf.t[:], in0=buf.t[:], in1=m.t[:], op=ALU.add), reads=[m], writes=[buf])
            cx.op('dve', lambda e: nc.vector.tensor_scalar(out=m.t[:], in0=buf.t[:], scalar1=-PI, scalar2=2 * PI, op0=ALU.is_lt, op1=ALU.mult),
                  reads=[buf], writes=[m])
            cx.op('dve', lambda e: nc.vector.tensor_tensor(out=buf.t[:], in0=buf.t[:], in1=m.t[:], op=ALU.add), reads=[m], writes=[buf])

        def body(b):
            cx.dma('sp', lambda e: e.dma_start(out=pi_.t[:], in_=pos[b:b + 1, :].broadcast_to([32, S])), writes=[pi_], sembuf=pi_)
            cx.op('dve', lambda e: nc.vector.tensor_copy(ang.t[:], pi_.t[:]), reads=[pi_], writes=[ang])
            cx.op('dve', lambda e: nc.vector.tensor_scalar(out=ang.t[:], in0=ang.t[:], scalar1=fr.t[:, 0:1], scalar2=None, op0=ALU.mult),
                  reads=[fr], writes=[ang])
            cx.op('dve', lambda e: nc.vector.tensor_scalar(out=tf.t[:], in0=ang.t[:], scalar1=1.0 / (2 * PI), scalar2=None, op0=ALU.mult),
                  reads=[ang], writes=[tf])
            cx.op('dve', lambda e: nc.vector.tensor_copy(ki.t[:], tf.t[:]), reads=[tf], writes=[ki])
            cx.op('dve', lambda e: nc.vector.tensor_copy(tf.t[:], ki.t[:]), reads=[ki], writes=[tf])
            cx.op('dve', lambda e: nc.vector.scalar_tensor_tensor(out=r.t[:], in0=tf.t[:], scalar=-2 * PI, in1=ang.t[:], op0=ALU.mult, op1=ALU.add),
                  reads=[tf, ang], writes=[r])
            wrap(r)
            cx.op('act', lambda e: nc.scalar.activation(out=outs[1].t[:], in_=r.t[:], func=AF.Sin), reads=[r], writes=[outs[1]])
            cx.op('act', lambda e: nc.scalar.activation(out=outs[2].t[:], in_=r.t[:], func=AF.Sin, scale=-1.0), reads=[r], writes=[outs[2]])
            cx.op('dve', lambda e: nc.vector.tensor_scalar(out=r.t[:], in0=r.t[:], scalar1=PI / 2, scalar2=None, op0=ALU.add),
                  reads=[outs[1], outs[2]], writes=[r])
            wrap(r)
            cx.op('act', lambda e: nc.scalar.activation(out=outs[0].t[:], in_=r.t[:], func=AF.Sin), reads=[r], writes=[outs[0]])
            for (src, j, r0) in ((outs[0], 0, 0), (outs[0], 0, 32), (outs[2], 1, 0), (outs[1], 1, 32)):
                cx.dma('sp', lambda e, src=src, j=j, r0=r0: e.dma_start(out=cs[b, j, r0:r0 + 32, :], in_=src.t[:]), reads=[src], sembuf=src)
        for b in range(NSEQ):
            body(b)
        cx.end_stage(blk)


NPAD = 4224
BIGW = ["ffn1_gate", "ffn1_up", "ffn1_down", "ffn2_gate", "ffn2_up", "ffn2_down", "w_out", "w_ukv",
        "decay_up", "iclr_up", "gate_up"]
BIGW_SHAPES = {"ffn1_gate": [D, DFF], "ffn1_up": [D, DFF], "ffn1_down": [DFF, D], "ffn2_gate": [D, DFF],
               "ffn2_up": [D, DFF], "ffn2_down": [DFF, D], "w_out": [D, D], "w_ukv": [256, 2048],
               "decay_up": [32, 512], "iclr_up": [32, 512], "gate_up": [96, 512]}


def build_program(nlayers=DEPTH, dbg=False):
    nc = bass.Bass("TRN2", target_bir_lowering=False)
    dt = lambda n, s, d=F32, kind="ExternalInput": nc.dram_tensor(n, s, d, kind=kind).ap()
    x = dt("x", [NT, D])
    pos = dt("pos", [NSEQ, S], I32)
    W = {n: dt(n, [DEPTH] + BIGW_SHAPES[n]) for n in BIGW}
    w_inx = dt("w_inx", [DEPTH, D, NPAD])
    wq = dt("wq", [DEPTH, 512, 2048])
    spk = dt("spk", [DEPTH, 128, 192])
    gfin = dt("gfin", [128, KC])
    consts = dt("consts", [128, 1024])
    maskd = dt("maskd", [128, 4, 512])
    invf = dt("invf", [32, 1])
    out = dt("out", [NT, D], kind="ExternalOutput")
    sk = "ExternalOutput" if dbg else "Internal"
    hT = dt("hT", [KC, 128, NT], kind=sk)
    pT = dt("pT", [NPAD, NT], kind=sk)
    yT = dt("yT", [D, NT], BF16, kind=sk)
    cs = dt("cs", [NSEQ, 2, 64, S], kind=sk)
    with contextlib.ExitStack() as es:
        sems = [es.enter_context(nc.semaphore(f"s{i}")) for i in range(96)]
        cx = Ctx(nc, sems)
        stage_rope(cx, pos, invf, cs)
        stage_in(cx, x, hT, consts[:, C_ID:C_ID + 128])
        for l in range(nlayers):
            sp = spk[l]
            stage_ffn(cx, hT, W["ffn1_gate"][l], W["ffn1_up"][l], W["ffn1_down"][l], sp[:, 0:16], consts)
            stage_proj(cx, hT, w_inx[l], sp[:, 16:32], consts, pT, NPAD)
            stage_mla(cx, pT, wq[l], W["w_ukv"][l], sp[:, 48:64], cs, consts, maskd, yT)
            stage_rwkv(cx, pT, sp[:, 64:160], W["decay_up"][l], W["iclr_up"][l], W["gate_up"][l], consts, yT)
            stage_conv(cx, pT, sp[:, 160:176].rearrange("p (a b) -> p a b", a=4), consts, yT)
            stage_wout(cx, hT, yT, W["w_out"][l])
            stage_ffn(cx, hT, W["ffn2_gate"][l], W["ffn2_up"][l], W["ffn2_down"][l], sp[:, 32:48], consts)
        stage_out(cx, hT, gfin, consts, consts[:, C_ID:C_ID + 128], out)
    return nc


def _fm(v, nch):
    return np.ascontiguousarray(np.asarray(v, np.float32).reshape(nch, 128).T)


def _hd(v):
    return np.ascontiguousarray(np.asarray(v, np.float32).reshape(8, 64).T)


def host_layout(inp):
    f32 = np.float32
    w_in = np.asarray(inp["w_in"], f32)
    w_inx = np.zeros((DEPTH, D, NPAD), f32)
    w_inx[:, :, :NIN] = w_in
    w_inx[:, :, NIN:NIN + 32] = w_in[:, :, 800:832]
    w_inx[:, :, NIN + 32:NIN + 64] = w_in[:, :, 768:800]
    w_uq = np.asarray(inp["w_uq"], f32).reshape(DEPTH, 512, 8, 192)
    wq = np.concatenate([w_uq[..., :128], w_uq[..., 128:192], w_uq[..., 160:192], w_uq[..., 128:160]], axis=-1)
    wq = np.ascontiguousarray(wq.reshape(DEPTH, 512, 2048))
    spk = np.zeros((DEPTH, 128, 192), f32)
    for l in range(DEPTH):
        spk[l, :, 0:16] = _fm(inp["norm_ffn1"][l], 16)
        spk[l, :, 16:32] = _fm(inp["norm_mix"][l], 16)
        spk[l, :, 32:48] = _fm(inp["norm_ffn2"][l], 16)
        spk[l, :, 48:52] = _fm(inp["q_norm"][l], 4)
        spk[l, :, 52:54] = _fm(inp["kv_norm"][l], 2)
        spk[l, :, 54:62] = _fm(inp["attn_out_norm"][l], 8)
        rp = spk[l, :, 64:160]
        mu = np.asarray(inp["shift_mu"][l], f32)
        rp[0:64, 0:8] = _hd(mu[0:512])
        rp[0:64, 8:16] = _hd(mu[512:1024])
        rp[0:64, 16:24] = _hd(mu[1024:1536])
        rp[0:64, 24:32] = _hd(inp["decay_w0"][l])
        rp[0:64, 32:40] = _hd(inp["iclr_a0"][l])
        rp[0:64, 40:48] = _hd(inp["k_k"][l])
        rp[0:64, 48:56] = _hd(inp["k_a"][l])
        rp[0:64, 56:64] = _hd(np.asarray(inp["r_k"][l], f32).reshape(512))
        rp[0:64, 64:72] = _hd(inp["lnx_gain"][l])
        rp[0:64, 72:80] = _hd(inp["lnx_bias"][l])
        rp[0:32, 80] = mu[1536:1568]
        rp[0:32, 81] = mu[1568:1600]
        rp[0:96, 82] = mu[1600:1696]
        cw = np.asarray(inp["conv_w"][l], f32)
        cp = np.zeros((128, 4, 4), f32)
        for k in range(3):
            cp[:, :, k] = cw[k].reshape(4, 128).T
        cp[:, :, 3] = np.asarray(inp["conv_out_norm"][l], f32).reshape(4, 128).T
        spk[l, :, 160:176] = cp.reshape(128, 16)
    consts = np.zeros((128, 1024), f32)
    consts[:, C_ONES:C_ONES + 128] = 1.0
    consts[:, C_EPS] = 1e-6
    consts[:, C_EPS + 1] = 64e-5
    consts[0:64, C_BD64:C_BD64 + 64] = 1.0
    consts[64:128, C_BD64 + 64:C_BD64 + 128] = 1.0
    consts[:, C_ID:C_ID + 128] = np.eye(128, dtype=f32)
    i = np.arange(64)
    consts[0:64, C_SU:C_SU + 64] = (i[:, None] < i[None, :])
    consts[0:64, C_SL:C_SL + 64] = (i[:, None] > i[None, :])
    consts[0:64, C_IU:C_IU + 64] = (i[:, None] <= i[None, :])
    rm = np.ones(128, f32)
    rm[0::64] = 0.0
    consts[:, C_RM:C_RM + 128] = rm[None, :]
    kl = np.arange(128)[:, None]
    ql = np.arange(512)[None, :]
    maskd = np.zeros((128, 4, 512), f32)
    for j in range(4):
        maskd[:, j, :] = np.where(128 * j + kl > ql, -30000.0, 0.0)
    invf = (1.0 / (np.float32(10000.0) ** (np.arange(0, 64, 2, dtype=f32) / np.float32(64)))).astype(f32).reshape(32, 1)
    shared = {n: np.ascontiguousarray(np.asarray(inp[n], f32)) for n in BIGW}
    shared.update({"w_inx": w_inx, "wq": wq, "spk": spk, "gfin": _fm(inp["norm_final"], 16), "consts": consts,
                   "maskd": maskd, "invf": invf})
    return shared


_PROG = {}


def kernel(**inp):
    shared = host_layout(inp)
    x = np.asarray(inp["x"], np.float32)
    pos = np.asarray(inp["positions"], np.int32)
    if "nc" not in _PROG:
        _PROG["nc"] = build_program()
    nc = _PROG["nc"]
    in_maps = []
    for c in range(8):
        m = dict(shared)
        m["x"] = np.ascontiguousarray(x[c * NSEQ:(c + 1) * NSEQ].reshape(NT, D))
        m["pos"] = np.ascontiguousarray(pos[c * NSEQ:(c + 1) * NSEQ])
        in_maps.append(m)
    res = run_bass_kernel_spmd(nc, in_maps, core_ids=list(range(8)))
    out = np.stack([np.asarray(r["out"]).reshape(NSEQ, S, D) for r in res.results], axis=0)
    return out.reshape(16, S, D).astype(np.float32)
```

```python
import contextlib
import numpy as np
import concourse.bass as bass
import concourse.mybir as mybir
from concourse.bass_utils import run_bass_kernel_spmd

F32 = mybir.dt.float32
BF16 = mybir.dt.bfloat16
I32 = mybir.dt.int32
AF = mybir.ActivationFunctionType
ALU = mybir.AluOpType
AX = mybir.AxisListType

D = 2048
DFF = 5632
S = 2048
NSEQ = 2
NT = NSEQ * S
DEPTH = 4
KC = D // 128
FC = DFF // 128
EPS = 1e-6
NIN = 4064
NINX = 4128


class Sem:
    def __init__(self, h):
        self.h = h
        self.n = 0


class Buf:
    def __init__(self, cx, t, dma=False):
        self.t = t
        self.wr = None
        self.rd = []
        self.sem = cx.new_sem() if dma else None

    def __getitem__(self, k):
        return self.t[k]


class Ctx:
    ENG = ('pe', 'act', 'dve', 'pool', 'sp')

    def __init__(self, nc, sem_handles):
        self.nc = nc
        self.sems = [Sem(h) for h in sem_handles]
        self.free = list(self.sems)
        self.esem = {e: self.new_sem() for e in ('pe', 'act', 'dve', 'pool')}
        self.q = {e: [] for e in self.ENG}
        self.seen = {e: {} for e in self.ENG}
        self.stage_sems = []
        self.pending_dma = {e: [] for e in self.ENG}

    def new_sem(self):
        return self.free.pop()

    def begin_stage(self):
        self.mark = len(self.free)
        self.taken = []

    def buf(self, t, dma=False):
        b = Buf.__new__(Buf)
        b.t = t
        b.wr = None
        b.rd = []
        b.sem = None
        if dma:
            b.sem = self.free.pop()
            self.taken.append(b.sem)
        return b

    def end_stage(self, block):
        for e in self.ENG:
            for tok in self.pending_dma[e]:
                self._wait(e, tok)
            self.pending_dma[e] = []
        self.flush(block)
        self.free.extend(self.taken)
        self.taken = []

    def _wait(self, eng, tok):
        if tok is None:
            return
        sem, val, src = tok
        if src == eng and src in ('pe', 'act', 'dve'):
            return
        if self.seen[eng].get(id(sem), 0) >= val:
            return
        self.seen[eng][id(sem)] = val
        self.q[eng].append(('wait', sem, val))

    def _deps(self, eng, reads, writes):
        best = {}
        toks = [b.wr for b in reads] + [b.wr for b in writes]
        for b in writes:
            toks.extend(b.rd)
        for t in toks:
            if t is None:
                continue
            k = id(t[0])
            if k not in best or best[k][1] < t[1]:
                best[k] = t
        for t in best.values():
            self._wait(eng, t)

    def op(self, eng, fn, reads=(), writes=(), sig=True):
        self._deps(eng, reads, writes)
        tok = None
        if sig:
            s = self.esem[eng]
            s.n += 1
            tok = (s, s.n, eng)
            self.q[eng].append(('op', fn, s, 1))
        else:
            self.q[eng].append(('op', fn, None, 0))
        for b in writes:
            b.wr = tok
            b.rd = []
        for b in reads:
            if tok is not None:
                b.rd = [t for t in b.rd if t[0] is not tok[0]] + [tok]
        return tok

    def dma(self, eng, fn, reads=(), writes=(), sembuf=None):
        s = sembuf.sem
        saved = []
        for b in writes:
            if b.wr is not None and b.wr[2] == 'dma' and b.wr[0] is s and not b.rd:
                saved.append((b, b.wr))
                b.wr = None
        self._deps(eng, reads, writes)
        for b, w in saved:
            b.wr = w
        s.n += 16
        tok = (s, s.n, 'dma')
        self.q[eng].append(('op', fn, s, 16))
        for b in writes:
            b.wr = tok
            b.rd = []
        for b in reads:
            b.rd = [t for t in b.rd if t[0] is not tok[0]] + [tok]
        if not writes:
            self.pending_dma[eng].append(tok)
        return tok

    def flush(self, block):
        m = {'pe': block.tensor, 'act': block.scalar, 'dve': block.vector,
             'pool': block.gpsimd, 'sp': block.sync}
        for e in self.ENG:
            lst = self.q[e]
            if not lst:
                continue

            def body(eng, lst=lst):
                for it in lst:
                    if it[0] == 'wait':
                        eng.wait_ge(it[1].h, it[2])
                    else:
                        ins = it[1](eng)
                        if it[2] is not None:
                            ins.then_inc(it[2].h, it[3])
            m[e](body)
            self.q[e] = []


class Ring:
    def __init__(self, bufs):
        self.bufs = bufs
        self.i = 0

    def next(self):
        b = self.bufs[self.i % len(self.bufs)]
        self.i += 1
        return b


def mm_group(cx, out_buf, pairs, reads, out_ap=None):
    nc = cx.nc
    oap = out_buf.t[:] if out_ap is None else out_ap
    n = len(pairs)

    def fn(eng):
        ins = None
        for i, (l, r) in enumerate(pairs):
            ins = nc.tensor.matmul(oap, l, r, start=(i == 0), stop=(i == n - 1))
        return ins
    return cx.op('pe', fn, reads=reads, writes=[out_buf])


_UID = [0]


def alloc(es, nc, name, shape, dt, psum=False):
    _UID[0] += 1
    name = f"{name}_{_UID[0]}"
    if psum:
        return es.enter_context(nc.psum_tensor(name, shape, dt))
    return es.enter_context(nc.sbuf_tensor(name, shape, dt))


def stage_in(cx, x, hT, ident_dram):
    nc = cx.nc
    with contextlib.ExitStack() as es:
        cx.begin_stage()
        ident = cx.buf(alloc(es, nc, "ident", [128, 128], F32), dma=True)
        xin = Ring([cx.buf(alloc(es, nc, f"xin{i}", [128, D], F32), dma=True) for i in range(3)])
        xo_t = [alloc(es, nc, f"xo{i}", [128, KC, 128], F32) for i in range(3)]
        xo = Ring([[cx.buf(t, dma=(q == 0)) for q in range(4)] for t in xo_t])
        ps = Ring([cx.buf(alloc(es, nc, f"ps{i}", [128, 512], F32, psum=True)) for i in range(4)])
        blk = es.enter_context(nc.Block())
        cx.dma('sp', lambda e: e.dma_start(out=ident.t[:], in_=ident_dram[:, :]), writes=[ident], sembuf=ident)
        for tt in range(NT // 128):
            xb = xin.next()
            cx.dma('sp', lambda e, xb=xb, tt=tt: e.dma_start(out=xb.t[:], in_=x[tt * 128:(tt + 1) * 128, :]),
                   writes=[xb], sembuf=xb)
            ob = xo.next()
            for g in range(KC // 4):
                p = ps.next()

                def fn(eng, p=p, xb=xb, g=g):
                    ins = None
                    for j in range(4):
                        kc = g * 4 + j
                        ins = nc.tensor.transpose(p.t[:, j * 128:(j + 1) * 128], xb.t[:, kc * 128:(kc + 1) * 128], ident.t[:])
                    return ins
                cx.op('pe', fn, reads=[xb, ident], writes=[p])
                eng = 'dve' if g % 2 == 0 else 'act'
                if eng == 'dve':
                    cx.op('dve', lambda e, p=p, ob=ob, g=g: nc.vector.tensor_copy(
                        ob[g].t[:, g * 4:(g + 1) * 4, :], p.t[:].rearrange("p (a b) -> p a b", a=4)), reads=[p], writes=[ob[g]])
                else:
                    cx.op('act', lambda e, p=p, ob=ob, g=g: nc.scalar.copy(
                        ob[g].t[:, g * 4:(g + 1) * 4, :], p.t[:].rearrange("p (a b) -> p a b", a=4)), reads=[p], writes=[ob[g]])
            cx.dma('sp', lambda e, ob=ob, tt=tt: e.dma_start(
                out=hT[:, :, tt * 128:(tt + 1) * 128].rearrange("k p t -> p k t"), in_=ob[0].t[:]),
                reads=list(ob), sembuf=ob[0])
        cx.end_stage(blk)


def rms_stats_gen(cx, nc, chunks_fn, nchunks, T, hring, sqring, ps_ss, ones_f, nfeat, eps_t, sd, rstd, src_rows=128, ldq='sp'):
    nsub = T // 512
    for kc in range(nchunks):
        hb = hring.next()
        cx.dma(ldq, chunks_fn(kc, hb), writes=[hb], sembuf=hb)
        sq = sqring.next()
        cx.op('act', lambda e, hb=hb, sq=sq: nc.scalar.activation(out=sq.t[:src_rows, :T], in_=hb.t[:src_rows, :T], func=AF.Square),
              reads=[hb], writes=[sq])
        for sub in range(nsub):
            p = ps_ss[sub]

            def fn(eng, p=p, sq=sq, sub=sub, kc=kc):
                return nc.tensor.matmul(p.t[:], ones_f.t[:src_rows, :], sq.t[:src_rows, sub * 512:(sub + 1) * 512],
                                        start=(kc == 0), stop=(kc == nchunks - 1))
            cx.op('pe', fn, reads=[sq, ones_f], writes=[p] if kc == 0 else [], sig=True)
            if kc != 0:
                pass
        if kc == nchunks - 1:
            last_tok = (cx.esem['pe'], cx.esem['pe'].n, 'pe')
            for sub in range(nsub):
                ps_ss[sub].wr = last_tok
        yield
    for sub in range(nsub):
        p = ps_ss[sub]
        cx.op('act', lambda e, p=p, sub=sub: nc.scalar.activation(
            out=sd.t[:, sub * 512:(sub + 1) * 512], in_=p.t[:], func=AF.Sqrt, bias=eps_t.t[:, 0:1], scale=1.0 / nfeat),
            reads=[p, eps_t], writes=[sd] if sub == 0 else [])
    sd.wr = (cx.esem['act'], cx.esem['act'].n, 'act')
    cx.op('dve', lambda e: nc.vector.reciprocal(rstd.t[:, :T], sd.t[:, :T]), reads=[sd], writes=[rstd])


def rms_stats(*a, **k):
    for _ in rms_stats_gen(*a, **k):
        pass


def stage_ffn(cx, hT, wg, wu, wd, gvec, consts, T=1024, tiles=None):
    nc = cx.nc
    nsub = T // 512
    NJ = 256
    with contextlib.ExitStack() as es:
        cx.begin_stage()
        ones_f = cx.buf(alloc(es, nc, "ones_f", [128, 128], F32), dma=True)
        eps_t = cx.buf(alloc(es, nc, "eps_t", [128, 1], F32), dma=True)
        g_t = cx.buf(alloc(es, nc, "g_t", [128, KC], F32), dma=True)
        xn = cx.buf(alloc(es, nc, "xn", [128, KC, T], BF16))
        act = [cx.buf(alloc(es, nc, f"act{j}", [128, T], BF16)) for j in range(FC)]
        wring = Ring([cx.buf(alloc(es, nc, f"w{i}", [128, 16, NJ], BF16), dma=True) for i in range(6)])
        hring = Ring([cx.buf(alloc(es, nc, f"hb{i}", [128, T], F32), dma=True) for i in range(2)])
        sqring = Ring([cx.buf(alloc(es, nc, f"sq{i}", [128, T], F32)) for i in range(2)])
        rstd = cx.buf(alloc(es, nc, "rstd", [128, T], F32))
        sd = rstd
        sgr = Ring([cx.buf(alloc(es, nc, f"sg{i}", [128, 512], F32)) for i in range(2)])
        outr = Ring([cx.buf(alloc(es, nc, f"ob{i}", [128, 512], F32), dma=True) for i in range(2)])
        hres = Ring([cx.buf(alloc(es, nc, f"hr{i}", [128, 512], F32), dma=True) for i in range(2)])
        ps_g = Ring([cx.buf(alloc(es, nc, f"psg{i}", [128, 512], F32, psum=True)) for i in range(2)])
        ps_u = Ring([cx.buf(alloc(es, nc, f"psu{i}", [128, 512], F32, psum=True)) for i in range(2)])
        ps_d = Ring([cx.buf(alloc(es, nc, f"psd{i}", [128, 512], F32, psum=True)) for i in range(2)])
        ps_ss = [cx.buf(alloc(es, nc, f"pss{i}", [128, 512], F32, psum=True)) for i in range(nsub)]
        blk = es.enter_context(nc.Block())

        cx.dma('sp', lambda e: e.dma_start(out=ones_f.t[:], in_=consts[:, 0:128]), writes=[ones_f], sembuf=ones_f)
        cx.dma('sp', lambda e: e.dma_start(out=eps_t.t[:], in_=consts[:, 128:129], allow_slow_non_contiguous=True), writes=[eps_t], sembuf=eps_t)
        cx.dma('sp', lambda e: e.dma_start(out=g_t.t[:], in_=gvec[:, :]), writes=[g_t], sembuf=g_t)
        wgv = wg.rearrange("(kc p) n -> p kc n", p=128)
        wuv = wu.rearrange("(kc p) n -> p kc n", p=128)
        wdv = wd.rearrange("(j p) n -> p j n", p=128)
        tl = list(range(NT // T)) if tiles is None else tiles
        def norm_gen(t0):
            yield from rms_stats_gen(cx, nc, lambda kc, hb: (lambda e: e.dma_start(out=hb.t[:, :T], in_=hT[kc, :, t0:t0 + T])),
                                     KC, T, hring, sqring, ps_ss, ones_f, D, eps_t, sd, rstd)
            for kc in range(KC):
                hb = hring.next()
                cx.dma('sp', lambda e, hb=hb, kc=kc: e.dma_start(out=hb.t[:, :T], in_=hT[kc, :, t0:t0 + T]),
                       writes=[hb], sembuf=hb)
                cx.op('dve', lambda e, hb=hb, kc=kc: nc.vector.scalar_tensor_tensor(
                    out=xn.t[:, kc, :], in0=hb.t[:, :T], scalar=g_t.t[:, kc:kc + 1], in1=rstd.t[:, :T],
                    op0=ALU.mult, op1=ALU.mult), reads=[hb, g_t, rstd], writes=[xn] if kc == 0 else [])
                xn.wr = (cx.esem['dve'], cx.esem['dve'].n, 'dve')
                yield

        def gateup(t0):
            for jt in range(DFF // NJ):
                wgb = wring.next()
                cx.dma('pool', lambda e, b=wgb, jt=jt: e.dma_start(out=b.t[:], in_=wgv[:, :, jt * NJ:(jt + 1) * NJ]),
                       writes=[wgb], sembuf=wgb)
                wub = wring.next()
                cx.dma('pool', lambda e, b=wub, jt=jt: e.dma_start(out=b.t[:], in_=wuv[:, :, jt * NJ:(jt + 1) * NJ]),
                       writes=[wub], sembuf=wub)
                for jj in range(NJ // 128):
                    j = jt * (NJ // 128) + jj
                    for sub in range(nsub):
                        pg = ps_g.next()
                        pu = ps_u.next()
                        mm_group(cx, pg, [(wgb.t[:, kc, jj * 128:(jj + 1) * 128], xn.t[:, kc, sub * 512:(sub + 1) * 512])
                                          for kc in range(KC)], reads=[wgb, xn])
                        mm_group(cx, pu, [(wub.t[:, kc, jj * 128:(jj + 1) * 128], xn.t[:, kc, sub * 512:(sub + 1) * 512])
                                          for kc in range(KC)], reads=[wub, xn])
                        sg = sgr.next()
                        cx.op('act', lambda e, pg=pg, sg=sg: nc.scalar.activation(out=sg.t[:], in_=pg.t[:], func=AF.Silu),
                              reads=[pg], writes=[sg])
                        cx.op('dve', lambda e, sg=sg, pu=pu, j=j, sub=sub: nc.vector.tensor_tensor(
                            out=act[j].t[:, sub * 512:(sub + 1) * 512], in0=sg.t[:], in1=pu.t[:], op=ALU.mult),
                            reads=[sg, pu], writes=[act[j]])
        def down(t0, nxt):
            JD = 16
            for ct in range(D // NJ):
                wds = []
                for q4 in range((FC + JD - 1) // JD):
                    wb = wring.next()
                    nj_ = min(JD, FC - q4 * JD)
                    cx.dma('pool', lambda e, b=wb, q4=q4, ct=ct, nj_=nj_: e.dma_start(
                        out=b.t[:, 0:nj_, :], in_=wdv[:, q4 * JD:q4 * JD + nj_, ct * NJ:(ct + 1) * NJ]),
                        writes=[wb], sembuf=wb)
                    wds.append(wb)
                for cc in range(NJ // 128):
                    c = ct * (NJ // 128) + cc
                    for sub in range(nsub):
                        hr = hres.next()
                        cx.dma('sp', lambda e, hr=hr, c=c, sub=sub: e.dma_start(
                            out=hr.t[:], in_=hT[c, :, t0 + sub * 512:t0 + (sub + 1) * 512]), writes=[hr], sembuf=hr)
                        pd = ps_d.next()
                        mm_group(cx, pd, [(wds[j // JD].t[:, j % JD, cc * 128:(cc + 1) * 128], act[j].t[:, sub * 512:(sub + 1) * 512])
                                          for j in range(FC)], reads=wds + act)
                        if nxt is not None:
                            next(nxt, None)
                            next(nxt, None)
                        ob = outr.next()
                        cx.op('dve', lambda e, pd=pd, hr=hr, ob=ob: nc.vector.scalar_tensor_tensor(
                            out=ob.t[:], in0=pd.t[:], scalar=0.5, in1=hr.t[:], op0=ALU.mult, op1=ALU.add),
                            reads=[pd, hr], writes=[ob])
                        cx.dma('sp', lambda e, ob=ob, c=c, sub=sub: e.dma_start(
                            out=hT[c, :, t0 + sub * 512:t0 + (sub + 1) * 512], in_=ob.t[:]), reads=[ob], sembuf=ob)
            if nxt is not None:
                for _ in nxt:
                    pass
        first = norm_gen(tl[0] * T)
        for _ in first:
            pass
        for idx, ti in enumerate(tl):
            gateup(ti * T)
            nxt = norm_gen(tl[idx + 1] * T) if idx + 1 < len(tl) else None
            down(ti * T, nxt)
        cx.end_stage(blk)


def stage_out(cx, hT, gvec, consts, ident_dram, out):
    nc = cx.nc
    T = 512
    with contextlib.ExitStack() as es:
        cx.begin_stage()
        ones_f = cx.buf(alloc(es, nc, "ones_f", [128, 128], F32), dma=True)
        ident = cx.buf(alloc(es, nc, "ident", [128, 128], F32), dma=True)
        eps_t = cx.buf(alloc(es, nc, "eps_t", [128, 1], F32), dma=True)
        g_t = cx.buf(alloc(es, nc, "g_t", [128, KC], F32), dma=True)
        hring = Ring([cx.buf(alloc(es, nc, f"hb{i}", [128, T], F32), dma=True) for i in range(3)])
        sqring = Ring([cx.buf(alloc(es, nc, f"sq{i}", [128, T], F32)) for i in range(2)])
        sd = cx.buf(alloc(es, nc, "sd", [128, T], F32))
        rstd = cx.buf(alloc(es, nc, "rstd", [128, T], F32))
        xn = cx.buf(alloc(es, nc, "xnf", [128, KC, T], F32))
        ps_ss = [cx.buf(alloc(es, nc, "pss0", [128, 512], F32, psum=True))]
        ps = Ring([cx.buf(alloc(es, nc, f"ps{i}", [128, 512], F32, psum=True)) for i in range(4)])
        orow_t = [alloc(es, nc, f"orow{i}", [128, D], F32) for i in range(3)]
        orow = Ring([[cx.buf(t, dma=(q == 0)) for q in range(4)] for t in orow_t])
        blk = es.enter_context(nc.Block())
        cx.dma('sp', lambda e: e.dma_start(out=ones_f.t[:], in_=consts[:, 0:128]), writes=[ones_f], sembuf=ones_f)
        cx.dma('sp', lambda e: e.dma_start(out=eps_t.t[:], in_=consts[:, 128:129], allow_slow_non_contiguous=True), writes=[eps_t], sembuf=eps_t)
        cx.dma('sp', lambda e: e.dma_start(out=g_t.t[:], in_=gvec[:, :]), writes=[g_t], sembuf=g_t)
        cx.dma('sp', lambda e: e.dma_start(out=ident.t[:], in_=ident_dram[:, :]), writes=[ident], sembuf=ident)
        def tile_body(t0):
            rms_stats(cx, nc, lambda kc, hb: (lambda e: e.dma_start(out=hb.t[:, :T], in_=hT[kc, :, t0:t0 + T])),
                      KC, T, hring, sqring, ps_ss, ones_f, D, eps_t, sd, rstd)
            for kc in range(KC):
                hb = hring.next()
                cx.dma('sp', lambda e, hb=hb, kc=kc: e.dma_start(out=hb.t[:, :T], in_=hT[kc, :, t0:t0 + T]),
                       writes=[hb], sembuf=hb)
                cx.op('dve', lambda e, hb=hb, kc=kc: nc.vector.scalar_tensor_tensor(
                    out=xn.t[:, kc, :], in0=hb.t[:, :T], scalar=g_t.t[:, kc:kc + 1], in1=rstd.t[:, :T],
                    op0=ALU.mult, op1=ALU.mult), reads=[hb, g_t, rstd], writes=[xn] if kc == 0 else [])
            xn.wr = (cx.esem['dve'], cx.esem['dve'].n, 'dve')
            for tb in range(T // 128):
                ob = orow.next()
                for g in range(KC // 4):
                    p = ps.next()

                    def fn(eng, p=p, g=g, tb=tb):
                        ins = None
                        for j in range(4):
                            kc = g * 4 + j
                            ins = nc.tensor.transpose(p.t[:, j * 128:(j + 1) * 128], xn.t[:, kc, tb * 128:(tb + 1) * 128], ident.t[:])
                        return ins
                    cx.op('pe', fn, reads=[xn, ident], writes=[p])
                    if g % 2 == 0:
                        cx.op('dve', lambda e, p=p, ob=ob, g=g: nc.vector.tensor_copy(ob[g].t[:, g * 512:(g + 1) * 512], p.t[:]),
                              reads=[p], writes=[ob[g]])
                    else:
                        cx.op('act', lambda e, p=p, ob=ob, g=g: nc.scalar.copy(ob[g].t[:, g * 512:(g + 1) * 512], p.t[:]),
                              reads=[p], writes=[ob[g]])
                cx.dma('sp', lambda e, ob=ob, tb=tb: e.dma_start(out=out[t0 + tb * 128:t0 + (tb + 1) * 128, :], in_=ob[0].t[:]),
                       reads=list(ob), sembuf=ob[0])
        for ti in range(NT // T):
            tile_body(ti * T)
        cx.end_stage(blk)


def load_consts(cx, es, nc, consts, ident_dram=None):
    ones_f = cx.buf(alloc(es, nc, "ones_f", [128, 128], F32), dma=True)
    eps_t = cx.buf(alloc(es, nc, "eps_t", [128, 1], F32), dma=True)
    cx.dma('sp', lambda e: e.dma_start(out=ones_f.t[:], in_=consts[:, 0:128]), writes=[ones_f], sembuf=ones_f)
    cx.dma('sp', lambda e: e.dma_start(out=eps_t.t[:], in_=consts[:, 128:129], allow_slow_non_contiguous=True),
           writes=[eps_t], sembuf=eps_t)
    return ones_f, eps_t


def stage_proj(cx, hT, w, gvec, consts, pT, ncols, T=1024):
    nc = cx.nc
    nsub = T // 512
    NJ = 384
    with contextlib.ExitStack() as es:
        cx.begin_stage()
        ones_f, eps_t = load_consts(cx, es, nc, consts)
        g_t = cx.buf(alloc(es, nc, "g_t", [128, KC], F32), dma=True)
        xns = [cx.buf(alloc(es, nc, f"xn{i}", [128, KC, T], BF16)) for i in range(2)]
        wring = Ring([cx.buf(alloc(es, nc, f"w{i}", [128, 16, NJ], BF16), dma=True) for i in range(3)])
        hring = Ring([cx.buf(alloc(es, nc, f"hb{i}", [128, T], F32), dma=True) for i in range(2)])
        sqring = Ring([cx.buf(alloc(es, nc, f"sq{i}", [128, T], F32)) for i in range(2)])
        rstd = cx.buf(alloc(es, nc, "rstd", [128, T], F32))
        outr = Ring([cx.buf(alloc(es, nc, f"ob{i}", [128, 512], F32), dma=True) for i in range(4)])
        ps_o = Ring([cx.buf(alloc(es, nc, f"pso{i}", [128, 512], F32, psum=True)) for i in range(4)])
        ps_ss = [cx.buf(alloc(es, nc, f"pss{i}", [128, 512], F32, psum=True)) for i in range(nsub)]
        blk = es.enter_context(nc.Block())
        cx.dma('sp', lambda e: e.dma_start(out=g_t.t[:], in_=gvec[:, :]), writes=[g_t], sembuf=g_t)
        wv = w.rearrange("(kc p) n -> p kc n", p=128)

        def norm_gen(t0, xn):
            yield from rms_stats_gen(cx, nc, lambda kc, hb: (lambda e: e.dma_start(out=hb.t[:, :T], in_=hT[kc, :, t0:t0 + T])),
                                     KC, T, hring, sqring, ps_ss, ones_f, D, eps_t, rstd, rstd)
            for kc in range(KC):
                hb = hring.next()
                cx.dma('sp', lambda e, hb=hb, kc=kc: e.dma_start(out=hb.t[:, :T], in_=hT[kc, :, t0:t0 + T]),
                       writes=[hb], sembuf=hb)
                cx.op('dve', lambda e, hb=hb, kc=kc: nc.vector.scalar_tensor_tensor(
                    out=xn.t[:, kc, :], in0=hb.t[:, :T], scalar=g_t.t[:, kc:kc + 1], in1=rstd.t[:, :T],
                    op0=ALU.mult, op1=ALU.mult), reads=[hb, g_t, rstd], writes=[xn] if kc == 0 else [])
                xn.wr = (cx.esem['dve'], cx.esem['dve'].n, 'dve')
                yield

        def main(t0, xn, nxt):
            gcount = 0
            for jt in range(ncols // NJ):
                wb = wring.next()
                cx.dma('pool', lambda e, b=wb, jt=jt: e.dma_start(out=b.t[:], in_=wv[:, :, jt * NJ:(jt + 1) * NJ]),
                       writes=[wb], sembuf=wb)
                for jj in range(NJ // 128):
                    r0 = jt * NJ + jj * 128
                    for sub in range(nsub):
                        po = ps_o.next()
                        mm_group(cx, po, [(wb.t[:, kc, jj * 128:(jj + 1) * 128], xn.t[:, kc, sub * 512:(sub + 1) * 512])
                                          for kc in range(KC)], reads=[wb, xn])
                        gcount += 1
                        if nxt is not None and gcount % 2 == 0:
                            next(nxt, None)
                        ob = outr.next()
                        if (jj + sub) % 2 == 0:
                            cx.op('act', lambda e, po=po, ob=ob: nc.scalar.copy(ob.t[:], po.t[:]), reads=[po], writes=[ob])
                        else:
                            cx.op('dve', lambda e, po=po, ob=ob: nc.vector.tensor_copy(ob.t[:], po.t[:]), reads=[po], writes=[ob])
                        cx.dma('sp', lambda e, ob=ob, r0=r0, sub=sub: e.dma_start(
                            out=pT[r0:r0 + 128, t0 + sub * 512:t0 + (sub + 1) * 512], in_=ob.t[:]), reads=[ob], sembuf=ob)
            if nxt is not None:
                for _ in nxt:
                    pass
        ntl = NT // T
        for _ in norm_gen(0, xns[0]):
            pass
        for ti in range(ntl):
            nxt = norm_gen((ti + 1) * T, xns[(ti + 1) % 2]) if ti + 1 < ntl else None
            main(ti * T, xns[ti % 2], nxt)
        cx.end_stage(blk)


def stage_wout(cx, hT, yT, w, T=1024):
    nc = cx.nc
    nsub = T // 512
    NJ = 256
    with contextlib.ExitStack() as es:
        cx.begin_stage()
        ybs = [cx.buf(alloc(es, nc, f"yb{i}", [128, KC, T], BF16), dma=True) for i in range(2)]
        wring = Ring([cx.buf(alloc(es, nc, f"w{i}", [128, 16, NJ], BF16), dma=True) for i in range(4)])
        outr = Ring([cx.buf(alloc(es, nc, f"ob{i}", [128, 512], F32), dma=True) for i in range(3)])
        hres = Ring([cx.buf(alloc(es, nc, f"hr{i}", [128, 512], F32), dma=True) for i in range(4)])
        ps_o = Ring([cx.buf(alloc(es, nc, f"pso{i}", [128, 512], F32, psum=True)) for i in range(4)])
        blk = es.enter_context(nc.Block())
        wv = w.rearrange("(kc p) n -> p kc n", p=128)
        yv = yT.rearrange("(kc p) t -> p kc t", p=128)
        ntl = NT // T

        def load_y(ti):
            yb = ybs[ti % 2]
            cx.dma('sp', lambda e: e.dma_start(out=yb.t[:], in_=yv[:, :, ti * T:(ti + 1) * T]), writes=[yb], sembuf=yb)

        def load_hr(t0, c, sub):
            hr = hres.next()
            cx.dma('sp', lambda e: e.dma_start(out=hr.t[:], in_=hT[c, :, t0 + sub * 512:t0 + (sub + 1) * 512]),
                   writes=[hr], sembuf=hr)
            return hr

        def tile_body(ti):
            t0 = ti * T
            yb = ybs[ti % 2]
            groups = [(jt, jj, sub) for jt in range(D // NJ) for jj in range(NJ // 128) for sub in range(nsub)]
            LA = 2
            hrs = {}
            for g in range(min(LA, len(groups))):
                jt, jj, sub = groups[g]
                hrs[g] = load_hr(t0, jt * (NJ // 128) + jj, sub)
            if ti + 1 < ntl:
                load_y(ti + 1)
            wb = None
            for g, (jt, jj, sub) in enumerate(groups):
                if jj == 0 and sub == 0:
                    wb = wring.next()
                    cx.dma('pool', lambda e, b=wb, jt=jt: e.dma_start(out=b.t[:], in_=wv[:, :, jt * NJ:(jt + 1) * NJ]),
                           writes=[wb], sembuf=wb)
                c = jt * (NJ // 128) + jj
                if g + LA < len(groups):
                    jt2, jj2, sub2 = groups[g + LA]
                    hrs[g + LA] = load_hr(t0, jt2 * (NJ // 128) + jj2, sub2)
                hr = hrs.pop(g)
                po = ps_o.next()
                mm_group(cx, po, [(wb.t[:, kc, jj * 128:(jj + 1) * 128], yb.t[:, kc, sub * 512:(sub + 1) * 512])
                                  for kc in range(KC)], reads=[wb, yb])
                ob = outr.next()
                cx.op('dve', lambda e, po=po, hr=hr, ob=ob: nc.vector.tensor_tensor(
                    out=ob.t[:], in0=po.t[:], in1=hr.t[:], op=ALU.add), reads=[po, hr], writes=[ob])
                cx.dma('sp', lambda e, ob=ob, c=c, sub=sub: e.dma_start(
                    out=hT[c, :, t0 + sub * 512:t0 + (sub + 1) * 512], in_=ob.t[:]), reads=[ob], sembuf=ob)
        load_y(0)
        for ti in range(ntl):
            tile_body(ti)
        cx.end_stage(blk)


C_ONES, C_EPS, C_BD64, C_ID, C_SU, C_SL, C_IU, C_RM = 0, 128, 256, 384, 512, 576, 640, 704
B0 = 832
C0 = 2528
R_KRS = 4064


def stage_conv(cx, pT, prm, consts, yT):
    nc = cx.nc
    with contextlib.ExitStack() as es:
        cx.begin_stage()
        bd = cx.buf(alloc(es, nc, "bd", [128, 128], F32), dma=True)
        eps_t = cx.buf(alloc(es, nc, "eps_t", [128, 1], F32), dma=True)
        pr = cx.buf(alloc(es, nc, "pr", [128, 4, 4], F32), dma=True)
        bg = Ring([cx.buf(alloc(es, nc, f"bg{i}", [128, S], F32), dma=True) for i in range(2)])
        cg = Ring([cx.buf(alloc(es, nc, f"cg{i}", [128, S], F32), dma=True) for i in range(2)])
        hh = Ring([cx.buf(alloc(es, nc, f"hh{i}", [128, S], F32), dma=True) for i in range(2)])
        u = cx.buf(alloc(es, nc, "u", [128, S + 2], F32))
        y = cx.buf(alloc(es, nc, "y", [128, S], F32))
        z = cx.buf(alloc(es, nc, "z", [128, S], F32))
        zsq = cx.buf(alloc(es, nc, "zsq", [128, S], F32))
        rs = cx.buf(alloc(es, nc, "rs", [128, S], F32))
        ob = Ring([cx.buf(alloc(es, nc, f"ob{i}", [128, S], BF16), dma=True) for i in range(2)])
        ps = Ring([cx.buf(alloc(es, nc, f"ps{i}", [128, 512], F32, psum=True)) for i in range(4)])
        blk = es.enter_context(nc.Block())
        cx.dma('sp', lambda e: e.dma_start(out=bd.t[:], in_=consts[:, C_BD64:C_BD64 + 128]), writes=[bd], sembuf=bd)
        cx.dma('sp', lambda e: e.dma_start(out=eps_t.t[:], in_=consts[:, C_EPS:C_EPS + 1], allow_slow_non_contiguous=True),
               writes=[eps_t], sembuf=eps_t)
        cx.dma('sp', lambda e: e.dma_start(out=pr.t[:], in_=prm[:, :, :]), writes=[pr], sembuf=pr)
        cx.op('dve', lambda e: nc.vector.memset(u.t[:, 0:2], 0.0), writes=[u])

        def body(b, ch):
            t0 = b * S
            bgb, cgb, hhb = bg.next(), cg.next(), hh.next()
            for (buf, r0) in ((bgb, C0 + ch * 128), (cgb, C0 + 512 + ch * 128), (hhb, C0 + 1024 + ch * 128)):
                cx.dma('sp', lambda e, buf=buf, r0=r0: e.dma_start(out=buf.t[:], in_=pT[r0:r0 + 128, t0:t0 + S]),
                       writes=[buf], sembuf=buf)
            cx.op('dve', lambda e: nc.vector.tensor_tensor(out=u.t[:, 2:S + 2], in0=cgb.t[:], in1=hhb.t[:], op=ALU.mult),
                  reads=[cgb, hhb], writes=[u])
            cx.op('act', lambda e: nc.scalar.activation(out=y.t[:], in_=u.t[:, 2:S + 2], func=AF.Copy, scale=pr.t[:, ch, 2:3]),
                  reads=[u, pr], writes=[y])
            cx.op('dve', lambda e: nc.vector.scalar_tensor_tensor(out=y.t[:], in0=u.t[:, 1:S + 1], scalar=pr.t[:, ch, 1:2],
                                                                  in1=y.t[:], op0=ALU.mult, op1=ALU.add), reads=[u, pr], writes=[y])
            cx.op('dve', lambda e: nc.vector.scalar_tensor_tensor(out=y.t[:], in0=u.t[:, 0:S], scalar=pr.t[:, ch, 0:1],
                                                                  in1=y.t[:], op0=ALU.mult, op1=ALU.add), reads=[u, pr], writes=[y])
            cx.op('dve', lambda e: nc.vector.tensor_tensor(out=z.t[:], in0=bgb.t[:], in1=y.t[:], op=ALU.mult),
                  reads=[bgb, y], writes=[z])
            cx.op('act', lambda e: nc.scalar.activation(out=zsq.t[:], in_=z.t[:], func=AF.Square), reads=[z], writes=[zsq])
            for sub in range(S // 512):
                p = ps.next()
                sl = slice(sub * 512, (sub + 1) * 512)
                cx.op('pe', lambda e, p=p, sl=sl: nc.tensor.matmul(p.t[:], bd.t[:], zsq.t[:, sl], start=True, stop=True),
                      reads=[bd, zsq], writes=[p])
                cx.op('act', lambda e, p=p, sl=sl: nc.scalar.activation(out=rs.t[:, sl], in_=p.t[:], func=AF.Sqrt,
                                                                        bias=eps_t.t[:, 0:1], scale=1.0 / 64),
                      reads=[p, eps_t], writes=[rs])
            cx.op('dve', lambda e: nc.vector.reciprocal(rs.t[:], rs.t[:]), reads=[rs], writes=[rs])
            o = ob.next()
            cx.op('dve', lambda e, o=o: nc.vector.scalar_tensor_tensor(out=o.t[:], in0=z.t[:], scalar=pr.t[:, ch, 3:4],
                                                                       in1=rs.t[:], op0=ALU.mult, op1=ALU.mult),
                  reads=[z, pr, rs], writes=[o])
            cx.dma('sp', lambda e, o=o: e.dma_start(out=yT[1536 + ch * 128:1536 + (ch + 1) * 128, t0:t0 + S], in_=o.t[:]),
                   reads=[o], sembuf=o)
        for b in range(NSEQ):
            for ch in range(4):
                body(b, ch)
        cx.end_stage(blk)


def stage_mla(cx, pT, wq, wkv, prm, cs, consts, maskd, yT):
    nc = cx.nc
    T = 1024
    scale = float((128 + 64) ** -0.5)
    with contextlib.ExitStack() as es:
        cx.begin_stage()
        ones_f, eps_t = load_consts(cx, es, nc, consts)
        ones_b = cx.buf(alloc(es, nc, "ones_b", [128, 128], BF16), dma=True)
        id_b = cx.buf(alloc(es, nc, "id_b", [128, 128], BF16), dma=True)
        mask = cx.buf(alloc(es, nc, "mask", [128, 4, 512], BF16), dma=True)
        pr = cx.buf(alloc(es, nc, "pr", [128, 16], F32), dma=True)
        wqb = cx.buf(alloc(es, nc, "wqb", [128, 4, 2048], BF16), dma=True)
        wkvb = cx.buf(alloc(es, nc, "wkvb", [128, 2, 2048], BF16), dma=True)
        cqn = cx.buf(alloc(es, nc, "cqn", [128, 4, S], BF16))
        ckvn = cx.buf(alloc(es, nc, "ckvn", [128, 2, S], BF16))
        cc = cx.buf(alloc(es, nc, "cc", [64, S], F32), dma=True)
        ss = cx.buf(alloc(es, nc, "ss", [64, S], F32), dma=True)
        kx = cx.buf(alloc(es, nc, "kx", [64, S], F32), dma=True)
        kxs = cx.buf(alloc(es, nc, "kxs", [64, S], F32), dma=True)
        k_r = cx.buf(alloc(es, nc, "k_r", [64, S], BF16))
        hsets = [dict(q_n=cx.buf(alloc(es, nc, "q_n", [128, S], BF16)), q_r=cx.buf(alloc(es, nc, "q_r", [64, S], BF16)),
                      k_n=cx.buf(alloc(es, nc, "k_n", [128, S], BF16)), V=cx.buf(alloc(es, nc, "V", [128, 16, 128], BF16)))
                 for _ in range(2)]
        xt = cx.buf(alloc(es, nc, "xt", [64, 512], F32))
        rt2 = cx.buf(alloc(es, nc, "rt2", [64, 512], F32))
        PT = Ring([cx.buf(alloc(es, nc, f"PT{i}", [128, 512], BF16)) for i in range(3)])
        hring = Ring([cx.buf(alloc(es, nc, f"hb{i}", [128, T], F32), dma=True) for i in range(2)])
        sqring = Ring([cx.buf(alloc(es, nc, f"sq{i}", [128, T], F32)) for i in range(2)])
        rstd = cx.buf(alloc(es, nc, "rstd", [128, T], F32))
        rinv = cx.buf(alloc(es, nc, "rinv", [128, 512], F32))
        yv = cx.buf(alloc(es, nc, "yv", [128, 512], F32))
        ysq = cx.buf(alloc(es, nc, "ysq", [128, 512], F32))
        sd2 = cx.buf(alloc(es, nc, "sd2", [128, 512], F32))
        obr = Ring([cx.buf(alloc(es, nc, f"ob{i}", [128, 512], BF16), dma=True) for i in range(2)])
        ps_p = Ring([cx.buf(alloc(es, nc, f"psp{i}", [128, 512], F32, psum=True)) for i in range(2)])
        ps_ss = ps_p.bufs
        ps_s = Ring([cx.buf(alloc(es, nc, f"pst{i}", [128, 512], F32, psum=True)) for i in range(2)])
        ps_o_r = Ring([cx.buf(alloc(es, nc, f"pso{i}", [128, 512], F32, psum=True)) for i in range(2)])
        ps_r_r = Ring([cx.buf(alloc(es, nc, f"psr{i}", [128, 512], F32, psum=True)) for i in range(2)])
        blk = es.enter_context(nc.Block())

        cx.dma('pool', lambda e: e.dma_start(out=ones_b.t[:], in_=consts[:, C_ONES:C_ONES + 128]), writes=[ones_b], sembuf=ones_b)
        cx.dma('pool', lambda e: e.dma_start(out=id_b.t[:], in_=consts[:, C_ID:C_ID + 128]), writes=[id_b], sembuf=id_b)
        cx.dma('pool', lambda e: e.dma_start(out=mask.t[:], in_=maskd[:, :, :]), writes=[mask], sembuf=mask)
        cx.dma('sp', lambda e: e.dma_start(out=pr.t[:], in_=prm[:, :]), writes=[pr], sembuf=pr)
        cx.dma('pool', lambda e: e.dma_start(out=wqb.t[:], in_=wq.rearrange("(kc p) n -> p kc n", p=128)), writes=[wqb], sembuf=wqb)
        cx.dma('pool', lambda e: e.dma_start(out=wkvb.t[:], in_=wkv.rearrange("(kc p) n -> p kc n", p=128)), writes=[wkvb], sembuf=wkvb)

        def norm_into(dst, row0, nch, gcol0, t0):
            for half in range(S // T):
                tt0 = t0 + half * T
                ld = lambda kc, hb, tt0=tt0: (lambda e: e.dma_start(out=hb.t[:, :T], in_=pT[row0 + kc * 128:row0 + (kc + 1) * 128, tt0:tt0 + T]))
                rms_stats(cx, nc, ld, nch, T, hring, sqring, ps_ss, ones_f, nch * 128, eps_t, rstd, rstd)
                for kc in range(nch):
                    hb = hring.next()
                    cx.dma('sp', ld(kc, hb), writes=[hb], sembuf=hb)
                    cx.op('dve', lambda e, hb=hb, kc=kc, half=half: nc.vector.scalar_tensor_tensor(
                        out=dst.t[:, kc, half * T:(half + 1) * T], in0=hb.t[:, :T], scalar=pr.t[:, gcol0 + kc:gcol0 + kc + 1],
                        in1=rstd.t[:, :T], op0=ALU.mult, op1=ALU.mult), reads=[hb, pr, rstd], writes=[dst])

        def rope(dst, x_ap, xs_ap, sl, xbufs):
            cx.op('dve', lambda e: nc.vector.tensor_tensor(out=xt.t[:, :sl.stop - sl.start], in0=xs_ap, in1=ss.t[:, sl], op=ALU.mult),
                  reads=xbufs + [ss], writes=[xt])
            cx.op('dve', lambda e: nc.vector.tensor_tensor(out=rt2.t[0:64, :sl.stop - sl.start], in0=x_ap, in1=cc.t[:, sl], op=ALU.mult),
                  reads=xbufs + [cc], writes=[rt2])
            cx.op('dve', lambda e: nc.vector.tensor_tensor(out=dst.t[0:64, sl], in0=rt2.t[0:64, :sl.stop - sl.start],
                                                           in1=xt.t[:, :sl.stop - sl.start], op=ALU.add),
                  reads=[rt2, xt], writes=[dst])

        def seq_body(b):
            t0 = b * S
            cx.dma('sp', lambda e: e.dma_start(out=cc.t[:], in_=cs[b, 0, :, :]), writes=[cc], sembuf=cc)
            cx.dma('sp', lambda e: e.dma_start(out=ss.t[:], in_=cs[b, 1, :, :]), writes=[ss], sembuf=ss)
            cx.dma('sp', lambda e: e.dma_start(out=kx.t[:], in_=pT[768:832, t0:t0 + S]), writes=[kx], sembuf=kx)
            cx.dma('sp', lambda e: e.dma_start(out=kxs.t[:], in_=pT[R_KRS:R_KRS + 64, t0:t0 + S]), writes=[kxs], sembuf=kxs)
            norm_into(cqn, 0, 4, 0, t0)
            norm_into(ckvn, 512, 2, 4, t0)
            for sub in range(4):
                sl = slice(sub * 512, (sub + 1) * 512)
                rope(k_r, kx.t[:, sl], kxs.t[:, sl], sl, [kx, kxs])
            first = proj_gen(0, t0, hsets[0])
            for _ in first:
                pass
            for h in range(8):
                nxt = proj_gen(h + 1, t0, hsets[(h + 1) % 2]) if h < 7 else None
                attn(h, t0, hsets[h % 2], nxt)
                if nxt is not None:
                    for _ in nxt:
                        pass

        def proj_gen(h, t0, hs):
            q_n, q_r, k_n, V = hs["q_n"], hs["q_r"], hs["k_n"], hs["V"]
            if True:
                c0 = h * 256
                for sub in range(4):
                    sl = slice(sub * 512, (sub + 1) * 512)
                    p = ps_p.next()
                    mm_group(cx, p, [(wqb.t[:, kc, c0:c0 + 128], cqn.t[:, kc, sl]) for kc in range(4)], reads=[wqb, cqn])
                    cx.op('act', lambda e, p=p, sl=sl: nc.scalar.copy(q_n.t[:, sl], p.t[:]), reads=[p], writes=[q_n])
                    yield
                    p = ps_p.next()
                    mm_group(cx, p, [(wkvb.t[:, kc, c0:c0 + 128], ckvn.t[:, kc, sl]) for kc in range(2)], reads=[wkvb, ckvn])
                    cx.op('act', lambda e, p=p, sl=sl: nc.scalar.copy(k_n.t[:, sl], p.t[:]), reads=[p], writes=[k_n])
                    yield
                    p1 = ps_p.next()
                    mm_group(cx, p1, [(wqb.t[:, kc, c0 + 128:c0 + 192], cqn.t[:, kc, sl]) for kc in range(4)], reads=[wqb, cqn],
                             out_ap=p1.t[0:64, :])
                    p2 = ps_p.next()
                    mm_group(cx, p2, [(wqb.t[:, kc, c0 + 192:c0 + 256], cqn.t[:, kc, sl]) for kc in range(4)], reads=[wqb, cqn],
                             out_ap=p2.t[0:64, :])
                    rope(q_r, p1.t[0:64, :], p2.t[0:64, :], sl, [p1, p2])
                    yield
                    p = ps_p.next()

                    def vfn(eng, p=p, sub=sub):
                        ins = None
                        for j in range(4):
                            tb = sub * 4 + j
                            for kc in range(2):
                                ins = nc.tensor.matmul(p.t[:, j * 128:(j + 1) * 128], ckvn.t[:, kc, tb * 128:(tb + 1) * 128],
                                                       wkvb.t[:, kc, c0 + 128:c0 + 256], start=(kc == 0), stop=(kc == 1))
                        return ins
                    cx.op('pe', vfn, reads=[ckvn, wkvb], writes=[p])
                    cx.op('act', lambda e, p=p, sub=sub: nc.scalar.copy(
                        V.t[:, sub * 4:(sub + 1) * 4, :], p.t[:].rearrange("p (a b) -> p a b", a=4)), reads=[p], writes=[V])
                    yield

        def attn(h, t0, hs, nxt):
            q_n, q_r, k_n, V = hs["q_n"], hs["q_r"], hs["k_n"], hs["V"]
            if True:
                items = [(qt, kb) for qt in range(4) for kb in range(4 * (qt + 1))]
                acc = {}

                def emit_st(qt, kb):
                    qs = slice(qt * 512, (qt + 1) * 512)
                    ks = slice(kb * 128, (kb + 1) * 128)
                    st = ps_s.next()
                    pairs = [(k_n.t[:, ks], q_n.t[:, qs]), (k_r.t[0:64, ks], q_r.t[0:64, qs])]
                    rds = [k_n, q_n, k_r, q_r]
                    if kb >= 4 * qt:
                        pairs.append((id_b.t[:], mask.t[:, kb - 4 * qt, :]))
                        rds += [id_b, mask]
                    mm_group(cx, st, pairs, reads=rds)
                    pt = PT.next()
                    cx.op('act', lambda e, st=st, pt=pt: nc.scalar.activation(out=pt.t[:], in_=st.t[:], func=AF.Exp, scale=scale),
                          reads=[st], writes=[pt])
                    return pt

                def emit_pv(qt, kb, pt):
                    nkb = 4 * (qt + 1)
                    if kb == 0:
                        acc[qt] = (ps_o_r.next(), ps_r_r.next())
                    ps_o, ps_r = acc[qt]

                    def pv(eng):
                        nc.tensor.matmul(ps_o.t[:], V.t[:, kb, :], pt.t[:], start=(kb == 0), stop=(kb == nkb - 1))
                        return nc.tensor.matmul(ps_r.t[:], ones_b.t[:], pt.t[:], start=(kb == 0), stop=(kb == nkb - 1))
                    cx.op('pe', pv, reads=[pt, V, ones_b], writes=[ps_o, ps_r])
                    if kb == nkb - 1:
                        finalize(qt, ps_o, ps_r)

                def finalize(qt, ps_o, ps_r):
                    cx.op('dve', lambda e: nc.vector.reciprocal(rinv.t[:], ps_r.t[:]), reads=[ps_r], writes=[rinv])
                    cx.op('dve', lambda e: nc.vector.tensor_tensor(out=yv.t[:], in0=ps_o.t[:], in1=rinv.t[:], op=ALU.mult),
                          reads=[ps_o, rinv], writes=[yv])
                    cx.op('act', lambda e: nc.scalar.activation(out=ysq.t[:], in_=yv.t[:], func=AF.Square), reads=[yv], writes=[ysq])
                    pm = ps_p.next()
                    cx.op('pe', lambda e: nc.tensor.matmul(pm.t[:], ones_f.t[:], ysq.t[:], start=True, stop=True),
                          reads=[ones_f, ysq], writes=[pm])
                    cx.op('act', lambda e: nc.scalar.activation(out=sd2.t[:], in_=pm.t[:], func=AF.Sqrt,
                                                                bias=eps_t.t[:, 0:1], scale=1.0 / 128),
                          reads=[pm, eps_t], writes=[sd2])
                    cx.op('dve', lambda e: nc.vector.reciprocal(sd2.t[:], sd2.t[:]), reads=[sd2], writes=[sd2])
                    o = obr.next()
                    cx.op('dve', lambda e: nc.vector.scalar_tensor_tensor(
                        out=o.t[:], in0=yv.t[:], scalar=pr.t[:, 6 + h:7 + h], in1=sd2.t[:], op0=ALU.mult, op1=ALU.mult),
                        reads=[yv, pr, sd2], writes=[o])
                    cx.dma('sp', lambda e: e.dma_start(
                        out=yT[h * 128:(h + 1) * 128, t0 + qt * 512:t0 + (qt + 1) * 512], in_=o.t[:]), reads=[o], sembuf=o)

                pending = None
                for ii, (qt, kb) in enumerate(items):
                    pt = emit_st(qt, kb)
                    if pending is not None:
                        emit_pv(*pending)
                    pending = (qt, kb, pt)
                    if nxt is not None and ii % 2 == 1:
                        next(nxt, None)
                emit_pv(*pending)
        for b in range(NSEQ):
            seq_body(b)
        cx.end_stage(blk)


def stage_rwkv(cx, pT, rp, dup, iup, gup, consts, yT):
    nc = cx.nc
    TB = 128
    c1 = -0.6065306597126334
    with contextlib.ExitStack() as es:
        cx.begin_stage()

        def sb(name, shape, dt=F32, dma=False):
            return cx.buf(alloc(es, nc, name, shape, dt), dma=dma)
        cst = sb("cst", [64, 1024], dma=True)
        rpt = sb("rpt", [128, 96], dma=True)
        dupt = sb("dupt", [32, 512], dma=True)
        iupt = sb("iupt", [32, 512], dma=True)
        gupt = sb("gupt", [96, 512], dma=True)
        omka = sb("omka", [64, 8])
        cur3 = sb("cur3", [64, 3, 8, TB], dma=True)
        prv3 = sb("prv3", [64, 3, 8, TB], dma=True)
        loc = sb("loc", [96, 3, TB], dma=True)
        lop = sb("lop", [96, 3, TB], dma=True)
        names = ["sw", "a", "g", "cum", "E1", "E2", "E3", "E4", "kk", "nr", "tmp", "kp", "bv", "aT", "bb", "bT", "bhT", "kT", "khT", "rT"]
        Tt = {n: sb(n, [64, 8, TB]) for n in names}
        cn = ["V", "Bh", "Kh", "N", "L", "AKu", "BRu", "KRu", "P", "Q", "Na", "La", "Nb", "Lb"]
        Cs = [{n: sb(n + str(c), [64, 8, 64]) for n in cn} for c in range(TB // 64)]
        Ct = {n: sb(n, [64, 8, 64]) for n in ["Zs", "Us", "H", "tH", "ysb", "yc", "sq", "sdv"]}
        obr = Ring([sb(f"ob{i}", [64, 8, 64], BF16, dma=True) for i in range(2)])
        pp = Ring([cx.buf(alloc(es, nc, f"pp{i}", [128, 512], F32, psum=True)) for i in range(8)])
        blk = es.enter_context(nc.Block())

        cx.dma('sp', lambda e: e.dma_start(out=cst.t[:], in_=consts[0:64, :]), writes=[cst], sembuf=cst)
        cx.dma('sp', lambda e: e.dma_start(out=rpt.t[:], in_=rp[:, :]), writes=[rpt], sembuf=rpt)
        cx.dma('sp', lambda e: e.dma_start(out=dupt.t[:], in_=dup[:, :]), writes=[dupt], sembuf=dupt)
        cx.dma('sp', lambda e: e.dma_start(out=iupt.t[:], in_=iup[:, :]), writes=[iupt], sembuf=iupt)
        cx.dma('sp', lambda e: e.dma_start(out=gupt.t[:], in_=gup[:, :]), writes=[gupt], sembuf=gupt)
        ones64 = cst.t[:, C_ONES:C_ONES + 64]
        id64 = cst.t[:, C_ID:C_ID + 64]
        eps64 = cst.t[:, C_EPS + 1:C_EPS + 2]

        def mk(c):
            return cst.t[:, c:c + 64].unsqueeze(1).broadcast_to([64, 8, 64])
        SUb, SLb, IUb, IDb = mk(C_SU), mk(C_SL), mk(C_IU), mk(C_ID)
        rmask = cst.t[:, C_RM:C_RM + TB]

        def pb(col):
            return rpt.t[0:64, col:col + 8]

        def bc(ap2, n):
            return ap2.unsqueeze(2).broadcast_to([64, 8, n])

        def TTo(o, oap, ins, op, eng='dve'):
            bufs = [x[0] for x in ins]
            aps = [x[1] for x in ins]
            cx.op(eng, lambda e: nc.vector.tensor_tensor(out=oap, in0=aps[0], in1=aps[1], op=op),
                  reads=[b for b in bufs if b is not None], writes=[o])

        def ACTo(o, oap, i, iap, func, scale=1.0, bias=None, extra=()):
            def fn(e):
                if bias is None:
                    return nc.scalar.activation(out=oap, in_=iap, func=func, scale=scale)
                return nc.scalar.activation(out=oap, in_=iap, func=func, scale=scale, bias=bias)
            cx.op('act', fn, reads=[i] + list(extra), writes=[o])

        for bt in (loc, lop):
            cx.op('dve', lambda e, bt=bt: nc.vector.memset(bt.t[:], 0.0), writes=[bt])
        cx.op('dve', lambda e: nc.vector.tensor_scalar(out=omka.t[:], in0=pb(48), scalar1=-1.0, scalar2=1.0,
                                                       op0=ALU.mult, op1=ALU.add), reads=[rpt], writes=[omka])

        def rows3(t_lo, t_hi):
            return [pT[B0 + kd * 512:B0 + (kd + 1) * 512, t_lo:t_hi].rearrange("(h f) t -> f h t", f=64) for kd in range(3)]

        def block_body(b, k):
            t0 = b * S + k * TB
            T = Tt
            for kd, src in enumerate(rows3(t0, t0 + TB)):
                cx.dma('sp', lambda e, kd=kd, src=src: e.dma_start(out=cur3.t[:, kd], in_=src), writes=[cur3], sembuf=cur3)
            cx.dma('sp', lambda e: e.dma_start(out=loc.t[0:32, 0, :], in_=pT[B0 + 1536:B0 + 1568, t0:t0 + TB]), writes=[loc], sembuf=loc)
            cx.dma('sp', lambda e: e.dma_start(out=loc.t[0:32, 1, :], in_=pT[B0 + 1568:B0 + 1600, t0:t0 + TB]), writes=[loc], sembuf=loc)
            cx.dma('sp', lambda e: e.dma_start(out=loc.t[0:96, 2, :], in_=pT[B0 + 1600:B0 + 1696, t0:t0 + TB]), writes=[loc], sembuf=loc)
            if k == 0:
                cx.op('dve', lambda e: nc.vector.memset(prv3.t[:, :, :, 0:1], 0.0), writes=[prv3])
                cx.op('dve', lambda e: nc.vector.memset(lop.t[:, :, 0:1], 0.0), writes=[lop])
                cx.op('dve', lambda e: nc.vector.memset(Ct["H"].t[:], 0.0), writes=[Ct["H"]])
                for kd, src in enumerate(rows3(t0, t0 + TB - 1)):
                    cx.dma('sp', lambda e, kd=kd, src=src: e.dma_start(out=prv3.t[:, kd, :, 1:TB], in_=src), writes=[prv3], sembuf=prv3)
                o1, lo_, hi_ = 1, t0, t0 + TB - 1
            else:
                for kd, src in enumerate(rows3(t0 - 1, t0 + TB - 1)):
                    cx.dma('sp', lambda e, kd=kd, src=src: e.dma_start(out=prv3.t[:, kd], in_=src), writes=[prv3], sembuf=prv3)
                o1, lo_, hi_ = 0, t0 - 1, t0 + TB - 1
            cx.dma('sp', lambda e: e.dma_start(out=lop.t[0:32, 0, o1:TB], in_=pT[B0 + 1536:B0 + 1568, lo_:hi_]), writes=[lop], sembuf=lop)
            cx.dma('sp', lambda e: e.dma_start(out=lop.t[0:32, 1, o1:TB], in_=pT[B0 + 1568:B0 + 1600, lo_:hi_]), writes=[lop], sembuf=lop)
            cx.dma('sp', lambda e: e.dma_start(out=lop.t[0:96, 2, o1:TB], in_=pT[B0 + 1600:B0 + 1696, lo_:hi_]), writes=[lop], sembuf=lop)
            c3 = cur3.t[:].rearrange("p a h t -> p (a h) t")
            p3 = prv3.t[:].rearrange("p a h t -> p (a h) t")
            mu3 = rpt.t[0:64, 0:24].unsqueeze(2).broadcast_to([64, 24, TB])
            TTo(prv3, p3, [(prv3, p3), (cur3, c3)], ALU.subtract)
            TTo(prv3, p3, [(prv3, p3), (rpt, mu3)], ALU.mult)
            TTo(prv3, p3, [(prv3, p3), (cur3, c3)], ALU.add)
            mul = rpt.t[0:96, 80:83].unsqueeze(2).broadcast_to([96, 3, TB])
            TTo(lop, lop.t[:], [(lop, lop.t[:]), (loc, loc.t[:])], ALU.subtract)
            TTo(lop, lop.t[:], [(lop, lop.t[:]), (rpt, mul)], ALU.mult)
            TTo(lop, lop.t[:], [(lop, lop.t[:]), (loc, loc.t[:])], ALU.add)
            rs_, ks_, vs_ = prv3.t[:, 0], prv3.t[:, 1], prv3.t[:, 2]
            ACTo(lop, lop.t[0:32, 0, :], lop, lop.t[0:32, 0, :], AF.Tanh)
            ACTo(lop, lop.t[0:96, 2, :], lop, lop.t[0:96, 2, :], AF.Sigmoid)
            for (dst, wt, kdim, li, bcol) in ((T["sw"], dupt, 32, 0, 24), (T["a"], iupt, 32, 1, 32), (T["g"], gupt, 96, 2, None)):
                for half in range(2):
                    p = pp.next()

                    def fn(e, p=p, wt=wt, kdim=kdim, li=li, half=half):
                        ins = None
                        for hh in range(4):
                            h = half * 4 + hh
                            ins = nc.tensor.matmul(p.t[0:64, hh * TB:(hh + 1) * TB], wt.t[0:kdim, h * 64:(h + 1) * 64],
                                                   lop.t[0:kdim, li, :], start=True, stop=True)
                        return ins
                    cx.op('pe', fn, reads=[wt, lop], writes=[p])
                    dap = dst.t[:, half * 4:(half + 1) * 4, :]
                    pap = p.t[0:64, :].rearrange("p (h t) -> p h t", h=4)
                    if bcol is None:
                        ACTo(dst, dap, p, pap, AF.Copy)
                    else:
                        bb_ = rpt.t[0:64, bcol + half * 4:bcol + half * 4 + 4].unsqueeze(2).broadcast_to([64, 4, TB])
                        TTo(dst, dap, [(p, pap), (rpt, bb_)], ALU.add)
                if bcol is not None:
                    ACTo(dst, dst.t[:], dst, dst.t[:], AF.Sigmoid)
            for h in range(8):
                cx.op('dve', lambda e, h=h: nc.vector.tensor_tensor_scan(
                    out=T["cum"].t[:, h, :], data0=rmask, data1=T["sw"].t[:, h, :], initial=0.0, op0=ALU.mult, op1=ALU.add),
                    reads=[cst, T["sw"]], writes=[T["cum"]])
            ACTo(T["E1"], T["E1"].t[:], T["cum"], T["cum"].t[:], AF.Exp, scale=c1)
            ACTo(T["E2"], T["E2"].t[:], T["cum"], T["cum"].t[:], AF.Exp, scale=-c1)
            TTo(T["E3"], T["E3"].t[:], [(T["cum"], T["cum"].t[:]), (T["sw"], T["sw"].t[:])], ALU.subtract)
            ACTo(T["E3"], T["E3"].t[:], T["E3"], T["E3"].t[:], AF.Exp, scale=c1)
            cum4 = T["cum"].t[:].rearrange("p h (c t) -> p h c t", t=64)
            cumC = cum4[:, :, :, 63:64].broadcast_to([64, 8, TB // 64, 64])
            e44 = T["E4"].t[:].rearrange("p h (c t) -> p h c t", t=64)
            TTo(T["E4"], e44, [(T["cum"], cumC), (T["cum"], cum4)], ALU.subtract)
            ACTo(T["E4"], T["E4"].t[:], T["E4"], T["E4"].t[:], AF.Exp, scale=c1)
            TTo(T["kk"], T["kk"].t[:], [(prv3, ks_), (rpt, bc(pb(40), TB))], ALU.mult)
            ACTo(T["nr"], T["nr"].t[:], T["kk"], T["kk"].t[:], AF.Square)
            for half in range(2):
                p = pp.next()
                hs = slice(half * 4, (half + 1) * 4)
                cx.op('pe', lambda e, p=p, hs=hs: nc.tensor.matmul(p.t[0:64, :], ones64, T["nr"].t[:, hs, :], start=True, stop=True),
                      reads=[cst, T["nr"]], writes=[p])
                ACTo(T["tmp"], T["tmp"].t[:, hs, :], p, p.t[0:64, :].rearrange("p (h t) -> p h t", h=4), AF.Sqrt)
            cx.op('dve', lambda e: nc.vector.tensor_scalar(out=T["tmp"].t[:], in0=T["tmp"].t[:], scalar1=1e-12, scalar2=None, op0=ALU.max),
                  reads=[T["tmp"]], writes=[T["tmp"]])
            cx.op('dve', lambda e: nc.vector.reciprocal(T["tmp"].t[:], T["tmp"].t[:]), reads=[T["tmp"]], writes=[T["tmp"]])
            TTo(T["kk"], T["kk"].t[:], [(T["kk"], T["kk"].t[:]), (T["tmp"], T["tmp"].t[:])], ALU.mult)
            TTo(T["tmp"], T["tmp"].t[:], [(T["a"], T["a"].t[:]), (rpt, bc(pb(48), TB))], ALU.mult)
            TTo(T["tmp"], T["tmp"].t[:], [(T["tmp"], T["tmp"].t[:]), (omka, bc(omka.t[:], TB))], ALU.add)
            TTo(T["kp"], T["kp"].t[:], [(prv3, ks_), (T["tmp"], T["tmp"].t[:])], ALU.mult)
            TTo(T["tmp"], T["tmp"].t[:], [(prv3, rs_), (T["kp"], T["kp"].t[:])], ALU.mult)
            TTo(T["tmp"], T["tmp"].t[:], [(T["tmp"], T["tmp"].t[:]), (rpt, bc(pb(56), TB))], ALU.mult)
            for half in range(2):
                p = pp.next()
                hs = slice(half * 4, (half + 1) * 4)
                cx.op('pe', lambda e, p=p, hs=hs: nc.tensor.matmul(p.t[0:64, :], ones64, T["tmp"].t[:, hs, :], start=True, stop=True),
                      reads=[cst, T["tmp"]], writes=[p])
                TTo(T["bv"], T["bv"].t[:, hs, :], [(p, p.t[0:64, :].rearrange("p (h t) -> p h t", h=4)), (prv3, prv3.t[:, 2, hs, :])], ALU.mult)
            cx.op('dve', lambda e: nc.vector.scalar_tensor_tensor(out=T["aT"].t[:], in0=T["kk"].t[:], scalar=-1.0, in1=T["E3"].t[:],
                                                                  op0=ALU.mult, op1=ALU.mult), reads=[T["kk"], T["E3"]], writes=[T["aT"]])
            TTo(T["bb"], T["bb"].t[:], [(T["kk"], T["kk"].t[:]), (T["a"], T["a"].t[:])], ALU.mult)
            TTo(T["bT"], T["bT"].t[:], [(T["bb"], T["bb"].t[:]), (T["E2"], T["E2"].t[:])], ALU.mult)
            TTo(T["bhT"], T["bhT"].t[:], [(T["bb"], T["bb"].t[:]), (T["E4"], T["E4"].t[:])], ALU.mult)
            TTo(T["kT"], T["kT"].t[:], [(T["kp"], T["kp"].t[:]), (T["E2"], T["E2"].t[:])], ALU.mult)
            TTo(T["khT"], T["khT"].t[:], [(T["kp"], T["kp"].t[:]), (T["E4"], T["E4"].t[:])], ALU.mult)
            TTo(T["rT"], T["rT"].t[:], [(prv3, rs_), (T["E1"], T["E1"].t[:])], ALU.mult)
            run_chunks(b, k)

        def pgroup(builder, reads):
            p = pp.next()

            def fn(e, p=p):
                ins = None
                for h in range(8):
                    ins = builder(p.t[0:64, h * 64:(h + 1) * 64], h)
                return ins
            cx.op('pe', fn, reads=reads, writes=[p])
            return p, p.t[0:64, :].rearrange("p (h t) -> p h t", h=8)

        def phase_a(b, k, c):
            T, C = Tt, Cs[c]
            cs_ = slice(c * 64, (c + 1) * 64)
            for dst, src_b, src_ap in ((C["V"], prv3, prv3.t[:, 2]), (C["Bh"], T["bhT"], T["bhT"].t[:]), (C["Kh"], T["khT"], T["khT"].t[:])):
                p, pv = pgroup(lambda o, h, src_ap=src_ap: nc.tensor.transpose(o, src_ap[:, h, cs_], id64), [src_b, cst])
                ACTo(dst, dst.t[:], p, pv, AF.Copy)
            yield

            def pw(dst, lt, rt, mb):
                p, pv = pgroup(lambda o, h: nc.tensor.matmul(o, lt.t[:, h, cs_], rt.t[:, h, cs_], start=True, stop=True), [lt, rt])
                TTo(dst, dst.t[:], [(p, pv), (cst, mb)], ALU.mult)
            pw(C["N"], T["bT"], T["aT"], SUb)
            pw(C["L"], T["aT"], T["bT"], SLb)
            yield
            pw(C["AKu"], T["kT"], T["aT"], SUb)
            pw(C["BRu"], T["bT"], T["rT"], IUb)
            pw(C["KRu"], T["kT"], T["rT"], IUb)
            TTo(C["P"], C["P"].t[:], [(C["N"], C["N"].t[:]), (cst, IDb)], ALU.add)
            TTo(C["Q"], C["Q"].t[:], [(C["L"], C["L"].t[:]), (cst, IDb)], ALU.add)
            yield
            Nc, Lc = C["N"], C["L"]
            nxt = [(C["Na"], C["La"]), (C["Nb"], C["Lb"])]
            for lvl in range(5):
                Nn, Ln = nxt[lvl % 2]
                p, pv = pgroup(lambda o, h, Nc=Nc, Lc=Lc: nc.tensor.matmul(o, Lc.t[:, h, :], Nc.t[:, h, :], start=True, stop=True), [Nc, Lc])
                if lvl < 4:
                    p2, pv2 = pgroup(lambda o, h, Nc=Nc, Lc=Lc: nc.tensor.matmul(o, Nc.t[:, h, :], Lc.t[:, h, :], start=True, stop=True), [Nc, Lc])
                ACTo(Nn, Nn.t[:], p, pv, AF.Copy)
                if lvl < 4:
                    cx.op('dve', lambda e, Ln=Ln, pv2=pv2: nc.vector.tensor_copy(Ln.t[:], pv2), reads=[p2], writes=[Ln])
                yield
                p, pv = pgroup(lambda o, h, Nn=Nn: nc.tensor.matmul(o, C["Q"].t[:, h, :], Nn.t[:, h, :], start=True, stop=True), [C["Q"], Nn])
                if lvl < 4:
                    p2, pv2 = pgroup(lambda o, h, Ln=Ln: nc.tensor.matmul(o, C["P"].t[:, h, :], Ln.t[:, h, :], start=True, stop=True), [C["P"], Ln])
                TTo(C["P"], C["P"].t[:], [(C["P"], C["P"].t[:]), (p, pv)], ALU.add)
                if lvl < 4:
                    TTo(C["Q"], C["Q"].t[:], [(C["Q"], C["Q"].t[:]), (p2, pv2)], ALU.add)
                Nc, Lc = Nn, Ln
                yield

        def phase_b(b, k, c):
            T, C = Tt, Cs[c]
            cs_ = slice(c * 64, (c + 1) * 64)
            tc0 = b * S + k * TB + c * 64
            H = Ct["H"]

            def zb(o, h):
                nc.tensor.matmul(o, T["aT"].t[:, h, cs_], H.t[:, h, :], start=True, stop=False)
                return nc.tensor.matmul(o, C["AKu"].t[:, h, :], C["V"].t[:, h, :], start=False, stop=True)
            p, pv = pgroup(zb, [T["aT"], H, C["AKu"], C["V"]])
            ACTo(Ct["Zs"], Ct["Zs"].t[:], p, pv, AF.Copy)
            p, pv = pgroup(lambda o, h: nc.tensor.matmul(o, C["P"].t[:, h, :], Ct["Zs"].t[:, h, :], start=True, stop=True), [C["P"], Ct["Zs"]])
            cx.op('dve', lambda e, pv=pv: nc.vector.tensor_copy(Ct["Us"].t[:], pv), reads=[p], writes=[Ct["Us"]])

            def yb_(o, h):
                nc.tensor.matmul(o, H.t[:, h, :], T["rT"].t[:, h, cs_], start=True, stop=False)
                nc.tensor.matmul(o, Ct["Us"].t[:, h, :], C["BRu"].t[:, h, :], start=False, stop=False)
                return nc.tensor.matmul(o, C["V"].t[:, h, :], C["KRu"].t[:, h, :], start=False, stop=True)

            def hb_(o, h):
                nc.tensor.matmul(o, C["Bh"].t[:, h, :], Ct["Us"].t[:, h, :], start=True, stop=False)
                return nc.tensor.matmul(o, C["Kh"].t[:, h, :], C["V"].t[:, h, :], start=False, stop=True)
            ph, phv = pgroup(hb_, [C["Bh"], Ct["Us"], C["Kh"], C["V"]])
            py, pyv = pgroup(yb_, [H, T["rT"], Ct["Us"], C["BRu"], C["V"], C["KRu"]])
            gC = T["E1"].t[:, :, c * 64 + 63:c * 64 + 64].broadcast_to([64, 8, 64])
            TTo(Ct["tH"], Ct["tH"].t[:], [(H, H.t[:]), (T["E1"], gC)], ALU.mult)
            TTo(H, H.t[:], [(Ct["tH"], Ct["tH"].t[:]), (ph, phv)], ALU.add)
            Cc = Ct
            ACTo(Cc["ysb"], Cc["ysb"].t[:], py, pyv, AF.Copy)
            ysf = Cc["ysb"].t[:].rearrange("p h t -> p (h t)")
            pm = pp.next()
            cx.op('pe', lambda e, pm=pm: nc.tensor.matmul(pm.t[0:64, :], ones64, ysf, start=True, stop=True), reads=[cst, Cc["ysb"]], writes=[pm])
            cx.op('dve', lambda e, pm=pm: nc.vector.scalar_tensor_tensor(
                out=Cc["yc"].t[:].rearrange("p h t -> p (h t)"), in0=pm.t[0:64, :], scalar=-1.0 / 64, in1=ysf, op0=ALU.mult, op1=ALU.add),
                reads=[pm, Cc["ysb"]], writes=[Cc["yc"]])
            ACTo(Cc["sq"], Cc["sq"].t[:], Cc["yc"], Cc["yc"].t[:], AF.Square)
            pv_ = pp.next()
            cx.op('pe', lambda e, pv_=pv_: nc.tensor.matmul(pv_.t[0:64, :], ones64, Cc["sq"].t[:].rearrange("p h t -> p (h t)"), start=True, stop=True),
                  reads=[cst, Cc["sq"]], writes=[pv_])
            ACTo(Cc["sdv"], Cc["sdv"].t[:].rearrange("p h t -> p (h t)"), pv_, pv_.t[0:64, :], AF.Sqrt, scale=1.0 / 64, bias=eps64, extra=[cst])
            cx.op('dve', lambda e: nc.vector.reciprocal(Cc["sdv"].t[:], Cc["sdv"].t[:]), reads=[Cc["sdv"]], writes=[Cc["sdv"]])
            yc = Cc["yc"]
            TTo(yc, yc.t[:], [(yc, yc.t[:]), (Cc["sdv"], Cc["sdv"].t[:])], ALU.mult)
            TTo(yc, yc.t[:], [(yc, yc.t[:]), (rpt, bc(pb(64), 64))], ALU.mult)
            TTo(yc, yc.t[:], [(yc, yc.t[:]), (rpt, bc(pb(72), 64))], ALU.add)
            TTo(yc, yc.t[:], [(yc, yc.t[:]), (T["bv"], T["bv"].t[:, :, cs_])], ALU.add)
            o = obr.next()
            TTo(o, o.t[:], [(yc, yc.t[:]), (T["g"], T["g"].t[:, :, cs_])], ALU.mult)
            cx.dma('sp', lambda e, o=o: e.dma_start(
                out=yT[1024:1536, tc0:tc0 + 64].rearrange("(h f) t -> f h t", f=64), in_=o.t[:]), reads=[o], sembuf=o)

        def run_chunks(b, k):
            gens = [phase_a(b, k, c) for c in range(TB // 64)]
            while gens:
                for g_ in list(gens):
                    try:
                        next(g_)
                    except StopIteration:
                        gens.remove(g_)
            for c in range(TB // 64):
                phase_b(b, k, c)

        for b in range(NSEQ):
            for k in range(S // TB):
                block_body(b, k)
        cx.end_stage(blk)


def stage_rope(cx, pos, invf, cs):
    nc = cx.nc
    PI = float(np.pi)
    with contextlib.ExitStack() as es:
        cx.begin_stage()
        pi_ = cx.buf(alloc(es, nc, "pi_", [32, S], I32), dma=True)
        fr = cx.buf(alloc(es, nc, "fr", [32, 1], F32), dma=True)
        ang = cx.buf(alloc(es, nc, "ang", [32, S], F32))
        tf = cx.buf(alloc(es, nc, "tf", [32, S], F32))
        ki = cx.buf(alloc(es, nc, "ki", [32, S], I32))
        r = cx.buf(alloc(es, nc, "r", [32, S], F32))
        m = cx.buf(alloc(es, nc, "m", [32, S], F32))
        outs = [cx.buf(alloc(es, nc, f"o{i}", [32, S], F32), dma=True) for i in range(3)]
        blk = es.enter_context(nc.Block())
        cx.dma('sp', lambda e: e.dma_start(out=fr.t[:], in_=invf[:, :]), writes=[fr], sembuf=fr)

        def wrap(buf):
            cx.op('dve', lambda e: nc.vector.tensor_scalar(out=m.t[:], in0=buf.t[:], scalar1=PI, scalar2=-2 * PI, op0=ALU.is_gt, op1=ALU.mult),
                  reads=[buf], writes=[m])
            cx.op('dve', lambda e: nc.vector.tensor_tensor(out=buf.t[:], in0=buf.t[:], in1=m.t[:], op=ALU.add), reads=[m], writes=[buf])
            cx.op('dve', lambda e: nc.vector.tensor_scalar(out=m.t[:], in0=buf.t[:], scalar1=-PI, scalar2=2 * PI, op0=ALU.is_lt, op1=ALU.mult),
                  reads=[buf], writes=[m])
            cx.op('dve', lambda e: nc.vector.tensor_tensor(out=buf.t[:], in0=buf.t[:], in1=m.t[:], op=ALU.add), reads=[m], writes=[buf])

        def body(b):
            cx.dma('sp', lambda e: e.dma_start(out=pi_.t[:], in_=pos[b:b + 1, :].broadcast_to([32, S])), writes=[pi_], sembuf=pi_)
            cx.op('dve', lambda e: nc.vector.tensor_copy(ang.t[:], pi_.t[:]), reads=[pi_], writes=[ang])
            cx.op('dve', lambda e: nc.vector.tensor_scalar(out=ang.t[:], in0=ang.t[:], scalar1=fr.t[:, 0:1], scalar2=None, op0=ALU.mult),
                  reads=[fr], writes=[ang])
            cx.op('dve', lambda e: nc.vector.tensor_scalar(out=tf.t[:], in0=ang.t[:], scalar1=1.0 / (2 * PI), scalar2=None, op0=ALU.mult),
                  reads=[ang], writes=[tf])
            cx.op('dve', lambda e: nc.vector.tensor_copy(ki.t[:], tf.t[:]), reads=[tf], writes=[ki])
            cx.op('dve', lambda e: nc.vector.tensor_copy(tf.t[:], ki.t[:]), reads=[ki], writes=[tf])
            cx.op('dve', lambda e: nc.vector.scalar_tensor_tensor(out=r.t[:], in0=tf.t[:], scalar=-2 * PI, in1=ang.t[:], op0=ALU.mult, op1=ALU.add),
                  reads=[tf, ang], writes=[r])
            wrap(r)
            cx.op('act', lambda e: nc.scalar.activation(out=outs[1].t[:], in_=r.t[:], func=AF.Sin), reads=[r], writes=[outs[1]])
            cx.op('act', lambda e: nc.scalar.activation(out=outs[2].t[:], in_=r.t[:], func=AF.Sin, scale=-1.0), reads=[r], writes=[outs[2]])
            cx.op('dve', lambda e: nc.vector.tensor_scalar(out=r.t[:], in0=r.t[:], scalar1=PI / 2, scalar2=None, op0=ALU.add),
                  reads=[outs[1], outs[2]], writes=[r])
            wrap(r)
            cx.op('act', lambda e: nc.scalar.activation(out=outs[0].t[:], in_=r.t[:], func=AF.Sin), reads=[r], writes=[outs[0]])
            for (src, j, r0) in ((outs[0], 0, 0), (outs[0], 0, 32), (outs[2], 1, 0), (outs[1], 1, 32)):
                cx.dma('sp', lambda e, src=src, j=j, r0=r0: e.dma_start(out=cs[b, j, r0:r0 + 32, :], in_=src.t[:]), reads=[src], sembuf=src)
        for b in range(NSEQ):
            body(b)
        cx.end_stage(blk)


NPAD = 4224
BIGW = ["ffn1_gate", "ffn1_up", "ffn1_down", "ffn2_gate", "ffn2_up", "ffn2_down", "w_out", "w_ukv",
        "decay_up", "iclr_up", "gate_up"]
BIGW_SHAPES = {"ffn1_gate": [D, DFF], "ffn1_up": [D, DFF], "ffn1_down": [DFF, D], "ffn2_gate": [D, DFF],
               "ffn2_up": [D, DFF], "ffn2_down": [DFF, D], "w_out": [D, D], "w_ukv": [256, 2048],
               "decay_up": [32, 512], "iclr_up": [32, 512], "gate_up": [96, 512]}


def build_program(nlayers=DEPTH, dbg=False):
    nc = bass.Bass("TRN2", target_bir_lowering=False)
    dt = lambda n, s, d=F32, kind="ExternalInput": nc.dram_tensor(n, s, d, kind=kind).ap()
    x = dt("x", [NT, D])
    pos = dt("pos", [NSEQ, S], I32)
    W = {n: dt(n, [DEPTH] + BIGW_SHAPES[n]) for n in BIGW}
    w_inx = dt("w_inx", [DEPTH, D, NPAD])
    wq = dt("wq", [DEPTH, 512, 2048])
    spk = dt("spk", [DEPTH, 128, 192])
    gfin = dt("gfin", [128, KC])
    consts = dt("consts", [128, 1024])
    maskd = dt("maskd", [128, 4, 512])
    invf = dt("invf", [32, 1])
    out = dt("out", [NT, D], kind="ExternalOutput")
    sk = "ExternalOutput" if dbg else "Internal"
    hT = dt("hT", [KC, 128, NT], kind=sk)
    pT = dt("pT", [NPAD, NT], kind=sk)
    yT = dt("yT", [D, NT], BF16, kind=sk)
    cs = dt("cs", [NSEQ, 2, 64, S], kind=sk)
    with contextlib.ExitStack() as es:
        sems = [es.enter_context(nc.semaphore(f"s{i}")) for i in range(96)]
        cx = Ctx(nc, sems)
        stage_rope(cx, pos, invf, cs)
        stage_in(cx, x, hT, consts[:, C_ID:C_ID + 128])
        for l in range(nlayers):
            sp = spk[l]
            stage_ffn(cx, hT, W["ffn1_gate"][l], W["ffn1_up"][l], W["ffn1_down"][l], sp[:, 0:16], consts)
            stage_proj(cx, hT, w_inx[l], sp[:, 16:32], consts, pT, NPAD)
            stage_mla(cx, pT, wq[l], W["w_ukv"][l], sp[:, 48:64], cs, consts, maskd, yT)
            stage_rwkv(cx, pT, sp[:, 64:160], W["decay_up"][l], W["iclr_up"][l], W["gate_up"][l], consts, yT)
            stage_conv(cx, pT, sp[:, 160:176].rearrange("p (a b) -> p a b", a=4), consts, yT)
            stage_wout(cx, hT, yT, W["w_out"][l])
            stage_ffn(cx, hT, W["ffn2_gate"][l], W["ffn2_up"][l], W["ffn2_down"][l], sp[:, 32:48], consts)
        stage_out(cx, hT, gfin, consts, consts[:, C_ID:C_ID + 128], out)
    return nc


def _fm(v, nch):
    return np.ascontiguousarray(np.asarray(v, np.float32).reshape(nch, 128).T)


def _hd(v):
    return np.ascontiguousarray(np.asarray(v, np.float32).reshape(8, 64).T)


def host_layout(inp):
    f32 = np.float32
    w_in = np.asarray(inp["w_in"], f32)
    w_inx = np.zeros((DEPTH, D, NPAD), f32)
    w_inx[:, :, :NIN] = w_in
    w_inx[:, :, NIN:NIN + 32] = w_in[:, :, 800:832]
    w_inx[:, :, NIN + 32:NIN + 64] = w_in[:, :, 768:800]
    w_uq = np.asarray(inp["w_uq"], f32).reshape(DEPTH, 512, 8, 192)
    wq = np.concatenate([w_uq[..., :128], w_uq[..., 128:192], w_uq[..., 160:192], w_uq[..., 128:160]], axis=-1)
    wq = np.ascontiguousarray(wq.reshape(DEPTH, 512, 2048))
    spk = np.zeros((DEPTH, 128, 192), f32)
    for l in range(DEPTH):
        spk[l, :, 0:16] = _fm(inp["norm_ffn1"][l], 16)
        spk[l, :, 16:32] = _fm(inp["norm_mix"][l], 16)
        spk[l, :, 32:48] = _fm(inp["norm_ffn2"][l], 16)
        spk[l, :, 48:52] = _fm(inp["q_norm"][l], 4)
        spk[l, :, 52:54] = _fm(inp["kv_norm"][l], 2)
        spk[l, :, 54:62] = _fm(inp["attn_out_norm"][l], 8)
        rp = spk[l, :, 64:160]
        mu = np.asarray(inp["shift_mu"][l], f32)
        rp[0:64, 0:8] = _hd(mu[0:512])
        rp[0:64, 8:16] = _hd(mu[512:1024])
        rp[0:64, 16:24] = _hd(mu[1024:1536])
        rp[0:64, 24:32] = _hd(inp["decay_w0"][l])
        rp[0:64, 32:40] = _hd(inp["iclr_a0"][l])
        rp[0:64, 40:48] = _hd(inp["k_k"][l])
        rp[0:64, 48:56] = _hd(inp["k_a"][l])
        rp[0:64, 56:64] = _hd(np.asarray(inp["r_k"][l], f32).reshape(512))
        rp[0:64, 64:72] = _hd(inp["lnx_gain"][l])
        rp[0:64, 72:80] = _hd(inp["lnx_bias"][l])
        rp[0:32, 80] = mu[1536:1568]
        rp[0:32, 81] = mu[1568:1600]
        rp[0:96, 82] = mu[1600:1696]
        cw = np.asarray(inp["conv_w"][l], f32)
        cp = np.zeros((128, 4, 4), f32)
        for k in range(3):
            cp[:, :, k] = cw[k].reshape(4, 128).T
        cp[:, :, 3] = np.asarray(inp["conv_out_norm"][l], f32).reshape(4, 128).T
        spk[l, :, 160:176] = cp.reshape(128, 16)
    consts = np.zeros((128, 1024), f32)
    consts[:, C_ONES:C_ONES + 128] = 1.0
    consts[:, C_EPS] = 1e-6
    consts[:, C_EPS + 1] = 64e-5
    consts[0:64, C_BD64:C_BD64 + 64] = 1.0
    consts[64:128, C_BD64 + 64:C_BD64 + 128] = 1.0
    consts[:, C_ID:C_ID + 128] = np.eye(128, dtype=f32)
    i = np.arange(64)
    consts[0:64, C_SU:C_SU + 64] = (i[:, None] < i[None, :])
    consts[0:64, C_SL:C_SL + 64] = (i[:, None] > i[None, :])
    consts[0:64, C_IU:C_IU + 64] = (i[:, None] <= i[None, :])
    rm = np.ones(128, f32)
    rm[0::64] = 0.0
    consts[:, C_RM:C_RM + 128] = rm[None, :]
    kl = np.arange(128)[:, None]
    ql = np.arange(512)[None, :]
    maskd = np.zeros((128, 4, 512), f32)
    for j in range(4):
        maskd[:, j, :] = np.where(128 * j + kl > ql, -30000.0, 0.0)
    invf = (1.0 / (np.float32(10000.0) ** (np.arange(0, 64, 2, dtype=f32) / np.float32(64)))).astype(f32).reshape(32, 1)
    shared = {n: np.ascontiguousarray(np.asarray(inp[n], f32)) for n in BIGW}
    shared.update({"w_inx": w_inx, "wq": wq, "spk": spk, "gfin": _fm(inp["norm_final"], 16), "consts": consts,
                   "maskd": maskd, "invf": invf})
    return shared


_PROG = {}


def kernel(**inp):
    shared = host_layout(inp)
    x = np.asarray(inp["x"], np.float32)
    pos = np.asarray(inp["positions"], np.int32)
    if "nc" not in _PROG:
        _PROG["nc"] = build_program()
    nc = _PROG["nc"]
    in_maps = []
    for c in range(8):
        m = dict(shared)
        m["x"] = np.ascontiguousarray(x[c * NSEQ:(c + 1) * NSEQ].reshape(NT, D))
        m["pos"] = np.ascontiguousarray(pos[c * NSEQ:(c + 1) * NSEQ])
        in_maps.append(m)
    res = run_bass_kernel_spmd(nc, in_maps, core_ids=list(range(8)))
    out = np.stack([np.asarray(r["out"]).reshape(NSEQ, S, D) for r in res.results], axis=0)
    return out.reshape(16, S, D).astype(np.float32)
```
